# Optimizing a Trainium2 kernel written in Bass

```python
import math
import jax
import jax.numpy as jnp
from jax import lax
import numpy as np

D_MODEL = 1024
BATCH = 8
SEQ = 2048
DEPTH = 4

GRID_W = 64
CTX_LEN = 256
N_MIXERS = 3
D_FF = 4 * D_MODEL
NORM_EPS = 1e-6
ROPE_BASE = 10000.0

A_HEADS = 8
A_DQK = D_MODEL // (2 * A_HEADS)
A_DV = D_MODEL // A_HEADS
A_CHUNK = 128
A_IN = 2 * A_HEADS * A_DQK + A_HEADS * A_DV + D_MODEL + 4 * A_HEADS

SWA_HEADS = 16
SWA_KV_HEADS = 4
SWA_DH = D_MODEL // SWA_HEADS
SWA_GROUP = SWA_HEADS // SWA_KV_HEADS
WINDOW = 128
SWA_BLOCK = WINDOW
SWA_IN = (SWA_HEADS + 2 * SWA_KV_HEADS) * SWA_DH

DIFF_HEADS = 8
DIFF_DH = D_MODEL // (2 * DIFF_HEADS)
DIFF_DV = 2 * DIFF_DH
DIFF_BLOCK = 128
DIFF_IN = 4 * DIFF_HEADS * DIFF_DH + DIFF_HEADS * DIFF_DV

N_MLSTM = (DEPTH + N_MIXERS - 1) // N_MIXERS
N_SWA = (DEPTH + N_MIXERS - 2) // N_MIXERS
N_DIFF = (DEPTH + N_MIXERS - 3) // N_MIXERS

kernel_name = "hybrid_mlstm_swa_diffattn_prefix_trunk"

F32 = jnp.float32


def _rms_norm(x, g):
    xf = x.astype(F32)
    y = xf * lax.rsqrt(jnp.mean(xf * xf, axis=-1, keepdims=True) + NORM_EPS)
    return (y * g.astype(F32)).astype(x.dtype)


def _modulate(x, shift, scale):
    return x * (1.0 + scale) + shift


def _sq_relu_mlp(u, w1, w2):
    return jnp.square(jax.nn.relu(u @ w1)) @ w2


def _axial_rope_tables(n_tok, head_dim):
    rows = n_tok // GRID_W
    row = jnp.repeat(jnp.arange(rows, dtype=jnp.int32), GRID_W).astype(F32)
    col = jnp.tile(jnp.arange(GRID_W, dtype=jnp.int32), rows).astype(F32)
    quarter = head_dim // 4
    inv = ROPE_BASE ** (-jnp.arange(quarter, dtype=F32) / quarter)
    ang = jnp.concatenate([row[:, None] * inv, col[:, None] * inv], axis=-1)
    return jnp.cos(ang), jnp.sin(ang)


def _apply_rope(x, cos, sin):
    half = x.shape[-1] // 2
    x1, x2 = x[..., :half], x[..., half:]
    cs = cos[:, None, :].astype(x.dtype)
    sn = sin[:, None, :].astype(x.dtype)
    return jnp.concatenate([x1 * cs - x2 * sn, x1 * sn + x2 * cs], axis=-1)


def _to_chunks(t):
    bsz, n, h = t.shape[:3]
    t = t.reshape((bsz, n // A_CHUNK, A_CHUNK, h) + t.shape[3:])
    return jnp.moveaxis(t, 3, 1)


def _from_chunks(t):
    t = jnp.moveaxis(t, 1, 3)
    return t.reshape((t.shape[0], t.shape[1] * t.shape[2]) + t.shape[3:])


def _mlstm_direction(q, k, v, log_i, log_f, state, need_out):
    k_c, v_c = _to_chunks(k), _to_chunks(v)
    li, lf = _to_chunks(log_i), _to_chunks(log_f)
    b = jnp.cumsum(lf, axis=-1)
    b_last = b[..., -1]
    a = b_last[..., None] - b + li
    g = jnp.max(a, axis=-1)
    w = jnp.exp(a - g[..., None])
    kv = jnp.einsum("bhcs,bhcsd,bhcse->bhcde", w, k_c, v_c)
    kn = jnp.einsum("bhcs,bhcsd->bhcd", w, k_c)

    def step(carry, xs):
        c_st, n_st, m_st = carry
        bl, gj, kvj, knj = xs
        m_new = jnp.maximum(bl + m_st, gj)
        decay = jnp.exp(bl + m_st - m_new)
        inject = jnp.exp(gj - m_new)
        c_new = decay[..., None, None] * c_st + inject[..., None, None] * kvj
        n_new = decay[..., None] * n_st + inject[..., None] * knj
        return (c_new, n_new, m_new), (c_st, n_st, m_st)

    xs = (jnp.moveaxis(b_last, 2, 0), jnp.moveaxis(g, 2, 0), jnp.moveaxis(kv, 2, 0), jnp.moveaxis(kn, 2, 0))
    final, starts = lax.scan(step, state, xs)
    if not need_out:
        return None, final
    c0 = jnp.moveaxis(starts[0], 0, 2)
    n0 = jnp.moveaxis(starts[1], 0, 2)
    m0 = jnp.moveaxis(starts[2], 0, 2)
    q_c = _to_chunks(q)
    L = q_c.shape[-2]
    d = b[..., :, None] - b[..., None, :] + li[..., None, :]
    d = jnp.where(jnp.tril(jnp.ones((L, L), dtype=bool)), d, -jnp.inf)
    inter = b + m0[..., None]
    m_t = jnp.maximum(jnp.max(d, axis=-1), inter)
    wq = jnp.exp(d - m_t[..., None]) * jnp.einsum("bhctd,bhcsd->bhcts", q_c, k_c)
    carry_w = jnp.exp(inter - m_t)
    num = jnp.einsum("bhcts,bhcse->bhcte", wq, v_c) + carry_w[..., None] * jnp.einsum("bhctd,bhcde->bhcte", q_c, c0)
    den = jnp.sum(wq, axis=-1) + carry_w * jnp.einsum("bhctd,bhcd->bhct", q_c, n0)
    h = num / jnp.maximum(jnp.abs(den), jnp.exp(-m_t))[..., None]
    return _from_chunks(h), final


def _mlstm_mixer(u_ctx, u_lat, w_in, gate_b, head_norm, w_out, need_ctx):
    qk = A_HEADS * A_DQK
    vd = A_HEADS * A_DV
    splits = [qk, 2 * qk, 2 * qk + vd, 2 * qk + vd + D_MODEL]

    def project(u):
        bsz, n = u.shape[:2]
        q, k, v, o, gates = jnp.split(u @ w_in, splits, axis=-1)
        q = q.astype(F32).reshape(bsz, n, A_HEADS, A_DQK)
        k = k.astype(F32).reshape(bsz, n, A_HEADS, A_DQK) * (A_DQK ** -0.5)
        v = v.astype(F32).reshape(bsz, n, A_HEADS, A_DV)
        gates = gates.astype(F32).reshape(bsz, n, 4, A_HEADS) + gate_b.astype(F32)
        fwd = (gates[:, :, 0], jax.nn.log_sigmoid(gates[:, :, 1]))
        bwd = (gates[:, :, 2], jax.nn.log_sigmoid(gates[:, :, 3]))
        return q, k, v, o, fwd, bwd

    def flip(t):
        return jnp.flip(t, axis=1)

    cq, ck, cv, co, cf, cb = project(u_ctx)
    lq, lk, lv, lo, lf, lb = project(u_lat)
    bsz = u_lat.shape[0]
    zero = (jnp.zeros((bsz, A_HEADS, A_DQK, A_DV), F32), jnp.zeros((bsz, A_HEADS, A_DQK), F32),
            jnp.zeros((bsz, A_HEADS), F32))
    hcf, st_f = _mlstm_direction(cq, ck, cv, cf[0], cf[1], zero, need_ctx)
    hcb, st_b = _mlstm_direction(flip(cq), flip(ck), flip(cv), flip(cb[0]), flip(cb[1]), zero, need_ctx)
    hlf, _ = _mlstm_direction(lq, lk, lv, lf[0], lf[1], st_f, True)
    hlb, _ = _mlstm_direction(flip(lq), flip(lk), flip(lv), flip(lb[0]), flip(lb[1]), st_b, True)

    def finish(hf, hb, o):
        hs = hf + flip(hb)
        hn = hs * lax.rsqrt(jnp.mean(hs * hs, axis=-1, keepdims=True) + NORM_EPS)
        hn = hn * head_norm.astype(F32).reshape(A_HEADS, A_DV)
        bsz_, n = hs.shape[:2]
        return (hn.reshape(bsz_, n, vd).astype(o.dtype) * jax.nn.sigmoid(o)) @ w_out

    y_lat = finish(hlf, hlb, lo)
    y_ctx = finish(hcf, hcb, co) if need_ctx else None
    return y_lat, y_ctx


def _swa_mixer(u_ctx, u_lat, w_in, sink, w_out, cos, sin, need_ctx):
    scale = SWA_DH ** -0.5
    sink_g = sink.astype(F32).reshape(SWA_KV_HEADS, SWA_GROUP)[:, :, None]

    def project(u):
        bsz, n = u.shape[:2]
        q, k, v = jnp.split(u @ w_in, [SWA_HEADS * SWA_DH, (SWA_HEADS + SWA_KV_HEADS) * SWA_DH], axis=-1)
        return (q.reshape(bsz, n, SWA_HEADS, SWA_DH), k.reshape(bsz, n, SWA_KV_HEADS, SWA_DH),
                v.reshape(bsz, n, SWA_KV_HEADS, SWA_DH))

    cq, ck, cv = project(u_ctx)
    lq, lk, lv = project(u_lat)
    lq = _apply_rope(lq, cos, sin)
    lk = _apply_rope(lk, cos, sin)
    bsz, n_tok = u_lat.shape[:2]
    nb = n_tok // SWA_BLOCK
    kp = jnp.pad(lk, ((0, 0), (SWA_BLOCK, SWA_BLOCK), (0, 0), (0, 0)))
    vp = jnp.pad(lv, ((0, 0), (SWA_BLOCK, SWA_BLOCK), (0, 0), (0, 0)))
    q_blocks = jnp.moveaxis(lq.reshape(bsz, nb, SWA_BLOCK, SWA_KV_HEADS, SWA_GROUP, SWA_DH), 1, 0)
    q_idx = jnp.arange(SWA_BLOCK)[:, None]
    k_idx = jnp.arange(3 * SWA_BLOCK)[None, :]
    rel = k_idx - SWA_BLOCK - q_idx

    def block(args):
        j, qb = args
        kb = lax.dynamic_slice_in_dim(kp, j * SWA_BLOCK, 3 * SWA_BLOCK, axis=1)
        vb = lax.dynamic_slice_in_dim(vp, j * SWA_BLOCK, 3 * SWA_BLOCK, axis=1)
        kpos = j * SWA_BLOCK - SWA_BLOCK + k_idx
        valid = (jnp.abs(rel) <= WINDOW) & (kpos >= 0) & (kpos < n_tok)
        s_lat = jnp.einsum("bqhgd,bkhd->bhgqk", qb, kb).astype(F32) * scale
        s_lat = jnp.where(valid, s_lat, -jnp.inf)
        s_ctx = jnp.einsum("bqhgd,bchd->bhgqc", qb, ck).astype(F32) * scale
        m = jnp.maximum(jnp.maximum(jnp.max(s_lat, -1), jnp.max(s_ctx, -1)), sink_g)
        p_lat = jnp.exp(s_lat - m[..., None])
        p_ctx = jnp.exp(s_ctx - m[..., None])
        den = jnp.sum(p_lat, -1) + jnp.sum(p_ctx, -1) + jnp.exp(sink_g - m)
        o = jnp.einsum("bhgqk,bkhd->bqhgd", p_lat, vb) + jnp.einsum("bhgqc,bchd->bqhgd", p_ctx, cv)
        return o / jnp.moveaxis(den, 3, 1)[..., None]

    o_lat = lax.map(block, (jnp.arange(nb), q_blocks))
    o_lat = jnp.moveaxis(o_lat, 0, 1).reshape(bsz, n_tok, SWA_HEADS * SWA_DH)
    y_lat = o_lat.astype(u_lat.dtype) @ w_out
    if not need_ctx:
        return y_lat, None
    n_ctx = u_ctx.shape[1]
    cqg = cq.reshape(bsz, n_ctx, SWA_KV_HEADS, SWA_GROUP, SWA_DH)
    s = jnp.einsum("bqhgd,bchd->bhgqc", cqg, ck).astype(F32) * scale
    m = jnp.maximum(jnp.max(s, -1), sink_g)
    p = jnp.exp(s - m[..., None])
    den = jnp.sum(p, -1) + jnp.exp(sink_g - m)
    o_ctx = jnp.einsum("bhgqc,bchd->bqhgd", p, cv) / jnp.moveaxis(den, 3, 1)[..., None]
    y_ctx = o_ctx.reshape(bsz, n_ctx, SWA_HEADS * SWA_DH).astype(u_ctx.dtype) @ w_out
    return y_lat, y_ctx


def _diff_mixer(u_ctx, u_lat, w_in, lam_q1, lam_k1, lam_q2, lam_k2, head_norm, w_out, cos, sin,
                layer_idx, need_ctx):
    scale = DIFF_DH ** -0.5
    lam_init = 0.8 - 0.6 * math.exp(-0.3 * layer_idx)
    lam = (jnp.exp(jnp.sum(lam_q1.astype(F32) * lam_k1.astype(F32)))
           - jnp.exp(jnp.sum(lam_q2.astype(F32) * lam_k2.astype(F32))) + lam_init)
    qk = 2 * DIFF_HEADS * DIFF_DH

    def project(u, rope):
        bsz, n = u.shape[:2]
        q, k, v = jnp.split(u @ w_in, [qk, 2 * qk], axis=-1)
        q = q.reshape(bsz, n, 2 * DIFF_HEADS, DIFF_DH)
        k = k.reshape(bsz, n, 2 * DIFF_HEADS, DIFF_DH)
        if rope:
            q = _apply_rope(q, cos, sin)
            k = _apply_rope(k, cos, sin)
        return (q.reshape(bsz, n, DIFF_HEADS, 2, DIFF_DH), k.reshape(bsz, n, DIFF_HEADS, 2, DIFF_DH),
                v.reshape(bsz, n, DIFF_HEADS, DIFF_DV))

    def finish(o, dtype):
        od = o[:, :, :, 0] - lam * o[:, :, :, 1]
        od = od * lax.rsqrt(jnp.mean(od * od, axis=-1, keepdims=True) + NORM_EPS)
        od = od * head_norm.astype(F32).reshape(DIFF_HEADS, DIFF_DV) * (1.0 - lam_init)
        bsz, n = od.shape[:2]
        return od.reshape(bsz, n, DIFF_HEADS * DIFF_DV).astype(dtype) @ w_out

    cq, ck, cv = project(u_ctx, False)
    lq, lk, lv = project(u_lat, True)
    bsz, n_tok = u_lat.shape[:2]
    nb = n_tok // DIFF_BLOCK
    q_blocks = jnp.moveaxis(lq.reshape(bsz, nb, DIFF_BLOCK, DIFF_HEADS, 2, DIFF_DH), 1, 0)

    def block(qb):
        s_lat = jnp.einsum("bqhmd,bkhmd->bhmqk", qb, lk).astype(F32) * scale
        s_ctx = jnp.einsum("bqhmd,bchmd->bhmqc", qb, ck).astype(F32) * scale
        mx = jnp.maximum(jnp.max(s_lat, -1), jnp.max(s_ctx, -1))[..., None]
        p_lat = jnp.exp(s_lat - mx)
        p_ctx = jnp.exp(s_ctx - mx)
        den = jnp.sum(p_lat, -1) + jnp.sum(p_ctx, -1)
        o = jnp.einsum("bhmqk,bkhe->bqhme", p_lat, lv) + jnp.einsum("bhmqc,bche->bqhme", p_ctx, cv)
        return o / jnp.moveaxis(den, 3, 1)[..., None]

    o_lat = lax.map(block, q_blocks)
    o_lat = jnp.moveaxis(o_lat, 0, 1).reshape(bsz, n_tok, DIFF_HEADS, 2, DIFF_DV)
    y_lat = finish(o_lat, u_lat.dtype)
    if not need_ctx:
        return y_lat, None
    s = jnp.einsum("bqhmd,bkhmd->bhmqk", cq, ck).astype(F32) * scale
    a = jax.nn.softmax(s, axis=-1)
    o_ctx = jnp.einsum("bhmqk,bkhe->bqhme", a, cv)
    return y_lat, finish(o_ctx, u_ctx.dtype)


def setup_inputs(seed: int = 0) -> dict:
    key = jax.random.key(seed)
    ks = jax.random.split(key, 25)
    D = D_MODEL

    def nrm(k, shape, scale):
        return jax.random.normal(k, shape, F32) * scale

    forget_bias = jnp.linspace(3.0, 6.0, A_HEADS, dtype=F32)
    zeros_h = jnp.zeros((A_HEADS,), F32)
    gate_base = jnp.stack([zeros_h, forget_bias, zeros_h, forget_bias])
    return {
        "x": nrm(ks[0], (BATCH, SEQ, D), 1.0),
        "c": nrm(ks[1], (BATCH, D), 1.0),
        "ctx": nrm(ks[2], (BATCH, CTX_LEN, D), 1.0),
        "c_ctx": nrm(ks[3], (D,), 1.0),
        "ada_w": nrm(ks[4], (DEPTH, D, 6 * D), 0.5 * D ** -0.5),
        "ada_b": nrm(ks[5], (DEPTH, 6 * D), 0.02),
        "norm_mix": 1.0 + nrm(ks[6], (DEPTH, D), 0.02),
        "norm_ffn": 1.0 + nrm(ks[7], (DEPTH, D), 0.02),
        "ffn_w1": nrm(ks[8], (DEPTH, D, D_FF), D ** -0.5),
        "ffn_w2": nrm(ks[9], (DEPTH, D_FF, D), D_FF ** -0.5),
        "mlstm_w_in": nrm(ks[10], (N_MLSTM, D, A_IN), D ** -0.5),
        "mlstm_gate_b": gate_base + nrm(ks[11], (N_MLSTM, 4, A_HEADS), 0.1),
        "mlstm_head_norm": 1.0 + nrm(ks[12], (N_MLSTM, A_HEADS * A_DV), 0.02),
        "mlstm_w_out": nrm(ks[13], (N_MLSTM, A_HEADS * A_DV, D), (A_HEADS * A_DV) ** -0.5),
        "swa_w_in": nrm(ks[14], (N_SWA, D, SWA_IN), D ** -0.5),
        "swa_sink": nrm(ks[15], (N_SWA, SWA_HEADS), 0.5),
        "swa_w_out": nrm(ks[16], (N_SWA, SWA_HEADS * SWA_DH, D), (SWA_HEADS * SWA_DH) ** -0.5),
        "diff_w_in": nrm(ks[17], (N_DIFF, D, DIFF_IN), D ** -0.5),
        "diff_lambda_q1": nrm(ks[18], (N_DIFF, DIFF_DH), 0.1),
        "diff_lambda_k1": nrm(ks[19], (N_DIFF, DIFF_DH), 0.1),
        "diff_lambda_q2": nrm(ks[20], (N_DIFF, DIFF_DH), 0.1),
        "diff_lambda_k2": nrm(ks[21], (N_DIFF, DIFF_DH), 0.1),
        "diff_head_norm": 1.0 + nrm(ks[22], (N_DIFF, DIFF_HEADS * DIFF_DV), 0.02),
        "diff_w_out": nrm(ks[23], (N_DIFF, DIFF_HEADS * DIFF_DV, D), (DIFF_HEADS * DIFF_DV) ** -0.5),
        "final_norm": 1.0 + nrm(ks[24], (D,), 0.02),
    }


def reference(x, c, ctx, c_ctx, ada_w, ada_b, norm_mix, norm_ffn, ffn_w1, ffn_w2,
              mlstm_w_in, mlstm_gate_b, mlstm_head_norm, mlstm_w_out,
              swa_w_in, swa_sink, swa_w_out,
              diff_w_in, diff_lambda_q1, diff_lambda_k1, diff_lambda_q2, diff_lambda_k2,
              diff_head_norm, diff_w_out, final_norm):
    n_tok = x.shape[1]
    cos_s, sin_s = _axial_rope_tables(n_tok, SWA_DH)
    cos_d, sin_d = _axial_rope_tables(n_tok, DIFF_DH)
    cond_lat = jax.nn.silu(c)[:, None, :]
    cond_ctx = jax.nn.silu(c_ctx)[None, None, :]
    h, hc = x, ctx
    for i in range(DEPTH):
        kind, slot = i % N_MIXERS, i // N_MIXERS
        need_ctx = i < DEPTH - 1
        mod_l = jnp.split(cond_lat @ ada_w[i] + ada_b[i], 6, axis=-1)
        mod_c = jnp.split(cond_ctx @ ada_w[i] + ada_b[i], 6, axis=-1)
        u = _modulate(_rms_norm(h, norm_mix[i]), mod_l[0], mod_l[1])
        uc = _modulate(_rms_norm(hc, norm_mix[i]), mod_c[0], mod_c[1])
        if kind == 0:
            y, yc = _mlstm_mixer(uc, u, mlstm_w_in[slot], mlstm_gate_b[slot], mlstm_head_norm[slot],
                                 mlstm_w_out[slot], need_ctx)
        elif kind == 1:
            y, yc = _swa_mixer(uc, u, swa_w_in[slot], swa_sink[slot], swa_w_out[slot], cos_s, sin_s, need_ctx)
        else:
            y, yc = _diff_mixer(uc, u, diff_w_in[slot], diff_lambda_q1[slot], diff_lambda_k1[slot],
                                diff_lambda_q2[slot], diff_lambda_k2[slot], diff_head_norm[slot],
                                diff_w_out[slot], cos_d, sin_d, i, need_ctx)
        h = h + mod_l[2] * y
        h = h + mod_l[5] * _sq_relu_mlp(_modulate(_rms_norm(h, norm_ffn[i]), mod_l[3], mod_l[4]),
                                        ffn_w1[i], ffn_w2[i])
        if need_ctx:
            hc = hc + mod_c[2] * yc
            hc = hc + mod_c[5] * _sq_relu_mlp(_modulate(_rms_norm(hc, norm_ffn[i]), mod_c[3], mod_c[4]),
                                              ffn_w1[i], ffn_w2[i])
    return _rms_norm(h, final_norm)
```

```python
import numpy as np
from contextlib import ExitStack
import concourse.bass as bass
import concourse.mybir as mybir
from concourse.bass_utils import run_bass_kernel_spmd

F32 = mybir.dt.float32
BF16 = mybir.dt.bfloat16
ALU = mybir.AluOpType
AF = mybir.ActivationFunctionType

D = 1024
NCTX = 256
NLAT = 2048
T = NCTX + NLAT
NT = T // 128
DEPTH = 4
EPS = 1e-6
NDS = 24
BLOCKS = [(0, 256, 1), (256, 512, 0), (768, 512, 0), (1280, 512, 0), (1792, 512, 0)]

SL = 64
S_FINAL = 4 * SL
S_MHN = S_FINAL + 8
S_MGB = S_MHN + 16
S_SINK = S_MGB + 64
S_DHN = S_SINK + 16
S_DLAM = S_DHN + 8
NS = S_DLAM + 256
NCONST = 5 * 128
PSORD = [0, 2, 1, 3]
FFN_ITEMS = 1000
SWA_STOP = 9
SWA_SUB = 9
SWA_QB = (0, 1)
SWA_NQ = 2


class Buf:
    __slots__ = ("name", "w", "r")

    def __init__(self, name):
        self.name = name
        self.w = None
        self.r = {}


class KB:
    def __init__(self, nc, es):
        self.nc = nc
        self.es = es
        self.E = {"pe": nc.tensor, "act": nc.scalar, "dve": nc.vector, "pool": nc.gpsimd, "sp": nc.sync}
        self.sem = {e: es.enter_context(nc.semaphore("s_" + e)) for e in self.E}
        self.cnt = {e: 0 for e in self.E}
        self.seen = {e: {} for e in self.E}
        self.dsem = [es.enter_context(nc.semaphore("d%d" % i)) for i in range(NDS)]
        self.dcnt = [0] * NDS
        self.dnext = {"sp": 0, "pool": 0}

    def _wait(self, eng, deps):
        for (sk, val) in deps:
            if sk == eng and eng == "pe":
                continue
            if self.seen[eng].get(sk, 0) >= val:
                continue
            semh = self.sem[sk] if isinstance(sk, str) else self.dsem[sk]
            self.E[eng].wait_ge(semh, val)
            self.seen[eng][sk] = val

    @staticmethod
    def _deps(reads, writes):
        deps = []
        for b in reads:
            if b.w is not None:
                deps.append(b.w)
        for b in writes:
            if b.w is not None:
                deps.append(b.w)
            deps.extend(b.r.values())
        return deps

    def op(self, eng, fn, reads=(), writes=(), inc=True):
        self._wait(eng, self._deps(reads, writes))
        ins = fn(self.E[eng])
        if inc:
            self.cnt[eng] += 1
            ins.then_inc(self.sem[eng], 1)
            tk = (eng, self.cnt[eng])
        else:
            tk = (eng, self.cnt[eng] + 1)
        for b in reads:
            b.r[eng] = tk
        for b in writes:
            b.w = tk
            b.r = {}

    def dma(self, q, out, in_, reads=(), writes=()):
        half = NDS // 2
        i = self.dnext[q] + (0 if q == "sp" else half)
        self.dnext[q] = (self.dnext[q] + 1) % half
        deps = self._deps(reads, writes)
        if self.dcnt[i] > 0:
            deps.append((i, self.dcnt[i]))
        self._wait(q, deps)
        self.E[q].dma_start(out=out, in_=in_).then_inc(self.dsem[i], 16)
        self.dcnt[i] += 16
        tk = (i, self.dcnt[i])
        for b in reads:
            b.r[("d", i)] = tk
        for b in writes:
            b.w = tk
            b.r = {}

    def wait_all_dma(self, eng):
        self._wait(eng, [(i, self.dcnt[i]) for i in range(NDS) if self.dcnt[i] > 0])

    def barrier(self):
        deps = [(e, self.cnt[e]) for e in self.E if e != "sp" and self.cnt[e] > 0]
        deps += [(i, self.dcnt[i]) for i in range(NDS) if self.dcnt[i] > 0]
        self._wait("sp", deps)
        self.cnt["sp"] += 1
        self.E["sp"].sem_inc(self.sem["sp"], 1)
        for e in self.E:
            if e != "sp":
                self._wait(e, [("sp", self.cnt["sp"])])


def build_program(layers=(0, 1, 2, 3), do_mixer=True, do_ffn=True, final_norm=True, in_ctx=True):
    nc = bass.Bass("TRN2", target_bir_lowering=False)

    declared = []
    only = None if len(layers) == DEPTH and do_mixer and do_ffn else set()

    class _Lazy:
        def __init__(self, name, shape):
            self.name, self.shape, self.ap_ = name, list(shape), None

        def get(self):
            if self.ap_ is None:
                self.ap_ = nc.dram_tensor(self.name, self.shape, F32, kind="ExternalInput").ap()
                declared.append(self.name)
            return self.ap_

        def __getitem__(self, k):
            return self.get()[k]

    def din(name, shape):
        return _Lazy(name, shape)

    xin = din("xin", [T, D]).get()
    cv_d = din("cv", [128, 16]).get()
    small_d = din("small", [128, NS]).get()
    cst_d = din("cst", [128, NCONST]).get()
    ropec_l = din("ropec", [128, NLAT])
    ropes_l = din("ropes", [128, NLAT])
    ada_w = din("ada_w", [DEPTH, D, 6 * D])
    w1_d = din("ffn_w1", [DEPTH, D, 4 * D])
    w2_d = din("ffn_w2", [DEPTH, 4 * D, D])
    wm_d = din("wm", [2, D, 3616])
    wmo_d = din("wmo", [2, D, D])
    ws_d = din("ws", [1, D, 3328])
    wso_d = din("wso", [1, D, D])
    wd_d = din("wd", [1, D, 5120])
    wdo_d = din("wdo", [1, D, D])
    out_d = nc.dram_tensor("out", [NLAT, D], F32, kind="ExternalOutput").ap()

    with ExitStack() as es:
        kb = KB(nc, es)
        op = kb.op
        dma = kb.dma
        cnt = [0]

        def sbt(es_, shape, dt, name=None):
            cnt[0] += 1
            return es_.enter_context(nc.sbuf_tensor("%s_%d" % (name or "t", cnt[0]), list(shape), dt))

        hT = sbt(es, [128, 8, T], F32, "hT")
        uT = sbt(es, [128, 8, T], BF16, "uT")
        hb = [Buf("h%d" % i) for i in range(NT)]
        ub = [Buf("u%d" % i) for i in range(len(BLOCKS))]
        cst = sbt(es, [128, NCONST], F32, "cst")
        small = sbt(es, [128, NS], F32, "small")
        cv = sbt(es, [128, 16], F32, "cv")
        cond = sbt(es, [128, 8, 2], BF16, "cond")
        ones_bf = sbt(es, [128, 128], BF16, "ones")
        maskf_bf = sbt(es, [128, 128], BF16, "maskf")
        maskb_bf = sbt(es, [128, 128], BF16, "maskb")
        nhalf = sbt(es, [128, 128], F32, "nhalf")
        modsb = sbt(es, [128, 48, 2], F32, "modsb")
        amix = sbt(es, [128, 8, 2], F32, "amix")
        affn = sbt(es, [128, 8, 2], F32, "affn")
        cbuf = Buf("consts")
        modb = Buf("mod")
        ident = cst[:, 0:128]
        maskf = cst[:, 128:256]
        maskb = cst[:, 256:384]
        ntricf = cst[:, 384:512]
        ntricb = cst[:, 512:640]
        pb = [es.enter_context(nc.psum_tensor("pb%d" % i, [128, 512], F32)) for i in range(8)]
        pbb = [Buf("pb%d" % i) for i in range(8)]

        def hbs(t0, n):
            return [hb[i] for i in range(t0 // 128, (t0 + n) // 128)]

        dma("sp", cst[:], cst_d, writes=[cbuf])
        dma("sp", small[:], small_d, writes=[cbuf])
        dma("sp", cv[:], cv_d, writes=[cbuf])
        op("dve", lambda e: e.memset(ones_bf[:], 1.0), writes=[cbuf])
        op("dve", lambda e: e.memset(nhalf[:], -0.5), writes=[cbuf])
        op("dve", lambda e: e.tensor_copy(out=maskf_bf[:], in_=maskf), reads=[cbuf], writes=[cbuf])
        op("dve", lambda e: e.tensor_copy(out=maskb_bf[:], in_=maskb), reads=[cbuf], writes=[cbuf])
        with ExitStack() as ps:
            tmpc = sbt(ps, [128, 16], F32, "tmpc")
            tb_ = Buf("tmpc")
            op("act", lambda e: e.activation(out=tmpc[:], in_=cv[:], func=AF.Exp, scale=-1.0), reads=[cbuf], writes=[tb_])
            op("dve", lambda e: e.tensor_scalar(out=tmpc[:], in0=tmpc[:], scalar1=1.0, scalar2=None, op0=ALU.add), reads=[tb_], writes=[tb_])
            op("dve", lambda e: e.reciprocal(out=tmpc[:], in_=tmpc[:]), reads=[tb_], writes=[tb_])
            op("dve", lambda e: e.tensor_tensor(out=cond[:].rearrange("p k j -> p (k j)"), in0=tmpc[:], in1=cv[:], op=ALU.mult),
               reads=[tb_, cbuf], writes=[cbuf])
            stg = [sbt(ps, [128, D], F32, "stg") for _ in range(2)]
            stb = [Buf("stg0"), Buf("stg1")]
            for tt in range(NT):
                s = tt % 2
                dma("sp", stg[s][:], xin[tt * 128:(tt + 1) * 128, :], writes=[stb[s]])
                for half in range(2):
                    bk = (tt * 2 + half) % 4
                    for c4 in range(4):
                        c = half * 4 + c4
                        op("pe", lambda e, c=c, c4=c4, bk=bk, s=s: e.transpose(pb[bk][:, c4 * 128:(c4 + 1) * 128], stg[s][:, c * 128:(c + 1) * 128], ident),
                           reads=[stb[s], cbuf], writes=[pbb[bk]], inc=(c4 == 3))
                    eng = "act" if half == 0 else "dve"
                    if eng == "act":
                        op("act", lambda e, bk=bk, half=half, tt=tt: e.activation(out=hT[:, half * 4:half * 4 + 4, tt * 128:(tt + 1) * 128],
                                                                                 in_=pb[bk][:].rearrange("p (c n) -> p c n", c=4), func=AF.Identity),
                           reads=[pbb[bk]], writes=[hb[tt]])
                    else:
                        op("dve", lambda e, bk=bk, half=half, tt=tt: e.tensor_copy(out=hT[:, half * 4:half * 4 + 4, tt * 128:(tt + 1) * 128],
                                                                                   in_=pb[bk][:].rearrange("p (c n) -> p c n", c=4)),
                           reads=[pbb[bk]], writes=[hb[tt]])
            kb.barrier()

        def rstd_from_psum(ps_ap, out_ap, n, dim, pbuf, obuf):
            op("act", lambda e: e.activation(out=out_ap, in_=ps_ap, func=AF.Ln, scale=1.0 / dim, bias=EPS), reads=[pbuf], writes=[obuf])
            op("act", lambda e: e.activation(out=out_ap, in_=out_ap, func=AF.Exp, scale=-0.5), reads=[obuf], writes=[obuf])

        def ada_phase(L):
            with ExitStack() as ps:
                slots = [sbt(ps, [128, 8, 512], BF16, "adaw") for _ in range(3)]
                sbf = [Buf("adaw%d" % i) for i in range(3)]
                awv = ada_w[L].rearrange("(c p) n -> p c n", p=128)
                pm = pb[7][:, 0:96].rearrange("p (m j) -> p m j", j=2)
                for g in range(12):
                    s = g % 3
                    dma("pool", slots[s][:], awv[:, :, g * 512:(g + 1) * 512], writes=[sbf[s]])
                    for jj in range(4):
                        m = g * 4 + jj
                        for k in range(8):
                            op("pe", lambda e, s=s, jj=jj, k=k, m=m: e.matmul(pm[:, m, :], lhsT=slots[s][:, k, jj * 128:(jj + 1) * 128], rhs=cond[:, k, :],
                                                                           start=(k == 0), stop=(k == 7)),
                               reads=[sbf[s], cbuf], writes=[pbb[7]], inc=(k == 7))
                base = L * SL
                op("dve", lambda e: e.tensor_tensor(out=modsb[:], in0=pm, in1=small[:, base + 16:base + 64].unsqueeze(2).broadcast_to([128, 48, 2]), op=ALU.add),
                   reads=[pbb[7], cbuf], writes=[modb])
                op("dve", lambda e: e.scalar_tensor_tensor(out=amix[:], in0=modsb[:, 8:16, :], scalar=1.0,
                                                           in1=small[:, base:base + 8].unsqueeze(2).broadcast_to([128, 8, 2]), op0=ALU.add, op1=ALU.mult),
                   reads=[modb, cbuf], writes=[modb])
                op("dve", lambda e: e.scalar_tensor_tensor(out=affn[:], in0=modsb[:, 32:40, :], scalar=1.0,
                                                           in1=small[:, base + 8:base + 16].unsqueeze(2).broadcast_to([128, 8, 2]), op0=ALU.add, op1=ALU.mult),
                   reads=[modb, cbuf], writes=[modb])
                kb.barrier()

        def norm_mod(a_t, shift_off, blocks):
            with ExitStack() as ps:
                sq = [sbt(ps, [128, 8, 512], BF16, "sq") for _ in range(2)]
                sqb = [Buf("sq0"), Buf("sq1")]
                rs = [sbt(ps, [128, 512], F32, "rs") for _ in range(2)]
                rsb = [Buf("rs0"), Buf("rs1")]
                tmp = [sbt(ps, [128, 4, 512], F32, "nt") for _ in range(2)]
                tmb = [Buf("nt0"), Buf("nt1")]
                for bi, (t0, n, j) in enumerate(blocks):
                    s = bi % 2
                    bk = 6 + s
                    bidx = BLOCKS.index((t0, n, j))
                    op("act", lambda e, s=s: e.activation(out=sq[s][:, :, 0:n], in_=hT[:, :, t0:t0 + n], func=AF.Square), reads=hbs(t0, n), writes=[sqb[s]])
                    for c in range(8):
                        op("pe", lambda e, s=s, c=c, bk=bk: e.matmul(pb[bk][:, 0:n], lhsT=ones_bf[:], rhs=sq[s][:, c, 0:n], start=(c == 0), stop=(c == 7)),
                           reads=[sqb[s], cbuf], writes=[pbb[bk]], inc=(c == 7))
                    rstd_from_psum(pb[bk][:, 0:n], rs[s][:, 0:n], n, float(D), pbb[bk], rsb[s])
                    for half in range(2):
                        op("dve", lambda e, s=s, half=half: e.tensor_tensor(out=tmp[half][:, :, 0:n], in0=hT[:, half * 4:half * 4 + 4, t0:t0 + n],
                                                                            in1=rs[s][:, 0:n].unsqueeze(1).broadcast_to([128, 4, n]), op=ALU.mult),
                           reads=hbs(t0, n) + [rsb[s]], writes=[tmb[half]])
                        for c4 in range(4):
                            c = half * 4 + c4
                            op("act", lambda e, half=half, c4=c4, c=c: e.activation(out=uT[:, c, t0:t0 + n], in_=tmp[half][:, c4, 0:n], func=AF.Identity,
                                                                                    scale=a_t[:, c, j:j + 1], bias=modsb[:, shift_off + c, j:j + 1]),
                               reads=[tmb[half], modb], writes=[ub[bidx]])
                kb.barrier()

        def ffn_phase(L, blocks):
            with ExitStack() as ps:
                w1s = [sbt(ps, [128, 8, 512], BF16, "w1s") for _ in range(3)]
                w2s = [sbt(ps, [128, 4, D], BF16, "w2s") for _ in range(3)]
                wb = [Buf("ffw%d" % i) for i in range(3)]
                hid = [sbt(ps, [128, 4, 512], BF16, "hid") for _ in range(2)]
                hib = [Buf("hid0"), Buf("hid1")]
                sqv = [sbt(ps, [128, 512], F32, "sqv") for _ in range(2)]
                sqvb = [Buf("sqv0"), Buf("sqv1")]
                w1v = w1_d[L].rearrange("(c p) n -> p c n", p=128)
                w2v = w2_d[L].rearrange("(c p) n -> p c n", p=128)
                items = [(e8, blk) for e8 in range(8) for blk in blocks][:FFN_ITEMS]
                loaded = set()

                def load(e8):
                    if e8 in loaded or e8 >= 8:
                        return
                    loaded.add(e8)
                    s = e8 % 3
                    dma("pool", w1s[s][:], w1v[:, :, e8 * 512:(e8 + 1) * 512], writes=[wb[s]])
                    dma("pool", w2s[s][:], w2v[:, e8 * 4:(e8 + 1) * 4, :], writes=[wb[s]])

                sqi = [0]

                def stage_h(i):
                    e8, (t0, n, j) = items[i]
                    s = e8 % 3
                    bidx = BLOCKS.index((t0, n, j))
                    for jj in range(4):
                        for k in range(8):
                            op("pe", lambda e, jj=jj, k=k: e.matmul(pb[jj][:, 0:n], lhsT=w1s[s][:, k, jj * 128:(jj + 1) * 128], rhs=uT[:, k, t0:t0 + n],
                                                                    start=(k == 0), stop=(k == 7)),
                               reads=[wb[s], ub[bidx]], writes=[pbb[jj]], inc=(k == 7))
                        q = sqi[0] % 2
                        sqi[0] += 1
                        op("act", lambda e, jj=jj, q=q: e.activation(out=sqv[q][:, 0:n], in_=pb[jj][:, 0:n], func=AF.Square), reads=[pbb[jj]], writes=[sqvb[q]])
                        op("dve", lambda e, jj=jj, q=q: e.scalar_tensor_tensor(out=hid[i % 2][:, jj, 0:n], in0=pb[jj][:, 0:n], scalar=0.0, in1=sqv[q][:, 0:n],
                                                                               op0=ALU.is_gt, op1=ALU.mult),
                           reads=[pbb[jj], sqvb[q]], writes=[hib[i % 2]])

                oi = [0]

                def stage_o(i):
                    e8, (t0, n, j) = items[i]
                    s = e8 % 3
                    for f in range(8):
                        bk = 4 + oi[0] % 3
                        oi[0] += 1
                        for jj in range(4):
                            op("pe", lambda e, jj=jj, f=f, bk=bk: e.matmul(pb[bk][:, 0:n], lhsT=w2s[s][:, jj, f * 128:(f + 1) * 128], rhs=hid[i % 2][:, jj, 0:n],
                                                                           start=(jj == 0), stop=(jj == 3)),
                               reads=[wb[s], hib[i % 2]], writes=[pbb[bk]], inc=(jj == 3))
                        op("dve", lambda e, f=f, bk=bk: e.scalar_tensor_tensor(out=hT[:, f, t0:t0 + n], in0=pb[bk][:, 0:n], scalar=modsb[:, 40 + f, j:j + 1],
                                                                               in1=hT[:, f, t0:t0 + n], op0=ALU.mult, op1=ALU.add),
                           reads=[pbb[bk], modb] + hbs(t0, n), writes=hbs(t0, n))

                load(0)
                load(1)
                stage_h(0)
                for i in range(len(items)):
                    if i + 1 < len(items):
                        load(items[i + 1][0] + 1)
                        stage_h(i + 1)
                    stage_o(i)
                kb.barrier()

        def outproj_acc(pbank, lhs_list, rhs_list, t0, n, j, reads, stop_bufs=None):
            for f in range(8):
                bk = pbank[f % len(pbank)]
                nmm = len(lhs_list)
                for i in range(nmm):
                    op("pe", lambda e, i=i, f=f, bk=bk: e.matmul(pb[bk][:, 0:n], lhsT=lhs_list[i][:, f * 128:(f + 1) * 128], rhs=rhs_list[i],
                                                                 start=(i == 0), stop=(i == nmm - 1)),
                       reads=reads, writes=[pbb[bk]], inc=(i == nmm - 1))
                op("dve", lambda e, f=f, bk=bk: e.scalar_tensor_tensor(out=hT[:, f, t0:t0 + n], in0=pb[bk][:, 0:n], scalar=modsb[:, 16 + f, j:j + 1],
                                                                       in1=hT[:, f, t0:t0 + n], op0=ALU.mult, op1=ALU.add),
                   reads=[pbb[bk], modb] + hbs(t0, n), writes=hbs(t0, n))

        def mlstm_phase(L, slot, need_ctx):
            out_blocks = BLOCKS if need_ctx else BLOCKS[1:]
            with ExitStack() as ps:
                G = sbt(ps, [128, NT, 32], F32, "G")
                SP = sbt(ps, [128, NT, 16], F32, "SP")
                LI = sbt(ps, [128, NT, 16], F32, "LI")
                KS = sbt(ps, [128, NT, 16], F32, "KS")
                KSb = sbt(ps, [128, NT, 16], BF16, "KSb")
                EBH = sbt(ps, [128, NT, 16], F32, "EBH")
                EBL = sbt(ps, [128, NT, 16], F32, "EBL")
                gbuf = Buf("gates")
                wg = sbt(ps, [128, 8, 32], BF16, "wg")
                wgb = Buf("wg")
                wmv = wm_d[slot].rearrange("(c p) n -> p c n", p=128)
                dma("pool", wg[:], wmv[:, :, 3584:3616], writes=[wgb])
                gbias = small[:, S_MGB + slot * 32:S_MGB + slot * 32 + 32]
                for tt in range(NT):
                    bk = 0 if tt < 16 else 1
                    o = (tt % 16) * 32
                    for k in range(8):
                        op("pe", lambda e, tt=tt, k=k, bk=bk, o=o: e.matmul(pb[bk][:, o:o + 32], lhsT=uT[:, k, tt * 128:(tt + 1) * 128], rhs=wg[:, k, :],
                                                                            start=(k == 0), stop=(k == 7)),
                           reads=[wgb] + ub, writes=[pbb[bk]], inc=(k == 7))
                op("dve", lambda e: e.tensor_tensor(out=G[:, 0:16, :], in0=pb[0][:].rearrange("p (t g) -> p t g", g=32),
                                                    in1=gbias.unsqueeze(1).broadcast_to([128, 16, 32]), op=ALU.add), reads=[pbb[0], cbuf], writes=[gbuf])
                op("dve", lambda e: e.tensor_tensor(out=G[:, 16:18, :], in0=pb[1][:, 0:64].rearrange("p (t g) -> p t g", g=32),
                                                    in1=gbias.unsqueeze(1).broadcast_to([128, 2, 32]), op=ALU.add), reads=[pbb[1], cbuf], writes=[gbuf])
                for d in range(2):
                    op("dve", lambda e, d=d: e.tensor_copy(out=LI[:, :, d * 8:d * 8 + 8], in_=G[:, :, d * 16:d * 16 + 8]), reads=[gbuf], writes=[gbuf])
                    op("act", lambda e, d=d: e.activation(out=SP[:, :, d * 8:d * 8 + 8], in_=G[:, :, d * 16 + 8:d * 16 + 16], func=AF.Exp, scale=-1.0),
                       reads=[gbuf], writes=[gbuf])
                op("act", lambda e: e.activation(out=SP[:], in_=SP[:], func=AF.Ln, scale=1.0, bias=1.0), reads=[gbuf], writes=[gbuf])
                for tt in range(NT):
                    bk = 2 if tt < 16 else 3
                    o = (tt % 16) * 32
                    op("pe", lambda e, tt=tt, bk=bk, o=o: e.matmul(pb[bk][:, o:o + 8], lhsT=ntricf, rhs=SP[:, tt, 0:8], start=True, stop=True),
                       reads=[gbuf, cbuf], writes=[pbb[bk]], inc=False)
                    op("pe", lambda e, tt=tt, bk=bk, o=o: e.matmul(pb[bk][:, o + 8:o + 16], lhsT=ntricb, rhs=SP[:, tt, 8:16], start=True, stop=True),
                       reads=[gbuf, cbuf], writes=[pbb[bk]], inc=False)
                    op("pe", lambda e, tt=tt, bk=bk, o=o: e.matmul(pb[bk][:, o + 16:o + 32], lhsT=nhalf[:], rhs=SP[:, tt, :], start=True, stop=True),
                       reads=[gbuf, cbuf], writes=[pbb[bk]], inc=True)
                for (bk, a, b_) in ((2, 0, 16), (3, 16, 18)):
                    nn = b_ - a
                    pv = pb[bk][:, 0:nn * 32].rearrange("p (t g) -> p t g", g=32)
                    op("dve", lambda e, pv=pv, a=a, b_=b_: e.tensor_tensor(out=KS[:, a:b_, :], in0=LI[:, a:b_, :], in1=pv[:, :, 0:16], op=ALU.subtract),
                       reads=[pbb[bk], gbuf], writes=[gbuf])
                    op("act", lambda e, pv=pv, a=a, b_=b_: e.activation(out=EBH[:, a:b_, :], in_=pv[:, :, 16:32], func=AF.Exp), reads=[pbb[bk]], writes=[gbuf])
                    op("act", lambda e, pv=pv, a=a, b_=b_: e.activation(out=EBL[:, a:b_, :], in_=pv[:, :, 16:32], func=AF.Exp, scale=2.0), reads=[pbb[bk]], writes=[gbuf])
                op("act", lambda e: e.activation(out=KS[:], in_=KS[:], func=AF.Exp), reads=[gbuf], writes=[gbuf])
                op("dve", lambda e: e.tensor_copy(out=KSb[:], in_=KS[:]), reads=[gbuf], writes=[gbuf])

                wh = [sbt(ps, [128, 8, 448], BF16, "wh") for _ in range(2)]
                whb = [Buf("wh0"), Buf("wh1")]
                wo = [sbt(ps, [128, D], BF16, "wo") for _ in range(2)]
                wob = [Buf("wo0"), Buf("wo1")]
                Qb = [sbt(ps, [64, T], BF16, "Qb") for _ in range(2)]
                qbb = [Buf("Qbf"), Buf("Qbb")]
                KT = sbt(ps, [64, T], BF16, "KT")
                ktb = Buf("KT")
                Ktok = sbt(ps, [128, NT, 64], BF16, "Ktok")
                kkb = Buf("Ktok")
                Va = [sbt(ps, [128, NT, 128], BF16, "Va") for _ in range(2)]
                vab = [Buf("Vaf"), Buf("Vab")]
                HS = sbt(ps, [128, T], F32, "HS")
                hsb = [Buf("hs%d" % i) for i in range(NT)]
                Cst = [sbt(ps, [64, 256], F32, "Cst") for _ in range(2)]
                csb = [Buf("Cf"), Buf("Cb")]
                Cbf = [sbt(ps, [64, 256], BF16, "Cbf") for _ in range(2)]
                cbb = [Buf("Cbf_f"), Buf("Cbf_b")]
                ctmp = [sbt(ps, [64, 256], F32, "ctmp") for _ in range(2)]
                ctb = [Buf("ct0"), Buf("ct1")]
                ebt = [sbt(ps, [64, 512], F32, "ebt") for _ in range(2)]
                ebb = [Buf("eb0"), Buf("eb1")]
                Pm = [sbt(ps, [128, 128], BF16, "Pm") for _ in range(4)]
                pmb = [Buf("Pm%d" % i) for i in range(4)]
                adn = [sbt(ps, [128, 128], F32, "adn") for _ in range(2)]
                adb = [Buf("ad0"), Buf("ad1")]
                htm = [sbt(ps, [128, 128], F32, "htm") for _ in range(2)]
                htb = [Buf("ht0"), Buf("ht1")]
                sqh = sbt(ps, [128, 512], BF16, "sqh")
                sqhb = Buf("sqh")
                rsh = sbt(ps, [128, 512], F32, "rsh")
                rshb = Buf("rsh")
                sig = sbt(ps, [128, 512], F32, "sig")
                sigb = Buf("sig")
                hn = sbt(ps, [128, 512], F32, "hn")
                hnb = Buf("hn")
                ao = [sbt(ps, [128, 512], BF16, "ao") for _ in range(2)]
                aob = [Buf("ao0"), Buf("ao1")]
                wov = wmo_d[slot]

                def load_head(h):
                    s = h % 2
                    dma("pool", wh[s][:], wmv[:, :, h * 448:(h + 1) * 448], writes=[whb[s]])
                    dma("pool", wo[s][:], wov[h * 128:(h + 1) * 128, :], writes=[wob[s]])

                order = [list(range(NT)), [1, 0] + list(range(NT - 1, 1, -1))]
                load_head(0)
                for h in range(8):
                    s = h % 2
                    if h + 1 < 8:
                        load_head(h + 1)
                    w = wh[s]
                    for (t0, n, j) in BLOCKS:
                        bidx = BLOCKS.index((t0, n, j))
                        for d in range(2):
                            ntr = ntricf if d == 0 else ntricb
                            for ti in range(n // 128):
                                tt = t0 // 128 + ti
                                op("pe", lambda e, d=d, tt=tt, ti=ti, ntr=ntr: e.matmul(pb[d][0:64, ti * 128:(ti + 1) * 128],
                                                                                         lhsT=SP[:, tt, d * 8 + h:d * 8 + h + 1].broadcast_to([128, 64]), rhs=ntr,
                                                                                         start=True, stop=True),
                                   reads=[gbuf, cbuf], writes=[pbb[d]], inc=(ti == n // 128 - 1))
                            op("act", lambda e, d=d: e.activation(out=ebt[d][:, 0:n], in_=pb[d][0:64, 0:n], func=AF.Exp), reads=[pbb[d]], writes=[ebb[d]])
                        for k in range(8):
                            op("pe", lambda e, k=k: e.matmul(pb[2][0:64, 0:n], lhsT=w[:, k, 0:64], rhs=uT[:, k, t0:t0 + n], start=(k == 0), stop=(k == 7)),
                               reads=[whb[s], ub[bidx]], writes=[pbb[2]], inc=(k == 7))
                        for d in range(2):
                            op("dve", lambda e, d=d: e.tensor_tensor(out=Qb[d][:, t0:t0 + n], in0=pb[2][0:64, 0:n], in1=ebt[d][:, 0:n], op=ALU.mult),
                               reads=[pbb[2], ebb[d]], writes=[qbb[d]])
                        for k in range(8):
                            op("pe", lambda e, k=k: e.matmul(pb[3][0:64, 0:n], lhsT=w[:, k, 64:128], rhs=uT[:, k, t0:t0 + n], start=(k == 0), stop=(k == 7)),
                               reads=[whb[s], ub[bidx]], writes=[pbb[3]], inc=(k == 7))
                        op("act", lambda e: e.activation(out=KT[:, t0:t0 + n], in_=pb[3][0:64, 0:n], func=AF.Identity, scale=0.125), reads=[pbb[3]], writes=[ktb])
                    for tt in range(NT):
                        bk = 4 + tt % 2
                        bidx = 0 if tt < 2 else 1 + (tt - 2) // 4
                        for k in range(8):
                            op("pe", lambda e, k=k, tt=tt, bk=bk: e.matmul(pb[bk][:, 0:192], lhsT=uT[:, k, tt * 128:(tt + 1) * 128], rhs=w[:, k, 128:320],
                                                                           start=(k == 0), stop=(k == 7)),
                               reads=[whb[s], ub[bidx]], writes=[pbb[bk]], inc=(k == 7))
                        op("act", lambda e, tt=tt, bk=bk: e.activation(out=Ktok[:, tt, :], in_=pb[bk][:, 0:64], func=AF.Identity, scale=0.125), reads=[pbb[bk]], writes=[kkb])
                        for d in range(2):
                            op("dve", lambda e, tt=tt, bk=bk, d=d: e.tensor_scalar(out=Va[d][:, tt, :], in0=pb[bk][:, 64:192], scalar1=KS[:, tt, d * 8 + h:d * 8 + h + 1],
                                                                                    scalar2=None, op0=ALU.mult),
                               reads=[pbb[bk], gbuf], writes=[vab[d]])
                    op("dve", lambda e: e.memset(HS[:], 0.0), writes=hsb)
                    for d in range(2):
                        op("dve", lambda e, d=d: e.memset(Cst[d][:], 0.0), writes=[csb[d]])
                    pmi = 0
                    for step in range(NT):
                        for d in range(2):
                            tt = order[d][step]
                            col = d * 8 + h
                            tsl = slice(tt * 128, (tt + 1) * 128)
                            ksl = KSb[:, tt, col:col + 1].broadcast_to([128, 128])
                            need_out = need_ctx or tt >= 2
                            if step > 0 and need_out:
                                op("dve", lambda e, d=d, tt=tt, col=col: e.tensor_scalar(out=Cbf[d][:], in0=Cst[d][:], scalar1=EBH[0:64, tt, col:col + 1], scalar2=None, op0=ALU.mult),
                                   reads=[csb[d], gbuf], writes=[cbb[d]])
                            if need_out:
                                bS = d
                                bN = 2 + d
                                bD = 4 + d
                                p_ = pmi % 4
                                pmi += 1
                                op("pe", lambda e, d=d, tsl=tsl, bS=bS: e.matmul(pb[bS][:, 0:128], lhsT=KT[:, tsl], rhs=Qb[d][:, tsl], start=True, stop=True),
                                   reads=[ktb, qbb[d]], writes=[pbb[bS]])
                                mk = maskf_bf if d == 0 else maskb_bf
                                op("dve", lambda e, p_=p_, bS=bS, mk=mk: e.tensor_tensor(out=Pm[p_][:], in0=pb[bS][:, 0:128], in1=mk[:], op=ALU.mult),
                                   reads=[pbb[bS], cbuf], writes=[pmb[p_]])
                                last = (step == 0)
                                op("pe", lambda e, d=d, tt=tt, p_=p_, bN=bN, last=last: e.matmul(pb[bN][:, 0:128], lhsT=Va[d][:, tt, :], rhs=Pm[p_][:], start=True, stop=last),
                                   reads=[vab[d], pmb[p_]], writes=[pbb[bN]], inc=last)
                                if not last:
                                    op("pe", lambda e, d=d, tsl=tsl, bN=bN: e.matmul(pb[bN][:, 0:128], lhsT=Cbf[d][:, 0:128], rhs=Qb[d][:, tsl], start=False, stop=True),
                                       reads=[cbb[d], qbb[d]], writes=[pbb[bN]])
                                op("pe", lambda e, ksl=ksl, p_=p_, bD=bD, last=last: e.matmul(pb[bD][:, 0:128], lhsT=ksl, rhs=Pm[p_][:], start=True, stop=last),
                                   reads=[gbuf, pmb[p_]], writes=[pbb[bD]], inc=last)
                                if not last:
                                    op("pe", lambda e, d=d, tsl=tsl, bD=bD: e.matmul(pb[bD][:, 0:128], lhsT=Cbf[d][:, 128:256], rhs=Qb[d][:, tsl], start=False, stop=True),
                                       reads=[cbb[d], qbb[d]], writes=[pbb[bD]])
                                op("act", lambda e, d=d, bD=bD: e.activation(out=adn[d][:], in_=pb[bD][:, 0:128], func=AF.Abs), reads=[pbb[bD]], writes=[adb[d]])
                                op("dve", lambda e, d=d: e.tensor_scalar(out=adn[d][:], in0=adn[d][:], scalar1=1.0, scalar2=None, op0=ALU.max), reads=[adb[d]], writes=[adb[d]])
                                op("dve", lambda e, d=d: e.reciprocal(out=adn[d][:], in_=adn[d][:]), reads=[adb[d]], writes=[adb[d]])
                                op("dve", lambda e, d=d, bN=bN: e.tensor_tensor(out=htm[d][:], in0=pb[bN][:, 0:128], in1=adn[d][:], op=ALU.mult),
                                   reads=[pbb[bN], adb[d]], writes=[htb[d]])
                                op("dve", lambda e, d=d, tsl=tsl: e.tensor_tensor(out=HS[:, tsl], in0=HS[:, tsl], in1=htm[d][:], op=ALU.add),
                                   reads=[htb[d], hsb[tt]], writes=[hsb[tt]])
                            if step < NT - 1:
                                bK = 6 + d
                                op("pe", lambda e, d=d, tt=tt, bK=bK: e.matmul(pb[bK][0:64, 0:128], lhsT=Ktok[:, tt, :], rhs=Va[d][:, tt, :], start=True, stop=True),
                                   reads=[kkb, vab[d]], writes=[pbb[bK]], inc=False)
                                op("pe", lambda e, tt=tt, ksl=ksl, bK=bK: e.matmul(pb[bK][0:64, 128:256], lhsT=Ktok[:, tt, :], rhs=ksl, start=True, stop=True),
                                   reads=[kkb, gbuf], writes=[pbb[bK]])
                                op("dve", lambda e, d=d, tt=tt, col=col, bK=bK: e.tensor_scalar(out=ctmp[d][:], in0=pb[bK][0:64, 0:256], scalar1=EBH[0:64, tt, col:col + 1],
                                                                                                scalar2=None, op0=ALU.mult),
                                   reads=[pbb[bK], gbuf], writes=[ctb[d]])
                                op("dve", lambda e, d=d, tt=tt, col=col: e.scalar_tensor_tensor(out=Cst[d][:], in0=Cst[d][:], scalar=EBL[0:64, tt, col:col + 1], in1=ctmp[d][:],
                                                                                                 op0=ALU.mult, op1=ALU.add),
                                   reads=[ctb[d], gbuf, csb[d]], writes=[csb[d]])
                    hnw = small[:, S_MHN + slot * 8 + h:S_MHN + slot * 8 + h + 1]
                    for bi, (t0, n, j) in enumerate(out_blocks):
                        bidx = BLOCKS.index((t0, n, j))
                        hsl = [hsb[i] for i in range(t0 // 128, (t0 + n) // 128)]
                        op("act", lambda e: e.activation(out=sqh[:, 0:n], in_=HS[:, t0:t0 + n], func=AF.Square), reads=hsl, writes=[sqhb])
                        op("pe", lambda e: e.matmul(pb[0][:, 0:n], lhsT=ones_bf[:], rhs=sqh[:, 0:n], start=True, stop=True), reads=[sqhb, cbuf], writes=[pbb[0]])
                        rstd_from_psum(pb[0][:, 0:n], rsh[:, 0:n], n, 128.0, pbb[0], rshb)
                        for k in range(8):
                            op("pe", lambda e, k=k: e.matmul(pb[1][:, 0:n], lhsT=w[:, k, 320:448], rhs=uT[:, k, t0:t0 + n], start=(k == 0), stop=(k == 7)),
                               reads=[whb[s], ub[bidx]], writes=[pbb[1]], inc=(k == 7))
                        op("act", lambda e: e.activation(out=sig[:, 0:n], in_=pb[1][:, 0:n], func=AF.Exp, scale=-1.0), reads=[pbb[1]], writes=[sigb])
                        op("dve", lambda e: e.tensor_scalar(out=sig[:, 0:n], in0=sig[:, 0:n], scalar1=1.0, scalar2=None, op0=ALU.add), reads=[sigb], writes=[sigb])
                        op("dve", lambda e: e.reciprocal(out=sig[:, 0:n], in_=sig[:, 0:n]), reads=[sigb], writes=[sigb])
                        op("dve", lambda e: e.scalar_tensor_tensor(out=hn[:, 0:n], in0=HS[:, t0:t0 + n], scalar=hnw, in1=rsh[:, 0:n], op0=ALU.mult, op1=ALU.mult),
                           reads=hsl + [rshb, cbuf], writes=[hnb])
                        a_ = bi % 2
                        op("dve", lambda e, a_=a_: e.tensor_tensor(out=ao[a_][:, 0:n], in0=hn[:, 0:n], in1=sig[:, 0:n], op=ALU.mult), reads=[hnb, sigb], writes=[aob[a_]])
                        outproj_acc([6, 7], [wo[s]], [ao[a_][:, 0:n]], t0, n, j, [wob[s], aob[a_]])
                kb.barrier()

        def proj_rope(ps_res, w, c0, cr0, dst_ap_fn, dbuf, wbuf, banks, tmps, tmpb, rope_tiles, scale_ctx_copy=True):
            for (t0, n, j) in BLOCKS:
                bidx = BLOCKS.index((t0, n, j))
                b0, b1 = banks
                for k in range(8):
                    op("pe", lambda e, k=k: e.matmul(pb[b0][:, 0:n], lhsT=w[:, k, c0:c0 + 128], rhs=uT[:, k, t0:t0 + n], start=(k == 0), stop=(k == 7)),
                       reads=[wbuf, ub[bidx]], writes=[pbb[b0]], inc=(k == 7))
                if j == 1:
                    op("act", lambda e: e.activation(out=dst_ap_fn(t0, n), in_=pb[b0][:, 0:n], func=AF.Identity), reads=[pbb[b0]], writes=[dbuf])
                    continue
                for k in range(8):
                    op("pe", lambda e, k=k: e.matmul(pb[b1][:, 0:n], lhsT=w[:, k, cr0:cr0 + 128], rhs=uT[:, k, t0:t0 + n], start=(k == 0), stop=(k == 7)),
                       reads=[wbuf, ub[bidx]], writes=[pbb[b1]], inc=(k == 7))
                rc, rs_, rb = rope_tiles(t0)
                op("dve", lambda e: e.tensor_tensor(out=tmps[0][:, 0:n], in0=pb[b0][:, 0:n], in1=rc, op=ALU.mult), reads=[pbb[b0], rb], writes=[tmpb[0]])
                op("dve", lambda e: e.tensor_tensor(out=tmps[1][:, 0:n], in0=pb[b1][:, 0:n], in1=rs_, op=ALU.mult), reads=[pbb[b1], rb], writes=[tmpb[1]])
                op("dve", lambda e: e.tensor_tensor(out=dst_ap_fn(t0, n), in0=tmps[0][:, 0:n], in1=tmps[1][:, 0:n], op=ALU.add), reads=[tmpb[0], tmpb[1]], writes=[dbuf])

        def load_rope(ps):
            rc = sbt(ps, [128, NLAT], F32, "ropec")
            rs_ = sbt(ps, [128, NLAT], F32, "ropes")
            rb = Buf("rope")
            dma("sp", rc[:], ropec_l.get(), writes=[rb])
            dma("sp", rs_[:], ropes_l.get(), writes=[rb])

            def tiles(t0):
                l0 = t0 - NCTX
                return rc[:, l0:l0 + 512], rs_[:, l0:l0 + 512], rb
            return tiles

        def swa_phase(L, need_ctx):
            with ExitStack() as ps:
                rope_tiles = load_rope(ps)
                wsv = ws_d[0].rearrange("(c p) n -> p c n", p=128)
                wov = wso_d[0].rearrange("(h p) n -> p h n", p=64)
                wg_ = [sbt(ps, [128, 8, 832], BF16, "wsg") for _ in range(1)]
                wgb = [Buf("wsg0"), Buf("wsg1")]
                wo = [sbt(ps, [64, 4, D], BF16, "wso") for _ in range(1)]
                wob = [Buf("wso0"), Buf("wso1")]
                QT = sbt(ps, [128, 2, T], BF16, "QT")
                qtb = Buf("QT")
                KT = sbt(ps, [128, T], BF16, "KT")
                ktb = Buf("KT")
                Va = sbt(ps, [128, NT, 128], BF16, "Va")
                vab = Buf("Va")
                aog = sbt(ps, [64, 4, T], BF16, "aog")
                aob = [Buf("aog%d" % i) for i in range(NT)]
                tmps = [sbt(ps, [128, 512], F32, "rt") for _ in range(2)]
                tmpb = [Buf("rt0"), Buf("rt1")]
                Pt = [sbt(ps, [128, 512], BF16, "Pt") for _ in range(3)]
                ptb = [Buf("Pt%d" % i) for i in range(3)]
                dn = sbt(ps, [128, 512], F32, "dn")
                dnb = Buf("dn")
                ES = sbt(ps, [128, 16], F32, "ES")
                esb = Buf("ES")
                op("act", lambda e: e.activation(out=ES[:], in_=small[:, S_SINK:S_SINK + 16], func=AF.Exp), reads=[cbuf], writes=[esb])
                op("dve", lambda e: e.memset(Va[:, :, 64:128], 1.0), writes=[vab])

                def load_g(g):
                    s = 0
                    dma("pool", wg_[s][:], wsv[:, :, g * 832:(g + 1) * 832], writes=[wgb[s]])
                    dma("pool", wo[s][:], wov[:, g * 4:(g + 1) * 4, :], writes=[wob[s]])

                pti = 0
                for g in range(4):
                    s = 0
                    load_g(g)
                    w = wg_[s]
                    if SWA_STOP < 1:
                        continue
                    proj_rope(ps, w, 512, 640, lambda t0, n: KT[:, t0:t0 + n], ktb, wgb[s], (0, 1), tmps, tmpb, rope_tiles)
                    for jq in range(SWA_NQ):
                        if SWA_SUB < 1:
                            continue
                        proj_rope(ps, w, jq * 128, 256 + jq * 128, lambda t0, n, jq=jq: QT[:, jq, t0:t0 + n], qtb, wgb[s], SWA_QB, tmps, tmpb, rope_tiles)
                    for tt in range(NT):
                        if SWA_SUB < 2:
                            continue
                        bk = 4 + (tt // 8) % 2
                        o = (tt % 8) * 64
                        bidx = 0 if tt < 2 else 1 + (tt - 2) // 4
                        for k in range(8):
                            op("pe", lambda e, k=k, tt=tt, bk=bk, o=o: e.matmul(pb[bk][:, o:o + 64], lhsT=uT[:, k, tt * 128:(tt + 1) * 128], rhs=w[:, k, 768:832],
                                                                                start=(k == 0), stop=(k == 7)),
                               reads=[wgb[s], ub[bidx]], writes=[pbb[bk]], inc=(k == 7))
                        if tt % 8 == 7 or tt == NT - 1:
                            a = (tt // 8) * 8
                            nn = tt + 1 - a
                            op("act", lambda e, a=a, nn=nn, bk=bk: e.activation(out=Va[:, a:a + nn, 0:64], in_=pb[bk][:, 0:nn * 64].rearrange("p (t d) -> p t d", d=64),
                                                                                func=AF.Identity), reads=[pbb[bk]], writes=[vab])
                    for qt in range(NT):
                        if SWA_STOP < 2:
                            continue
                        if qt < 2:
                            if not need_ctx:
                                continue
                            kts = [0, 1]
                        else:
                            kts = [0, 1] + [kt for kt in (qt - 1, qt, qt + 1) if 2 <= kt < NT]
                        qsl = slice(qt * 128, (qt + 1) * 128)
                        bO = 6 + qt % 2
                        for ki, kt in enumerate(kts):
                            bS = (pti % 2) * 2
                            p_ = pti % 3
                            pti += 1
                            ksl = slice(kt * 128, (kt + 1) * 128)
                            op("pe", lambda e, bS=bS, ksl=ksl, qsl=qsl: e.matmul(pb[bS][:, 0:256], lhsT=KT[0:64, ksl], rhs=QT[0:64, :, qsl], start=True, stop=True),
                               reads=[ktb, qtb], writes=[pbb[bS]])
                            op("pe", lambda e, bS=bS, ksl=ksl, qsl=qsl: e.matmul(pb[bS + 1][:, 0:256], lhsT=KT[64:128, ksl], rhs=QT[64:128, :, qsl], start=True, stop=True),
                               reads=[ktb, qtb], writes=[pbb[bS + 1]])
                            op("act", lambda e, bS=bS, p_=p_: e.activation(out=Pt[p_][:, 0:256], in_=pb[bS][:, 0:256], func=AF.Exp, scale=0.125), reads=[pbb[bS]], writes=[ptb[p_]])
                            op("act", lambda e, bS=bS, p_=p_: e.activation(out=Pt[p_][:, 256:512], in_=pb[bS + 1][:, 0:256], func=AF.Exp, scale=0.125), reads=[pbb[bS + 1]], writes=[ptb[p_]])
                            if qt >= 2 and kt >= 2 and kt != qt:
                                mk = maskb_bf if kt == qt - 1 else maskf_bf
                                op("dve", lambda e, p_=p_, mk=mk: e.tensor_tensor(out=Pt[p_][:].rearrange("p (h q) -> p h q", h=4), in0=Pt[p_][:].rearrange("p (h q) -> p h q", h=4),
                                                                                  in1=mk[:].unsqueeze(1).broadcast_to([128, 4, 128]), op=ALU.mult),
                                   reads=[ptb[p_], cbuf], writes=[ptb[p_]])
                            op("pe", lambda e, kt=kt, p_=p_, bO=bO, ki=ki: e.matmul(pb[bO][:], lhsT=Va[:, kt, :], rhs=Pt[p_][:], start=(ki == 0), stop=(ki == len(kts) - 1)),
                               reads=[vab, ptb[p_]], writes=[pbb[bO]], inc=(ki == len(kts) - 1))
                        op("dve", lambda e, bO=bO: e.tensor_tensor(out=dn[64:128, :].rearrange("p (h q) -> p h q", h=4), in0=pb[bO][64:128, :].rearrange("p (h q) -> p h q", h=4),
                                                                   in1=ES[64:128, g * 4:(g + 1) * 4].unsqueeze(2).broadcast_to([64, 4, 128]), op=ALU.add),
                           reads=[pbb[bO], esb], writes=[dnb])
                        op("dve", lambda e: e.reciprocal(out=dn[64:128, :], in_=dn[64:128, :]), reads=[dnb], writes=[dnb])
                        op("dve", lambda e, bO=bO, qsl=qsl: e.tensor_tensor(out=aog[:, :, qsl], in0=pb[bO][0:64, :].rearrange("p (h q) -> p h q", h=4),
                                                                            in1=dn[64:128, :].rearrange("p (h q) -> p h q", h=4), op=ALU.mult),
                           reads=[pbb[bO], dnb], writes=[aob[qt]])
                    for (t0, n, j) in (BLOCKS if need_ctx else BLOCKS[1:]):
                        if SWA_STOP < 3:
                            continue
                        ab = [aob[i] for i in range(t0 // 128, (t0 + n) // 128)]
                        outproj_acc([0, 1, 2, 3], [wo[s][:, PSORD[pos], :] for pos in range(4)], [aog[:, pos, t0:t0 + n] for pos in range(4)], t0, n, j, [wob[s]] + ab)
                kb.barrier()

        def diff_phase(L, need_ctx):
            lam_init = 0.8 - 0.6 * float(np.exp(-0.3 * L))
            with ExitStack() as ps:
                rope_tiles = load_rope(ps)
                wdv = wd_d[0].rearrange("(c p) n -> p c n", p=128)
                wov = wdo_d[0]
                wh = [sbt(ps, [128, 8, 640], BF16, "wdh") for _ in range(2)]
                whb = [Buf("wdh0"), Buf("wdh1")]
                wo = [sbt(ps, [128, D], BF16, "wdo") for _ in range(2)]
                wob = [Buf("wdo0"), Buf("wdo1")]
                QT = sbt(ps, [128, T], BF16, "QT")
                qtb = Buf("QT")
                KT = sbt(ps, [128, T], BF16, "KT")
                ktb = Buf("KT")
                Vt = sbt(ps, [128, NT, 128], BF16, "Vt")
                vtb = Buf("Vt")
                tmps = [sbt(ps, [128, 512], F32, "rt") for _ in range(2)]
                tmpb = [Buf("rt0"), Buf("rt1")]
                Pt = [sbt(ps, [128, 512], BF16, "Pt") for _ in range(4)]
                ptb = [Buf("Pt%d" % i) for i in range(4)]
                rr = [sbt(ps, [128, 512], F32, "rr") for _ in range(2)]
                rrb = [Buf("rr0"), Buf("rr1")]
                od = sbt(ps, [128, 512], F32, "od")
                odb = Buf("od")
                sqh = sbt(ps, [128, 512], BF16, "sqh")
                sqhb = Buf("sqh")
                rsh = sbt(ps, [128, 512], F32, "rsh")
                rshb = Buf("rsh")
                ao = [sbt(ps, [128, 512], BF16, "ao") for _ in range(2)]
                aob = [Buf("ao0"), Buf("ao1")]
                lam = sbt(ps, [128, 8], F32, "lam")
                lamb = Buf("lam")
                hnw = sbt(ps, [128, 8], F32, "hnw")
                lv = small[:, S_DLAM:S_DLAM + 256]
                op("dve", lambda e: e.tensor_tensor(out=tmps[0][:, 0:64], in0=lv[:, 0:64], in1=lv[:, 64:128], op=ALU.mult), reads=[cbuf], writes=[tmpb[0]])
                op("dve", lambda e: e.tensor_tensor(out=tmps[0][:, 64:128], in0=lv[:, 128:192], in1=lv[:, 192:256], op=ALU.mult), reads=[cbuf], writes=[tmpb[0]])
                op("dve", lambda e: e.tensor_reduce(out=lam[:, 0:2], in_=tmps[0][:, 0:128].rearrange("p (a b) -> p a b", a=2), axis=mybir.AxisListType.X, op=ALU.add),
                   reads=[tmpb[0]], writes=[lamb])
                op("act", lambda e: e.activation(out=lam[:, 0:2], in_=lam[:, 0:2], func=AF.Exp), reads=[lamb], writes=[lamb])
                op("dve", lambda e: e.tensor_tensor(out=lam[:, 2:3], in0=lam[:, 1:2], in1=lam[:, 0:1], op=ALU.subtract), reads=[lamb], writes=[lamb])
                op("dve", lambda e: e.tensor_scalar(out=lam[:, 3:4], in0=lam[:, 2:3], scalar1=-lam_init, scalar2=None, op0=ALU.add), reads=[lamb], writes=[lamb])
                op("dve", lambda e: e.tensor_scalar(out=hnw[:], in0=small[:, S_DHN:S_DHN + 8], scalar1=(1.0 - lam_init), scalar2=None, op0=ALU.mult), reads=[cbuf], writes=[lamb])
                nlam = lam[:, 3:4]

                def load_h(h):
                    s = h % 2
                    dma("pool", wh[s][:], wdv[:, :, h * 640:(h + 1) * 640], writes=[whb[s]])
                    dma("pool", wo[s][:], wov[h * 128:(h + 1) * 128, :], writes=[wob[s]])

                load_h(0)
                pti = 0
                oi = 0
                for h in range(8):
                    s = h % 2
                    if h + 1 < 8:
                        load_h(h + 1)
                    w = wh[s]
                    proj_rope(ps, w, 256, 384, lambda t0, n: KT[:, t0:t0 + n], ktb, whb[s], (0, 1), tmps, tmpb, rope_tiles)
                    proj_rope(ps, w, 0, 128, lambda t0, n: QT[:, t0:t0 + n], qtb, whb[s], (2, 3), tmps, tmpb, rope_tiles)
                    for tt in range(NT):
                        bk = 4 + (tt // 4) % 2
                        o = (tt % 4) * 128
                        bidx = 0 if tt < 2 else 1 + (tt - 2) // 4
                        for k in range(8):
                            op("pe", lambda e, k=k, tt=tt, bk=bk, o=o: e.matmul(pb[bk][:, o:o + 128], lhsT=uT[:, k, tt * 128:(tt + 1) * 128], rhs=w[:, k, 512:640],
                                                                                start=(k == 0), stop=(k == 7)),
                               reads=[whb[s], ub[bidx]], writes=[pbb[bk]], inc=(k == 7))
                        if tt % 4 == 3 or tt == NT - 1:
                            a = (tt // 4) * 4
                            nn = tt + 1 - a
                            op("act", lambda e, a=a, nn=nn, bk=bk: e.activation(out=Vt[:, a:a + nn, :], in_=pb[bk][:, 0:nn * 128].rearrange("p (t d) -> p t d", d=128),
                                                                                func=AF.Identity), reads=[pbb[bk]], writes=[vtb])
                    for bi, (t0, n, j) in enumerate(BLOCKS if need_ctx else BLOCKS[1:]):
                        kts = [0, 1] if j == 1 else list(range(NT))
                        for ki, kt in enumerate(kts):
                            ksl = slice(kt * 128, (kt + 1) * 128)
                            first = (ki == 0)
                            lastk = (ki == len(kts) - 1)
                            for m in range(2):
                                bS = m * 2 + (ki % 2)
                                p_ = pti % 4
                                pti += 1
                                rsl = slice(m * 64, (m + 1) * 64)
                                op("pe", lambda e, bS=bS, ksl=ksl, rsl=rsl: e.matmul(pb[bS][:, 0:n], lhsT=KT[rsl, ksl], rhs=QT[rsl, t0:t0 + n], start=True, stop=True),
                                   reads=[ktb, qtb], writes=[pbb[bS]])
                                op("act", lambda e, bS=bS, p_=p_: e.activation(out=Pt[p_][:, 0:n], in_=pb[bS][:, 0:n], func=AF.Exp, scale=0.125), reads=[pbb[bS]], writes=[ptb[p_]])
                                op("pe", lambda e, kt=kt, p_=p_, m=m: e.matmul(pb[4 + m][:, 0:n], lhsT=Vt[:, kt, :], rhs=Pt[p_][:, 0:n], start=first, stop=lastk),
                                   reads=[vtb, ptb[p_]], writes=[pbb[4 + m]], inc=lastk)
                                op("pe", lambda e, p_=p_, m=m: e.matmul(pb[6 + m][:, 0:n], lhsT=ones_bf[:], rhs=Pt[p_][:, 0:n], start=first, stop=lastk),
                                   reads=[cbuf, ptb[p_]], writes=[pbb[6 + m]], inc=lastk)
                        for m in range(2):
                            op("dve", lambda e, m=m: e.reciprocal(out=rr[m][:, 0:n], in_=pb[6 + m][:, 0:n]), reads=[pbb[6 + m]], writes=[rrb[m]])
                            op("dve", lambda e, m=m: e.tensor_tensor(out=rr[m][:, 0:n], in0=pb[4 + m][:, 0:n], in1=rr[m][:, 0:n], op=ALU.mult), reads=[pbb[4 + m], rrb[m]], writes=[rrb[m]])
                        op("dve", lambda e: e.scalar_tensor_tensor(out=od[:, 0:n], in0=rr[1][:, 0:n], scalar=nlam, in1=rr[0][:, 0:n], op0=ALU.mult, op1=ALU.add),
                           reads=[rrb[0], rrb[1], lamb], writes=[odb])
                        op("act", lambda e: e.activation(out=sqh[:, 0:n], in_=od[:, 0:n], func=AF.Square), reads=[odb], writes=[sqhb])
                        op("pe", lambda e: e.matmul(pb[0][:, 0:n], lhsT=ones_bf[:], rhs=sqh[:, 0:n], start=True, stop=True), reads=[sqhb, cbuf], writes=[pbb[0]])
                        rstd_from_psum(pb[0][:, 0:n], rsh[:, 0:n], n, 128.0, pbb[0], rshb)
                        a_ = oi % 2
                        oi += 1
                        op("dve", lambda e, a_=a_: e.scalar_tensor_tensor(out=ao[a_][:, 0:n], in0=od[:, 0:n], scalar=hnw[:, h:h + 1], in1=rsh[:, 0:n], op0=ALU.mult, op1=ALU.mult),
                           reads=[odb, rshb, lamb], writes=[aob[a_]])
                        outproj_acc([1, 2, 3], [wo[s]], [ao[a_][:, 0:n]], t0, n, j, [wob[s], aob[a_]])
                kb.barrier()

        def final_phase(do_norm):
            with ExitStack() as ps:
                sq = sbt(ps, [128, 8, 512], BF16, "fsq")
                sqb = Buf("fsq")
                rs = sbt(ps, [128, 512], F32, "frs")
                rsb = Buf("frs")
                tmp = sbt(ps, [128, 8, 512], F32, "ftmp")
                tmb = Buf("ftmp")
                ost = [sbt(ps, [128, D], F32, "ost") for _ in range(2)]
                osb = [Buf("ost0"), Buf("ost1")]
                oi = 0
                for (t0, n, j) in BLOCKS[1:]:
                    if do_norm:
                        op("act", lambda e: e.activation(out=sq[:, :, 0:n], in_=hT[:, :, t0:t0 + n], func=AF.Square), reads=hbs(t0, n), writes=[sqb])
                        for c in range(8):
                            op("pe", lambda e, c=c: e.matmul(pb[7][:, 0:n], lhsT=ones_bf[:], rhs=sq[:, c, 0:n], start=(c == 0), stop=(c == 7)),
                               reads=[sqb, cbuf], writes=[pbb[7]], inc=(c == 7))
                        rstd_from_psum(pb[7][:, 0:n], rs[:, 0:n], n, float(D), pbb[7], rsb)
                        for c in range(8):
                            op("dve", lambda e, c=c: e.scalar_tensor_tensor(out=tmp[:, c, 0:n], in0=hT[:, c, t0:t0 + n], scalar=small[:, S_FINAL + c:S_FINAL + c + 1],
                                                                            in1=rs[:, 0:n], op0=ALU.mult, op1=ALU.mult),
                               reads=hbs(t0, n) + [rsb, cbuf], writes=[tmb])
                    else:
                        op("dve", lambda e: e.tensor_copy(out=tmp[:, :, 0:n], in_=hT[:, :, t0:t0 + n]), reads=hbs(t0, n), writes=[tmb])
                    for ti in range(n // 128):
                        o_ = oi % 2
                        oi += 1
                        for half in range(2):
                            bk = (oi * 2 + half) % 4
                            for c4 in range(4):
                                c = half * 4 + c4
                                op("pe", lambda e, c=c, c4=c4, bk=bk, ti=ti: e.transpose(pb[bk][:, c4 * 128:(c4 + 1) * 128], tmp[:, c, ti * 128:(ti + 1) * 128], ident),
                                   reads=[tmb, cbuf], writes=[pbb[bk]], inc=(c4 == 3))
                            if half == 0:
                                op("act", lambda e, bk=bk, o_=o_, half=half: e.activation(out=ost[o_][:, half * 512:(half + 1) * 512], in_=pb[bk][:], func=AF.Identity),
                                   reads=[pbb[bk]], writes=[osb[o_]])
                            else:
                                op("dve", lambda e, bk=bk, o_=o_, half=half: e.tensor_copy(out=ost[o_][:, half * 512:(half + 1) * 512], in_=pb[bk][:]),
                                   reads=[pbb[bk]], writes=[osb[o_]])
                        r0 = t0 - NCTX + ti * 128
                        dma("sp", out_d[r0:r0 + 128, :], ost[o_][:], reads=[osb[o_]])
                kb.wait_all_dma("sp")

        for L in layers:
            kind, slot = L % 3, L // 3
            need_ctx = L < DEPTH - 1
            blocks = BLOCKS if need_ctx else BLOCKS[1:]
            ada_phase(L)
            if do_mixer:
                norm_mod(amix, 0, BLOCKS)
                if kind == 0:
                    mlstm_phase(L, slot, need_ctx)
                elif kind == 1:
                    swa_phase(L, need_ctx)
                else:
                    diff_phase(L, need_ctx)
            if do_ffn:
                norm_mod(affn, 24, blocks)
                if do_ffn != "norm":
                    ffn_phase(L, blocks)
        final_phase(final_norm)
    nc._declared_inputs = list(declared)
    return nc


def _pcol(v):
    return np.ascontiguousarray(np.asarray(v, np.float32).reshape(-1, 128).T)


def _rot_idx(base):
    return list(range(base + 32, base + 64)) + list(range(base, base + 32))


def _consts():
    i = np.arange(128)
    ident = (i[:, None] == i[None, :]).astype(np.float32)
    maskf = (i[:, None] <= i[None, :]).astype(np.float32)
    maskb = (i[:, None] >= i[None, :]).astype(np.float32)
    return np.ascontiguousarray(np.concatenate([ident, maskf, maskb, -(maskf - 0.5), -(maskb - 0.5)], axis=1))


def _rope_tables():
    rows = NLAT // 64
    row = np.repeat(np.arange(rows), 64).astype(np.float32)
    col = np.tile(np.arange(64), rows).astype(np.float32)
    quarter = 16
    inv = (np.float32(10000.0) ** (-np.arange(quarter, dtype=np.float32) / np.float32(quarter))).astype(np.float32)
    ang = np.concatenate([row[:, None] * inv, col[:, None] * inv], axis=-1).astype(np.float32)
    cos = np.cos(ang).astype(np.float32).T
    sin = np.sin(ang).astype(np.float32).T
    c64 = np.concatenate([cos, cos], 0)
    s64 = np.concatenate([-sin, sin], 0)
    return np.ascontiguousarray(np.concatenate([c64, c64], 0)), np.ascontiguousarray(np.concatenate([s64, s64], 0))


def prepare_shared(inp):
    sh = {}
    sh["cst"] = _consts()
    sh["ropec"], sh["ropes"] = _rope_tables()
    sh["ada_w"] = np.ascontiguousarray(inp["ada_w"], dtype=np.float32)
    sh["ffn_w1"] = np.ascontiguousarray(inp["ffn_w1"], dtype=np.float32)
    sh["ffn_w2"] = np.ascontiguousarray(inp["ffn_w2"], dtype=np.float32)
    wm = []
    for s in range(inp["mlstm_w_in"].shape[0]):
        W = inp["mlstm_w_in"][s]
        cols = []
        for h in range(8):
            cols += list(range(h * 64, h * 64 + 64))
            cols += list(range(512 + h * 64, 512 + h * 64 + 64))
            cols += list(range(512 + h * 64, 512 + h * 64 + 64))
            cols += list(range(1024 + h * 128, 1024 + h * 128 + 128))
            cols += list(range(2048 + h * 128, 2048 + h * 128 + 128))
        cols += list(range(3072, 3104))
        wm.append(W[:, cols])
    sh["wm"] = np.ascontiguousarray(np.stack(wm), dtype=np.float32)
    sh["wmo"] = np.ascontiguousarray(inp["mlstm_w_out"], dtype=np.float32)
    W = inp["swa_w_in"][0]
    cols = []
    for g in range(4):
        q = []
        qr = []
        for hh in range(4):
            b = (4 * g + hh) * 64
            q += list(range(b, b + 64))
            qr += _rot_idx(b)
        kb_ = 1024 + g * 64
        k = list(range(kb_, kb_ + 64))
        kr = _rot_idx(kb_)
        v = list(range(1280 + g * 64, 1280 + g * 64 + 64))
        cols += q + qr + k + k + kr + kr + v
    sh["ws"] = np.ascontiguousarray(W[:, cols][None], dtype=np.float32)
    sh["wso"] = np.ascontiguousarray(inp["swa_w_out"], dtype=np.float32)
    W = inp["diff_w_in"][0]
    cols = []
    for h in range(8):
        qb_ = h * 128
        kb_ = 1024 + h * 128
        cols += list(range(qb_, qb_ + 128)) + _rot_idx(qb_) + _rot_idx(qb_ + 64)
        cols += list(range(kb_, kb_ + 128)) + _rot_idx(kb_) + _rot_idx(kb_ + 64)
        cols += list(range(2048 + h * 128, 2048 + h * 128 + 128))
    sh["wd"] = np.ascontiguousarray(W[:, cols][None], dtype=np.float32)
    sh["wdo"] = np.ascontiguousarray(inp["diff_w_out"], dtype=np.float32)
    small = np.zeros((128, NS), np.float32)
    for L in range(DEPTH):
        small[:, L * SL:L * SL + 8] = _pcol(inp["norm_mix"][L])
        small[:, L * SL + 8:L * SL + 16] = _pcol(inp["norm_ffn"][L])
        small[:, L * SL + 16:L * SL + 64] = _pcol(inp["ada_b"][L])
    small[:, S_FINAL:S_FINAL + 8] = _pcol(inp["final_norm"])
    for s in range(inp["mlstm_head_norm"].shape[0]):
        small[:, S_MHN + s * 8:S_MHN + s * 8 + 8] = _pcol(inp["mlstm_head_norm"][s])
        small[:, S_MGB + s * 32:S_MGB + s * 32 + 32] = np.broadcast_to(np.asarray(inp["mlstm_gate_b"][s], np.float32).reshape(1, 32), (128, 32))
    sink = np.asarray(inp["swa_sink"][0], np.float32)
    sord = [4 * g + PSORD[p] for g in range(4) for p in range(4)]
    small[:, S_SINK:S_SINK + 16] = np.broadcast_to(sink[sord][None, :], (128, 16))
    small[:, S_DHN:S_DHN + 8] = _pcol(inp["diff_head_norm"][0])
    lamv = np.concatenate([np.asarray(inp[k][0], np.float32) for k in ("diff_lambda_q1", "diff_lambda_k1", "diff_lambda_q2", "diff_lambda_k2")])
    small[:, S_DLAM:S_DLAM + 256] = np.broadcast_to(lamv[None, :], (128, 256))
    sh["small"] = small
    return sh


def prepare_core(inp, b):
    xin = np.ascontiguousarray(np.concatenate([inp["ctx"][b], inp["x"][b]], axis=0), dtype=np.float32)
    cv = np.zeros((128, 16), np.float32)
    cv[:, 0::2] = _pcol(inp["c"][b])
    cv[:, 1::2] = _pcol(inp["c_ctx"])
    return {"xin": xin, "cv": cv}


_NC_CACHE = {}


def kernel(**inputs):
    inp = {k: np.asarray(v) for k, v in inputs.items()}
    B = inp["x"].shape[0]
    shared = prepare_shared(inp)
    if "full" not in _NC_CACHE:
        _NC_CACHE["full"] = build_program()
    nc = _NC_CACHE["full"]
    in_maps = []
    for b in range(B):
        m = dict(shared)
        m.update(prepare_core(inp, b))
        in_maps.append({k: m[k] for k in nc._declared_inputs})
    res = run_bass_kernel_spmd(nc, in_maps, core_ids=list(range(B)))
    out = np.stack([np.asarray(r["out"], dtype=np.float32) for r in res.results], axis=0)
    return out
```

```python
import numpy as np
from contextlib import ExitStack
import concourse.bass as bass
import concourse.mybir as mybir
from concourse.bass_utils import run_bass_kernel_spmd

F32 = mybir.dt.float32
BF16 = mybir.dt.bfloat16
ALU = mybir.AluOpType
AF = mybir.ActivationFunctionType

D = 1024
NCTX = 256
NLAT = 2048
T = NCTX + NLAT
NT = T // 128
DEPTH = 4
EPS = 1e-6
NDS = 24
BLOCKS = [(0, 256, 1), (256, 512, 0), (768, 512, 0), (1280, 512, 0), (1792, 512, 0)]

SL = 64
S_FINAL = 4 * SL
S_MHN = S_FINAL + 8
S_MGB = S_MHN + 16
S_SINK = S_MGB + 64
S_DHN = S_SINK + 16
S_DLAM = S_DHN + 8
NS = S_DLAM + 256
NCONST = 5 * 128
PSORD = [0, 2, 1, 3]
FFN_ITEMS = 1000
SWA_STOP = 9
SWA_SUB = 9
SWA_QB = (0, 1)
SWA_NQ = 2


class Buf:
    __slots__ = ("name", "w", "r")

    def __init__(self, name):
        self.name = name
        self.w = None
        self.r = {}


class KB:
    def __init__(self, nc, es):
        self.nc = nc
        self.es = es
        self.E = {"pe": nc.tensor, "act": nc.scalar, "dve": nc.vector, "pool": nc.gpsimd, "sp": nc.sync}
        self.sem = {e: es.enter_context(nc.semaphore("s_" + e)) for e in self.E}
        self.cnt = {e: 0 for e in self.E}
        self.seen = {e: {} for e in self.E}
        self.dsem = [es.enter_context(nc.semaphore("d%d" % i)) for i in range(NDS)]
        self.dcnt = [0] * NDS
        self.dnext = {"sp": 0, "pool": 0}

    def _wait(self, eng, deps):
        for (sk, val) in deps:
            if sk == eng and eng == "pe":
                continue
            if self.seen[eng].get(sk, 0) >= val:
                continue
            semh = self.sem[sk] if isinstance(sk, str) else self.dsem[sk]
            self.E[eng].wait_ge(semh, val)
            self.seen[eng][sk] = val

    @staticmethod
    def _deps(reads, writes):
        deps = []
        for b in reads:
            if b.w is not None:
                deps.append(b.w)
        for b in writes:
            if b.w is not None:
                deps.append(b.w)
            deps.extend(b.r.values())
        return deps

    def op(self, eng, fn, reads=(), writes=(), inc=True):
        self._wait(eng, self._deps(reads, writes))
        ins = fn(self.E[eng])
        if inc:
            self.cnt[eng] += 1
            ins.then_inc(self.sem[eng], 1)
            tk = (eng, self.cnt[eng])
        else:
            tk = (eng, self.cnt[eng] + 1)
        for b in reads:
            b.r[eng] = tk
        for b in writes:
            b.w = tk
            b.r = {}

    def dma(self, q, out, in_, reads=(), writes=()):
        half = NDS // 2
        i = self.dnext[q] + (0 if q == "sp" else half)
        self.dnext[q] = (self.dnext[q] + 1) % half
        deps = self._deps(reads, writes)
        if self.dcnt[i] > 0:
            deps.append((i, self.dcnt[i]))
        self._wait(q, deps)
        self.E[q].dma_start(out=out, in_=in_).then_inc(self.dsem[i], 16)
        self.dcnt[i] += 16
        tk = (i, self.dcnt[i])
        for b in reads:
            b.r[("d", i)] = tk
        for b in writes:
            b.w = tk
            b.r = {}

    def wait_all_dma(self, eng):
        self._wait(eng, [(i, self.dcnt[i]) for i in range(NDS) if self.dcnt[i] > 0])

    def barrier(self):
        deps = [(e, self.cnt[e]) for e in self.E if e != "sp" and self.cnt[e] > 0]
        deps += [(i, self.dcnt[i]) for i in range(NDS) if self.dcnt[i] > 0]
        self._wait("sp", deps)
        self.cnt["sp"] += 1
        self.E["sp"].sem_inc(self.sem["sp"], 1)
        for e in self.E:
            if e != "sp":
                self._wait(e, [("sp", self.cnt["sp"])])


def build_program(layers=(0, 1, 2, 3), do_mixer=True, do_ffn=True, final_norm=True, in_ctx=True):
    nc = bass.Bass("TRN2", target_bir_lowering=False)

    declared = []
    only = None if len(layers) == DEPTH and do_mixer and do_ffn else set()

    class _Lazy:
        def __init__(self, name, shape):
            self.name, self.shape, self.ap_ = name, list(shape), None

        def get(self):
            if self.ap_ is None:
                self.ap_ = nc.dram_tensor(self.name, self.shape, F32, kind="ExternalInput").ap()
                declared.append(self.name)
            return self.ap_

        def __getitem__(self, k):
            return self.get()[k]

    def din(name, shape):
        return _Lazy(name, shape)

    xin = din("xin", [T, D]).get()
    cv_d = din("cv", [128, 16]).get()
    small_d = din("small", [128, NS]).get()
    cst_d = din("cst", [128, NCONST]).get()
    ropec_l = din("ropec", [128, NLAT])
    ropes_l = din("ropes", [128, NLAT])
    ada_w = din("ada_w", [DEPTH, D, 6 * D])
    w1_d = din("ffn_w1", [DEPTH, D, 4 * D])
    w2_d = din("ffn_w2", [DEPTH, 4 * D, D])
    wm_d = din("wm", [2, D, 3616])
    wmo_d = din("wmo", [2, D, D])
    ws_d = din("ws", [1, D, 3328])
    wso_d = din("wso", [1, D, D])
    wd_d = din("wd", [1, D, 5120])
    wdo_d = din("wdo", [1, D, D])
    out_d = nc.dram_tensor("out", [NLAT, D], F32, kind="ExternalOutput").ap()

    with ExitStack() as es:
        kb = KB(nc, es)
        op = kb.op
        dma = kb.dma
        cnt = [0]

        def sbt(es_, shape, dt, name=None):
            cnt[0] += 1
            return es_.enter_context(nc.sbuf_tensor("%s_%d" % (name or "t", cnt[0]), list(shape), dt))

        hT = sbt(es, [128, 8, T], F32, "hT")
        uT = sbt(es, [128, 8, T], BF16, "uT")
        hb = [Buf("h%d" % i) for i in range(NT)]
        ub = [Buf("u%d" % i) for i in range(len(BLOCKS))]
        cst = sbt(es, [128, NCONST], F32, "cst")
        small = sbt(es, [128, NS], F32, "small")
        cv = sbt(es, [128, 16], F32, "cv")
        cond = sbt(es, [128, 8, 2], BF16, "cond")
        ones_bf = sbt(es, [128, 128], BF16, "ones")
        maskf_bf = sbt(es, [128, 128], BF16, "maskf")
        maskb_bf = sbt(es, [128, 128], BF16, "maskb")
        nhalf = sbt(es, [128, 128], F32, "nhalf")
        modsb = sbt(es, [128, 48, 2], F32, "modsb")
        amix = sbt(es, [128, 8, 2], F32, "amix")
        affn = sbt(es, [128, 8, 2], F32, "affn")
        cbuf = Buf("consts")
        modb = Buf("mod")
        ident = cst[:, 0:128]
        maskf = cst[:, 128:256]
        maskb = cst[:, 256:384]
        ntricf = cst[:, 384:512]
        ntricb = cst[:, 512:640]
        pb = [es.enter_context(nc.psum_tensor("pb%d" % i, [128, 512], F32)) for i in range(8)]
        pbb = [Buf("pb%d" % i) for i in range(8)]

        def hbs(t0, n):
            return [hb[i] for i in range(t0 // 128, (t0 + n) // 128)]

        dma("sp", cst[:], cst_d, writes=[cbuf])
        dma("sp", small[:], small_d, writes=[cbuf])
        dma("sp", cv[:], cv_d, writes=[cbuf])
        op("dve", lambda e: e.memset(ones_bf[:], 1.0), writes=[cbuf])
        op("dve", lambda e: e.memset(nhalf[:], -0.5), writes=[cbuf])
        op("dve", lambda e: e.tensor_copy(out=maskf_bf[:], in_=maskf), reads=[cbuf], writes=[cbuf])
        op("dve", lambda e: e.tensor_copy(out=maskb_bf[:], in_=maskb), reads=[cbuf], writes=[cbuf])
        with ExitStack() as ps:
            tmpc = sbt(ps, [128, 16], F32, "tmpc")
            tb_ = Buf("tmpc")
            op("act", lambda e: e.activation(out=tmpc[:], in_=cv[:], func=AF.Exp, scale=-1.0), reads=[cbuf], writes=[tb_])
            op("dve", lambda e: e.tensor_scalar(out=tmpc[:], in0=tmpc[:], scalar1=1.0, scalar2=None, op0=ALU.add), reads=[tb_], writes=[tb_])
            op("dve", lambda e: e.reciprocal(out=tmpc[:], in_=tmpc[:]), reads=[tb_], writes=[tb_])
            op("dve", lambda e: e.tensor_tensor(out=cond[:].rearrange("p k j -> p (k j)"), in0=tmpc[:], in1=cv[:], op=ALU.mult),
               reads=[tb_, cbuf], writes=[cbuf])
            stg = [sbt(ps, [128, D], F32, "stg") for _ in range(2)]
            stb = [Buf("stg0"), Buf("stg1")]
            for tt in range(NT):
                s = tt % 2
                dma("sp", stg[s][:], xin[tt * 128:(tt + 1) * 128, :], writes=[stb[s]])
                for half in range(2):
                    bk = (tt * 2 + half) % 4
                    for c4 in range(4):
                        c = half * 4 + c4
                        op("pe", lambda e, c=c, c4=c4, bk=bk, s=s: e.transpose(pb[bk][:, c4 * 128:(c4 + 1) * 128], stg[s][:, c * 128:(c + 1) * 128], ident),
                           reads=[stb[s], cbuf], writes=[pbb[bk]], inc=(c4 == 3))
                    eng = "act" if half == 0 else "dve"
                    if eng == "act":
                        op("act", lambda e, bk=bk, half=half, tt=tt: e.activation(out=hT[:, half * 4:half * 4 + 4, tt * 128:(tt + 1) * 128],
                                                                                 in_=pb[bk][:].rearrange("p (c n) -> p c n", c=4), func=AF.Identity),
                           reads=[pbb[bk]], writes=[hb[tt]])
                    else:
                        op("dve", lambda e, bk=bk, half=half, tt=tt: e.tensor_copy(out=hT[:, half * 4:half * 4 + 4, tt * 128:(tt + 1) * 128],
                                                                                   in_=pb[bk][:].rearrange("p (c n) -> p c n", c=4)),
                           reads=[pbb[bk]], writes=[hb[tt]])
            kb.barrier()

        def rstd_from_psum(ps_ap, out_ap, n, dim, pbuf, obuf):
            op("act", lambda e: e.activation(out=out_ap, in_=ps_ap, func=AF.Ln, scale=1.0 / dim, bias=EPS), reads=[pbuf], writes=[obuf])
            op("act", lambda e: e.activation(out=out_ap, in_=out_ap, func=AF.Exp, scale=-0.5), reads=[obuf], writes=[obuf])

        def ada_phase(L):
            with ExitStack() as ps:
                slots = [sbt(ps, [128, 8, 512], BF16, "adaw") for _ in range(3)]
                sbf = [Buf("adaw%d" % i) for i in range(3)]
                awv = ada_w[L].rearrange("(c p) n -> p c n", p=128)
                pm = pb[7][:, 0:96].rearrange("p (m j) -> p m j", j=2)
                for g in range(12):
                    s = g % 3
                    dma("pool", slots[s][:], awv[:, :, g * 512:(g + 1) * 512], writes=[sbf[s]])
                    for jj in range(4):
                        m = g * 4 + jj
                        for k in range(8):
                            op("pe", lambda e, s=s, jj=jj, k=k, m=m: e.matmul(pm[:, m, :], lhsT=slots[s][:, k, jj * 128:(jj + 1) * 128], rhs=cond[:, k, :],
                                                                           start=(k == 0), stop=(k == 7)),
                               reads=[sbf[s], cbuf], writes=[pbb[7]], inc=(k == 7))
                base = L * SL
                op("dve", lambda e: e.tensor_tensor(out=modsb[:], in0=pm, in1=small[:, base + 16:base + 64].unsqueeze(2).broadcast_to([128, 48, 2]), op=ALU.add),
                   reads=[pbb[7], cbuf], writes=[modb])
                op("dve", lambda e: e.scalar_tensor_tensor(out=amix[:], in0=modsb[:, 8:16, :], scalar=1.0,
                                                           in1=small[:, base:base + 8].unsqueeze(2).broadcast_to([128, 8, 2]), op0=ALU.add, op1=ALU.mult),
                   reads=[modb, cbuf], writes=[modb])
                op("dve", lambda e: e.scalar_tensor_tensor(out=affn[:], in0=modsb[:, 32:40, :], scalar=1.0,
                                                           in1=small[:, base + 8:base + 16].unsqueeze(2).broadcast_to([128, 8, 2]), op0=ALU.add, op1=ALU.mult),
                   reads=[modb, cbuf], writes=[modb])
                kb.barrier()

        def norm_mod(a_t, shift_off, blocks):
            with ExitStack() as ps:
                sq = [sbt(ps, [128, 8, 512], BF16, "sq") for _ in range(2)]
                sqb = [Buf("sq0"), Buf("sq1")]
                rs = [sbt(ps, [128, 512], F32, "rs") for _ in range(2)]
                rsb = [Buf("rs0"), Buf("rs1")]
                tmp = [sbt(ps, [128, 4, 512], F32, "nt") for _ in range(2)]
                tmb = [Buf("nt0"), Buf("nt1")]
                for bi, (t0, n, j) in enumerate(blocks):
                    s = bi % 2
                    bk = 6 + s
                    bidx = BLOCKS.index((t0, n, j))
                    op("act", lambda e, s=s: e.activation(out=sq[s][:, :, 0:n], in_=hT[:, :, t0:t0 + n], func=AF.Square), reads=hbs(t0, n), writes=[sqb[s]])
                    for c in range(8):
                        op("pe", lambda e, s=s, c=c, bk=bk: e.matmul(pb[bk][:, 0:n], lhsT=ones_bf[:], rhs=sq[s][:, c, 0:n], start=(c == 0), stop=(c == 7)),
                           reads=[sqb[s], cbuf], writes=[pbb[bk]], inc=(c == 7))
                    rstd_from_psum(pb[bk][:, 0:n], rs[s][:, 0:n], n, float(D), pbb[bk], rsb[s])
                    for half in range(2):
                        op("dve", lambda e, s=s, half=half: e.tensor_tensor(out=tmp[half][:, :, 0:n], in0=hT[:, half * 4:half * 4 + 4, t0:t0 + n],
                                                                            in1=rs[s][:, 0:n].unsqueeze(1).broadcast_to([128, 4, n]), op=ALU.mult),
                           reads=hbs(t0, n) + [rsb[s]], writes=[tmb[half]])
                        for c4 in range(4):
                            c = half * 4 + c4
                            op("act", lambda e, half=half, c4=c4, c=c: e.activation(out=uT[:, c, t0:t0 + n], in_=tmp[half][:, c4, 0:n], func=AF.Identity,
                                                                                    scale=a_t[:, c, j:j + 1], bias=modsb[:, shift_off + c, j:j + 1]),
                               reads=[tmb[half], modb], writes=[ub[bidx]])
                kb.barrier()

        def ffn_phase(L, blocks):
            with ExitStack() as ps:
                w1s = [sbt(ps, [128, 8, 512], BF16, "w1s") for _ in range(3)]
                w2s = [sbt(ps, [128, 4, D], BF16, "w2s") for _ in range(3)]
                wb = [Buf("ffw%d" % i) for i in range(3)]
                hid = [sbt(ps, [128, 4, 512], BF16, "hid") for _ in range(2)]
                hib = [Buf("hid0"), Buf("hid1")]
                sqv = [sbt(ps, [128, 512], F32, "sqv") for _ in range(2)]
                sqvb = [Buf("sqv0"), Buf("sqv1")]
                w1v = w1_d[L].rearrange("(c p) n -> p c n", p=128)
                w2v = w2_d[L].rearrange("(c p) n -> p c n", p=128)
                items = [(e8, blk) for e8 in range(8) for blk in blocks][:FFN_ITEMS]
                loaded = set()

                def load(e8):
                    if e8 in loaded or e8 >= 8:
                        return
                    loaded.add(e8)
                    s = e8 % 3
                    dma("pool", w1s[s][:], w1v[:, :, e8 * 512:(e8 + 1) * 512], writes=[wb[s]])
                    dma("pool", w2s[s][:], w2v[:, e8 * 4:(e8 + 1) * 4, :], writes=[wb[s]])

                sqi = [0]

                def stage_h(i):
                    e8, (t0, n, j) = items[i]
                    s = e8 % 3
                    bidx = BLOCKS.index((t0, n, j))
                    for jj in range(4):
                        for k in range(8):
                            op("pe", lambda e, jj=jj, k=k: e.matmul(pb[jj][:, 0:n], lhsT=w1s[s][:, k, jj * 128:(jj + 1) * 128], rhs=uT[:, k, t0:t0 + n],
                                                                    start=(k == 0), stop=(k == 7)),
                               reads=[wb[s], ub[bidx]], writes=[pbb[jj]], inc=(k == 7))
                        q = sqi[0] % 2
                        sqi[0] += 1
                        op("act", lambda e, jj=jj, q=q: e.activation(out=sqv[q][:, 0:n], in_=pb[jj][:, 0:n], func=AF.Square), reads=[pbb[jj]], writes=[sqvb[q]])
                        op("dve", lambda e, jj=jj, q=q: e.scalar_tensor_tensor(out=hid[i % 2][:, jj, 0:n], in0=pb[jj][:, 0:n], scalar=0.0, in1=sqv[q][:, 0:n],
                                                                               op0=ALU.is_gt, op1=ALU.mult),
                           reads=[pbb[jj], sqvb[q]], writes=[hib[i % 2]])

                oi = [0]

                def stage_o(i):
                    e8, (t0, n, j) = items[i]
                    s = e8 % 3
                    for f in range(8):
                        bk = 4 + oi[0] % 3
                        oi[0] += 1
                        for jj in range(4):
                            op("pe", lambda e, jj=jj, f=f, bk=bk: e.matmul(pb[bk][:, 0:n], lhsT=w2s[s][:, jj, f * 128:(f + 1) * 128], rhs=hid[i % 2][:, jj, 0:n],
                                                                           start=(jj == 0), stop=(jj == 3)),
                               reads=[wb[s], hib[i % 2]], writes=[pbb[bk]], inc=(jj == 3))
                        op("dve", lambda e, f=f, bk=bk: e.scalar_tensor_tensor(out=hT[:, f, t0:t0 + n], in0=pb[bk][:, 0:n], scalar=modsb[:, 40 + f, j:j + 1],
                                                                               in1=hT[:, f, t0:t0 + n], op0=ALU.mult, op1=ALU.add),
                           reads=[pbb[bk], modb] + hbs(t0, n), writes=hbs(t0, n))

                load(0)
                load(1)
                stage_h(0)
                for i in range(len(items)):
                    if i + 1 < len(items):
                        load(items[i + 1][0] + 1)
                        stage_h(i + 1)
                    stage_o(i)
                kb.barrier()

        def outproj_acc(pbank, lhs_list, rhs_list, t0, n, j, reads, stop_bufs=None):
            for f in range(8):
                bk = pbank[f % len(pbank)]
                nmm = len(lhs_list)
                for i in range(nmm):
                    op("pe", lambda e, i=i, f=f, bk=bk: e.matmul(pb[bk][:, 0:n], lhsT=lhs_list[i][:, f * 128:(f + 1) * 128], rhs=rhs_list[i],
                                                                 start=(i == 0), stop=(i == nmm - 1)),
                       reads=reads, writes=[pbb[bk]], inc=(i == nmm - 1))
                op("dve", lambda e, f=f, bk=bk: e.scalar_tensor_tensor(out=hT[:, f, t0:t0 + n], in0=pb[bk][:, 0:n], scalar=modsb[:, 16 + f, j:j + 1],
                                                                       in1=hT[:, f, t0:t0 + n], op0=ALU.mult, op1=ALU.add),
                   reads=[pbb[bk], modb] + hbs(t0, n), writes=hbs(t0, n))

        def mlstm_phase(L, slot, need_ctx):
            out_blocks = BLOCKS if need_ctx else BLOCKS[1:]
            with ExitStack() as ps:
                G = sbt(ps, [128, NT, 32], F32, "G")
                SP = sbt(ps, [128, NT, 16], F32, "SP")
                LI = sbt(ps, [128, NT, 16], F32, "LI")
                KS = sbt(ps, [128, NT, 16], F32, "KS")
                KSb = sbt(ps, [128, NT, 16], BF16, "KSb")
                EBH = sbt(ps, [128, NT, 16], F32, "EBH")
                EBL = sbt(ps, [128, NT, 16], F32, "EBL")
                gbuf = Buf("gates")
                wg = sbt(ps, [128, 8, 32], BF16, "wg")
                wgb = Buf("wg")
                wmv = wm_d[slot].rearrange("(c p) n -> p c n", p=128)
                dma("pool", wg[:], wmv[:, :, 3584:3616], writes=[wgb])
                gbias = small[:, S_MGB + slot * 32:S_MGB + slot * 32 + 32]
                for tt in range(NT):
                    bk = 0 if tt < 16 else 1
                    o = (tt % 16) * 32
                    for k in range(8):
                        op("pe", lambda e, tt=tt, k=k, bk=bk, o=o: e.matmul(pb[bk][:, o:o + 32], lhsT=uT[:, k, tt * 128:(tt + 1) * 128], rhs=wg[:, k, :],
                                                                            start=(k == 0), stop=(k == 7)),
                           reads=[wgb] + ub, writes=[pbb[bk]], inc=(k == 7))
                op("dve", lambda e: e.tensor_tensor(out=G[:, 0:16, :], in0=pb[0][:].rearrange("p (t g) -> p t g", g=32),
                                                    in1=gbias.unsqueeze(1).broadcast_to([128, 16, 32]), op=ALU.add), reads=[pbb[0], cbuf], writes=[gbuf])
                op("dve", lambda e: e.tensor_tensor(out=G[:, 16:18, :], in0=pb[1][:, 0:64].rearrange("p (t g) -> p t g", g=32),
                                                    in1=gbias.unsqueeze(1).broadcast_to([128, 2, 32]), op=ALU.add), reads=[pbb[1], cbuf], writes=[gbuf])
                for d in range(2):
                    op("dve", lambda e, d=d: e.tensor_copy(out=LI[:, :, d * 8:d * 8 + 8], in_=G[:, :, d * 16:d * 16 + 8]), reads=[gbuf], writes=[gbuf])
                    op("act", lambda e, d=d: e.activation(out=SP[:, :, d * 8:d * 8 + 8], in_=G[:, :, d * 16 + 8:d * 16 + 16], func=AF.Exp, scale=-1.0),
                       reads=[gbuf], writes=[gbuf])
                op("act", lambda e: e.activation(out=SP[:], in_=SP[:], func=AF.Ln, scale=1.0, bias=1.0), reads=[gbuf], writes=[gbuf])
                for tt in range(NT):
                    bk = 2 if tt < 16 else 3
                    o = (tt % 16) * 32
                    op("pe", lambda e, tt=tt, bk=bk, o=o: e.matmul(pb[bk][:, o:o + 8], lhsT=ntricf, rhs=SP[:, tt, 0:8], start=True, stop=True),
                       reads=[gbuf, cbuf], writes=[pbb[bk]], inc=False)
                    op("pe", lambda e, tt=tt, bk=bk, o=o: e.matmul(pb[bk][:, o + 8:o + 16], lhsT=ntricb, rhs=SP[:, tt, 8:16], start=True, stop=True),
                       reads=[gbuf, cbuf], writes=[pbb[bk]], inc=False)
                    op("pe", lambda e, tt=tt, bk=bk, o=o: e.matmul(pb[bk][:, o + 16:o + 32], lhsT=nhalf[:], rhs=SP[:, tt, :], start=True, stop=True),
                       reads=[gbuf, cbuf], writes=[pbb[bk]], inc=True)
                for (bk, a, b_) in ((2, 0, 16), (3, 16, 18)):
                    nn = b_ - a
                    pv = pb[bk][:, 0:nn * 32].rearrange("p (t g) -> p t g", g=32)
                    op("dve", lambda e, pv=pv, a=a, b_=b_: e.tensor_tensor(out=KS[:, a:b_, :], in0=LI[:, a:b_, :], in1=pv[:, :, 0:16], op=ALU.subtract),
                       reads=[pbb[bk], gbuf], writes=[gbuf])
                    op("act", lambda e, pv=pv, a=a, b_=b_: e.activation(out=EBH[:, a:b_, :], in_=pv[:, :, 16:32], func=AF.Exp), reads=[pbb[bk]], writes=[gbuf])
                    op("act", lambda e, pv=pv, a=a, b_=b_: e.activation(out=EBL[:, a:b_, :], in_=pv[:, :, 16:32], func=AF.Exp, scale=2.0), reads=[pbb[bk]], writes=[gbuf])
                op("act", lambda e: e.activation(out=KS[:], in_=KS[:], func=AF.Exp), reads=[gbuf], writes=[gbuf])
                op("dve", lambda e: e.tensor_copy(out=KSb[:], in_=KS[:]), reads=[gbuf], writes=[gbuf])

                wh = [sbt(ps, [128, 8, 448], BF16, "wh") for _ in range(2)]
                whb = [Buf("wh0"), Buf("wh1")]
                wo = [sbt(ps, [128, D], BF16, "wo") for _ in range(2)]
                wob = [Buf("wo0"), Buf("wo1")]
                Qb = [sbt(ps, [64, T], BF16, "Qb") for _ in range(2)]
                qbb = [Buf("Qbf"), Buf("Qbb")]
                KT = sbt(ps, [64, T], BF16, "KT")
                ktb = Buf("KT")
                Ktok = sbt(ps, [128, NT, 64], BF16, "Ktok")
                kkb = Buf("Ktok")
                Va = [sbt(ps, [128, NT, 128], BF16, "Va") for _ in range(2)]
                vab = [Buf("Vaf"), Buf("Vab")]
                HS = sbt(ps, [128, T], F32, "HS")
                hsb = [Buf("hs%d" % i) for i in range(NT)]
                Cst = [sbt(ps, [64, 256], F32, "Cst") for _ in range(2)]
                csb = [Buf("Cf"), Buf("Cb")]
                Cbf = [sbt(ps, [64, 256], BF16, "Cbf") for _ in range(2)]
                cbb = [Buf("Cbf_f"), Buf("Cbf_b")]
                ctmp = [sbt(ps, [64, 256], F32, "ctmp") for _ in range(2)]
                ctb = [Buf("ct0"), Buf("ct1")]
                ebt = [sbt(ps, [64, 512], F32, "ebt") for _ in range(2)]
                ebb = [Buf("eb0"), Buf("eb1")]
                Pm = [sbt(ps, [128, 128], BF16, "Pm") for _ in range(4)]
                pmb = [Buf("Pm%d" % i) for i in range(4)]
                adn = [sbt(ps, [128, 128], F32, "adn") for _ in range(2)]
                adb = [Buf("ad0"), Buf("ad1")]
                htm = [sbt(ps, [128, 128], F32, "htm") for _ in range(2)]
                htb = [Buf("ht0"), Buf("ht1")]
                sqh = sbt(ps, [128, 512], BF16, "sqh")
                sqhb = Buf("sqh")
                rsh = sbt(ps, [128, 512], F32, "rsh")
                rshb = Buf("rsh")
                sig = sbt(ps, [128, 512], F32, "sig")
                sigb = Buf("sig")
                hn = sbt(ps, [128, 512], F32, "hn")
                hnb = Buf("hn")
                ao = [sbt(ps, [128, 512], BF16, "ao") for _ in range(2)]
                aob = [Buf("ao0"), Buf("ao1")]
                wov = wmo_d[slot]

                def load_head(h):
                    s = h % 2
                    dma("pool", wh[s][:], wmv[:, :, h * 448:(h + 1) * 448], writes=[whb[s]])
                    dma("pool", wo[s][:], wov[h * 128:(h + 1) * 128, :], writes=[wob[s]])

                order = [list(range(NT)), [1, 0] + list(range(NT - 1, 1, -1))]
                load_head(0)
                for h in range(8):
                    s = h % 2
                    if h + 1 < 8:
                        load_head(h + 1)
                    w = wh[s]
                    for (t0, n, j) in BLOCKS:
                        bidx = BLOCKS.index((t0, n, j))
                        for d in range(2):
                            ntr = ntricf if d == 0 else ntricb
                            for ti in range(n // 128):
                                tt = t0 // 128 + ti
                                op("pe", lambda e, d=d, tt=tt, ti=ti, ntr=ntr: e.matmul(pb[d][0:64, ti * 128:(ti + 1) * 128],
                                                                                         lhsT=SP[:, tt, d * 8 + h:d * 8 + h + 1].broadcast_to([128, 64]), rhs=ntr,
                                                                                         start=True, stop=True),
                                   reads=[gbuf, cbuf], writes=[pbb[d]], inc=(ti == n // 128 - 1))
                            op("act", lambda e, d=d: e.activation(out=ebt[d][:, 0:n], in_=pb[d][0:64, 0:n], func=AF.Exp), reads=[pbb[d]], writes=[ebb[d]])
                        for k in range(8):
                            op("pe", lambda e, k=k: e.matmul(pb[2][0:64, 0:n], lhsT=w[:, k, 0:64], rhs=uT[:, k, t0:t0 + n], start=(k == 0), stop=(k == 7)),
                               reads=[whb[s], ub[bidx]], writes=[pbb[2]], inc=(k == 7))
                        for d in range(2):
                            op("dve", lambda e, d=d: e.tensor_tensor(out=Qb[d][:, t0:t0 + n], in0=pb[2][0:64, 0:n], in1=ebt[d][:, 0:n], op=ALU.mult),
                               reads=[pbb[2], ebb[d]], writes=[qbb[d]])
                        for k in range(8):
                            op("pe", lambda e, k=k: e.matmul(pb[3][0:64, 0:n], lhsT=w[:, k, 64:128], rhs=uT[:, k, t0:t0 + n], start=(k == 0), stop=(k == 7)),
                               reads=[whb[s], ub[bidx]], writes=[pbb[3]], inc=(k == 7))
                        op("act", lambda e: e.activation(out=KT[:, t0:t0 + n], in_=pb[3][0:64, 0:n], func=AF.Identity, scale=0.125), reads=[pbb[3]], writes=[ktb])
                    for tt in range(NT):
                        bk = 4 + tt % 2
                        bidx = 0 if tt < 2 else 1 + (tt - 2) // 4
                        for k in range(8):
                            op("pe", lambda e, k=k, tt=tt, bk=bk: e.matmul(pb[bk][:, 0:192], lhsT=uT[:, k, tt * 128:(tt + 1) * 128], rhs=w[:, k, 128:320],
                                                                           start=(k == 0), stop=(k == 7)),
                               reads=[whb[s], ub[bidx]], writes=[pbb[bk]], inc=(k == 7))
                        op("act", lambda e, tt=tt, bk=bk: e.activation(out=Ktok[:, tt, :], in_=pb[bk][:, 0:64], func=AF.Identity, scale=0.125), reads=[pbb[bk]], writes=[kkb])
                        for d in range(2):
                            op("dve", lambda e, tt=tt, bk=bk, d=d: e.tensor_scalar(out=Va[d][:, tt, :], in0=pb[bk][:, 64:192], scalar1=KS[:, tt, d * 8 + h:d * 8 + h + 1],
                                                                                    scalar2=None, op0=ALU.mult),
                               reads=[pbb[bk], gbuf], writes=[vab[d]])
                    op("dve", lambda e: e.memset(HS[:], 0.0), writes=hsb)
                    for d in range(2):
                        op("dve", lambda e, d=d: e.memset(Cst[d][:], 0.0), writes=[csb[d]])
                    pmi = 0
                    for step in range(NT):
                        for d in range(2):
                            tt = order[d][step]
                            col = d * 8 + h
                            tsl = slice(tt * 128, (tt + 1) * 128)
                            ksl = KSb[:, tt, col:col + 1].broadcast_to([128, 128])
                            need_out = need_ctx or tt >= 2
                            if step > 0 and need_out:
                                op("dve", lambda e, d=d, tt=tt, col=col: e.tensor_scalar(out=Cbf[d][:], in0=Cst[d][:], scalar1=EBH[0:64, tt, col:col + 1], scalar2=None, op0=ALU.mult),
                                   reads=[csb[d], gbuf], writes=[cbb[d]])
                            if need_out:
                                bS = d
                                bN = 2 + d
                                bD = 4 + d
                                p_ = pmi % 4
                                pmi += 1
                                op("pe", lambda e, d=d, tsl=tsl, bS=bS: e.matmul(pb[bS][:, 0:128], lhsT=KT[:, tsl], rhs=Qb[d][:, tsl], start=True, stop=True),
                                   reads=[ktb, qbb[d]], writes=[pbb[bS]])
                                mk = maskf_bf if d == 0 else maskb_bf
                                op("dve", lambda e, p_=p_, bS=bS, mk=mk: e.tensor_tensor(out=Pm[p_][:], in0=pb[bS][:, 0:128], in1=mk[:], op=ALU.mult),
                                   reads=[pbb[bS], cbuf], writes=[pmb[p_]])
                                last = (step == 0)
                                op("pe", lambda e, d=d, tt=tt, p_=p_, bN=bN, last=last: e.matmul(pb[bN][:, 0:128], lhsT=Va[d][:, tt, :], rhs=Pm[p_][:], start=True, stop=last),
                                   reads=[vab[d], pmb[p_]], writes=[pbb[bN]], inc=last)
                                if not last:
                                    op("pe", lambda e, d=d, tsl=tsl, bN=bN: e.matmul(pb[bN][:, 0:128], lhsT=Cbf[d][:, 0:128], rhs=Qb[d][:, tsl], start=False, stop=True),
                                       reads=[cbb[d], qbb[d]], writes=[pbb[bN]])
                                op("pe", lambda e, ksl=ksl, p_=p_, bD=bD, last=last: e.matmul(pb[bD][:, 0:128], lhsT=ksl, rhs=Pm[p_][:], start=True, stop=last),
                                   reads=[gbuf, pmb[p_]], writes=[pbb[bD]], inc=last)
                                if not last:
                                    op("pe", lambda e, d=d, tsl=tsl, bD=bD: e.matmul(pb[bD][:, 0:128], lhsT=Cbf[d][:, 128:256], rhs=Qb[d][:, tsl], start=False, stop=True),
                                       reads=[cbb[d], qbb[d]], writes=[pbb[bD]])
                                op("act", lambda e, d=d, bD=bD: e.activation(out=adn[d][:], in_=pb[bD][:, 0:128], func=AF.Abs), reads=[pbb[bD]], writes=[adb[d]])
                                op("dve", lambda e, d=d: e.tensor_scalar(out=adn[d][:], in0=adn[d][:], scalar1=1.0, scalar2=None, op0=ALU.max), reads=[adb[d]], writes=[adb[d]])
                                op("dve", lambda e, d=d: e.reciprocal(out=adn[d][:], in_=adn[d][:]), reads=[adb[d]], writes=[adb[d]])
                                op("dve", lambda e, d=d, bN=bN: e.tensor_tensor(out=htm[d][:], in0=pb[bN][:, 0:128], in1=adn[d][:], op=ALU.mult),
                                   reads=[pbb[bN], adb[d]], writes=[htb[d]])
                                op("dve", lambda e, d=d, tsl=tsl: e.tensor_tensor(out=HS[:, tsl], in0=HS[:, tsl], in1=htm[d][:], op=ALU.add),
                                   reads=[htb[d], hsb[tt]], writes=[hsb[tt]])
                            if step < NT - 1:
                                bK = 6 + d
                                op("pe", lambda e, d=d, tt=tt, bK=bK: e.matmul(pb[bK][0:64, 0:128], lhsT=Ktok[:, tt, :], rhs=Va[d][:, tt, :], start=True, stop=True),
                                   reads=[kkb, vab[d]], writes=[pbb[bK]], inc=False)
                                op("pe", lambda e, tt=tt, ksl=ksl, bK=bK: e.matmul(pb[bK][0:64, 128:256], lhsT=Ktok[:, tt, :], rhs=ksl, start=True, stop=True),
                                   reads=[kkb, gbuf], writes=[pbb[bK]])
                                op("dve", lambda e, d=d, tt=tt, col=col, bK=bK: e.tensor_scalar(out=ctmp[d][:], in0=pb[bK][0:64, 0:256], scalar1=EBH[0:64, tt, col:col + 1],
                                                                                                scalar2=None, op0=ALU.mult),
                                   reads=[pbb[bK], gbuf], writes=[ctb[d]])
                                op("dve", lambda e, d=d, tt=tt, col=col: e.scalar_tensor_tensor(out=Cst[d][:], in0=Cst[d][:], scalar=EBL[0:64, tt, col:col + 1], in1=ctmp[d][:],
                                                                                                 op0=ALU.mult, op1=ALU.add),
                                   reads=[ctb[d], gbuf, csb[d]], writes=[csb[d]])
                    hnw = small[:, S_MHN + slot * 8 + h:S_MHN + slot * 8 + h + 1]
                    for bi, (t0, n, j) in enumerate(out_blocks):
                        bidx = BLOCKS.index((t0, n, j))
                        hsl = [hsb[i] for i in range(t0 // 128, (t0 + n) // 128)]
                        op("act", lambda e: e.activation(out=sqh[:, 0:n], in_=HS[:, t0:t0 + n], func=AF.Square), reads=hsl, writes=[sqhb])
                        op("pe", lambda e: e.matmul(pb[0][:, 0:n], lhsT=ones_bf[:], rhs=sqh[:, 0:n], start=True, stop=True), reads=[sqhb, cbuf], writes=[pbb[0]])
                        rstd_from_psum(pb[0][:, 0:n], rsh[:, 0:n], n, 128.0, pbb[0], rshb)
                        for k in range(8):
                            op("pe", lambda e, k=k: e.matmul(pb[1][:, 0:n], lhsT=w[:, k, 320:448], rhs=uT[:, k, t0:t0 + n], start=(k == 0), stop=(k == 7)),
                               reads=[whb[s], ub[bidx]], writes=[pbb[1]], inc=(k == 7))
                        op("act", lambda e: e.activation(out=sig[:, 0:n], in_=pb[1][:, 0:n], func=AF.Exp, scale=-1.0), reads=[pbb[1]], writes=[sigb])
                        op("dve", lambda e: e.tensor_scalar(out=sig[:, 0:n], in0=sig[:, 0:n], scalar1=1.0, scalar2=None, op0=ALU.add), reads=[sigb], writes=[sigb])
                        op("dve", lambda e: e.reciprocal(out=sig[:, 0:n], in_=sig[:, 0:n]), reads=[sigb], writes=[sigb])
                        op("dve", lambda e: e.scalar_tensor_tensor(out=hn[:, 0:n], in0=HS[:, t0:t0 + n], scalar=hnw, in1=rsh[:, 0:n], op0=ALU.mult, op1=ALU.mult),
                           reads=hsl + [rshb, cbuf], writes=[hnb])
                        a_ = bi % 2
                        op("dve", lambda e, a_=a_: e.tensor_tensor(out=ao[a_][:, 0:n], in0=hn[:, 0:n], in1=sig[:, 0:n], op=ALU.mult), reads=[hnb, sigb], writes=[aob[a_]])
                        outproj_acc([6, 7], [wo[s]], [ao[a_][:, 0:n]], t0, n, j, [wob[s], aob[a_]])
                kb.barrier()

        def proj_rope(ps_res, w, c0, cr0, dst_ap_fn, dbuf, wbuf, banks, tmps, tmpb, rope_tiles, scale_ctx_copy=True):
            for (t0, n, j) in BLOCKS:
                bidx = BLOCKS.index((t0, n, j))
                b0, b1 = banks
                for k in range(8):
                    op("pe", lambda e, k=k: e.matmul(pb[b0][:, 0:n], lhsT=w[:, k, c0:c0 + 128], rhs=uT[:, k, t0:t0 + n], start=(k == 0), stop=(k == 7)),
                       reads=[wbuf, ub[bidx]], writes=[pbb[b0]], inc=(k == 7))
                if j == 1:
                    op("act", lambda e: e.activation(out=dst_ap_fn(t0, n), in_=pb[b0][:, 0:n], func=AF.Identity), reads=[pbb[b0]], writes=[dbuf])
                    continue
                for k in range(8):
                    op("pe", lambda e, k=k: e.matmul(pb[b1][:, 0:n], lhsT=w[:, k, cr0:cr0 + 128], rhs=uT[:, k, t0:t0 + n], start=(k == 0), stop=(k == 7)),
                       reads=[wbuf, ub[bidx]], writes=[pbb[b1]], inc=(k == 7))
                rc, rs_, rb = rope_tiles(t0)
                op("dve", lambda e: e.tensor_tensor(out=tmps[0][:, 0:n], in0=pb[b0][:, 0:n], in1=rc, op=ALU.mult), reads=[pbb[b0], rb], writes=[tmpb[0]])
                op("dve", lambda e: e.tensor_tensor(out=tmps[1][:, 0:n], in0=pb[b1][:, 0:n], in1=rs_, op=ALU.mult), reads=[pbb[b1], rb], writes=[tmpb[1]])
                op("dve", lambda e: e.tensor_tensor(out=dst_ap_fn(t0, n), in0=tmps[0][:, 0:n], in1=tmps[1][:, 0:n], op=ALU.add), reads=[tmpb[0], tmpb[1]], writes=[dbuf])

        def load_rope(ps):
            rc = sbt(ps, [128, NLAT], F32, "ropec")
            rs_ = sbt(ps, [128, NLAT], F32, "ropes")
            rb = Buf("rope")
            dma("sp", rc[:], ropec_l.get(), writes=[rb])
            dma("sp", rs_[:], ropes_l.get(), writes=[rb])

            def tiles(t0):
                l0 = t0 - NCTX
                return rc[:, l0:l0 + 512], rs_[:, l0:l0 + 512], rb
            return tiles

        def swa_phase(L, need_ctx):
            with ExitStack() as ps:
                rope_tiles = load_rope(ps)
                wsv = ws_d[0].rearrange("(c p) n -> p c n", p=128)
                wov = wso_d[0].rearrange("(h p) n -> p h n", p=64)
                wg_ = [sbt(ps, [128, 8, 832], BF16, "wsg") for _ in range(1)]
                wgb = [Buf("wsg0"), Buf("wsg1")]
                wo = [sbt(ps, [64, 4, D], BF16, "wso") for _ in range(1)]
                wob = [Buf("wso0"), Buf("wso1")]
                QT = sbt(ps, [128, 2, T], BF16, "QT")
                qtb = Buf("QT")
                KT = sbt(ps, [128, T], BF16, "KT")
                ktb = Buf("KT")
                Va = sbt(ps, [128, NT, 128], BF16, "Va")
                vab = Buf("Va")
                aog = sbt(ps, [64, 4, T], BF16, "aog")
                aob = [Buf("aog%d" % i) for i in range(NT)]
                tmps = [sbt(ps, [128, 512], F32, "rt") for _ in range(2)]
                tmpb = [Buf("rt0"), Buf("rt1")]
                Pt = [sbt(ps, [128, 512], BF16, "Pt") for _ in range(3)]
                ptb = [Buf("Pt%d" % i) for i in range(3)]
                dn = sbt(ps, [128, 512], F32, "dn")
                dnb = Buf("dn")
                ES = sbt(ps, [128, 16], F32, "ES")
                esb = Buf("ES")
                op("act", lambda e: e.activation(out=ES[:], in_=small[:, S_SINK:S_SINK + 16], func=AF.Exp), reads=[cbuf], writes=[esb])
                op("dve", lambda e: e.memset(Va[:, :, 64:128], 1.0), writes=[vab])

                def load_g(g):
                    s = 0
                    dma("pool", wg_[s][:], wsv[:, :, g * 832:(g + 1) * 832], writes=[wgb[s]])
                    dma("pool", wo[s][:], wov[:, g * 4:(g + 1) * 4, :], writes=[wob[s]])

                pti = 0
                for g in range(4):
                    s = 0
                    load_g(g)
                    w = wg_[s]
                    if SWA_STOP < 1:
                        continue
                    proj_rope(ps, w, 512, 640, lambda t0, n: KT[:, t0:t0 + n], ktb, wgb[s], (0, 1), tmps, tmpb, rope_tiles)
                    for jq in range(SWA_NQ):
                        if SWA_SUB < 1:
                            continue
                        proj_rope(ps, w, jq * 128, 256 + jq * 128, lambda t0, n, jq=jq: QT[:, jq, t0:t0 + n], qtb, wgb[s], SWA_QB, tmps, tmpb, rope_tiles)
                    for tt in range(NT):
                        if SWA_SUB < 2:
                            continue
                        bk = 4 + (tt // 8) % 2
                        o = (tt % 8) * 64
                        bidx = 0 if tt < 2 else 1 + (tt - 2) // 4
                        for k in range(8):
                            op("pe", lambda e, k=k, tt=tt, bk=bk, o=o: e.matmul(pb[bk][:, o:o + 64], lhsT=uT[:, k, tt * 128:(tt + 1) * 128], rhs=w[:, k, 768:832],
                                                                                start=(k == 0), stop=(k == 7)),
                               reads=[wgb[s], ub[bidx]], writes=[pbb[bk]], inc=(k == 7))
                        if tt % 8 == 7 or tt == NT - 1:
                            a = (tt // 8) * 8
                            nn = tt + 1 - a
                            op("act", lambda e, a=a, nn=nn, bk=bk: e.activation(out=Va[:, a:a + nn, 0:64], in_=pb[bk][:, 0:nn * 64].rearrange("p (t d) -> p t d", d=64),
                                                                                func=AF.Identity), reads=[pbb[bk]], writes=[vab])
                    aitems = []
                    for qt in range(NT):
                        if SWA_STOP < 2:
                            continue
                        if qt < 2:
                            if not need_ctx:
                                continue
                            kts = [0, 1]
                        else:
                            kts = [0, 1] + [kt for kt in (qt - 1, qt, qt + 1) if 2 <= kt < NT]
                        for ki, kt in enumerate(kts):
                            aitems.append((qt, kt, ki, len(kts)))

                    def s_stage(i):
                        qt, kt, ki, nk = aitems[i]
                        qsl = slice(qt * 128, (qt + 1) * 128)
                        ksl = slice(kt * 128, (kt + 1) * 128)
                        bS = (i % 2) * 2
                        p_ = i % 3
                        op("pe", lambda e: e.matmul(pb[bS][:, 0:256], lhsT=KT[0:64, ksl], rhs=QT[0:64, :, qsl], start=True, stop=True),
                           reads=[ktb, qtb], writes=[pbb[bS]])
                        op("pe", lambda e: e.matmul(pb[bS + 1][:, 0:256], lhsT=KT[64:128, ksl], rhs=QT[64:128, :, qsl], start=True, stop=True),
                           reads=[ktb, qtb], writes=[pbb[bS + 1]])
                        op("act", lambda e: e.activation(out=Pt[p_][:, 0:256], in_=pb[bS][:, 0:256], func=AF.Exp, scale=0.125), reads=[pbb[bS]], writes=[ptb[p_]])
                        op("act", lambda e: e.activation(out=Pt[p_][:, 256:512], in_=pb[bS + 1][:, 0:256], func=AF.Exp, scale=0.125), reads=[pbb[bS + 1]], writes=[ptb[p_]])
                        if qt >= 2 and kt >= 2 and kt != qt:
                            mk = maskb_bf if kt == qt - 1 else maskf_bf
                            op("dve", lambda e: e.tensor_tensor(out=Pt[p_][:].rearrange("p (h q) -> p h q", h=4), in0=Pt[p_][:].rearrange("p (h q) -> p h q", h=4),
                                                                in1=mk[:].unsqueeze(1).broadcast_to([128, 4, 128]), op=ALU.mult),
                               reads=[ptb[p_], cbuf], writes=[ptb[p_]])

                    def pv_stage(i):
                        qt, kt, ki, nk = aitems[i]
                        qsl = slice(qt * 128, (qt + 1) * 128)
                        p_ = i % 3
                        bO = 6 + qt % 2
                        op("pe", lambda e: e.matmul(pb[bO][:], lhsT=Va[:, kt, :], rhs=Pt[p_][:], start=(ki == 0), stop=(ki == nk - 1)),
                           reads=[vab, ptb[p_]], writes=[pbb[bO]], inc=(ki == nk - 1))
                        if ki == nk - 1:
                            op("dve", lambda e: e.tensor_tensor(out=dn[64:128, :].rearrange("p (h q) -> p h q", h=4), in0=pb[bO][64:128, :].rearrange("p (h q) -> p h q", h=4),
                                                                in1=ES[64:128, g * 4:(g + 1) * 4].unsqueeze(2).broadcast_to([64, 4, 128]), op=ALU.add),
                               reads=[pbb[bO], esb], writes=[dnb])
                            op("dve", lambda e: e.reciprocal(out=dn[64:128, :], in_=dn[64:128, :]), reads=[dnb], writes=[dnb])
                            op("dve", lambda e: e.tensor_tensor(out=aog[:, :, qsl], in0=pb[bO][0:64, :].rearrange("p (h q) -> p h q", h=4),
                                                                in1=dn[64:128, :].rearrange("p (h q) -> p h q", h=4), op=ALU.mult),
                               reads=[pbb[bO], dnb], writes=[aob[qt]])

                    if aitems:
                        s_stage(0)
                    for i in range(len(aitems)):
                        if i + 1 < len(aitems):
                            s_stage(i + 1)
                        pv_stage(i)
                    for (t0, n, j) in (BLOCKS if need_ctx else BLOCKS[1:]):
                        if SWA_STOP < 3:
                            continue
                        ab = [aob[i] for i in range(t0 // 128, (t0 + n) // 128)]
                        outproj_acc([0, 1, 2, 3], [wo[s][:, PSORD[pos], :] for pos in range(4)], [aog[:, pos, t0:t0 + n] for pos in range(4)], t0, n, j, [wob[s]] + ab)
                kb.barrier()

        def diff_phase(L, need_ctx):
            lam_init = 0.8 - 0.6 * float(np.exp(-0.3 * L))
            with ExitStack() as ps:
                rope_tiles = load_rope(ps)
                wdv = wd_d[0].rearrange("(c p) n -> p c n", p=128)
                wov = wdo_d[0]
                wh = [sbt(ps, [128, 8, 640], BF16, "wdh") for _ in range(2)]
                whb = [Buf("wdh0"), Buf("wdh1")]
                wo = [sbt(ps, [128, D], BF16, "wdo") for _ in range(2)]
                wob = [Buf("wdo0"), Buf("wdo1")]
                QT = sbt(ps, [128, T], BF16, "QT")
                qtb = Buf("QT")
                KT = sbt(ps, [128, T], BF16, "KT")
                ktb = Buf("KT")
                Vt = sbt(ps, [128, NT, 128], BF16, "Vt")
                vtb = Buf("Vt")
                tmps = [sbt(ps, [128, 512], F32, "rt") for _ in range(2)]
                tmpb = [Buf("rt0"), Buf("rt1")]
                Pt = [sbt(ps, [128, 512], BF16, "Pt") for _ in range(4)]
                ptb = [Buf("Pt%d" % i) for i in range(4)]
                rr = [sbt(ps, [128, 512], F32, "rr") for _ in range(2)]
                rrb = [Buf("rr0"), Buf("rr1")]
                od = sbt(ps, [128, 512], F32, "od")
                odb = Buf("od")
                sqh = sbt(ps, [128, 512], BF16, "sqh")
                sqhb = Buf("sqh")
                rsh = sbt(ps, [128, 512], F32, "rsh")
                rshb = Buf("rsh")
                ao = [sbt(ps, [128, 512], BF16, "ao") for _ in range(2)]
                aob = [Buf("ao0"), Buf("ao1")]
                lam = sbt(ps, [128, 8], F32, "lam")
                lamb = Buf("lam")
                hnw = sbt(ps, [128, 8], F32, "hnw")
                lv = small[:, S_DLAM:S_DLAM + 256]
                op("dve", lambda e: e.tensor_tensor(out=tmps[0][:, 0:64], in0=lv[:, 0:64], in1=lv[:, 64:128], op=ALU.mult), reads=[cbuf], writes=[tmpb[0]])
                op("dve", lambda e: e.tensor_tensor(out=tmps[0][:, 64:128], in0=lv[:, 128:192], in1=lv[:, 192:256], op=ALU.mult), reads=[cbuf], writes=[tmpb[0]])
                op("dve", lambda e: e.tensor_reduce(out=lam[:, 0:2], in_=tmps[0][:, 0:128].rearrange("p (a b) -> p a b", a=2), axis=mybir.AxisListType.X, op=ALU.add),
                   reads=[tmpb[0]], writes=[lamb])
                op("act", lambda e: e.activation(out=lam[:, 0:2], in_=lam[:, 0:2], func=AF.Exp), reads=[lamb], writes=[lamb])
                op("dve", lambda e: e.tensor_tensor(out=lam[:, 2:3], in0=lam[:, 1:2], in1=lam[:, 0:1], op=ALU.subtract), reads=[lamb], writes=[lamb])
                op("dve", lambda e: e.tensor_scalar(out=lam[:, 3:4], in0=lam[:, 2:3], scalar1=-lam_init, scalar2=None, op0=ALU.add), reads=[lamb], writes=[lamb])
                op("dve", lambda e: e.tensor_scalar(out=hnw[:], in0=small[:, S_DHN:S_DHN + 8], scalar1=(1.0 - lam_init), scalar2=None, op0=ALU.mult), reads=[cbuf], writes=[lamb])
                nlam = lam[:, 3:4]

                def load_h(h):
                    s = h % 2
                    dma("pool", wh[s][:], wdv[:, :, h * 640:(h + 1) * 640], writes=[whb[s]])
                    dma("pool", wo[s][:], wov[h * 128:(h + 1) * 128, :], writes=[wob[s]])

                load_h(0)
                pti = 0
                oi = 0
                for h in range(8):
                    s = h % 2
                    if h + 1 < 8:
                        load_h(h + 1)
                    w = wh[s]
                    proj_rope(ps, w, 256, 384, lambda t0, n: KT[:, t0:t0 + n], ktb, whb[s], (0, 1), tmps, tmpb, rope_tiles)
                    proj_rope(ps, w, 0, 128, lambda t0, n: QT[:, t0:t0 + n], qtb, whb[s], (2, 3), tmps, tmpb, rope_tiles)
                    for tt in range(NT):
                        bk = 4 + (tt // 4) % 2
                        o = (tt % 4) * 128
                        bidx = 0 if tt < 2 else 1 + (tt - 2) // 4
                        for k in range(8):
                            op("pe", lambda e, k=k, tt=tt, bk=bk, o=o: e.matmul(pb[bk][:, o:o + 128], lhsT=uT[:, k, tt * 128:(tt + 1) * 128], rhs=w[:, k, 512:640],
                                                                                start=(k == 0), stop=(k == 7)),
                               reads=[whb[s], ub[bidx]], writes=[pbb[bk]], inc=(k == 7))
                        if tt % 4 == 3 or tt == NT - 1:
                            a = (tt // 4) * 4
                            nn = tt + 1 - a
                            op("act", lambda e, a=a, nn=nn, bk=bk: e.activation(out=Vt[:, a:a + nn, :], in_=pb[bk][:, 0:nn * 128].rearrange("p (t d) -> p t d", d=128),
                                                                                func=AF.Identity), reads=[pbb[bk]], writes=[vtb])
                    for bi, (t0, n, j) in enumerate(BLOCKS if need_ctx else BLOCKS[1:]):
                        kts = [0, 1] if j == 1 else list(range(NT))
                        nk = len(kts)

                        def s_stage(ki):
                            kt = kts[ki]
                            ksl = slice(kt * 128, (kt + 1) * 128)
                            for m in range(2):
                                bS = m * 2 + (ki % 2)
                                p_ = (ki % 2) * 2 + m
                                rsl = slice(m * 64, (m + 1) * 64)
                                op("pe", lambda e, bS=bS, ksl=ksl, rsl=rsl: e.matmul(pb[bS][:, 0:n], lhsT=KT[rsl, ksl], rhs=QT[rsl, t0:t0 + n], start=True, stop=True),
                                   reads=[ktb, qtb], writes=[pbb[bS]])
                            for m in range(2):
                                bS = m * 2 + (ki % 2)
                                p_ = (ki % 2) * 2 + m
                                op("act", lambda e, bS=bS, p_=p_: e.activation(out=Pt[p_][:, 0:n], in_=pb[bS][:, 0:n], func=AF.Exp, scale=0.125), reads=[pbb[bS]], writes=[ptb[p_]])

                        def pv_stage(ki):
                            kt = kts[ki]
                            first = (ki == 0)
                            lastk = (ki == nk - 1)
                            for m in range(2):
                                p_ = (ki % 2) * 2 + m
                                op("pe", lambda e, kt=kt, p_=p_, m=m: e.matmul(pb[4 + m][:, 0:n], lhsT=Vt[:, kt, :], rhs=Pt[p_][:, 0:n], start=first, stop=lastk),
                                   reads=[vtb, ptb[p_]], writes=[pbb[4 + m]], inc=lastk)
                                op("pe", lambda e, p_=p_, m=m: e.matmul(pb[6 + m][:, 0:n], lhsT=ones_bf[:], rhs=Pt[p_][:, 0:n], start=first, stop=lastk),
                                   reads=[cbuf, ptb[p_]], writes=[pbb[6 + m]], inc=lastk)

                        s_stage(0)
                        for ki in range(nk):
                            if ki + 1 < nk:
                                s_stage(ki + 1)
                            pv_stage(ki)
                        for m in range(2):
                            op("dve", lambda e, m=m: e.reciprocal(out=rr[m][:, 0:n], in_=pb[6 + m][:, 0:n]), reads=[pbb[6 + m]], writes=[rrb[m]])
                            op("dve", lambda e, m=m: e.tensor_tensor(out=rr[m][:, 0:n], in0=pb[4 + m][:, 0:n], in1=rr[m][:, 0:n], op=ALU.mult), reads=[pbb[4 + m], rrb[m]], writes=[rrb[m]])
                        op("dve", lambda e: e.scalar_tensor_tensor(out=od[:, 0:n], in0=rr[1][:, 0:n], scalar=nlam, in1=rr[0][:, 0:n], op0=ALU.mult, op1=ALU.add),
                           reads=[rrb[0], rrb[1], lamb], writes=[odb])
                        op("act", lambda e: e.activation(out=sqh[:, 0:n], in_=od[:, 0:n], func=AF.Square), reads=[odb], writes=[sqhb])
                        op("pe", lambda e: e.matmul(pb[0][:, 0:n], lhsT=ones_bf[:], rhs=sqh[:, 0:n], start=True, stop=True), reads=[sqhb, cbuf], writes=[pbb[0]])
                        rstd_from_psum(pb[0][:, 0:n], rsh[:, 0:n], n, 128.0, pbb[0], rshb)
                        a_ = oi % 2
                        oi += 1
                        op("dve", lambda e, a_=a_: e.scalar_tensor_tensor(out=ao[a_][:, 0:n], in0=od[:, 0:n], scalar=hnw[:, h:h + 1], in1=rsh[:, 0:n], op0=ALU.mult, op1=ALU.mult),
                           reads=[odb, rshb, lamb], writes=[aob[a_]])
                        outproj_acc([1, 2, 3], [wo[s]], [ao[a_][:, 0:n]], t0, n, j, [wob[s], aob[a_]])
                kb.barrier()

        def final_phase(do_norm):
            with ExitStack() as ps:
                sq = sbt(ps, [128, 8, 512], BF16, "fsq")
                sqb = Buf("fsq")
                rs = sbt(ps, [128, 512], F32, "frs")
                rsb = Buf("frs")
                tmp = sbt(ps, [128, 8, 512], F32, "ftmp")
                tmb = Buf("ftmp")
                ost = [sbt(ps, [128, D], F32, "ost") for _ in range(2)]
                osb = [Buf("ost0"), Buf("ost1")]
                oi = 0
                for (t0, n, j) in BLOCKS[1:]:
                    if do_norm:
                        op("act", lambda e: e.activation(out=sq[:, :, 0:n], in_=hT[:, :, t0:t0 + n], func=AF.Square), reads=hbs(t0, n), writes=[sqb])
                        for c in range(8):
                            op("pe", lambda e, c=c: e.matmul(pb[7][:, 0:n], lhsT=ones_bf[:], rhs=sq[:, c, 0:n], start=(c == 0), stop=(c == 7)),
                               reads=[sqb, cbuf], writes=[pbb[7]], inc=(c == 7))
                        rstd_from_psum(pb[7][:, 0:n], rs[:, 0:n], n, float(D), pbb[7], rsb)
                        for c in range(8):
                            op("dve", lambda e, c=c: e.scalar_tensor_tensor(out=tmp[:, c, 0:n], in0=hT[:, c, t0:t0 + n], scalar=small[:, S_FINAL + c:S_FINAL + c + 1],
                                                                            in1=rs[:, 0:n], op0=ALU.mult, op1=ALU.mult),
                               reads=hbs(t0, n) + [rsb, cbuf], writes=[tmb])
                    else:
                        op("dve", lambda e: e.tensor_copy(out=tmp[:, :, 0:n], in_=hT[:, :, t0:t0 + n]), reads=hbs(t0, n), writes=[tmb])
                    for ti in range(n // 128):
                        o_ = oi % 2
                        oi += 1
                        for half in range(2):
                            bk = (oi * 2 + half) % 4
                            for c4 in range(4):
                                c = half * 4 + c4
                                op("pe", lambda e, c=c, c4=c4, bk=bk, ti=ti: e.transpose(pb[bk][:, c4 * 128:(c4 + 1) * 128], tmp[:, c, ti * 128:(ti + 1) * 128], ident),
                                   reads=[tmb, cbuf], writes=[pbb[bk]], inc=(c4 == 3))
                            if half == 0:
                                op("act", lambda e, bk=bk, o_=o_, half=half: e.activation(out=ost[o_][:, half * 512:(half + 1) * 512], in_=pb[bk][:], func=AF.Identity),
                                   reads=[pbb[bk]], writes=[osb[o_]])
                            else:
                                op("dve", lambda e, bk=bk, o_=o_, half=half: e.tensor_copy(out=ost[o_][:, half * 512:(half + 1) * 512], in_=pb[bk][:]),
                                   reads=[pbb[bk]], writes=[osb[o_]])
                        r0 = t0 - NCTX + ti * 128
                        dma("sp", out_d[r0:r0 + 128, :], ost[o_][:], reads=[osb[o_]])
                kb.wait_all_dma("sp")

        for L in layers:
            kind, slot = L % 3, L // 3
            need_ctx = L < DEPTH - 1
            blocks = BLOCKS if need_ctx else BLOCKS[1:]
            ada_phase(L)
            if do_mixer:
                norm_mod(amix, 0, BLOCKS)
                if kind == 0:
                    mlstm_phase(L, slot, need_ctx)
                elif kind == 1:
                    swa_phase(L, need_ctx)
                else:
                    diff_phase(L, need_ctx)
            if do_ffn:
                norm_mod(affn, 24, blocks)
                if do_ffn != "norm":
                    ffn_phase(L, blocks)
        final_phase(final_norm)
    nc._declared_inputs = list(declared)
    return nc


def _pcol(v):
    return np.ascontiguousarray(np.asarray(v, np.float32).reshape(-1, 128).T)


def _rot_idx(base):
    return list(range(base + 32, base + 64)) + list(range(base, base + 32))


def _consts():
    i = np.arange(128)
    ident = (i[:, None] == i[None, :]).astype(np.float32)
    maskf = (i[:, None] <= i[None, :]).astype(np.float32)
    maskb = (i[:, None] >= i[None, :]).astype(np.float32)
    return np.ascontiguousarray(np.concatenate([ident, maskf, maskb, -(maskf - 0.5), -(maskb - 0.5)], axis=1))


def _rope_tables():
    rows = NLAT // 64
    row = np.repeat(np.arange(rows), 64).astype(np.float32)
    col = np.tile(np.arange(64), rows).astype(np.float32)
    quarter = 16
    inv = (np.float32(10000.0) ** (-np.arange(quarter, dtype=np.float32) / np.float32(quarter))).astype(np.float32)
    ang = np.concatenate([row[:, None] * inv, col[:, None] * inv], axis=-1).astype(np.float32)
    cos = np.cos(ang).astype(np.float32).T
    sin = np.sin(ang).astype(np.float32).T
    c64 = np.concatenate([cos, cos], 0)
    s64 = np.concatenate([-sin, sin], 0)
    return np.ascontiguousarray(np.concatenate([c64, c64], 0)), np.ascontiguousarray(np.concatenate([s64, s64], 0))


def prepare_shared(inp):
    sh = {}
    sh["cst"] = _consts()
    sh["ropec"], sh["ropes"] = _rope_tables()
    sh["ada_w"] = np.ascontiguousarray(inp["ada_w"], dtype=np.float32)
    sh["ffn_w1"] = np.ascontiguousarray(inp["ffn_w1"], dtype=np.float32)
    sh["ffn_w2"] = np.ascontiguousarray(inp["ffn_w2"], dtype=np.float32)
    wm = []
    for s in range(inp["mlstm_w_in"].shape[0]):
        W = inp["mlstm_w_in"][s]
        cols = []
        for h in range(8):
            cols += list(range(h * 64, h * 64 + 64))
            cols += list(range(512 + h * 64, 512 + h * 64 + 64))
            cols += list(range(512 + h * 64, 512 + h * 64 + 64))
            cols += list(range(1024 + h * 128, 1024 + h * 128 + 128))
            cols += list(range(2048 + h * 128, 2048 + h * 128 + 128))
        cols += list(range(3072, 3104))
        wm.append(W[:, cols])
    sh["wm"] = np.ascontiguousarray(np.stack(wm), dtype=np.float32)
    sh["wmo"] = np.ascontiguousarray(inp["mlstm_w_out"], dtype=np.float32)
    W = inp["swa_w_in"][0]
    cols = []
    for g in range(4):
        q = []
        qr = []
        for hh in range(4):
            b = (4 * g + hh) * 64
            q += list(range(b, b + 64))
            qr += _rot_idx(b)
        kb_ = 1024 + g * 64
        k = list(range(kb_, kb_ + 64))
        kr = _rot_idx(kb_)
        v = list(range(1280 + g * 64, 1280 + g * 64 + 64))
        cols += q + qr + k + k + kr + kr + v
    sh["ws"] = np.ascontiguousarray(W[:, cols][None], dtype=np.float32)
    sh["wso"] = np.ascontiguousarray(inp["swa_w_out"], dtype=np.float32)
    W = inp["diff_w_in"][0]
    cols = []
    for h in range(8):
        qb_ = h * 128
        kb_ = 1024 + h * 128
        cols += list(range(qb_, qb_ + 128)) + _rot_idx(qb_) + _rot_idx(qb_ + 64)
        cols += list(range(kb_, kb_ + 128)) + _rot_idx(kb_) + _rot_idx(kb_ + 64)
        cols += list(range(2048 + h * 128, 2048 + h * 128 + 128))
    sh["wd"] = np.ascontiguousarray(W[:, cols][None], dtype=np.float32)
    sh["wdo"] = np.ascontiguousarray(inp["diff_w_out"], dtype=np.float32)
    small = np.zeros((128, NS), np.float32)
    for L in range(DEPTH):
        small[:, L * SL:L * SL + 8] = _pcol(inp["norm_mix"][L])
        small[:, L * SL + 8:L * SL + 16] = _pcol(inp["norm_ffn"][L])
        small[:, L * SL + 16:L * SL + 64] = _pcol(inp["ada_b"][L])
    small[:, S_FINAL:S_FINAL + 8] = _pcol(inp["final_norm"])
    for s in range(inp["mlstm_head_norm"].shape[0]):
        small[:, S_MHN + s * 8:S_MHN + s * 8 + 8] = _pcol(inp["mlstm_head_norm"][s])
        small[:, S_MGB + s * 32:S_MGB + s * 32 + 32] = np.broadcast_to(np.asarray(inp["mlstm_gate_b"][s], np.float32).reshape(1, 32), (128, 32))
    sink = np.asarray(inp["swa_sink"][0], np.float32)
    sord = [4 * g + PSORD[p] for g in range(4) for p in range(4)]
    small[:, S_SINK:S_SINK + 16] = np.broadcast_to(sink[sord][None, :], (128, 16))
    small[:, S_DHN:S_DHN + 8] = _pcol(inp["diff_head_norm"][0])
    lamv = np.concatenate([np.asarray(inp[k][0], np.float32) for k in ("diff_lambda_q1", "diff_lambda_k1", "diff_lambda_q2", "diff_lambda_k2")])
    small[:, S_DLAM:S_DLAM + 256] = np.broadcast_to(lamv[None, :], (128, 256))
    sh["small"] = small
    return sh


def prepare_core(inp, b):
    xin = np.ascontiguousarray(np.concatenate([inp["ctx"][b], inp["x"][b]], axis=0), dtype=np.float32)
    cv = np.zeros((128, 16), np.float32)
    cv[:, 0::2] = _pcol(inp["c"][b])
    cv[:, 1::2] = _pcol(inp["c_ctx"])
    return {"xin": xin, "cv": cv}


_NC_CACHE = {}


def kernel(**inputs):
    inp = {k: np.asarray(v) for k, v in inputs.items()}
    B = inp["x"].shape[0]
    shared = prepare_shared(inp)
    if "full" not in _NC_CACHE:
        _NC_CACHE["full"] = build_program()
    nc = _NC_CACHE["full"]
    in_maps = []
    for b in range(B):
        m = dict(shared)
        m.update(prepare_core(inp, b))
        in_maps.append({k: m[k] for k in nc._declared_inputs})
    res = run_bass_kernel_spmd(nc, in_maps, core_ids=list(range(B)))
    out = np.stack([np.asarray(r["out"], dtype=np.float32) for r in res.results], axis=0)
    return out
```

```python
import numpy as np
from contextlib import ExitStack
import concourse.bass as bass
import concourse.mybir as mybir
from concourse.bass_utils import run_bass_kernel_spmd

F32 = mybir.dt.float32
BF16 = mybir.dt.bfloat16
ALU = mybir.AluOpType
AF = mybir.ActivationFunctionType

D = 1024
NCTX = 256
NLAT = 2048
T = NCTX + NLAT
NT = T // 128
DEPTH = 4
EPS = 1e-6
NDS = 24
BLOCKS = [(0, 256, 1), (256, 512, 0), (768, 512, 0), (1280, 512, 0), (1792, 512, 0)]

SL = 64
S_FINAL = 4 * SL
S_MHN = S_FINAL + 8
S_MGB = S_MHN + 16
S_SINK = S_MGB + 64
S_DHN = S_SINK + 16
S_DLAM = S_DHN + 8
NS = S_DLAM + 256
NCONST = 5 * 128
PSORD = [0, 2, 1, 3]
FFN_ITEMS = 1000
SWA_STOP = 9
SWA_SUB = 9
SWA_QB = (0, 1)
SWA_NQ = 2


class Buf:
    __slots__ = ("name", "w", "r")

    def __init__(self, name):
        self.name = name
        self.w = None
        self.r = {}


class KB:
    def __init__(self, nc, es):
        self.nc = nc
        self.es = es
        self.E = {"pe": nc.tensor, "act": nc.scalar, "dve": nc.vector, "pool": nc.gpsimd, "sp": nc.sync}
        self.sem = {e: es.enter_context(nc.semaphore("s_" + e)) for e in self.E}
        self.cnt = {e: 0 for e in self.E}
        self.seen = {e: {} for e in self.E}
        self.dsem = [es.enter_context(nc.semaphore("d%d" % i)) for i in range(NDS)]
        self.dcnt = [0] * NDS
        self.dnext = {"sp": 0, "pool": 0}

    def _wait(self, eng, deps):
        for (sk, val) in deps:
            if sk == eng and eng == "pe":
                continue
            if self.seen[eng].get(sk, 0) >= val:
                continue
            semh = self.sem[sk] if isinstance(sk, str) else self.dsem[sk]
            self.E[eng].wait_ge(semh, val)
            self.seen[eng][sk] = val

    @staticmethod
    def _deps(reads, writes):
        deps = []
        for b in reads:
            if b.w is not None:
                deps.append(b.w)
        for b in writes:
            if b.w is not None:
                deps.append(b.w)
            deps.extend(b.r.values())
        return deps

    def op(self, eng, fn, reads=(), writes=(), inc=True):
        self._wait(eng, self._deps(reads, writes))
        ins = fn(self.E[eng])
        if inc:
            self.cnt[eng] += 1
            ins.then_inc(self.sem[eng], 1)
            tk = (eng, self.cnt[eng])
        else:
            tk = (eng, self.cnt[eng] + 1)
        for b in reads:
            b.r[eng] = tk
        for b in writes:
            b.w = tk
            b.r = {}

    def dma(self, q, out, in_, reads=(), writes=()):
        half = NDS // 2
        i = self.dnext[q] + (0 if q == "sp" else half)
        self.dnext[q] = (self.dnext[q] + 1) % half
        deps = self._deps(reads, writes)
        if self.dcnt[i] > 0:
            deps.append((i, self.dcnt[i]))
        self._wait(q, deps)
        self.E[q].dma_start(out=out, in_=in_).then_inc(self.dsem[i], 16)
        self.dcnt[i] += 16
        tk = (i, self.dcnt[i])
        for b in reads:
            b.r[("d", i)] = tk
        for b in writes:
            b.w = tk
            b.r = {}

    def wait_all_dma(self, eng):
        self._wait(eng, [(i, self.dcnt[i]) for i in range(NDS) if self.dcnt[i] > 0])

    def barrier(self):
        deps = [(e, self.cnt[e]) for e in self.E if e != "sp" and self.cnt[e] > 0]
        deps += [(i, self.dcnt[i]) for i in range(NDS) if self.dcnt[i] > 0]
        self._wait("sp", deps)
        self.cnt["sp"] += 1
        self.E["sp"].sem_inc(self.sem["sp"], 1)
        for e in self.E:
            if e != "sp":
                self._wait(e, [("sp", self.cnt["sp"])])


def build_program(layers=(0, 1, 2, 3), do_mixer=True, do_ffn=True, final_norm=True, in_ctx=True):
    nc = bass.Bass("TRN2", target_bir_lowering=False)

    declared = []
    only = None if len(layers) == DEPTH and do_mixer and do_ffn else set()

    class _Lazy:
        def __init__(self, name, shape):
            self.name, self.shape, self.ap_ = name, list(shape), None

        def get(self):
            if self.ap_ is None:
                self.ap_ = nc.dram_tensor(self.name, self.shape, F32, kind="ExternalInput").ap()
                declared.append(self.name)
            return self.ap_

        def __getitem__(self, k):
            return self.get()[k]

    def din(name, shape):
        return _Lazy(name, shape)

    xin = din("xin", [T, D]).get()
    cv_d = din("cv", [128, 16]).get()
    small_d = din("small", [128, NS]).get()
    cst_d = din("cst", [128, NCONST]).get()
    ropec_l = din("ropec", [128, NLAT])
    ropes_l = din("ropes", [128, NLAT])
    ada_w = din("ada_w", [DEPTH, D, 6 * D])
    w1_d = din("ffn_w1", [DEPTH, D, 4 * D])
    w2_d = din("ffn_w2", [DEPTH, 4 * D, D])
    wm_d = din("wm", [2, D, 3616])
    wmo_d = din("wmo", [2, D, D])
    ws_d = din("ws", [1, D, 3328])
    wso_d = din("wso", [1, D, D])
    wd_d = din("wd", [1, D, 5120])
    wdo_d = din("wdo", [1, D, D])
    out_d = nc.dram_tensor("out", [NLAT, D], F32, kind="ExternalOutput").ap()

    with ExitStack() as es:
        kb = KB(nc, es)
        op = kb.op
        dma = kb.dma
        cnt = [0]

        def sbt(es_, shape, dt, name=None):
            cnt[0] += 1
            return es_.enter_context(nc.sbuf_tensor("%s_%d" % (name or "t", cnt[0]), list(shape), dt))

        hT = sbt(es, [128, 8, T], F32, "hT")
        uT = sbt(es, [128, 8, T], BF16, "uT")
        hb = [Buf("h%d" % i) for i in range(NT)]
        ub = [Buf("u%d" % i) for i in range(len(BLOCKS))]
        cst = sbt(es, [128, NCONST], F32, "cst")
        small = sbt(es, [128, NS], F32, "small")
        cv = sbt(es, [128, 16], F32, "cv")
        cond = sbt(es, [128, 8, 2], BF16, "cond")
        ones_bf = sbt(es, [128, 128], BF16, "ones")
        maskf_bf = sbt(es, [128, 128], BF16, "maskf")
        maskb_bf = sbt(es, [128, 128], BF16, "maskb")
        nhalf = sbt(es, [128, 128], F32, "nhalf")
        modsb = sbt(es, [128, 48, 2], F32, "modsb")
        amix = sbt(es, [128, 8, 2], F32, "amix")
        affn = sbt(es, [128, 8, 2], F32, "affn")
        cbuf = Buf("consts")
        modb = Buf("mod")
        ident = cst[:, 0:128]
        maskf = cst[:, 128:256]
        maskb = cst[:, 256:384]
        ntricf = cst[:, 384:512]
        ntricb = cst[:, 512:640]
        pb = [es.enter_context(nc.psum_tensor("pb%d" % i, [128, 512], F32)) for i in range(8)]
        pbb = [Buf("pb%d" % i) for i in range(8)]

        def hbs(t0, n):
            return [hb[i] for i in range(t0 // 128, (t0 + n) // 128)]

        dma("sp", cst[:], cst_d, writes=[cbuf])
        dma("sp", small[:], small_d, writes=[cbuf])
        dma("sp", cv[:], cv_d, writes=[cbuf])
        op("dve", lambda e: e.memset(ones_bf[:], 1.0), writes=[cbuf])
        op("dve", lambda e: e.memset(nhalf[:], -0.5), writes=[cbuf])
        op("dve", lambda e: e.tensor_copy(out=maskf_bf[:], in_=maskf), reads=[cbuf], writes=[cbuf])
        op("dve", lambda e: e.tensor_copy(out=maskb_bf[:], in_=maskb), reads=[cbuf], writes=[cbuf])
        with ExitStack() as ps:
            tmpc = sbt(ps, [128, 16], F32, "tmpc")
            tb_ = Buf("tmpc")
            op("act", lambda e: e.activation(out=tmpc[:], in_=cv[:], func=AF.Exp, scale=-1.0), reads=[cbuf], writes=[tb_])
            op("dve", lambda e: e.tensor_scalar(out=tmpc[:], in0=tmpc[:], scalar1=1.0, scalar2=None, op0=ALU.add), reads=[tb_], writes=[tb_])
            op("dve", lambda e: e.reciprocal(out=tmpc[:], in_=tmpc[:]), reads=[tb_], writes=[tb_])
            op("dve", lambda e: e.tensor_tensor(out=cond[:].rearrange("p k j -> p (k j)"), in0=tmpc[:], in1=cv[:], op=ALU.mult),
               reads=[tb_, cbuf], writes=[cbuf])
            stg = [sbt(ps, [128, D], F32, "stg") for _ in range(2)]
            stb = [Buf("stg0"), Buf("stg1")]
            for tt in range(NT):
                s = tt % 2
                dma("sp", stg[s][:], xin[tt * 128:(tt + 1) * 128, :], writes=[stb[s]])
                for half in range(2):
                    bk = (tt * 2 + half) % 4
                    for c4 in range(4):
                        c = half * 4 + c4
                        op("pe", lambda e, c=c, c4=c4, bk=bk, s=s: e.transpose(pb[bk][:, c4 * 128:(c4 + 1) * 128], stg[s][:, c * 128:(c + 1) * 128], ident),
                           reads=[stb[s], cbuf], writes=[pbb[bk]], inc=(c4 == 3))
                    eng = "act" if half == 0 else "dve"
                    if eng == "act":
                        op("act", lambda e, bk=bk, half=half, tt=tt: e.activation(out=hT[:, half * 4:half * 4 + 4, tt * 128:(tt + 1) * 128],
                                                                                 in_=pb[bk][:].rearrange("p (c n) -> p c n", c=4), func=AF.Identity),
                           reads=[pbb[bk]], writes=[hb[tt]])
                    else:
                        op("dve", lambda e, bk=bk, half=half, tt=tt: e.tensor_copy(out=hT[:, half * 4:half * 4 + 4, tt * 128:(tt + 1) * 128],
                                                                                   in_=pb[bk][:].rearrange("p (c n) -> p c n", c=4)),
                           reads=[pbb[bk]], writes=[hb[tt]])
            kb.barrier()

        def rstd_from_psum(ps_ap, out_ap, n, dim, pbuf, obuf):
            op("act", lambda e: e.activation(out=out_ap, in_=ps_ap, func=AF.Ln, scale=1.0 / dim, bias=EPS), reads=[pbuf], writes=[obuf])
            op("act", lambda e: e.activation(out=out_ap, in_=out_ap, func=AF.Exp, scale=-0.5), reads=[obuf], writes=[obuf])

        def ada_phase(L):
            with ExitStack() as ps:
                slots = [sbt(ps, [128, 8, 512], BF16, "adaw") for _ in range(3)]
                sbf = [Buf("adaw%d" % i) for i in range(3)]
                awv = ada_w[L].rearrange("(c p) n -> p c n", p=128)
                pm = pb[7][:, 0:96].rearrange("p (m j) -> p m j", j=2)
                for g in range(12):
                    s = g % 3
                    dma("pool", slots[s][:], awv[:, :, g * 512:(g + 1) * 512], writes=[sbf[s]])
                    for jj in range(4):
                        m = g * 4 + jj
                        for k in range(8):
                            op("pe", lambda e, s=s, jj=jj, k=k, m=m: e.matmul(pm[:, m, :], lhsT=slots[s][:, k, jj * 128:(jj + 1) * 128], rhs=cond[:, k, :],
                                                                           start=(k == 0), stop=(k == 7)),
                               reads=[sbf[s], cbuf], writes=[pbb[7]], inc=(k == 7))
                base = L * SL
                op("dve", lambda e: e.tensor_tensor(out=modsb[:], in0=pm, in1=small[:, base + 16:base + 64].unsqueeze(2).broadcast_to([128, 48, 2]), op=ALU.add),
                   reads=[pbb[7], cbuf], writes=[modb])
                op("dve", lambda e: e.scalar_tensor_tensor(out=amix[:], in0=modsb[:, 8:16, :], scalar=1.0,
                                                           in1=small[:, base:base + 8].unsqueeze(2).broadcast_to([128, 8, 2]), op0=ALU.add, op1=ALU.mult),
                   reads=[modb, cbuf], writes=[modb])
                op("dve", lambda e: e.scalar_tensor_tensor(out=affn[:], in0=modsb[:, 32:40, :], scalar=1.0,
                                                           in1=small[:, base + 8:base + 16].unsqueeze(2).broadcast_to([128, 8, 2]), op0=ALU.add, op1=ALU.mult),
                   reads=[modb, cbuf], writes=[modb])
                kb.barrier()

        def norm_mod(a_t, shift_off, blocks):
            with ExitStack() as ps:
                sq = [sbt(ps, [128, 8, 512], BF16, "sq") for _ in range(2)]
                sqb = [Buf("sq0"), Buf("sq1")]
                rs = [sbt(ps, [128, 512], F32, "rs") for _ in range(2)]
                rsb = [Buf("rs0"), Buf("rs1")]
                tmp = [sbt(ps, [128, 4, 512], F32, "nt") for _ in range(2)]
                tmb = [Buf("nt0"), Buf("nt1")]
                for bi, (t0, n, j) in enumerate(blocks):
                    s = bi % 2
                    bk = 6 + s
                    bidx = BLOCKS.index((t0, n, j))
                    op("act", lambda e, s=s: e.activation(out=sq[s][:, :, 0:n], in_=hT[:, :, t0:t0 + n], func=AF.Square), reads=hbs(t0, n), writes=[sqb[s]])
                    for c in range(8):
                        op("pe", lambda e, s=s, c=c, bk=bk: e.matmul(pb[bk][:, 0:n], lhsT=ones_bf[:], rhs=sq[s][:, c, 0:n], start=(c == 0), stop=(c == 7)),
                           reads=[sqb[s], cbuf], writes=[pbb[bk]], inc=(c == 7))
                    rstd_from_psum(pb[bk][:, 0:n], rs[s][:, 0:n], n, float(D), pbb[bk], rsb[s])
                    for half in range(2):
                        op("dve", lambda e, s=s, half=half: e.tensor_tensor(out=tmp[half][:, :, 0:n], in0=hT[:, half * 4:half * 4 + 4, t0:t0 + n],
                                                                            in1=rs[s][:, 0:n].unsqueeze(1).broadcast_to([128, 4, n]), op=ALU.mult),
                           reads=hbs(t0, n) + [rsb[s]], writes=[tmb[half]])
                        for c4 in range(4):
                            c = half * 4 + c4
                            op("act", lambda e, half=half, c4=c4, c=c: e.activation(out=uT[:, c, t0:t0 + n], in_=tmp[half][:, c4, 0:n], func=AF.Identity,
                                                                                    scale=a_t[:, c, j:j + 1], bias=modsb[:, shift_off + c, j:j + 1]),
                               reads=[tmb[half], modb], writes=[ub[bidx]])
                kb.barrier()

        def ffn_phase(L, blocks):
            with ExitStack() as ps:
                w1s = [sbt(ps, [128, 8, 512], BF16, "w1s") for _ in range(3)]
                w2s = [sbt(ps, [128, 4, D], BF16, "w2s") for _ in range(3)]
                wb = [Buf("ffw%d" % i) for i in range(3)]
                hid = [sbt(ps, [128, 4, 512], BF16, "hid") for _ in range(2)]
                hib = [Buf("hid0"), Buf("hid1")]
                sqv = [sbt(ps, [128, 512], F32, "sqv") for _ in range(2)]
                sqvb = [Buf("sqv0"), Buf("sqv1")]
                w1v = w1_d[L].rearrange("(c p) n -> p c n", p=128)
                w2v = w2_d[L].rearrange("(c p) n -> p c n", p=128)
                items = [(e8, blk) for e8 in range(8) for blk in blocks][:FFN_ITEMS]
                loaded = set()

                def load(e8):
                    if e8 in loaded or e8 >= 8:
                        return
                    loaded.add(e8)
                    s = e8 % 3
                    dma("pool", w1s[s][:], w1v[:, :, e8 * 512:(e8 + 1) * 512], writes=[wb[s]])
                    dma("pool", w2s[s][:], w2v[:, e8 * 4:(e8 + 1) * 4, :], writes=[wb[s]])

                sqi = [0]

                def stage_h(i):
                    e8, (t0, n, j) = items[i]
                    s = e8 % 3
                    bidx = BLOCKS.index((t0, n, j))
                    for jj in range(4):
                        for k in range(8):
                            op("pe", lambda e, jj=jj, k=k: e.matmul(pb[jj][:, 0:n], lhsT=w1s[s][:, k, jj * 128:(jj + 1) * 128], rhs=uT[:, k, t0:t0 + n],
                                                                    start=(k == 0), stop=(k == 7)),
                               reads=[wb[s], ub[bidx]], writes=[pbb[jj]], inc=(k == 7))
                        q = sqi[0] % 2
                        sqi[0] += 1
                        op("act", lambda e, jj=jj, q=q: e.activation(out=sqv[q][:, 0:n], in_=pb[jj][:, 0:n], func=AF.Square), reads=[pbb[jj]], writes=[sqvb[q]])
                        op("dve", lambda e, jj=jj, q=q: e.scalar_tensor_tensor(out=hid[i % 2][:, jj, 0:n], in0=pb[jj][:, 0:n], scalar=0.0, in1=sqv[q][:, 0:n],
                                                                               op0=ALU.is_gt, op1=ALU.mult),
                           reads=[pbb[jj], sqvb[q]], writes=[hib[i % 2]])

                oi = [0]

                def stage_o(i):
                    e8, (t0, n, j) = items[i]
                    s = e8 % 3
                    for f in range(8):
                        bk = 4 + oi[0] % 3
                        oi[0] += 1
                        for jj in range(4):
                            op("pe", lambda e, jj=jj, f=f, bk=bk: e.matmul(pb[bk][:, 0:n], lhsT=w2s[s][:, jj, f * 128:(f + 1) * 128], rhs=hid[i % 2][:, jj, 0:n],
                                                                           start=(jj == 0), stop=(jj == 3)),
                               reads=[wb[s], hib[i % 2]], writes=[pbb[bk]], inc=(jj == 3))
                        op("dve", lambda e, f=f, bk=bk: e.scalar_tensor_tensor(out=hT[:, f, t0:t0 + n], in0=pb[bk][:, 0:n], scalar=modsb[:, 40 + f, j:j + 1],
                                                                               in1=hT[:, f, t0:t0 + n], op0=ALU.mult, op1=ALU.add),
                           reads=[pbb[bk], modb] + hbs(t0, n), writes=hbs(t0, n))

                load(0)
                load(1)
                stage_h(0)
                for i in range(len(items)):
                    if i + 1 < len(items):
                        load(items[i + 1][0] + 1)
                        stage_h(i + 1)
                    stage_o(i)
                kb.barrier()

        def outproj_acc(pbank, lhs_list, rhs_list, t0, n, j, reads, stop_bufs=None):
            for f in range(8):
                bk = pbank[f % len(pbank)]
                nmm = len(lhs_list)
                for i in range(nmm):
                    op("pe", lambda e, i=i, f=f, bk=bk: e.matmul(pb[bk][:, 0:n], lhsT=lhs_list[i][:, f * 128:(f + 1) * 128], rhs=rhs_list[i],
                                                                 start=(i == 0), stop=(i == nmm - 1)),
                       reads=reads, writes=[pbb[bk]], inc=(i == nmm - 1))
                op("dve", lambda e, f=f, bk=bk: e.scalar_tensor_tensor(out=hT[:, f, t0:t0 + n], in0=pb[bk][:, 0:n], scalar=modsb[:, 16 + f, j:j + 1],
                                                                       in1=hT[:, f, t0:t0 + n], op0=ALU.mult, op1=ALU.add),
                   reads=[pbb[bk], modb] + hbs(t0, n), writes=hbs(t0, n))

        def mlstm_phase(L, slot, need_ctx):
            out_blocks = BLOCKS if need_ctx else BLOCKS[1:]
            with ExitStack() as ps:
                G = sbt(ps, [128, NT, 32], F32, "G")
                SP = sbt(ps, [128, NT, 16], F32, "SP")
                LI = sbt(ps, [128, NT, 16], F32, "LI")
                KS = sbt(ps, [128, NT, 16], F32, "KS")
                KSb = sbt(ps, [128, NT, 16], BF16, "KSb")
                EBH = sbt(ps, [128, NT, 16], F32, "EBH")
                EBL = sbt(ps, [128, NT, 16], F32, "EBL")
                gbuf = Buf("gates")
                wg = sbt(ps, [128, 8, 32], BF16, "wg")
                wgb = Buf("wg")
                wmv = wm_d[slot].rearrange("(c p) n -> p c n", p=128)
                dma("pool", wg[:], wmv[:, :, 3584:3616], writes=[wgb])
                gbias = small[:, S_MGB + slot * 32:S_MGB + slot * 32 + 32]
                for tt in range(NT):
                    bk = 0 if tt < 16 else 1
                    o = (tt % 16) * 32
                    for k in range(8):
                        op("pe", lambda e, tt=tt, k=k, bk=bk, o=o: e.matmul(pb[bk][:, o:o + 32], lhsT=uT[:, k, tt * 128:(tt + 1) * 128], rhs=wg[:, k, :],
                                                                            start=(k == 0), stop=(k == 7)),
                           reads=[wgb] + ub, writes=[pbb[bk]], inc=(k == 7))
                op("dve", lambda e: e.tensor_tensor(out=G[:, 0:16, :], in0=pb[0][:].rearrange("p (t g) -> p t g", g=32),
                                                    in1=gbias.unsqueeze(1).broadcast_to([128, 16, 32]), op=ALU.add), reads=[pbb[0], cbuf], writes=[gbuf])
                op("dve", lambda e: e.tensor_tensor(out=G[:, 16:18, :], in0=pb[1][:, 0:64].rearrange("p (t g) -> p t g", g=32),
                                                    in1=gbias.unsqueeze(1).broadcast_to([128, 2, 32]), op=ALU.add), reads=[pbb[1], cbuf], writes=[gbuf])
                for d in range(2):
                    op("dve", lambda e, d=d: e.tensor_copy(out=LI[:, :, d * 8:d * 8 + 8], in_=G[:, :, d * 16:d * 16 + 8]), reads=[gbuf], writes=[gbuf])
                    op("act", lambda e, d=d: e.activation(out=SP[:, :, d * 8:d * 8 + 8], in_=G[:, :, d * 16 + 8:d * 16 + 16], func=AF.Exp, scale=-1.0),
                       reads=[gbuf], writes=[gbuf])
                op("act", lambda e: e.activation(out=SP[:], in_=SP[:], func=AF.Ln, scale=1.0, bias=1.0), reads=[gbuf], writes=[gbuf])
                for tt in range(NT):
                    bk = 2 if tt < 16 else 3
                    o = (tt % 16) * 32
                    op("pe", lambda e, tt=tt, bk=bk, o=o: e.matmul(pb[bk][:, o:o + 8], lhsT=ntricf, rhs=SP[:, tt, 0:8], start=True, stop=True),
                       reads=[gbuf, cbuf], writes=[pbb[bk]], inc=False)
                    op("pe", lambda e, tt=tt, bk=bk, o=o: e.matmul(pb[bk][:, o + 8:o + 16], lhsT=ntricb, rhs=SP[:, tt, 8:16], start=True, stop=True),
                       reads=[gbuf, cbuf], writes=[pbb[bk]], inc=False)
                    op("pe", lambda e, tt=tt, bk=bk, o=o: e.matmul(pb[bk][:, o + 16:o + 32], lhsT=nhalf[:], rhs=SP[:, tt, :], start=True, stop=True),
                       reads=[gbuf, cbuf], writes=[pbb[bk]], inc=True)
                for (bk, a, b_) in ((2, 0, 16), (3, 16, 18)):
                    nn = b_ - a
                    pv = pb[bk][:, 0:nn * 32].rearrange("p (t g) -> p t g", g=32)
                    op("dve", lambda e, pv=pv, a=a, b_=b_: e.tensor_tensor(out=KS[:, a:b_, :], in0=LI[:, a:b_, :], in1=pv[:, :, 0:16], op=ALU.subtract),
                       reads=[pbb[bk], gbuf], writes=[gbuf])
                    op("act", lambda e, pv=pv, a=a, b_=b_: e.activation(out=EBH[:, a:b_, :], in_=pv[:, :, 16:32], func=AF.Exp), reads=[pbb[bk]], writes=[gbuf])
                    op("act", lambda e, pv=pv, a=a, b_=b_: e.activation(out=EBL[:, a:b_, :], in_=pv[:, :, 16:32], func=AF.Exp, scale=2.0), reads=[pbb[bk]], writes=[gbuf])
                op("act", lambda e: e.activation(out=KS[:], in_=KS[:], func=AF.Exp), reads=[gbuf], writes=[gbuf])
                op("dve", lambda e: e.tensor_copy(out=KSb[:], in_=KS[:]), reads=[gbuf], writes=[gbuf])

                wh = [sbt(ps, [128, 8, 448], BF16, "wh") for _ in range(2)]
                whb = [Buf("wh0"), Buf("wh1")]
                wo = [sbt(ps, [128, D], BF16, "wo") for _ in range(2)]
                wob = [Buf("wo0"), Buf("wo1")]
                Qb = [sbt(ps, [64, T], BF16, "Qb") for _ in range(2)]
                qbb = [Buf("Qbf"), Buf("Qbb")]
                KT = sbt(ps, [64, T], BF16, "KT")
                ktb = Buf("KT")
                Ktok = sbt(ps, [128, NT, 64], BF16, "Ktok")
                kkb = Buf("Ktok")
                Va = [sbt(ps, [128, NT, 128], BF16, "Va") for _ in range(2)]
                vab = [Buf("Vaf"), Buf("Vab")]
                HS = sbt(ps, [128, T], F32, "HS")
                hsb = [Buf("hs%d" % i) for i in range(NT)]
                Cst = [sbt(ps, [64, 256], F32, "Cst") for _ in range(2)]
                csb = [Buf("Cf"), Buf("Cb")]
                Cbf = [[sbt(ps, [64, 256], BF16, "Cbf") for _ in range(2)] for _ in range(2)]
                cbb = [[Buf("Cbf%d_%d" % (d_, i)) for i in range(2)] for d_ in range(2)]
                ctmp = [sbt(ps, [64, 256], F32, "ctmp") for _ in range(2)]
                ctb = [Buf("ct0"), Buf("ct1")]
                ebt = [sbt(ps, [64, 512], F32, "ebt") for _ in range(2)]
                ebb = [Buf("eb0"), Buf("eb1")]
                Pm = [sbt(ps, [128, 128], BF16, "Pm") for _ in range(4)]
                pmb = [Buf("Pm%d" % i) for i in range(4)]
                adn = [sbt(ps, [128, 128], F32, "adn") for _ in range(2)]
                adb = [Buf("ad0"), Buf("ad1")]
                htm = [sbt(ps, [128, 128], F32, "htm") for _ in range(2)]
                htb = [Buf("ht0"), Buf("ht1")]
                sqh = sbt(ps, [128, 512], BF16, "sqh")
                sqhb = Buf("sqh")
                rsh = sbt(ps, [128, 512], F32, "rsh")
                rshb = Buf("rsh")
                sig = sbt(ps, [128, 512], F32, "sig")
                sigb = Buf("sig")
                hn = sbt(ps, [128, 512], F32, "hn")
                hnb = Buf("hn")
                ao = [sbt(ps, [128, 512], BF16, "ao") for _ in range(2)]
                aob = [Buf("ao0"), Buf("ao1")]
                wov = wmo_d[slot]

                def load_head(h):
                    s = h % 2
                    dma("pool", wh[s][:], wmv[:, :, h * 448:(h + 1) * 448], writes=[whb[s]])
                    dma("pool", wo[s][:], wov[h * 128:(h + 1) * 128, :], writes=[wob[s]])

                order = [list(range(NT)), [1, 0] + list(range(NT - 1, 1, -1))]
                load_head(0)
                for h in range(8):
                    s = h % 2
                    if h + 1 < 8:
                        load_head(h + 1)
                    w = wh[s]
                    for (t0, n, j) in BLOCKS:
                        bidx = BLOCKS.index((t0, n, j))
                        for d in range(2):
                            ntr = ntricf if d == 0 else ntricb
                            for ti in range(n // 128):
                                tt = t0 // 128 + ti
                                op("pe", lambda e, d=d, tt=tt, ti=ti, ntr=ntr: e.matmul(pb[d][0:64, ti * 128:(ti + 1) * 128],
                                                                                         lhsT=SP[:, tt, d * 8 + h:d * 8 + h + 1].broadcast_to([128, 64]), rhs=ntr,
                                                                                         start=True, stop=True),
                                   reads=[gbuf, cbuf], writes=[pbb[d]], inc=(ti == n // 128 - 1))
                            op("act", lambda e, d=d: e.activation(out=ebt[d][:, 0:n], in_=pb[d][0:64, 0:n], func=AF.Exp), reads=[pbb[d]], writes=[ebb[d]])
                        for k in range(8):
                            op("pe", lambda e, k=k: e.matmul(pb[2][0:64, 0:n], lhsT=w[:, k, 0:64], rhs=uT[:, k, t0:t0 + n], start=(k == 0), stop=(k == 7)),
                               reads=[whb[s], ub[bidx]], writes=[pbb[2]], inc=(k == 7))
                        for d in range(2):
                            op("dve", lambda e, d=d: e.tensor_tensor(out=Qb[d][:, t0:t0 + n], in0=pb[2][0:64, 0:n], in1=ebt[d][:, 0:n], op=ALU.mult),
                               reads=[pbb[2], ebb[d]], writes=[qbb[d]])
                        for k in range(8):
                            op("pe", lambda e, k=k: e.matmul(pb[3][0:64, 0:n], lhsT=w[:, k, 64:128], rhs=uT[:, k, t0:t0 + n], start=(k == 0), stop=(k == 7)),
                               reads=[whb[s], ub[bidx]], writes=[pbb[3]], inc=(k == 7))
                        op("act", lambda e: e.activation(out=KT[:, t0:t0 + n], in_=pb[3][0:64, 0:n], func=AF.Identity, scale=0.125), reads=[pbb[3]], writes=[ktb])
                    for tt in range(NT):
                        bk = 4 + tt % 2
                        bidx = 0 if tt < 2 else 1 + (tt - 2) // 4
                        for k in range(8):
                            op("pe", lambda e, k=k, tt=tt, bk=bk: e.matmul(pb[bk][:, 0:192], lhsT=uT[:, k, tt * 128:(tt + 1) * 128], rhs=w[:, k, 128:320],
                                                                           start=(k == 0), stop=(k == 7)),
                               reads=[whb[s], ub[bidx]], writes=[pbb[bk]], inc=(k == 7))
                        op("act", lambda e, tt=tt, bk=bk: e.activation(out=Ktok[:, tt, :], in_=pb[bk][:, 0:64], func=AF.Identity, scale=0.125), reads=[pbb[bk]], writes=[kkb])
                        for d in range(2):
                            op("dve", lambda e, tt=tt, bk=bk, d=d: e.tensor_scalar(out=Va[d][:, tt, :], in0=pb[bk][:, 64:192], scalar1=KS[:, tt, d * 8 + h:d * 8 + h + 1],
                                                                                    scalar2=None, op0=ALU.mult),
                               reads=[pbb[bk], gbuf], writes=[vab[d]])
                    op("dve", lambda e: e.memset(HS[:], 0.0), writes=hsb)
                    for d in range(2):
                        op("dve", lambda e, d=d: e.memset(Cst[d][:], 0.0), writes=[csb[d]])
                    def s1(step):
                        for d in range(2):
                            tt = order[d][step]
                            col = d * 8 + h
                            tsl = slice(tt * 128, (tt + 1) * 128)
                            if not (need_ctx or tt >= 2):
                                continue
                            cq = step % 2
                            if step > 0:
                                op("dve", lambda e: e.tensor_scalar(out=Cbf[d][cq][:], in0=Cst[d][:], scalar1=EBH[0:64, tt, col:col + 1], scalar2=None, op0=ALU.mult),
                                   reads=[csb[d], gbuf], writes=[cbb[d][cq]])
                            bS = d
                            p_ = (step % 2) * 2 + d
                            op("pe", lambda e: e.matmul(pb[bS][:, 0:128], lhsT=KT[:, tsl], rhs=Qb[d][:, tsl], start=True, stop=True),
                               reads=[ktb, qbb[d]], writes=[pbb[bS]])
                            mk = maskf_bf if d == 0 else maskb_bf
                            op("dve", lambda e: e.tensor_tensor(out=Pm[p_][:], in0=pb[bS][:, 0:128], in1=mk[:], op=ALU.mult),
                               reads=[pbb[bS], cbuf], writes=[pmb[p_]])

                    def s3(step):
                        if step >= NT - 1:
                            return
                        for d in range(2):
                            tt = order[d][step]
                            col = d * 8 + h
                            ksl = KSb[:, tt, col:col + 1].broadcast_to([128, 128])
                            bK = 6 + d
                            op("pe", lambda e: e.matmul(pb[bK][0:64, 0:128], lhsT=Ktok[:, tt, :], rhs=Va[d][:, tt, :], start=True, stop=True),
                               reads=[kkb, vab[d]], writes=[pbb[bK]], inc=False)
                            op("pe", lambda e: e.matmul(pb[bK][0:64, 128:256], lhsT=Ktok[:, tt, :], rhs=ksl, start=True, stop=True),
                               reads=[kkb, gbuf], writes=[pbb[bK]])
                            op("dve", lambda e: e.tensor_scalar(out=ctmp[d][:], in0=pb[bK][0:64, 0:256], scalar1=EBH[0:64, tt, col:col + 1], scalar2=None, op0=ALU.mult),
                               reads=[pbb[bK], gbuf], writes=[ctb[d]])
                            op("dve", lambda e: e.scalar_tensor_tensor(out=Cst[d][:], in0=Cst[d][:], scalar=EBL[0:64, tt, col:col + 1], in1=ctmp[d][:], op0=ALU.mult, op1=ALU.add),
                               reads=[ctb[d], gbuf, csb[d]], writes=[csb[d]])

                    def s2(step):
                        for d in range(2):
                            tt = order[d][step]
                            col = d * 8 + h
                            tsl = slice(tt * 128, (tt + 1) * 128)
                            if not (need_ctx or tt >= 2):
                                continue
                            ksl = KSb[:, tt, col:col + 1].broadcast_to([128, 128])
                            cq = step % 2
                            bN = 2 + d
                            bD = 4 + d
                            p_ = (step % 2) * 2 + d
                            last = (step == 0)
                            op("pe", lambda e: e.matmul(pb[bN][:, 0:128], lhsT=Va[d][:, tt, :], rhs=Pm[p_][:], start=True, stop=last),
                               reads=[vab[d], pmb[p_]], writes=[pbb[bN]], inc=last)
                            if not last:
                                op("pe", lambda e: e.matmul(pb[bN][:, 0:128], lhsT=Cbf[d][cq][:, 0:128], rhs=Qb[d][:, tsl], start=False, stop=True),
                                   reads=[cbb[d][cq], qbb[d]], writes=[pbb[bN]])
                            op("pe", lambda e: e.matmul(pb[bD][:, 0:128], lhsT=ksl, rhs=Pm[p_][:], start=True, stop=last),
                               reads=[gbuf, pmb[p_]], writes=[pbb[bD]], inc=last)
                            if not last:
                                op("pe", lambda e: e.matmul(pb[bD][:, 0:128], lhsT=Cbf[d][cq][:, 128:256], rhs=Qb[d][:, tsl], start=False, stop=True),
                                   reads=[cbb[d][cq], qbb[d]], writes=[pbb[bD]])
                            op("act", lambda e: e.activation(out=adn[d][:], in_=pb[bD][:, 0:128], func=AF.Abs), reads=[pbb[bD]], writes=[adb[d]])
                            op("dve", lambda e: e.tensor_scalar(out=adn[d][:], in0=adn[d][:], scalar1=1.0, scalar2=None, op0=ALU.max), reads=[adb[d]], writes=[adb[d]])
                            op("dve", lambda e: e.reciprocal(out=adn[d][:], in_=adn[d][:]), reads=[adb[d]], writes=[adb[d]])
                            op("dve", lambda e: e.tensor_tensor(out=htm[d][:], in0=pb[bN][:, 0:128], in1=adn[d][:], op=ALU.mult),
                               reads=[pbb[bN], adb[d]], writes=[htb[d]])
                            op("dve", lambda e: e.tensor_tensor(out=HS[:, tsl], in0=HS[:, tsl], in1=htm[d][:], op=ALU.add),
                               reads=[htb[d], hsb[tt]], writes=[hsb[tt]])

                    s1(0)
                    for step in range(NT):
                        s3(step)
                        if step + 1 < NT:
                            s1(step + 1)
                        s2(step)
                    hnw = small[:, S_MHN + slot * 8 + h:S_MHN + slot * 8 + h + 1]
                    for bi, (t0, n, j) in enumerate(out_blocks):
                        bidx = BLOCKS.index((t0, n, j))
                        hsl = [hsb[i] for i in range(t0 // 128, (t0 + n) // 128)]
                        op("act", lambda e: e.activation(out=sqh[:, 0:n], in_=HS[:, t0:t0 + n], func=AF.Square), reads=hsl, writes=[sqhb])
                        op("pe", lambda e: e.matmul(pb[0][:, 0:n], lhsT=ones_bf[:], rhs=sqh[:, 0:n], start=True, stop=True), reads=[sqhb, cbuf], writes=[pbb[0]])
                        rstd_from_psum(pb[0][:, 0:n], rsh[:, 0:n], n, 128.0, pbb[0], rshb)
                        for k in range(8):
                            op("pe", lambda e, k=k: e.matmul(pb[1][:, 0:n], lhsT=w[:, k, 320:448], rhs=uT[:, k, t0:t0 + n], start=(k == 0), stop=(k == 7)),
                               reads=[whb[s], ub[bidx]], writes=[pbb[1]], inc=(k == 7))
                        op("act", lambda e: e.activation(out=sig[:, 0:n], in_=pb[1][:, 0:n], func=AF.Exp, scale=-1.0), reads=[pbb[1]], writes=[sigb])
                        op("dve", lambda e: e.tensor_scalar(out=sig[:, 0:n], in0=sig[:, 0:n], scalar1=1.0, scalar2=None, op0=ALU.add), reads=[sigb], writes=[sigb])
                        op("dve", lambda e: e.reciprocal(out=sig[:, 0:n], in_=sig[:, 0:n]), reads=[sigb], writes=[sigb])
                        op("dve", lambda e: e.scalar_tensor_tensor(out=hn[:, 0:n], in0=HS[:, t0:t0 + n], scalar=hnw, in1=rsh[:, 0:n], op0=ALU.mult, op1=ALU.mult),
                           reads=hsl + [rshb, cbuf], writes=[hnb])
                        a_ = bi % 2
                        op("dve", lambda e, a_=a_: e.tensor_tensor(out=ao[a_][:, 0:n], in0=hn[:, 0:n], in1=sig[:, 0:n], op=ALU.mult), reads=[hnb, sigb], writes=[aob[a_]])
                        outproj_acc([6, 7], [wo[s]], [ao[a_][:, 0:n]], t0, n, j, [wob[s], aob[a_]])
                kb.barrier()

        def proj_rope(ps_res, w, c0, cr0, dst_ap_fn, dbuf, wbuf, banks, tmps, tmpb, rope_tiles, scale_ctx_copy=True):
            for (t0, n, j) in BLOCKS:
                bidx = BLOCKS.index((t0, n, j))
                b0, b1 = banks
                for k in range(8):
                    op("pe", lambda e, k=k: e.matmul(pb[b0][:, 0:n], lhsT=w[:, k, c0:c0 + 128], rhs=uT[:, k, t0:t0 + n], start=(k == 0), stop=(k == 7)),
                       reads=[wbuf, ub[bidx]], writes=[pbb[b0]], inc=(k == 7))
                if j == 1:
                    op("act", lambda e: e.activation(out=dst_ap_fn(t0, n), in_=pb[b0][:, 0:n], func=AF.Identity), reads=[pbb[b0]], writes=[dbuf])
                    continue
                for k in range(8):
                    op("pe", lambda e, k=k: e.matmul(pb[b1][:, 0:n], lhsT=w[:, k, cr0:cr0 + 128], rhs=uT[:, k, t0:t0 + n], start=(k == 0), stop=(k == 7)),
                       reads=[wbuf, ub[bidx]], writes=[pbb[b1]], inc=(k == 7))
                rc, rs_, rb = rope_tiles(t0)
                op("dve", lambda e: e.tensor_tensor(out=tmps[0][:, 0:n], in0=pb[b0][:, 0:n], in1=rc, op=ALU.mult), reads=[pbb[b0], rb], writes=[tmpb[0]])
                op("dve", lambda e: e.tensor_tensor(out=tmps[1][:, 0:n], in0=pb[b1][:, 0:n], in1=rs_, op=ALU.mult), reads=[pbb[b1], rb], writes=[tmpb[1]])
                op("dve", lambda e: e.tensor_tensor(out=dst_ap_fn(t0, n), in0=tmps[0][:, 0:n], in1=tmps[1][:, 0:n], op=ALU.add), reads=[tmpb[0], tmpb[1]], writes=[dbuf])

        def load_rope(ps):
            rc = sbt(ps, [128, NLAT], F32, "ropec")
            rs_ = sbt(ps, [128, NLAT], F32, "ropes")
            rb = Buf("rope")
            dma("sp", rc[:], ropec_l.get(), writes=[rb])
            dma("sp", rs_[:], ropes_l.get(), writes=[rb])

            def tiles(t0):
                l0 = t0 - NCTX
                return rc[:, l0:l0 + 512], rs_[:, l0:l0 + 512], rb
            return tiles

        def swa_phase(L, need_ctx):
            with ExitStack() as ps:
                rope_tiles = load_rope(ps)
                wsv = ws_d[0].rearrange("(c p) n -> p c n", p=128)
                wov = wso_d[0].rearrange("(h p) n -> p h n", p=64)
                wg_ = [sbt(ps, [128, 8, 832], BF16, "wsg") for _ in range(1)]
                wgb = [Buf("wsg0"), Buf("wsg1")]
                wo = [sbt(ps, [64, 4, D], BF16, "wso") for _ in range(1)]
                wob = [Buf("wso0"), Buf("wso1")]
                QT = sbt(ps, [128, 2, T], BF16, "QT")
                qtb = Buf("QT")
                KT = sbt(ps, [128, T], BF16, "KT")
                ktb = Buf("KT")
                Va = sbt(ps, [128, NT, 128], BF16, "Va")
                vab = Buf("Va")
                aog = sbt(ps, [64, 4, T], BF16, "aog")
                aob = [Buf("aog%d" % i) for i in range(NT)]
                tmps = [sbt(ps, [128, 512], F32, "rt") for _ in range(2)]
                tmpb = [Buf("rt0"), Buf("rt1")]
                Pt = [sbt(ps, [128, 512], BF16, "Pt") for _ in range(3)]
                ptb = [Buf("Pt%d" % i) for i in range(3)]
                dn = sbt(ps, [128, 512], F32, "dn")
                dnb = Buf("dn")
                ES = sbt(ps, [128, 16], F32, "ES")
                esb = Buf("ES")
                op("act", lambda e: e.activation(out=ES[:], in_=small[:, S_SINK:S_SINK + 16], func=AF.Exp), reads=[cbuf], writes=[esb])
                op("dve", lambda e: e.memset(Va[:, :, 64:128], 1.0), writes=[vab])

                def load_g(g):
                    s = 0
                    dma("pool", wg_[s][:], wsv[:, :, g * 832:(g + 1) * 832], writes=[wgb[s]])
                    dma("pool", wo[s][:], wov[:, g * 4:(g + 1) * 4, :], writes=[wob[s]])

                pti = 0
                for g in range(4):
                    s = 0
                    load_g(g)
                    w = wg_[s]
                    if SWA_STOP < 1:
                        continue
                    proj_rope(ps, w, 512, 640, lambda t0, n: KT[:, t0:t0 + n], ktb, wgb[s], (0, 1), tmps, tmpb, rope_tiles)
                    for jq in range(SWA_NQ):
                        if SWA_SUB < 1:
                            continue
                        proj_rope(ps, w, jq * 128, 256 + jq * 128, lambda t0, n, jq=jq: QT[:, jq, t0:t0 + n], qtb, wgb[s], SWA_QB, tmps, tmpb, rope_tiles)
                    for tt in range(NT):
                        if SWA_SUB < 2:
                            continue
                        bk = 4 + (tt // 8) % 2
                        o = (tt % 8) * 64
                        bidx = 0 if tt < 2 else 1 + (tt - 2) // 4
                        for k in range(8):
                            op("pe", lambda e, k=k, tt=tt, bk=bk, o=o: e.matmul(pb[bk][:, o:o + 64], lhsT=uT[:, k, tt * 128:(tt + 1) * 128], rhs=w[:, k, 768:832],
                                                                                start=(k == 0), stop=(k == 7)),
                               reads=[wgb[s], ub[bidx]], writes=[pbb[bk]], inc=(k == 7))
                        if tt % 8 == 7 or tt == NT - 1:
                            a = (tt // 8) * 8
                            nn = tt + 1 - a
                            op("act", lambda e, a=a, nn=nn, bk=bk: e.activation(out=Va[:, a:a + nn, 0:64], in_=pb[bk][:, 0:nn * 64].rearrange("p (t d) -> p t d", d=64),
                                                                                func=AF.Identity), reads=[pbb[bk]], writes=[vab])
                    aitems = []
                    for qt in range(NT):
                        if SWA_STOP < 2:
                            continue
                        if qt < 2:
                            if not need_ctx:
                                continue
                            kts = [0, 1]
                        else:
                            kts = [0, 1] + [kt for kt in (qt - 1, qt, qt + 1) if 2 <= kt < NT]
                        for ki, kt in enumerate(kts):
                            aitems.append((qt, kt, ki, len(kts)))

                    def s_stage(i):
                        qt, kt, ki, nk = aitems[i]
                        qsl = slice(qt * 128, (qt + 1) * 128)
                        ksl = slice(kt * 128, (kt + 1) * 128)
                        bS = (i % 2) * 2
                        p_ = i % 3
                        op("pe", lambda e: e.matmul(pb[bS][:, 0:256], lhsT=KT[0:64, ksl], rhs=QT[0:64, :, qsl], start=True, stop=True),
                           reads=[ktb, qtb], writes=[pbb[bS]])
                        op("pe", lambda e: e.matmul(pb[bS + 1][:, 0:256], lhsT=KT[64:128, ksl], rhs=QT[64:128, :, qsl], start=True, stop=True),
                           reads=[ktb, qtb], writes=[pbb[bS + 1]])
                        op("act", lambda e: e.activation(out=Pt[p_][:, 0:256], in_=pb[bS][:, 0:256], func=AF.Exp, scale=0.125), reads=[pbb[bS]], writes=[ptb[p_]])
                        op("act", lambda e: e.activation(out=Pt[p_][:, 256:512], in_=pb[bS + 1][:, 0:256], func=AF.Exp, scale=0.125), reads=[pbb[bS + 1]], writes=[ptb[p_]])
                        if qt >= 2 and kt >= 2 and kt != qt:
                            mk = maskb_bf if kt == qt - 1 else maskf_bf
                            op("dve", lambda e: e.tensor_tensor(out=Pt[p_][:].rearrange("p (h q) -> p h q", h=4), in0=Pt[p_][:].rearrange("p (h q) -> p h q", h=4),
                                                                in1=mk[:].unsqueeze(1).broadcast_to([128, 4, 128]), op=ALU.mult),
                               reads=[ptb[p_], cbuf], writes=[ptb[p_]])

                    def pv_stage(i):
                        qt, kt, ki, nk = aitems[i]
                        qsl = slice(qt * 128, (qt + 1) * 128)
                        p_ = i % 3
                        bO = 6 + qt % 2
                        op("pe", lambda e: e.matmul(pb[bO][:], lhsT=Va[:, kt, :], rhs=Pt[p_][:], start=(ki == 0), stop=(ki == nk - 1)),
                           reads=[vab, ptb[p_]], writes=[pbb[bO]], inc=(ki == nk - 1))
                        if ki == nk - 1:
                            op("dve", lambda e: e.tensor_tensor(out=dn[64:128, :].rearrange("p (h q) -> p h q", h=4), in0=pb[bO][64:128, :].rearrange("p (h q) -> p h q", h=4),
                                                                in1=ES[64:128, g * 4:(g + 1) * 4].unsqueeze(2).broadcast_to([64, 4, 128]), op=ALU.add),
                               reads=[pbb[bO], esb], writes=[dnb])
                            op("dve", lambda e: e.reciprocal(out=dn[64:128, :], in_=dn[64:128, :]), reads=[dnb], writes=[dnb])
                            op("dve", lambda e: e.tensor_tensor(out=aog[:, :, qsl], in0=pb[bO][0:64, :].rearrange("p (h q) -> p h q", h=4),
                                                                in1=dn[64:128, :].rearrange("p (h q) -> p h q", h=4), op=ALU.mult),
                               reads=[pbb[bO], dnb], writes=[aob[qt]])

                    if aitems:
                        s_stage(0)
                    for i in range(len(aitems)):
                        if i + 1 < len(aitems):
                            s_stage(i + 1)
                        pv_stage(i)
                    for (t0, n, j) in (BLOCKS if need_ctx else BLOCKS[1:]):
                        if SWA_STOP < 3:
                            continue
                        ab = [aob[i] for i in range(t0 // 128, (t0 + n) // 128)]
                        outproj_acc([0, 1, 2, 3], [wo[s][:, PSORD[pos], :] for pos in range(4)], [aog[:, pos, t0:t0 + n] for pos in range(4)], t0, n, j, [wob[s]] + ab)
                kb.barrier()

        def diff_phase(L, need_ctx):
            lam_init = 0.8 - 0.6 * float(np.exp(-0.3 * L))
            with ExitStack() as ps:
                rope_tiles = load_rope(ps)
                wdv = wd_d[0].rearrange("(c p) n -> p c n", p=128)
                wov = wdo_d[0]
                wh = [sbt(ps, [128, 8, 640], BF16, "wdh") for _ in range(2)]
                whb = [Buf("wdh0"), Buf("wdh1")]
                wo = [sbt(ps, [128, D], BF16, "wdo") for _ in range(2)]
                wob = [Buf("wdo0"), Buf("wdo1")]
                QT = sbt(ps, [128, T], BF16, "QT")
                qtb = Buf("QT")
                KT = sbt(ps, [128, T], BF16, "KT")
                ktb = Buf("KT")
                Vt = sbt(ps, [128, NT, 128], BF16, "Vt")
                vtb = Buf("Vt")
                tmps = [sbt(ps, [128, 512], F32, "rt") for _ in range(2)]
                tmpb = [Buf("rt0"), Buf("rt1")]
                Pt = [sbt(ps, [128, 512], BF16, "Pt") for _ in range(4)]
                ptb = [Buf("Pt%d" % i) for i in range(4)]
                rr = [sbt(ps, [128, 512], F32, "rr") for _ in range(2)]
                rrb = [Buf("rr0"), Buf("rr1")]
                od = sbt(ps, [128, 512], F32, "od")
                odb = Buf("od")
                sqh = sbt(ps, [128, 512], BF16, "sqh")
                sqhb = Buf("sqh")
                rsh = sbt(ps, [128, 512], F32, "rsh")
                rshb = Buf("rsh")
                ao = [sbt(ps, [128, 512], BF16, "ao") for _ in range(2)]
                aob = [Buf("ao0"), Buf("ao1")]
                lam = sbt(ps, [128, 8], F32, "lam")
                lamb = Buf("lam")
                hnw = sbt(ps, [128, 8], F32, "hnw")
                lv = small[:, S_DLAM:S_DLAM + 256]
                op("dve", lambda e: e.tensor_tensor(out=tmps[0][:, 0:64], in0=lv[:, 0:64], in1=lv[:, 64:128], op=ALU.mult), reads=[cbuf], writes=[tmpb[0]])
                op("dve", lambda e: e.tensor_tensor(out=tmps[0][:, 64:128], in0=lv[:, 128:192], in1=lv[:, 192:256], op=ALU.mult), reads=[cbuf], writes=[tmpb[0]])
                op("dve", lambda e: e.tensor_reduce(out=lam[:, 0:2], in_=tmps[0][:, 0:128].rearrange("p (a b) -> p a b", a=2), axis=mybir.AxisListType.X, op=ALU.add),
                   reads=[tmpb[0]], writes=[lamb])
                op("act", lambda e: e.activation(out=lam[:, 0:2], in_=lam[:, 0:2], func=AF.Exp), reads=[lamb], writes=[lamb])
                op("dve", lambda e: e.tensor_tensor(out=lam[:, 2:3], in0=lam[:, 1:2], in1=lam[:, 0:1], op=ALU.subtract), reads=[lamb], writes=[lamb])
                op("dve", lambda e: e.tensor_scalar(out=lam[:, 3:4], in0=lam[:, 2:3], scalar1=-lam_init, scalar2=None, op0=ALU.add), reads=[lamb], writes=[lamb])
                op("dve", lambda e: e.tensor_scalar(out=hnw[:], in0=small[:, S_DHN:S_DHN + 8], scalar1=(1.0 - lam_init), scalar2=None, op0=ALU.mult), reads=[cbuf], writes=[lamb])
                nlam = lam[:, 3:4]

                def load_h(h):
                    s = h % 2
                    dma("pool", wh[s][:], wdv[:, :, h * 640:(h + 1) * 640], writes=[whb[s]])
                    dma("pool", wo[s][:], wov[h * 128:(h + 1) * 128, :], writes=[wob[s]])

                load_h(0)
                pti = 0
                oi = 0
                for h in range(8):
                    s = h % 2
                    if h + 1 < 8:
                        load_h(h + 1)
                    w = wh[s]
                    proj_rope(ps, w, 256, 384, lambda t0, n: KT[:, t0:t0 + n], ktb, whb[s], (0, 1), tmps, tmpb, rope_tiles)
                    proj_rope(ps, w, 0, 128, lambda t0, n: QT[:, t0:t0 + n], qtb, whb[s], (2, 3), tmps, tmpb, rope_tiles)
                    for tt in range(NT):
                        bk = 4 + (tt // 4) % 2
                        o = (tt % 4) * 128
                        bidx = 0 if tt < 2 else 1 + (tt - 2) // 4
                        for k in range(8):
                            op("pe", lambda e, k=k, tt=tt, bk=bk, o=o: e.matmul(pb[bk][:, o:o + 128], lhsT=uT[:, k, tt * 128:(tt + 1) * 128], rhs=w[:, k, 512:640],
                                                                                start=(k == 0), stop=(k == 7)),
                               reads=[whb[s], ub[bidx]], writes=[pbb[bk]], inc=(k == 7))
                        if tt % 4 == 3 or tt == NT - 1:
                            a = (tt // 4) * 4
                            nn = tt + 1 - a
                            op("act", lambda e, a=a, nn=nn, bk=bk: e.activation(out=Vt[:, a:a + nn, :], in_=pb[bk][:, 0:nn * 128].rearrange("p (t d) -> p t d", d=128),
                                                                                func=AF.Identity), reads=[pbb[bk]], writes=[vtb])
                    for bi, (t0, n, j) in enumerate(BLOCKS if need_ctx else BLOCKS[1:]):
                        kts = [0, 1] if j == 1 else list(range(NT))
                        nk = len(kts)

                        def s_stage(ki):
                            kt = kts[ki]
                            ksl = slice(kt * 128, (kt + 1) * 128)
                            for m in range(2):
                                bS = m * 2 + (ki % 2)
                                p_ = (ki % 2) * 2 + m
                                rsl = slice(m * 64, (m + 1) * 64)
                                op("pe", lambda e, bS=bS, ksl=ksl, rsl=rsl: e.matmul(pb[bS][:, 0:n], lhsT=KT[rsl, ksl], rhs=QT[rsl, t0:t0 + n], start=True, stop=True),
                                   reads=[ktb, qtb], writes=[pbb[bS]])
                            for m in range(2):
                                bS = m * 2 + (ki % 2)
                                p_ = (ki % 2) * 2 + m
                                op("act", lambda e, bS=bS, p_=p_: e.activation(out=Pt[p_][:, 0:n], in_=pb[bS][:, 0:n], func=AF.Exp, scale=0.125), reads=[pbb[bS]], writes=[ptb[p_]])

                        def pv_stage(ki):
                            kt = kts[ki]
                            first = (ki == 0)
                            lastk = (ki == nk - 1)
                            for m in range(2):
                                p_ = (ki % 2) * 2 + m
                                op("pe", lambda e, kt=kt, p_=p_, m=m: e.matmul(pb[4 + m][:, 0:n], lhsT=Vt[:, kt, :], rhs=Pt[p_][:, 0:n], start=first, stop=lastk),
                                   reads=[vtb, ptb[p_]], writes=[pbb[4 + m]], inc=lastk)
                                op("pe", lambda e, p_=p_, m=m: e.matmul(pb[6 + m][:, 0:n], lhsT=ones_bf[:], rhs=Pt[p_][:, 0:n], start=first, stop=lastk),
                                   reads=[cbuf, ptb[p_]], writes=[pbb[6 + m]], inc=lastk)

                        s_stage(0)
                        for ki in range(nk):
                            if ki + 1 < nk:
                                s_stage(ki + 1)
                            pv_stage(ki)
                        for m in range(2):
                            op("dve", lambda e, m=m: e.reciprocal(out=rr[m][:, 0:n], in_=pb[6 + m][:, 0:n]), reads=[pbb[6 + m]], writes=[rrb[m]])
                            op("dve", lambda e, m=m: e.tensor_tensor(out=rr[m][:, 0:n], in0=pb[4 + m][:, 0:n], in1=rr[m][:, 0:n], op=ALU.mult), reads=[pbb[4 + m], rrb[m]], writes=[rrb[m]])
                        op("dve", lambda e: e.scalar_tensor_tensor(out=od[:, 0:n], in0=rr[1][:, 0:n], scalar=nlam, in1=rr[0][:, 0:n], op0=ALU.mult, op1=ALU.add),
                           reads=[rrb[0], rrb[1], lamb], writes=[odb])
                        op("act", lambda e: e.activation(out=sqh[:, 0:n], in_=od[:, 0:n], func=AF.Square), reads=[odb], writes=[sqhb])
                        op("pe", lambda e: e.matmul(pb[0][:, 0:n], lhsT=ones_bf[:], rhs=sqh[:, 0:n], start=True, stop=True), reads=[sqhb, cbuf], writes=[pbb[0]])
                        rstd_from_psum(pb[0][:, 0:n], rsh[:, 0:n], n, 128.0, pbb[0], rshb)
                        a_ = oi % 2
                        oi += 1
                        op("dve", lambda e, a_=a_: e.scalar_tensor_tensor(out=ao[a_][:, 0:n], in0=od[:, 0:n], scalar=hnw[:, h:h + 1], in1=rsh[:, 0:n], op0=ALU.mult, op1=ALU.mult),
                           reads=[odb, rshb, lamb], writes=[aob[a_]])
                        outproj_acc([1, 2, 3], [wo[s]], [ao[a_][:, 0:n]], t0, n, j, [wob[s], aob[a_]])
                kb.barrier()

        def final_phase(do_norm):
            with ExitStack() as ps:
                sq = sbt(ps, [128, 8, 512], BF16, "fsq")
                sqb = Buf("fsq")
                rs = sbt(ps, [128, 512], F32, "frs")
                rsb = Buf("frs")
                tmp = sbt(ps, [128, 8, 512], F32, "ftmp")
                tmb = Buf("ftmp")
                ost = [sbt(ps, [128, D], F32, "ost") for _ in range(2)]
                osb = [Buf("ost0"), Buf("ost1")]
                oi = 0
                for (t0, n, j) in BLOCKS[1:]:
                    if do_norm:
                        op("act", lambda e: e.activation(out=sq[:, :, 0:n], in_=hT[:, :, t0:t0 + n], func=AF.Square), reads=hbs(t0, n), writes=[sqb])
                        for c in range(8):
                            op("pe", lambda e, c=c: e.matmul(pb[7][:, 0:n], lhsT=ones_bf[:], rhs=sq[:, c, 0:n], start=(c == 0), stop=(c == 7)),
                               reads=[sqb, cbuf], writes=[pbb[7]], inc=(c == 7))
                        rstd_from_psum(pb[7][:, 0:n], rs[:, 0:n], n, float(D), pbb[7], rsb)
                        for c in range(8):
                            op("dve", lambda e, c=c: e.scalar_tensor_tensor(out=tmp[:, c, 0:n], in0=hT[:, c, t0:t0 + n], scalar=small[:, S_FINAL + c:S_FINAL + c + 1],
                                                                            in1=rs[:, 0:n], op0=ALU.mult, op1=ALU.mult),
                               reads=hbs(t0, n) + [rsb, cbuf], writes=[tmb])
                    else:
                        op("dve", lambda e: e.tensor_copy(out=tmp[:, :, 0:n], in_=hT[:, :, t0:t0 + n]), reads=hbs(t0, n), writes=[tmb])
                    for ti in range(n // 128):
                        o_ = oi % 2
                        oi += 1
                        for half in range(2):
                            bk = (oi * 2 + half) % 4
                            for c4 in range(4):
                                c = half * 4 + c4
                                op("pe", lambda e, c=c, c4=c4, bk=bk, ti=ti: e.transpose(pb[bk][:, c4 * 128:(c4 + 1) * 128], tmp[:, c, ti * 128:(ti + 1) * 128], ident),
                                   reads=[tmb, cbuf], writes=[pbb[bk]], inc=(c4 == 3))
                            if half == 0:
                                op("act", lambda e, bk=bk, o_=o_, half=half: e.activation(out=ost[o_][:, half * 512:(half + 1) * 512], in_=pb[bk][:], func=AF.Identity),
                                   reads=[pbb[bk]], writes=[osb[o_]])
                            else:
                                op("dve", lambda e, bk=bk, o_=o_, half=half: e.tensor_copy(out=ost[o_][:, half * 512:(half + 1) * 512], in_=pb[bk][:]),
                                   reads=[pbb[bk]], writes=[osb[o_]])
                        r0 = t0 - NCTX + ti * 128
                        dma("sp", out_d[r0:r0 + 128, :], ost[o_][:], reads=[osb[o_]])
                kb.wait_all_dma("sp")

        for L in layers:
            kind, slot = L % 3, L // 3
            need_ctx = L < DEPTH - 1
            blocks = BLOCKS if need_ctx else BLOCKS[1:]
            ada_phase(L)
            if do_mixer:
                norm_mod(amix, 0, BLOCKS)
                if kind == 0:
                    mlstm_phase(L, slot, need_ctx)
                elif kind == 1:
                    swa_phase(L, need_ctx)
                else:
                    diff_phase(L, need_ctx)
            if do_ffn:
                norm_mod(affn, 24, blocks)
                if do_ffn != "norm":
                    ffn_phase(L, blocks)
        final_phase(final_norm)
    nc._declared_inputs = list(declared)
    return nc


def _pcol(v):
    return np.ascontiguousarray(np.asarray(v, np.float32).reshape(-1, 128).T)


def _rot_idx(base):
    return list(range(base + 32, base + 64)) + list(range(base, base + 32))


def _consts():
    i = np.arange(128)
    ident = (i[:, None] == i[None, :]).astype(np.float32)
    maskf = (i[:, None] <= i[None, :]).astype(np.float32)
    maskb = (i[:, None] >= i[None, :]).astype(np.float32)
    return np.ascontiguousarray(np.concatenate([ident, maskf, maskb, -(maskf - 0.5), -(maskb - 0.5)], axis=1))


def _rope_tables():
    rows = NLAT // 64
    row = np.repeat(np.arange(rows), 64).astype(np.float32)
    col = np.tile(np.arange(64), rows).astype(np.float32)
    quarter = 16
    inv = (np.float32(10000.0) ** (-np.arange(quarter, dtype=np.float32) / np.float32(quarter))).astype(np.float32)
    ang = np.concatenate([row[:, None] * inv, col[:, None] * inv], axis=-1).astype(np.float32)
    cos = np.cos(ang).astype(np.float32).T
    sin = np.sin(ang).astype(np.float32).T
    c64 = np.concatenate([cos, cos], 0)
    s64 = np.concatenate([-sin, sin], 0)
    return np.ascontiguousarray(np.concatenate([c64, c64], 0)), np.ascontiguousarray(np.concatenate([s64, s64], 0))


def prepare_shared(inp):
    sh = {}
    sh["cst"] = _consts()
    sh["ropec"], sh["ropes"] = _rope_tables()
    sh["ada_w"] = np.ascontiguousarray(inp["ada_w"], dtype=np.float32)
    sh["ffn_w1"] = np.ascontiguousarray(inp["ffn_w1"], dtype=np.float32)
    sh["ffn_w2"] = np.ascontiguousarray(inp["ffn_w2"], dtype=np.float32)
    wm = []
    for s in range(inp["mlstm_w_in"].shape[0]):
        W = inp["mlstm_w_in"][s]
        cols = []
        for h in range(8):
            cols += list(range(h * 64, h * 64 + 64))
            cols += list(range(512 + h * 64, 512 + h * 64 + 64))
            cols += list(range(512 + h * 64, 512 + h * 64 + 64))
            cols += list(range(1024 + h * 128, 1024 + h * 128 + 128))
            cols += list(range(2048 + h * 128, 2048 + h * 128 + 128))
        cols += list(range(3072, 3104))
        wm.append(W[:, cols])
    sh["wm"] = np.ascontiguousarray(np.stack(wm), dtype=np.float32)
    sh["wmo"] = np.ascontiguousarray(inp["mlstm_w_out"], dtype=np.float32)
    W = inp["swa_w_in"][0]
    cols = []
    for g in range(4):
        q = []
        qr = []
        for hh in range(4):
            b = (4 * g + hh) * 64
            q += list(range(b, b + 64))
            qr += _rot_idx(b)
        kb_ = 1024 + g * 64
        k = list(range(kb_, kb_ + 64))
        kr = _rot_idx(kb_)
        v = list(range(1280 + g * 64, 1280 + g * 64 + 64))
        cols += q + qr + k + k + kr + kr + v
    sh["ws"] = np.ascontiguousarray(W[:, cols][None], dtype=np.float32)
    sh["wso"] = np.ascontiguousarray(inp["swa_w_out"], dtype=np.float32)
    W = inp["diff_w_in"][0]
    cols = []
    for h in range(8):
        qb_ = h * 128
        kb_ = 1024 + h * 128
        cols += list(range(qb_, qb_ + 128)) + _rot_idx(qb_) + _rot_idx(qb_ + 64)
        cols += list(range(kb_, kb_ + 128)) + _rot_idx(kb_) + _rot_idx(kb_ + 64)
        cols += list(range(2048 + h * 128, 2048 + h * 128 + 128))
    sh["wd"] = np.ascontiguousarray(W[:, cols][None], dtype=np.float32)
    sh["wdo"] = np.ascontiguousarray(inp["diff_w_out"], dtype=np.float32)
    small = np.zeros((128, NS), np.float32)
    for L in range(DEPTH):
        small[:, L * SL:L * SL + 8] = _pcol(inp["norm_mix"][L])
        small[:, L * SL + 8:L * SL + 16] = _pcol(inp["norm_ffn"][L])
        small[:, L * SL + 16:L * SL + 64] = _pcol(inp["ada_b"][L])
    small[:, S_FINAL:S_FINAL + 8] = _pcol(inp["final_norm"])
    for s in range(inp["mlstm_head_norm"].shape[0]):
        small[:, S_MHN + s * 8:S_MHN + s * 8 + 8] = _pcol(inp["mlstm_head_norm"][s])
        small[:, S_MGB + s * 32:S_MGB + s * 32 + 32] = np.broadcast_to(np.asarray(inp["mlstm_gate_b"][s], np.float32).reshape(1, 32), (128, 32))
    sink = np.asarray(inp["swa_sink"][0], np.float32)
    sord = [4 * g + PSORD[p] for g in range(4) for p in range(4)]
    small[:, S_SINK:S_SINK + 16] = np.broadcast_to(sink[sord][None, :], (128, 16))
    small[:, S_DHN:S_DHN + 8] = _pcol(inp["diff_head_norm"][0])
    lamv = np.concatenate([np.asarray(inp[k][0], np.float32) for k in ("diff_lambda_q1", "diff_lambda_k1", "diff_lambda_q2", "diff_lambda_k2")])
    small[:, S_DLAM:S_DLAM + 256] = np.broadcast_to(lamv[None, :], (128, 256))
    sh["small"] = small
    return sh


def prepare_core(inp, b):
    xin = np.ascontiguousarray(np.concatenate([inp["ctx"][b], inp["x"][b]], axis=0), dtype=np.float32)
    cv = np.zeros((128, 16), np.float32)
    cv[:, 0::2] = _pcol(inp["c"][b])
    cv[:, 1::2] = _pcol(inp["c_ctx"])
    return {"xin": xin, "cv": cv}


_NC_CACHE = {}


def kernel(**inputs):
    inp = {k: np.asarray(v) for k, v in inputs.items()}
    B = inp["x"].shape[0]
    shared = prepare_shared(inp)
    if "full" not in _NC_CACHE:
        _NC_CACHE["full"] = build_program()
    nc = _NC_CACHE["full"]
    in_maps = []
    for b in range(B):
        m = dict(shared)
        m.update(prepare_core(inp, b))
        in_maps.append({k: m[k] for k in nc._declared_inputs})
    res = run_bass_kernel_spmd(nc, in_maps, core_ids=list(range(B)))
    out = np.stack([np.asarray(r["out"], dtype=np.float32) for r in res.results], axis=0)
    return out
```

```python
import numpy as np
from contextlib import ExitStack
import concourse.bass as bass
import concourse.mybir as mybir
from concourse.bass_utils import run_bass_kernel_spmd

F32 = mybir.dt.float32
BF16 = mybir.dt.bfloat16
ALU = mybir.AluOpType
AF = mybir.ActivationFunctionType

D = 1024
NCTX = 256
NLAT = 2048
T = NCTX + NLAT
NT = T // 128
DEPTH = 4
EPS = 1e-6
NDS = 24
BLOCKS = [(0, 256, 1), (256, 512, 0), (768, 512, 0), (1280, 512, 0), (1792, 512, 0)]

SL = 64
S_FINAL = 4 * SL
S_MHN = S_FINAL + 8
S_MGB = S_MHN + 16
S_SINK = S_MGB + 64
S_DHN = S_SINK + 16
S_DLAM = S_DHN + 8
NS = S_DLAM + 256
NCONST = 5 * 128
PSORD = [0, 2, 1, 3]
FFN_ITEMS = 1000
SWA_STOP = 9
SWA_SUB = 9
SWA_QB = (0, 1)
SWA_NQ = 2


class Buf:
    __slots__ = ("name", "w", "r")

    def __init__(self, name):
        self.name = name
        self.w = None
        self.r = {}


class KB:
    def __init__(self, nc, es):
        self.nc = nc
        self.es = es
        self.E = {"pe": nc.tensor, "act": nc.scalar, "dve": nc.vector, "pool": nc.gpsimd, "sp": nc.sync}
        self.sem = {e: es.enter_context(nc.semaphore("s_" + e)) for e in self.E}
        self.cnt = {e: 0 for e in self.E}
        self.seen = {e: {} for e in self.E}
        self.dsem = [es.enter_context(nc.semaphore("d%d" % i)) for i in range(NDS)]
        self.dcnt = [0] * NDS
        self.dnext = {"sp": 0, "pool": 0}

    def _wait(self, eng, deps):
        for (sk, val) in deps:
            if sk == eng and eng == "pe":
                continue
            if self.seen[eng].get(sk, 0) >= val:
                continue
            semh = self.sem[sk] if isinstance(sk, str) else self.dsem[sk]
            self.E[eng].wait_ge(semh, val)
            self.seen[eng][sk] = val

    @staticmethod
    def _deps(reads, writes):
        deps = []
        for b in reads:
            if b.w is not None:
                deps.append(b.w)
        for b in writes:
            if b.w is not None:
                deps.append(b.w)
            deps.extend(b.r.values())
        return deps

    def op(self, eng, fn, reads=(), writes=(), inc=True):
        self._wait(eng, self._deps(reads, writes))
        ins = fn(self.E[eng])
        if inc:
            self.cnt[eng] += 1
            ins.then_inc(self.sem[eng], 1)
            tk = (eng, self.cnt[eng])
        else:
            tk = (eng, self.cnt[eng] + 1)
        for b in reads:
            b.r[eng] = tk
        for b in writes:
            b.w = tk
            b.r = {}

    def dma(self, q, out, in_, reads=(), writes=()):
        half = NDS // 2
        i = self.dnext[q] + (0 if q == "sp" else half)
        self.dnext[q] = (self.dnext[q] + 1) % half
        deps = self._deps(reads, writes)
        if self.dcnt[i] > 0:
            deps.append((i, self.dcnt[i]))
        self._wait(q, deps)
        self.E[q].dma_start(out=out, in_=in_).then_inc(self.dsem[i], 16)
        self.dcnt[i] += 16
        tk = (i, self.dcnt[i])
        for b in reads:
            b.r[("d", i)] = tk
        for b in writes:
            b.w = tk
            b.r = {}

    def wait_all_dma(self, eng):
        self._wait(eng, [(i, self.dcnt[i]) for i in range(NDS) if self.dcnt[i] > 0])

    def barrier(self):
        deps = [(e, self.cnt[e]) for e in self.E if e != "sp" and self.cnt[e] > 0]
        deps += [(i, self.dcnt[i]) for i in range(NDS) if self.dcnt[i] > 0]
        self._wait("sp", deps)
        self.cnt["sp"] += 1
        self.E["sp"].sem_inc(self.sem["sp"], 1)
        for e in self.E:
            if e != "sp":
                self._wait(e, [("sp", self.cnt["sp"])])


def build_program(layers=(0, 1, 2, 3), do_mixer=True, do_ffn=True, final_norm=True, in_ctx=True):
    nc = bass.Bass("TRN2", target_bir_lowering=False)

    declared = []
    only = None if len(layers) == DEPTH and do_mixer and do_ffn else set()

    class _Lazy:
        def __init__(self, name, shape):
            self.name, self.shape, self.ap_ = name, list(shape), None

        def get(self):
            if self.ap_ is None:
                self.ap_ = nc.dram_tensor(self.name, self.shape, F32, kind="ExternalInput").ap()
                declared.append(self.name)
            return self.ap_

        def __getitem__(self, k):
            return self.get()[k]

    def din(name, shape):
        return _Lazy(name, shape)

    xin = din("xin", [T, D]).get()
    cv_d = din("cv", [128, 16]).get()
    small_d = din("small", [128, NS]).get()
    cst_d = din("cst", [128, NCONST]).get()
    ropec_l = din("ropec", [128, NLAT])
    ropes_l = din("ropes", [128, NLAT])
    ada_w = din("ada_w", [DEPTH, D, 6 * D])
    w1_d = din("ffn_w1", [DEPTH, D, 4 * D])
    w2_d = din("ffn_w2", [DEPTH, 4 * D, D])
    wm_d = din("wm", [2, D, 3616])
    wmo_d = din("wmo", [2, D, D])
    ws_d = din("ws", [1, D, 3328])
    wso_d = din("wso", [1, D, D])
    wd_d = din("wd", [1, D, 5120])
    wdo_d = din("wdo", [1, D, D])
    out_d = nc.dram_tensor("out", [NLAT, D], F32, kind="ExternalOutput").ap()

    with ExitStack() as es:
        kb = KB(nc, es)
        op = kb.op
        dma = kb.dma
        cnt = [0]

        def sbt(es_, shape, dt, name=None):
            cnt[0] += 1
            return es_.enter_context(nc.sbuf_tensor("%s_%d" % (name or "t", cnt[0]), list(shape), dt))

        hT = sbt(es, [128, 8, T], F32, "hT")
        uT = sbt(es, [128, 8, T], BF16, "uT")
        hb = [Buf("h%d" % i) for i in range(NT)]
        ub = [Buf("u%d" % i) for i in range(len(BLOCKS))]
        cst = sbt(es, [128, NCONST], F32, "cst")
        small = sbt(es, [128, NS], F32, "small")
        cv = sbt(es, [128, 16], F32, "cv")
        cond = sbt(es, [128, 8, 2], BF16, "cond")
        ones_bf = sbt(es, [128, 128], BF16, "ones")
        maskf_bf = sbt(es, [128, 128], BF16, "maskf")
        maskb_bf = sbt(es, [128, 128], BF16, "maskb")
        nhalf = sbt(es, [128, 128], F32, "nhalf")
        modsbs = [sbt(es, [128, 48, 2], F32, "modsb") for _ in range(2)]
        amixs = [sbt(es, [128, 8, 2], F32, "amix") for _ in range(2)]
        affns = [sbt(es, [128, 8, 2], F32, "affn") for _ in range(2)]
        modbs = [Buf("mod0"), Buf("mod1")]
        cbuf = Buf("consts")

        class M:
            pass

        def set_mod(L):
            M.modsb, M.amix, M.affn, M.modb = modsbs[L % 2], amixs[L % 2], affns[L % 2], modbs[L % 2]
        ident = cst[:, 0:128]
        maskf = cst[:, 128:256]
        maskb = cst[:, 256:384]
        ntricf = cst[:, 384:512]
        ntricb = cst[:, 512:640]
        pb = [es.enter_context(nc.psum_tensor("pb%d" % i, [128, 512], F32)) for i in range(8)]
        pbb = [Buf("pb%d" % i) for i in range(8)]

        def hbs(t0, n):
            return [hb[i] for i in range(t0 // 128, (t0 + n) // 128)]

        dma("sp", cst[:], cst_d, writes=[cbuf])
        dma("sp", small[:], small_d, writes=[cbuf])
        dma("sp", cv[:], cv_d, writes=[cbuf])
        op("dve", lambda e: e.memset(ones_bf[:], 1.0), writes=[cbuf])
        op("dve", lambda e: e.memset(nhalf[:], -0.5), writes=[cbuf])
        op("dve", lambda e: e.tensor_copy(out=maskf_bf[:], in_=maskf), reads=[cbuf], writes=[cbuf])
        op("dve", lambda e: e.tensor_copy(out=maskb_bf[:], in_=maskb), reads=[cbuf], writes=[cbuf])
        with ExitStack() as ps:
            tmpc = sbt(ps, [128, 16], F32, "tmpc")
            tb_ = Buf("tmpc")
            op("act", lambda e: e.activation(out=tmpc[:], in_=cv[:], func=AF.Exp, scale=-1.0), reads=[cbuf], writes=[tb_])
            op("dve", lambda e: e.tensor_scalar(out=tmpc[:], in0=tmpc[:], scalar1=1.0, scalar2=None, op0=ALU.add), reads=[tb_], writes=[tb_])
            op("dve", lambda e: e.reciprocal(out=tmpc[:], in_=tmpc[:]), reads=[tb_], writes=[tb_])
            op("dve", lambda e: e.tensor_tensor(out=cond[:].rearrange("p k j -> p (k j)"), in0=tmpc[:], in1=cv[:], op=ALU.mult),
               reads=[tb_, cbuf], writes=[cbuf])
            stg = [sbt(ps, [128, D], F32, "stg") for _ in range(2)]
            stb = [Buf("stg0"), Buf("stg1")]
            for tt in range(NT):
                s = tt % 2
                dma("sp", stg[s][:], xin[tt * 128:(tt + 1) * 128, :], writes=[stb[s]])
                for half in range(2):
                    bk = (tt * 2 + half) % 4
                    for c4 in range(4):
                        c = half * 4 + c4
                        op("pe", lambda e, c=c, c4=c4, bk=bk, s=s: e.transpose(pb[bk][:, c4 * 128:(c4 + 1) * 128], stg[s][:, c * 128:(c + 1) * 128], ident),
                           reads=[stb[s], cbuf], writes=[pbb[bk]], inc=(c4 == 3))
                    eng = "act" if half == 0 else "dve"
                    if eng == "act":
                        op("act", lambda e, bk=bk, half=half, tt=tt: e.activation(out=hT[:, half * 4:half * 4 + 4, tt * 128:(tt + 1) * 128],
                                                                                 in_=pb[bk][:].rearrange("p (c n) -> p c n", c=4), func=AF.Identity),
                           reads=[pbb[bk]], writes=[hb[tt]])
                    else:
                        op("dve", lambda e, bk=bk, half=half, tt=tt: e.tensor_copy(out=hT[:, half * 4:half * 4 + 4, tt * 128:(tt + 1) * 128],
                                                                                   in_=pb[bk][:].rearrange("p (c n) -> p c n", c=4)),
                           reads=[pbb[bk]], writes=[hb[tt]])
            kb.barrier()

        def rstd_from_psum(ps_ap, out_ap, n, dim, pbuf, obuf):
            op("act", lambda e: e.activation(out=out_ap, in_=ps_ap, func=AF.Ln, scale=1.0 / dim, bias=EPS), reads=[pbuf], writes=[obuf])
            op("act", lambda e: e.activation(out=out_ap, in_=out_ap, func=AF.Exp, scale=-0.5), reads=[obuf], writes=[obuf])

        def ada_parts(L, scope):
            slots = [sbt(scope, [128, 8, 512], BF16, "adaw") for _ in range(3)]
            sbf = [Buf("adaw%d" % i) for i in range(3)]
            awv = ada_w[L].rearrange("(c p) n -> p c n", p=128)
            pm = pb[7][:, 0:96].rearrange("p (m j) -> p m j", j=2)
            msb, amx, afn, mb = modsbs[L % 2], amixs[L % 2], affns[L % 2], modbs[L % 2]

            def a_dma(g):
                s = g % 3
                dma("pool", slots[s][:], awv[:, :, g * 512:(g + 1) * 512], writes=[sbf[s]])

            def a_mm(g):
                s = g % 3
                for jj in range(4):
                    m = g * 4 + jj
                    for k in range(8):
                        op("pe", lambda e: e.matmul(pm[:, m, :], lhsT=slots[s][:, k, jj * 128:(jj + 1) * 128], rhs=cond[:, k, :], start=(k == 0), stop=(k == 7)),
                           reads=[sbf[s], cbuf], writes=[pbb[7]], inc=(k == 7))

            def a_fin():
                base = L * SL
                op("dve", lambda e: e.tensor_tensor(out=msb[:], in0=pm, in1=small[:, base + 16:base + 64].unsqueeze(2).broadcast_to([128, 48, 2]), op=ALU.add),
                   reads=[pbb[7], cbuf], writes=[mb])
                op("dve", lambda e: e.scalar_tensor_tensor(out=amx[:], in0=msb[:, 8:16, :], scalar=1.0,
                                                           in1=small[:, base:base + 8].unsqueeze(2).broadcast_to([128, 8, 2]), op0=ALU.add, op1=ALU.mult),
                   reads=[mb, cbuf], writes=[mb])
                op("dve", lambda e: e.scalar_tensor_tensor(out=afn[:], in0=msb[:, 32:40, :], scalar=1.0,
                                                           in1=small[:, base + 8:base + 16].unsqueeze(2).broadcast_to([128, 8, 2]), op0=ALU.add, op1=ALU.mult),
                   reads=[mb, cbuf], writes=[mb])

            return a_dma, a_mm, a_fin

        def ada_phase(L):
            with ExitStack() as ps:
                a_dma, a_mm, a_fin = ada_parts(L, ps)
                for g in range(12):
                    a_dma(g)
                    a_mm(g)
                a_fin()
                kb.barrier()

        def norm_mod(a_t, shift_off, blocks):
            with ExitStack() as ps:
                sq = [sbt(ps, [128, 8, 512], BF16, "sq") for _ in range(2)]
                sqb = [Buf("sq0"), Buf("sq1")]
                rs = [sbt(ps, [128, 512], F32, "rs") for _ in range(2)]
                rsb = [Buf("rs0"), Buf("rs1")]
                tmp = [sbt(ps, [128, 4, 512], F32, "nt") for _ in range(2)]
                tmb = [Buf("nt0"), Buf("nt1")]
                for bi, (t0, n, j) in enumerate(blocks):
                    s = bi % 2
                    bk = 6 + s
                    bidx = BLOCKS.index((t0, n, j))
                    op("act", lambda e, s=s: e.activation(out=sq[s][:, :, 0:n], in_=hT[:, :, t0:t0 + n], func=AF.Square), reads=hbs(t0, n), writes=[sqb[s]])
                    for c in range(8):
                        op("pe", lambda e, s=s, c=c, bk=bk: e.matmul(pb[bk][:, 0:n], lhsT=ones_bf[:], rhs=sq[s][:, c, 0:n], start=(c == 0), stop=(c == 7)),
                           reads=[sqb[s], cbuf], writes=[pbb[bk]], inc=(c == 7))
                    rstd_from_psum(pb[bk][:, 0:n], rs[s][:, 0:n], n, float(D), pbb[bk], rsb[s])
                    for half in range(2):
                        op("dve", lambda e, s=s, half=half: e.tensor_tensor(out=tmp[half][:, :, 0:n], in0=hT[:, half * 4:half * 4 + 4, t0:t0 + n],
                                                                            in1=rs[s][:, 0:n].unsqueeze(1).broadcast_to([128, 4, n]), op=ALU.mult),
                           reads=hbs(t0, n) + [rsb[s]], writes=[tmb[half]])
                        for c4 in range(4):
                            c = half * 4 + c4
                            op("act", lambda e, half=half, c4=c4, c=c: e.activation(out=uT[:, c, t0:t0 + n], in_=tmp[half][:, c4, 0:n], func=AF.Identity,
                                                                                    scale=a_t()[:, c, j:j + 1], bias=M.modsb[:, shift_off + c, j:j + 1]),
                               reads=[tmb[half], M.modb], writes=[ub[bidx]])
                kb.barrier()

        def ffn_phase(L, blocks, prefetch=None):
            with ExitStack() as ps:
                pre = ada_parts(prefetch, ps) if prefetch is not None else None
                w1s = [sbt(ps, [128, 8, 512], BF16, "w1s") for _ in range(3)]
                w2s = [sbt(ps, [128, 4, D], BF16, "w2s") for _ in range(3)]
                wb = [Buf("ffw%d" % i) for i in range(3)]
                hid = [sbt(ps, [128, 4, 512], BF16, "hid") for _ in range(2)]
                hib = [Buf("hid0"), Buf("hid1")]
                sqv = [sbt(ps, [128, 512], F32, "sqv") for _ in range(2)]
                sqvb = [Buf("sqv0"), Buf("sqv1")]
                w1v = w1_d[L].rearrange("(c p) n -> p c n", p=128)
                w2v = w2_d[L].rearrange("(c p) n -> p c n", p=128)
                items = [(e8, blk) for e8 in range(8) for blk in blocks][:FFN_ITEMS]
                loaded = set()

                def load(e8):
                    if e8 in loaded or e8 >= 8:
                        return
                    loaded.add(e8)
                    s = e8 % 3
                    dma("pool", w1s[s][:], w1v[:, :, e8 * 512:(e8 + 1) * 512], writes=[wb[s]])
                    dma("pool", w2s[s][:], w2v[:, e8 * 4:(e8 + 1) * 4, :], writes=[wb[s]])

                sqi = [0]

                def stage_h(i):
                    e8, (t0, n, j) = items[i]
                    s = e8 % 3
                    bidx = BLOCKS.index((t0, n, j))
                    for jj in range(4):
                        for k in range(8):
                            op("pe", lambda e, jj=jj, k=k: e.matmul(pb[jj][:, 0:n], lhsT=w1s[s][:, k, jj * 128:(jj + 1) * 128], rhs=uT[:, k, t0:t0 + n],
                                                                    start=(k == 0), stop=(k == 7)),
                               reads=[wb[s], ub[bidx]], writes=[pbb[jj]], inc=(k == 7))
                        q = sqi[0] % 2
                        sqi[0] += 1
                        op("act", lambda e, jj=jj, q=q: e.activation(out=sqv[q][:, 0:n], in_=pb[jj][:, 0:n], func=AF.Square), reads=[pbb[jj]], writes=[sqvb[q]])
                        op("dve", lambda e, jj=jj, q=q: e.scalar_tensor_tensor(out=hid[i % 2][:, jj, 0:n], in0=pb[jj][:, 0:n], scalar=0.0, in1=sqv[q][:, 0:n],
                                                                               op0=ALU.is_gt, op1=ALU.mult),
                           reads=[pbb[jj], sqvb[q]], writes=[hib[i % 2]])

                oi = [0]

                def stage_o(i):
                    e8, (t0, n, j) = items[i]
                    s = e8 % 3
                    for f in range(8):
                        bk = 4 + oi[0] % 3
                        oi[0] += 1
                        for jj in range(4):
                            op("pe", lambda e, jj=jj, f=f, bk=bk: e.matmul(pb[bk][:, 0:n], lhsT=w2s[s][:, jj, f * 128:(f + 1) * 128], rhs=hid[i % 2][:, jj, 0:n],
                                                                           start=(jj == 0), stop=(jj == 3)),
                               reads=[wb[s], hib[i % 2]], writes=[pbb[bk]], inc=(jj == 3))
                        op("dve", lambda e, f=f, bk=bk: e.scalar_tensor_tensor(out=hT[:, f, t0:t0 + n], in0=pb[bk][:, 0:n], scalar=M.modsb[:, 40 + f, j:j + 1],
                                                                               in1=hT[:, f, t0:t0 + n], op0=ALU.mult, op1=ALU.add),
                           reads=[pbb[bk], M.modb] + hbs(t0, n), writes=hbs(t0, n))

                load(0)
                load(1)
                stage_h(0)
                for i in range(len(items)):
                    if pre is not None:
                        if i % 3 == 0 and i // 3 < 12:
                            pre[0](i // 3)
                        if i >= 6 and (i - 6) % 3 == 0 and (i - 6) // 3 < 12:
                            pre[1]((i - 6) // 3)
                    if i + 1 < len(items):
                        load(items[i + 1][0] + 1)
                        stage_h(i + 1)
                    stage_o(i)
                if pre is not None:
                    pre[2]()
                kb.barrier()

        def outproj_acc(pbank, lhs_list, rhs_list, t0, n, j, reads, stop_bufs=None):
            for f in range(8):
                bk = pbank[f % len(pbank)]
                nmm = len(lhs_list)
                for i in range(nmm):
                    op("pe", lambda e, i=i, f=f, bk=bk: e.matmul(pb[bk][:, 0:n], lhsT=lhs_list[i][:, f * 128:(f + 1) * 128], rhs=rhs_list[i],
                                                                 start=(i == 0), stop=(i == nmm - 1)),
                       reads=reads, writes=[pbb[bk]], inc=(i == nmm - 1))
                op("dve", lambda e, f=f, bk=bk: e.scalar_tensor_tensor(out=hT[:, f, t0:t0 + n], in0=pb[bk][:, 0:n], scalar=M.modsb[:, 16 + f, j:j + 1],
                                                                       in1=hT[:, f, t0:t0 + n], op0=ALU.mult, op1=ALU.add),
                   reads=[pbb[bk], M.modb] + hbs(t0, n), writes=hbs(t0, n))

        def mlstm_phase(L, slot, need_ctx):
            out_blocks = BLOCKS if need_ctx else BLOCKS[1:]
            with ExitStack() as ps:
                G = sbt(ps, [128, NT, 32], F32, "G")
                SP = sbt(ps, [128, NT, 16], F32, "SP")
                LI = sbt(ps, [128, NT, 16], F32, "LI")
                KS = sbt(ps, [128, NT, 16], F32, "KS")
                KSb = sbt(ps, [128, NT, 16], BF16, "KSb")
                EBH = sbt(ps, [128, NT, 16], F32, "EBH")
                EBL = sbt(ps, [128, NT, 16], F32, "EBL")
                gbuf = Buf("gates")
                wg = sbt(ps, [128, 8, 32], BF16, "wg")
                wgb = Buf("wg")
                wmv = wm_d[slot].rearrange("(c p) n -> p c n", p=128)
                dma("pool", wg[:], wmv[:, :, 3584:3616], writes=[wgb])
                gbias = small[:, S_MGB + slot * 32:S_MGB + slot * 32 + 32]
                for tt in range(NT):
                    bk = 0 if tt < 16 else 1
                    o = (tt % 16) * 32
                    for k in range(8):
                        op("pe", lambda e, tt=tt, k=k, bk=bk, o=o: e.matmul(pb[bk][:, o:o + 32], lhsT=uT[:, k, tt * 128:(tt + 1) * 128], rhs=wg[:, k, :],
                                                                            start=(k == 0), stop=(k == 7)),
                           reads=[wgb] + ub, writes=[pbb[bk]], inc=(k == 7))
                op("dve", lambda e: e.tensor_tensor(out=G[:, 0:16, :], in0=pb[0][:].rearrange("p (t g) -> p t g", g=32),
                                                    in1=gbias.unsqueeze(1).broadcast_to([128, 16, 32]), op=ALU.add), reads=[pbb[0], cbuf], writes=[gbuf])
                op("dve", lambda e: e.tensor_tensor(out=G[:, 16:18, :], in0=pb[1][:, 0:64].rearrange("p (t g) -> p t g", g=32),
                                                    in1=gbias.unsqueeze(1).broadcast_to([128, 2, 32]), op=ALU.add), reads=[pbb[1], cbuf], writes=[gbuf])
                for d in range(2):
                    op("dve", lambda e, d=d: e.tensor_copy(out=LI[:, :, d * 8:d * 8 + 8], in_=G[:, :, d * 16:d * 16 + 8]), reads=[gbuf], writes=[gbuf])
                    op("act", lambda e, d=d: e.activation(out=SP[:, :, d * 8:d * 8 + 8], in_=G[:, :, d * 16 + 8:d * 16 + 16], func=AF.Exp, scale=-1.0),
                       reads=[gbuf], writes=[gbuf])
                op("act", lambda e: e.activation(out=SP[:], in_=SP[:], func=AF.Ln, scale=1.0, bias=1.0), reads=[gbuf], writes=[gbuf])
                for tt in range(NT):
                    bk = 2 if tt < 16 else 3
                    o = (tt % 16) * 32
                    op("pe", lambda e, tt=tt, bk=bk, o=o: e.matmul(pb[bk][:, o:o + 8], lhsT=ntricf, rhs=SP[:, tt, 0:8], start=True, stop=True),
                       reads=[gbuf, cbuf], writes=[pbb[bk]], inc=False)
                    op("pe", lambda e, tt=tt, bk=bk, o=o: e.matmul(pb[bk][:, o + 8:o + 16], lhsT=ntricb, rhs=SP[:, tt, 8:16], start=True, stop=True),
                       reads=[gbuf, cbuf], writes=[pbb[bk]], inc=False)
                    op("pe", lambda e, tt=tt, bk=bk, o=o: e.matmul(pb[bk][:, o + 16:o + 32], lhsT=nhalf[:], rhs=SP[:, tt, :], start=True, stop=True),
                       reads=[gbuf, cbuf], writes=[pbb[bk]], inc=True)
                for (bk, a, b_) in ((2, 0, 16), (3, 16, 18)):
                    nn = b_ - a
                    pv = pb[bk][:, 0:nn * 32].rearrange("p (t g) -> p t g", g=32)
                    op("dve", lambda e, pv=pv, a=a, b_=b_: e.tensor_tensor(out=KS[:, a:b_, :], in0=LI[:, a:b_, :], in1=pv[:, :, 0:16], op=ALU.subtract),
                       reads=[pbb[bk], gbuf], writes=[gbuf])
                    op("act", lambda e, pv=pv, a=a, b_=b_: e.activation(out=EBH[:, a:b_, :], in_=pv[:, :, 16:32], func=AF.Exp), reads=[pbb[bk]], writes=[gbuf])
                    op("act", lambda e, pv=pv, a=a, b_=b_: e.activation(out=EBL[:, a:b_, :], in_=pv[:, :, 16:32], func=AF.Exp, scale=2.0), reads=[pbb[bk]], writes=[gbuf])
                op("act", lambda e: e.activation(out=KS[:], in_=KS[:], func=AF.Exp), reads=[gbuf], writes=[gbuf])
                op("dve", lambda e: e.tensor_copy(out=KSb[:], in_=KS[:]), reads=[gbuf], writes=[gbuf])

                wh = [sbt(ps, [128, 8, 448], BF16, "wh") for _ in range(2)]
                whb = [Buf("wh0"), Buf("wh1")]
                wo = [sbt(ps, [128, D], BF16, "wo") for _ in range(2)]
                wob = [Buf("wo0"), Buf("wo1")]
                Qb = [sbt(ps, [64, T], BF16, "Qb") for _ in range(2)]
                qbb = [Buf("Qbf"), Buf("Qbb")]
                KT = sbt(ps, [64, T], BF16, "KT")
                ktb = Buf("KT")
                Ktok = sbt(ps, [128, NT, 64], BF16, "Ktok")
                kkb = Buf("Ktok")
                Va = [sbt(ps, [128, NT, 128], BF16, "Va") for _ in range(2)]
                vab = [Buf("Vaf"), Buf("Vab")]
                HS = sbt(ps, [128, T], F32, "HS")
                hsb = [Buf("hs%d" % i) for i in range(NT)]
                Cst = [sbt(ps, [64, 256], F32, "Cst") for _ in range(2)]
                csb = [Buf("Cf"), Buf("Cb")]
                Cbf = [[sbt(ps, [64, 256], BF16, "Cbf") for _ in range(2)] for _ in range(2)]
                cbb = [[Buf("Cbf%d_%d" % (d_, i)) for i in range(2)] for d_ in range(2)]
                ctmp = [sbt(ps, [64, 256], F32, "ctmp") for _ in range(2)]
                ctb = [Buf("ct0"), Buf("ct1")]
                ebt = [sbt(ps, [64, 512], F32, "ebt") for _ in range(2)]
                ebb = [Buf("eb0"), Buf("eb1")]
                Pm = [sbt(ps, [128, 128], BF16, "Pm") for _ in range(4)]
                pmb = [Buf("Pm%d" % i) for i in range(4)]
                adn = [sbt(ps, [128, 128], F32, "adn") for _ in range(2)]
                adb = [Buf("ad0"), Buf("ad1")]
                htm = [sbt(ps, [128, 128], F32, "htm") for _ in range(2)]
                htb = [Buf("ht0"), Buf("ht1")]
                sqh = sbt(ps, [128, 512], BF16, "sqh")
                sqhb = Buf("sqh")
                rsh = sbt(ps, [128, 512], F32, "rsh")
                rshb = Buf("rsh")
                sig = sbt(ps, [128, 512], F32, "sig")
                sigb = Buf("sig")
                hn = sbt(ps, [128, 512], F32, "hn")
                hnb = Buf("hn")
                ao = [sbt(ps, [128, 512], BF16, "ao") for _ in range(2)]
                aob = [Buf("ao0"), Buf("ao1")]
                wov = wmo_d[slot]

                def load_head(h):
                    s = h % 2
                    dma("pool", wh[s][:], wmv[:, :, h * 448:(h + 1) * 448], writes=[whb[s]])
                    dma("pool", wo[s][:], wov[h * 128:(h + 1) * 128, :], writes=[wob[s]])

                order = [list(range(NT)), [1, 0] + list(range(NT - 1, 1, -1))]
                load_head(0)
                for h in range(8):
                    s = h % 2
                    if h + 1 < 8:
                        load_head(h + 1)
                    w = wh[s]
                    for (t0, n, j) in BLOCKS:
                        bidx = BLOCKS.index((t0, n, j))
                        for d in range(2):
                            ntr = ntricf if d == 0 else ntricb
                            for ti in range(n // 128):
                                tt = t0 // 128 + ti
                                op("pe", lambda e, d=d, tt=tt, ti=ti, ntr=ntr: e.matmul(pb[d][0:64, ti * 128:(ti + 1) * 128],
                                                                                         lhsT=SP[:, tt, d * 8 + h:d * 8 + h + 1].broadcast_to([128, 64]), rhs=ntr,
                                                                                         start=True, stop=True),
                                   reads=[gbuf, cbuf], writes=[pbb[d]], inc=(ti == n // 128 - 1))
                            op("act", lambda e, d=d: e.activation(out=ebt[d][:, 0:n], in_=pb[d][0:64, 0:n], func=AF.Exp), reads=[pbb[d]], writes=[ebb[d]])
                        for k in range(8):
                            op("pe", lambda e, k=k: e.matmul(pb[2][0:64, 0:n], lhsT=w[:, k, 0:64], rhs=uT[:, k, t0:t0 + n], start=(k == 0), stop=(k == 7)),
                               reads=[whb[s], ub[bidx]], writes=[pbb[2]], inc=(k == 7))
                        for d in range(2):
                            op("dve", lambda e, d=d: e.tensor_tensor(out=Qb[d][:, t0:t0 + n], in0=pb[2][0:64, 0:n], in1=ebt[d][:, 0:n], op=ALU.mult),
                               reads=[pbb[2], ebb[d]], writes=[qbb[d]])
                        for k in range(8):
                            op("pe", lambda e, k=k: e.matmul(pb[3][0:64, 0:n], lhsT=w[:, k, 64:128], rhs=uT[:, k, t0:t0 + n], start=(k == 0), stop=(k == 7)),
                               reads=[whb[s], ub[bidx]], writes=[pbb[3]], inc=(k == 7))
                        op("act", lambda e: e.activation(out=KT[:, t0:t0 + n], in_=pb[3][0:64, 0:n], func=AF.Identity, scale=0.125), reads=[pbb[3]], writes=[ktb])
                    for tt in range(NT):
                        bk = 4 + tt % 2
                        bidx = 0 if tt < 2 else 1 + (tt - 2) // 4
                        for k in range(8):
                            op("pe", lambda e, k=k, tt=tt, bk=bk: e.matmul(pb[bk][:, 0:192], lhsT=uT[:, k, tt * 128:(tt + 1) * 128], rhs=w[:, k, 128:320],
                                                                           start=(k == 0), stop=(k == 7)),
                               reads=[whb[s], ub[bidx]], writes=[pbb[bk]], inc=(k == 7))
                        op("act", lambda e, tt=tt, bk=bk: e.activation(out=Ktok[:, tt, :], in_=pb[bk][:, 0:64], func=AF.Identity, scale=0.125), reads=[pbb[bk]], writes=[kkb])
                        for d in range(2):
                            op("dve", lambda e, tt=tt, bk=bk, d=d: e.tensor_scalar(out=Va[d][:, tt, :], in0=pb[bk][:, 64:192], scalar1=KS[:, tt, d * 8 + h:d * 8 + h + 1],
                                                                                    scalar2=None, op0=ALU.mult),
                               reads=[pbb[bk], gbuf], writes=[vab[d]])
                    op("dve", lambda e: e.memset(HS[:], 0.0), writes=hsb)
                    for d in range(2):
                        op("dve", lambda e, d=d: e.memset(Cst[d][:], 0.0), writes=[csb[d]])
                    def s1(step):
                        for d in range(2):
                            tt = order[d][step]
                            col = d * 8 + h
                            tsl = slice(tt * 128, (tt + 1) * 128)
                            if not (need_ctx or tt >= 2):
                                continue
                            cq = step % 2
                            if step > 0:
                                op("dve", lambda e: e.tensor_scalar(out=Cbf[d][cq][:], in0=Cst[d][:], scalar1=EBH[0:64, tt, col:col + 1], scalar2=None, op0=ALU.mult),
                                   reads=[csb[d], gbuf], writes=[cbb[d][cq]])
                            bS = d
                            p_ = (step % 2) * 2 + d
                            op("pe", lambda e: e.matmul(pb[bS][:, 0:128], lhsT=KT[:, tsl], rhs=Qb[d][:, tsl], start=True, stop=True),
                               reads=[ktb, qbb[d]], writes=[pbb[bS]])
                            mk = maskf_bf if d == 0 else maskb_bf
                            op("dve", lambda e: e.tensor_tensor(out=Pm[p_][:], in0=pb[bS][:, 0:128], in1=mk[:], op=ALU.mult),
                               reads=[pbb[bS], cbuf], writes=[pmb[p_]])

                    def s3(step):
                        if step >= NT - 1:
                            return
                        for d in range(2):
                            tt = order[d][step]
                            col = d * 8 + h
                            ksl = KSb[:, tt, col:col + 1].broadcast_to([128, 128])
                            bK = 6 + d
                            op("pe", lambda e: e.matmul(pb[bK][0:64, 0:128], lhsT=Ktok[:, tt, :], rhs=Va[d][:, tt, :], start=True, stop=True),
                               reads=[kkb, vab[d]], writes=[pbb[bK]], inc=False)
                            op("pe", lambda e: e.matmul(pb[bK][0:64, 128:256], lhsT=Ktok[:, tt, :], rhs=ksl, start=True, stop=True),
                               reads=[kkb, gbuf], writes=[pbb[bK]])
                            op("dve", lambda e: e.tensor_scalar(out=ctmp[d][:], in0=pb[bK][0:64, 0:256], scalar1=EBH[0:64, tt, col:col + 1], scalar2=None, op0=ALU.mult),
                               reads=[pbb[bK], gbuf], writes=[ctb[d]])
                            op("dve", lambda e: e.scalar_tensor_tensor(out=Cst[d][:], in0=Cst[d][:], scalar=EBL[0:64, tt, col:col + 1], in1=ctmp[d][:], op0=ALU.mult, op1=ALU.add),
                               reads=[ctb[d], gbuf, csb[d]], writes=[csb[d]])

                    def s2(step):
                        for d in range(2):
                            tt = order[d][step]
                            col = d * 8 + h
                            tsl = slice(tt * 128, (tt + 1) * 128)
                            if not (need_ctx or tt >= 2):
                                continue
                            ksl = KSb[:, tt, col:col + 1].broadcast_to([128, 128])
                            cq = step % 2
                            bN = 2 + d
                            bD = 4 + d
                            p_ = (step % 2) * 2 + d
                            last = (step == 0)
                            op("pe", lambda e: e.matmul(pb[bN][:, 0:128], lhsT=Va[d][:, tt, :], rhs=Pm[p_][:], start=True, stop=last),
                               reads=[vab[d], pmb[p_]], writes=[pbb[bN]], inc=last)
                            if not last:
                                op("pe", lambda e: e.matmul(pb[bN][:, 0:128], lhsT=Cbf[d][cq][:, 0:128], rhs=Qb[d][:, tsl], start=False, stop=True),
                                   reads=[cbb[d][cq], qbb[d]], writes=[pbb[bN]])
                            op("pe", lambda e: e.matmul(pb[bD][:, 0:128], lhsT=ksl, rhs=Pm[p_][:], start=True, stop=last),
                               reads=[gbuf, pmb[p_]], writes=[pbb[bD]], inc=last)
                            if not last:
                                op("pe", lambda e: e.matmul(pb[bD][:, 0:128], lhsT=Cbf[d][cq][:, 128:256], rhs=Qb[d][:, tsl], start=False, stop=True),
                                   reads=[cbb[d][cq], qbb[d]], writes=[pbb[bD]])
                            op("act", lambda e: e.activation(out=adn[d][:], in_=pb[bD][:, 0:128], func=AF.Abs), reads=[pbb[bD]], writes=[adb[d]])
                            op("dve", lambda e: e.tensor_scalar(out=adn[d][:], in0=adn[d][:], scalar1=1.0, scalar2=None, op0=ALU.max), reads=[adb[d]], writes=[adb[d]])
                            op("dve", lambda e: e.reciprocal(out=adn[d][:], in_=adn[d][:]), reads=[adb[d]], writes=[adb[d]])
                            op("dve", lambda e: e.tensor_tensor(out=htm[d][:], in0=pb[bN][:, 0:128], in1=adn[d][:], op=ALU.mult),
                               reads=[pbb[bN], adb[d]], writes=[htb[d]])
                            op("dve", lambda e: e.tensor_tensor(out=HS[:, tsl], in0=HS[:, tsl], in1=htm[d][:], op=ALU.add),
                               reads=[htb[d], hsb[tt]], writes=[hsb[tt]])

                    s1(0)
                    for step in range(NT):
                        s3(step)
                        if step + 1 < NT:
                            s1(step + 1)
                        s2(step)
                    hnw = small[:, S_MHN + slot * 8 + h:S_MHN + slot * 8 + h + 1]
                    for bi, (t0, n, j) in enumerate(out_blocks):
                        bidx = BLOCKS.index((t0, n, j))
                        hsl = [hsb[i] for i in range(t0 // 128, (t0 + n) // 128)]
                        op("act", lambda e: e.activation(out=sqh[:, 0:n], in_=HS[:, t0:t0 + n], func=AF.Square), reads=hsl, writes=[sqhb])
                        op("pe", lambda e: e.matmul(pb[0][:, 0:n], lhsT=ones_bf[:], rhs=sqh[:, 0:n], start=True, stop=True), reads=[sqhb, cbuf], writes=[pbb[0]])
                        rstd_from_psum(pb[0][:, 0:n], rsh[:, 0:n], n, 128.0, pbb[0], rshb)
                        for k in range(8):
                            op("pe", lambda e, k=k: e.matmul(pb[1][:, 0:n], lhsT=w[:, k, 320:448], rhs=uT[:, k, t0:t0 + n], start=(k == 0), stop=(k == 7)),
                               reads=[whb[s], ub[bidx]], writes=[pbb[1]], inc=(k == 7))
                        op("act", lambda e: e.activation(out=sig[:, 0:n], in_=pb[1][:, 0:n], func=AF.Exp, scale=-1.0), reads=[pbb[1]], writes=[sigb])
                        op("dve", lambda e: e.tensor_scalar(out=sig[:, 0:n], in0=sig[:, 0:n], scalar1=1.0, scalar2=None, op0=ALU.add), reads=[sigb], writes=[sigb])
                        op("dve", lambda e: e.reciprocal(out=sig[:, 0:n], in_=sig[:, 0:n]), reads=[sigb], writes=[sigb])
                        op("dve", lambda e: e.scalar_tensor_tensor(out=hn[:, 0:n], in0=HS[:, t0:t0 + n], scalar=hnw, in1=rsh[:, 0:n], op0=ALU.mult, op1=ALU.mult),
                           reads=hsl + [rshb, cbuf], writes=[hnb])
                        a_ = bi % 2
                        op("dve", lambda e, a_=a_: e.tensor_tensor(out=ao[a_][:, 0:n], in0=hn[:, 0:n], in1=sig[:, 0:n], op=ALU.mult), reads=[hnb, sigb], writes=[aob[a_]])
                        outproj_acc([6, 7], [wo[s]], [ao[a_][:, 0:n]], t0, n, j, [wob[s], aob[a_]])
                kb.barrier()

        def proj_rope(ps_res, w, c0, cr0, dst_ap_fn, dbuf, wbuf, banks, tmps, tmpb, rope_tiles, scale_ctx_copy=True):
            for (t0, n, j) in BLOCKS:
                bidx = BLOCKS.index((t0, n, j))
                b0, b1 = banks
                for k in range(8):
                    op("pe", lambda e, k=k: e.matmul(pb[b0][:, 0:n], lhsT=w[:, k, c0:c0 + 128], rhs=uT[:, k, t0:t0 + n], start=(k == 0), stop=(k == 7)),
                       reads=[wbuf, ub[bidx]], writes=[pbb[b0]], inc=(k == 7))
                if j == 1:
                    op("act", lambda e: e.activation(out=dst_ap_fn(t0, n), in_=pb[b0][:, 0:n], func=AF.Identity), reads=[pbb[b0]], writes=[dbuf])
                    continue
                for k in range(8):
                    op("pe", lambda e, k=k: e.matmul(pb[b1][:, 0:n], lhsT=w[:, k, cr0:cr0 + 128], rhs=uT[:, k, t0:t0 + n], start=(k == 0), stop=(k == 7)),
                       reads=[wbuf, ub[bidx]], writes=[pbb[b1]], inc=(k == 7))
                rc, rs_, rb = rope_tiles(t0)
                op("dve", lambda e: e.tensor_tensor(out=tmps[0][:, 0:n], in0=pb[b0][:, 0:n], in1=rc, op=ALU.mult), reads=[pbb[b0], rb], writes=[tmpb[0]])
                op("dve", lambda e: e.tensor_tensor(out=tmps[1][:, 0:n], in0=pb[b1][:, 0:n], in1=rs_, op=ALU.mult), reads=[pbb[b1], rb], writes=[tmpb[1]])
                op("dve", lambda e: e.tensor_tensor(out=dst_ap_fn(t0, n), in0=tmps[0][:, 0:n], in1=tmps[1][:, 0:n], op=ALU.add), reads=[tmpb[0], tmpb[1]], writes=[dbuf])

        def load_rope(ps):
            rc = sbt(ps, [128, NLAT], F32, "ropec")
            rs_ = sbt(ps, [128, NLAT], F32, "ropes")
            rb = Buf("rope")
            dma("sp", rc[:], ropec_l.get(), writes=[rb])
            dma("sp", rs_[:], ropes_l.get(), writes=[rb])

            def tiles(t0):
                l0 = t0 - NCTX
                return rc[:, l0:l0 + 512], rs_[:, l0:l0 + 512], rb
            return tiles

        def swa_phase(L, need_ctx):
            with ExitStack() as ps:
                rope_tiles = load_rope(ps)
                wsv = ws_d[0].rearrange("(c p) n -> p c n", p=128)
                wov = wso_d[0].rearrange("(h p) n -> p h n", p=64)
                wg_ = [sbt(ps, [128, 8, 832], BF16, "wsg") for _ in range(1)]
                wgb = [Buf("wsg0"), Buf("wsg1")]
                wo = [sbt(ps, [64, 4, D], BF16, "wso") for _ in range(1)]
                wob = [Buf("wso0"), Buf("wso1")]
                QT = sbt(ps, [128, 2, T], BF16, "QT")
                qtb = Buf("QT")
                KT = sbt(ps, [128, T], BF16, "KT")
                ktb = Buf("KT")
                Va = sbt(ps, [128, NT, 128], BF16, "Va")
                vab = Buf("Va")
                aog = sbt(ps, [64, 4, T], BF16, "aog")
                aob = [Buf("aog%d" % i) for i in range(NT)]
                tmps = [sbt(ps, [128, 512], F32, "rt") for _ in range(2)]
                tmpb = [Buf("rt0"), Buf("rt1")]
                Pt = [sbt(ps, [128, 512], BF16, "Pt") for _ in range(3)]
                ptb = [Buf("Pt%d" % i) for i in range(3)]
                dn = sbt(ps, [128, 512], F32, "dn")
                dnb = Buf("dn")
                ES = sbt(ps, [128, 16], F32, "ES")
                esb = Buf("ES")
                op("act", lambda e: e.activation(out=ES[:], in_=small[:, S_SINK:S_SINK + 16], func=AF.Exp), reads=[cbuf], writes=[esb])
                op("dve", lambda e: e.memset(Va[:, :, 64:128], 1.0), writes=[vab])

                def load_g(g):
                    s = 0
                    dma("pool", wg_[s][:], wsv[:, :, g * 832:(g + 1) * 832], writes=[wgb[s]])
                    dma("pool", wo[s][:], wov[:, g * 4:(g + 1) * 4, :], writes=[wob[s]])

                pti = 0
                for g in range(4):
                    s = 0
                    load_g(g)
                    w = wg_[s]
                    if SWA_STOP < 1:
                        continue
                    proj_rope(ps, w, 512, 640, lambda t0, n: KT[:, t0:t0 + n], ktb, wgb[s], (0, 1), tmps, tmpb, rope_tiles)
                    for jq in range(SWA_NQ):
                        if SWA_SUB < 1:
                            continue
                        proj_rope(ps, w, jq * 128, 256 + jq * 128, lambda t0, n, jq=jq: QT[:, jq, t0:t0 + n], qtb, wgb[s], SWA_QB, tmps, tmpb, rope_tiles)
                    for tt in range(NT):
                        if SWA_SUB < 2:
                            continue
                        bk = 4 + (tt // 8) % 2
                        o = (tt % 8) * 64
                        bidx = 0 if tt < 2 else 1 + (tt - 2) // 4
                        for k in range(8):
                            op("pe", lambda e, k=k, tt=tt, bk=bk, o=o: e.matmul(pb[bk][:, o:o + 64], lhsT=uT[:, k, tt * 128:(tt + 1) * 128], rhs=w[:, k, 768:832],
                                                                                start=(k == 0), stop=(k == 7)),
                               reads=[wgb[s], ub[bidx]], writes=[pbb[bk]], inc=(k == 7))
                        if tt % 8 == 7 or tt == NT - 1:
                            a = (tt // 8) * 8
                            nn = tt + 1 - a
                            op("act", lambda e, a=a, nn=nn, bk=bk: e.activation(out=Va[:, a:a + nn, 0:64], in_=pb[bk][:, 0:nn * 64].rearrange("p (t d) -> p t d", d=64),
                                                                                func=AF.Identity), reads=[pbb[bk]], writes=[vab])
                    aitems = []
                    for qt in range(NT):
                        if SWA_STOP < 2:
                            continue
                        if qt < 2:
                            if not need_ctx:
                                continue
                            kts = [0, 1]
                        else:
                            kts = [0, 1] + [kt for kt in (qt - 1, qt, qt + 1) if 2 <= kt < NT]
                        for ki, kt in enumerate(kts):
                            aitems.append((qt, kt, ki, len(kts)))

                    def s_stage(i):
                        qt, kt, ki, nk = aitems[i]
                        qsl = slice(qt * 128, (qt + 1) * 128)
                        ksl = slice(kt * 128, (kt + 1) * 128)
                        bS = (i % 2) * 2
                        p_ = i % 3
                        op("pe", lambda e: e.matmul(pb[bS][:, 0:256], lhsT=KT[0:64, ksl], rhs=QT[0:64, :, qsl], start=True, stop=True),
                           reads=[ktb, qtb], writes=[pbb[bS]])
                        op("pe", lambda e: e.matmul(pb[bS + 1][:, 0:256], lhsT=KT[64:128, ksl], rhs=QT[64:128, :, qsl], start=True, stop=True),
                           reads=[ktb, qtb], writes=[pbb[bS + 1]])
                        op("act", lambda e: e.activation(out=Pt[p_][:, 0:256], in_=pb[bS][:, 0:256], func=AF.Exp, scale=0.125), reads=[pbb[bS]], writes=[ptb[p_]])
                        op("act", lambda e: e.activation(out=Pt[p_][:, 256:512], in_=pb[bS + 1][:, 0:256], func=AF.Exp, scale=0.125), reads=[pbb[bS + 1]], writes=[ptb[p_]])
                        if qt >= 2 and kt >= 2 and kt != qt:
                            mk = maskb_bf if kt == qt - 1 else maskf_bf
                            op("dve", lambda e: e.tensor_tensor(out=Pt[p_][:].rearrange("p (h q) -> p h q", h=4), in0=Pt[p_][:].rearrange("p (h q) -> p h q", h=4),
                                                                in1=mk[:].unsqueeze(1).broadcast_to([128, 4, 128]), op=ALU.mult),
                               reads=[ptb[p_], cbuf], writes=[ptb[p_]])

                    def pv_stage(i):
                        qt, kt, ki, nk = aitems[i]
                        qsl = slice(qt * 128, (qt + 1) * 128)
                        p_ = i % 3
                        bO = 6 + qt % 2
                        op("pe", lambda e: e.matmul(pb[bO][:], lhsT=Va[:, kt, :], rhs=Pt[p_][:], start=(ki == 0), stop=(ki == nk - 1)),
                           reads=[vab, ptb[p_]], writes=[pbb[bO]], inc=(ki == nk - 1))
                        if ki == nk - 1:
                            op("dve", lambda e: e.tensor_tensor(out=dn[64:128, :].rearrange("p (h q) -> p h q", h=4), in0=pb[bO][64:128, :].rearrange("p (h q) -> p h q", h=4),
                                                                in1=ES[64:128, g * 4:(g + 1) * 4].unsqueeze(2).broadcast_to([64, 4, 128]), op=ALU.add),
                               reads=[pbb[bO], esb], writes=[dnb])
                            op("dve", lambda e: e.reciprocal(out=dn[64:128, :], in_=dn[64:128, :]), reads=[dnb], writes=[dnb])
                            op("dve", lambda e: e.tensor_tensor(out=aog[:, :, qsl], in0=pb[bO][0:64, :].rearrange("p (h q) -> p h q", h=4),
                                                                in1=dn[64:128, :].rearrange("p (h q) -> p h q", h=4), op=ALU.mult),
                               reads=[pbb[bO], dnb], writes=[aob[qt]])

                    if aitems:
                        s_stage(0)
                    for i in range(len(aitems)):
                        if i + 1 < len(aitems):
                            s_stage(i + 1)
                        pv_stage(i)
                    for (t0, n, j) in (BLOCKS if need_ctx else BLOCKS[1:]):
                        if SWA_STOP < 3:
                            continue
                        ab = [aob[i] for i in range(t0 // 128, (t0 + n) // 128)]
                        outproj_acc([0, 1, 2, 3], [wo[s][:, PSORD[pos], :] for pos in range(4)], [aog[:, pos, t0:t0 + n] for pos in range(4)], t0, n, j, [wob[s]] + ab)
                kb.barrier()

        def diff_phase(L, need_ctx):
            lam_init = 0.8 - 0.6 * float(np.exp(-0.3 * L))
            with ExitStack() as ps:
                rope_tiles = load_rope(ps)
                wdv = wd_d[0].rearrange("(c p) n -> p c n", p=128)
                wov = wdo_d[0]
                wh = [sbt(ps, [128, 8, 640], BF16, "wdh") for _ in range(2)]
                whb = [Buf("wdh0"), Buf("wdh1")]
                wo = [sbt(ps, [128, D], BF16, "wdo") for _ in range(2)]
                wob = [Buf("wdo0"), Buf("wdo1")]
                QT = sbt(ps, [128, T], BF16, "QT")
                qtb = Buf("QT")
                KT = sbt(ps, [128, T], BF16, "KT")
                ktb = Buf("KT")
                Vt = sbt(ps, [128, NT, 128], BF16, "Vt")
                vtb = Buf("Vt")
                tmps = [sbt(ps, [128, 512], F32, "rt") for _ in range(2)]
                tmpb = [Buf("rt0"), Buf("rt1")]
                Pt = [sbt(ps, [128, 512], BF16, "Pt") for _ in range(4)]
                ptb = [Buf("Pt%d" % i) for i in range(4)]
                rr = [sbt(ps, [128, 512], F32, "rr") for _ in range(2)]
                rrb = [Buf("rr0"), Buf("rr1")]
                od = sbt(ps, [128, 512], F32, "od")
                odb = Buf("od")
                sqh = sbt(ps, [128, 512], BF16, "sqh")
                sqhb = Buf("sqh")
                rsh = sbt(ps, [128, 512], F32, "rsh")
                rshb = Buf("rsh")
                ao = [sbt(ps, [128, 512], BF16, "ao") for _ in range(2)]
                aob = [Buf("ao0"), Buf("ao1")]
                lam = sbt(ps, [128, 8], F32, "lam")
                lamb = Buf("lam")
                hnw = sbt(ps, [128, 8], F32, "hnw")
                lv = small[:, S_DLAM:S_DLAM + 256]
                op("dve", lambda e: e.tensor_tensor(out=tmps[0][:, 0:64], in0=lv[:, 0:64], in1=lv[:, 64:128], op=ALU.mult), reads=[cbuf], writes=[tmpb[0]])
                op("dve", lambda e: e.tensor_tensor(out=tmps[0][:, 64:128], in0=lv[:, 128:192], in1=lv[:, 192:256], op=ALU.mult), reads=[cbuf], writes=[tmpb[0]])
                op("dve", lambda e: e.tensor_reduce(out=lam[:, 0:2], in_=tmps[0][:, 0:128].rearrange("p (a b) -> p a b", a=2), axis=mybir.AxisListType.X, op=ALU.add),
                   reads=[tmpb[0]], writes=[lamb])
                op("act", lambda e: e.activation(out=lam[:, 0:2], in_=lam[:, 0:2], func=AF.Exp), reads=[lamb], writes=[lamb])
                op("dve", lambda e: e.tensor_tensor(out=lam[:, 2:3], in0=lam[:, 1:2], in1=lam[:, 0:1], op=ALU.subtract), reads=[lamb], writes=[lamb])
                op("dve", lambda e: e.tensor_scalar(out=lam[:, 3:4], in0=lam[:, 2:3], scalar1=-lam_init, scalar2=None, op0=ALU.add), reads=[lamb], writes=[lamb])
                op("dve", lambda e: e.tensor_scalar(out=hnw[:], in0=small[:, S_DHN:S_DHN + 8], scalar1=(1.0 - lam_init), scalar2=None, op0=ALU.mult), reads=[cbuf], writes=[lamb])
                nlam = lam[:, 3:4]

                def load_h(h):
                    s = h % 2
                    dma("pool", wh[s][:], wdv[:, :, h * 640:(h + 1) * 640], writes=[whb[s]])
                    dma("pool", wo[s][:], wov[h * 128:(h + 1) * 128, :], writes=[wob[s]])

                load_h(0)
                pti = 0
                oi = 0
                for h in range(8):
                    s = h % 2
                    if h + 1 < 8:
                        load_h(h + 1)
                    w = wh[s]
                    proj_rope(ps, w, 256, 384, lambda t0, n: KT[:, t0:t0 + n], ktb, whb[s], (0, 1), tmps, tmpb, rope_tiles)
                    proj_rope(ps, w, 0, 128, lambda t0, n: QT[:, t0:t0 + n], qtb, whb[s], (2, 3), tmps, tmpb, rope_tiles)
                    for tt in range(NT):
                        bk = 4 + (tt // 4) % 2
                        o = (tt % 4) * 128
                        bidx = 0 if tt < 2 else 1 + (tt - 2) // 4
                        for k in range(8):
                            op("pe", lambda e, k=k, tt=tt, bk=bk, o=o: e.matmul(pb[bk][:, o:o + 128], lhsT=uT[:, k, tt * 128:(tt + 1) * 128], rhs=w[:, k, 512:640],
                                                                                start=(k == 0), stop=(k == 7)),
                               reads=[whb[s], ub[bidx]], writes=[pbb[bk]], inc=(k == 7))
                        if tt % 4 == 3 or tt == NT - 1:
                            a = (tt // 4) * 4
                            nn = tt + 1 - a
                            op("act", lambda e, a=a, nn=nn, bk=bk: e.activation(out=Vt[:, a:a + nn, :], in_=pb[bk][:, 0:nn * 128].rearrange("p (t d) -> p t d", d=128),
                                                                                func=AF.Identity), reads=[pbb[bk]], writes=[vtb])
                    for bi, (t0, n, j) in enumerate(BLOCKS if need_ctx else BLOCKS[1:]):
                        kts = [0, 1] if j == 1 else list(range(NT))
                        nk = len(kts)

                        def s_stage(ki):
                            kt = kts[ki]
                            ksl = slice(kt * 128, (kt + 1) * 128)
                            for m in range(2):
                                bS = m * 2 + (ki % 2)
                                p_ = (ki % 2) * 2 + m
                                rsl = slice(m * 64, (m + 1) * 64)
                                op("pe", lambda e, bS=bS, ksl=ksl, rsl=rsl: e.matmul(pb[bS][:, 0:n], lhsT=KT[rsl, ksl], rhs=QT[rsl, t0:t0 + n], start=True, stop=True),
                                   reads=[ktb, qtb], writes=[pbb[bS]])
                            for m in range(2):
                                bS = m * 2 + (ki % 2)
                                p_ = (ki % 2) * 2 + m
                                op("act", lambda e, bS=bS, p_=p_: e.activation(out=Pt[p_][:, 0:n], in_=pb[bS][:, 0:n], func=AF.Exp, scale=0.125), reads=[pbb[bS]], writes=[ptb[p_]])

                        def pv_stage(ki):
                            kt = kts[ki]
                            first = (ki == 0)
                            lastk = (ki == nk - 1)
                            for m in range(2):
                                p_ = (ki % 2) * 2 + m
                                op("pe", lambda e, kt=kt, p_=p_, m=m: e.matmul(pb[4 + m][:, 0:n], lhsT=Vt[:, kt, :], rhs=Pt[p_][:, 0:n], start=first, stop=lastk),
                                   reads=[vtb, ptb[p_]], writes=[pbb[4 + m]], inc=lastk)
                                op("pe", lambda e, p_=p_, m=m: e.matmul(pb[6 + m][:, 0:n], lhsT=ones_bf[:], rhs=Pt[p_][:, 0:n], start=first, stop=lastk),
                                   reads=[cbuf, ptb[p_]], writes=[pbb[6 + m]], inc=lastk)

                        s_stage(0)
                        for ki in range(nk):
                            if ki + 1 < nk:
                                s_stage(ki + 1)
                            pv_stage(ki)
                        for m in range(2):
                            op("dve", lambda e, m=m: e.reciprocal(out=rr[m][:, 0:n], in_=pb[6 + m][:, 0:n]), reads=[pbb[6 + m]], writes=[rrb[m]])
                            op("dve", lambda e, m=m: e.tensor_tensor(out=rr[m][:, 0:n], in0=pb[4 + m][:, 0:n], in1=rr[m][:, 0:n], op=ALU.mult), reads=[pbb[4 + m], rrb[m]], writes=[rrb[m]])
                        op("dve", lambda e: e.scalar_tensor_tensor(out=od[:, 0:n], in0=rr[1][:, 0:n], scalar=nlam, in1=rr[0][:, 0:n], op0=ALU.mult, op1=ALU.add),
                           reads=[rrb[0], rrb[1], lamb], writes=[odb])
                        op("act", lambda e: e.activation(out=sqh[:, 0:n], in_=od[:, 0:n], func=AF.Square), reads=[odb], writes=[sqhb])
                        op("pe", lambda e: e.matmul(pb[0][:, 0:n], lhsT=ones_bf[:], rhs=sqh[:, 0:n], start=True, stop=True), reads=[sqhb, cbuf], writes=[pbb[0]])
                        rstd_from_psum(pb[0][:, 0:n], rsh[:, 0:n], n, 128.0, pbb[0], rshb)
                        a_ = oi % 2
                        oi += 1
                        op("dve", lambda e, a_=a_: e.scalar_tensor_tensor(out=ao[a_][:, 0:n], in0=od[:, 0:n], scalar=hnw[:, h:h + 1], in1=rsh[:, 0:n], op0=ALU.mult, op1=ALU.mult),
                           reads=[odb, rshb, lamb], writes=[aob[a_]])
                        outproj_acc([1, 2, 3], [wo[s]], [ao[a_][:, 0:n]], t0, n, j, [wob[s], aob[a_]])
                kb.barrier()

        def final_phase(do_norm):
            with ExitStack() as ps:
                sq = sbt(ps, [128, 8, 512], BF16, "fsq")
                sqb = Buf("fsq")
                rs = sbt(ps, [128, 512], F32, "frs")
                rsb = Buf("frs")
                tmp = sbt(ps, [128, 8, 512], F32, "ftmp")
                tmb = Buf("ftmp")
                ost = [sbt(ps, [128, D], F32, "ost") for _ in range(2)]
                osb = [Buf("ost0"), Buf("ost1")]
                oi = 0
                for (t0, n, j) in BLOCKS[1:]:
                    if do_norm:
                        op("act", lambda e: e.activation(out=sq[:, :, 0:n], in_=hT[:, :, t0:t0 + n], func=AF.Square), reads=hbs(t0, n), writes=[sqb])
                        for c in range(8):
                            op("pe", lambda e, c=c: e.matmul(pb[7][:, 0:n], lhsT=ones_bf[:], rhs=sq[:, c, 0:n], start=(c == 0), stop=(c == 7)),
                               reads=[sqb, cbuf], writes=[pbb[7]], inc=(c == 7))
                        rstd_from_psum(pb[7][:, 0:n], rs[:, 0:n], n, float(D), pbb[7], rsb)
                        for c in range(8):
                            op("dve", lambda e, c=c: e.scalar_tensor_tensor(out=tmp[:, c, 0:n], in0=hT[:, c, t0:t0 + n], scalar=small[:, S_FINAL + c:S_FINAL + c + 1],
                                                                            in1=rs[:, 0:n], op0=ALU.mult, op1=ALU.mult),
                               reads=hbs(t0, n) + [rsb, cbuf], writes=[tmb])
                    else:
                        op("dve", lambda e: e.tensor_copy(out=tmp[:, :, 0:n], in_=hT[:, :, t0:t0 + n]), reads=hbs(t0, n), writes=[tmb])
                    for ti in range(n // 128):
                        o_ = oi % 2
                        oi += 1
                        for half in range(2):
                            bk = (oi * 2 + half) % 4
                            for c4 in range(4):
                                c = half * 4 + c4
                                op("pe", lambda e, c=c, c4=c4, bk=bk, ti=ti: e.transpose(pb[bk][:, c4 * 128:(c4 + 1) * 128], tmp[:, c, ti * 128:(ti + 1) * 128], ident),
                                   reads=[tmb, cbuf], writes=[pbb[bk]], inc=(c4 == 3))
                            if half == 0:
                                op("act", lambda e, bk=bk, o_=o_, half=half: e.activation(out=ost[o_][:, half * 512:(half + 1) * 512], in_=pb[bk][:], func=AF.Identity),
                                   reads=[pbb[bk]], writes=[osb[o_]])
                            else:
                                op("dve", lambda e, bk=bk, o_=o_, half=half: e.tensor_copy(out=ost[o_][:, half * 512:(half + 1) * 512], in_=pb[bk][:]),
                                   reads=[pbb[bk]], writes=[osb[o_]])
                        r0 = t0 - NCTX + ti * 128
                        dma("sp", out_d[r0:r0 + 128, :], ost[o_][:], reads=[osb[o_]])
                kb.wait_all_dma("sp")

        prefetched = set()
        for li, L in enumerate(layers):
            kind, slot = L % 3, L // 3
            need_ctx = L < DEPTH - 1
            blocks = BLOCKS if need_ctx else BLOCKS[1:]
            set_mod(L)
            if L not in prefetched:
                ada_phase(L)
            if do_mixer:
                norm_mod(lambda: M.amix, 0, BLOCKS)
                if kind == 0:
                    mlstm_phase(L, slot, need_ctx)
                elif kind == 1:
                    swa_phase(L, need_ctx)
                else:
                    diff_phase(L, need_ctx)
            if do_ffn:
                norm_mod(lambda: M.affn, 24, blocks)
                if do_ffn != "norm":
                    nxt = layers[li + 1] if li + 1 < len(layers) else None
                    can = nxt is not None and len(blocks) == len(BLOCKS) and FFN_ITEMS >= 40
                    ffn_phase(L, blocks, prefetch=nxt if can else None)
                    if can:
                        prefetched.add(nxt)
        final_phase(final_norm)
    nc._declared_inputs = list(declared)
    return nc


def _pcol(v):
    return np.ascontiguousarray(np.asarray(v, np.float32).reshape(-1, 128).T)


def _rot_idx(base):
    return list(range(base + 32, base + 64)) + list(range(base, base + 32))


def _consts():
    i = np.arange(128)
    ident = (i[:, None] == i[None, :]).astype(np.float32)
    maskf = (i[:, None] <= i[None, :]).astype(np.float32)
    maskb = (i[:, None] >= i[None, :]).astype(np.float32)
    return np.ascontiguousarray(np.concatenate([ident, maskf, maskb, -(maskf - 0.5), -(maskb - 0.5)], axis=1))


def _rope_tables():
    rows = NLAT // 64
    row = np.repeat(np.arange(rows), 64).astype(np.float32)
    col = np.tile(np.arange(64), rows).astype(np.float32)
    quarter = 16
    inv = (np.float32(10000.0) ** (-np.arange(quarter, dtype=np.float32) / np.float32(quarter))).astype(np.float32)
    ang = np.concatenate([row[:, None] * inv, col[:, None] * inv], axis=-1).astype(np.float32)
    cos = np.cos(ang).astype(np.float32).T
    sin = np.sin(ang).astype(np.float32).T
    c64 = np.concatenate([cos, cos], 0)
    s64 = np.concatenate([-sin, sin], 0)
    return np.ascontiguousarray(np.concatenate([c64, c64], 0)), np.ascontiguousarray(np.concatenate([s64, s64], 0))


def prepare_shared(inp):
    sh = {}
    sh["cst"] = _consts()
    sh["ropec"], sh["ropes"] = _rope_tables()
    sh["ada_w"] = np.ascontiguousarray(inp["ada_w"], dtype=np.float32)
    sh["ffn_w1"] = np.ascontiguousarray(inp["ffn_w1"], dtype=np.float32)
    sh["ffn_w2"] = np.ascontiguousarray(inp["ffn_w2"], dtype=np.float32)
    wm = []
    for s in range(inp["mlstm_w_in"].shape[0]):
        W = inp["mlstm_w_in"][s]
        cols = []
        for h in range(8):
            cols += list(range(h * 64, h * 64 + 64))
            cols += list(range(512 + h * 64, 512 + h * 64 + 64))
            cols += list(range(512 + h * 64, 512 + h * 64 + 64))
            cols += list(range(1024 + h * 128, 1024 + h * 128 + 128))
            cols += list(range(2048 + h * 128, 2048 + h * 128 + 128))
        cols += list(range(3072, 3104))
        wm.append(W[:, cols])
    sh["wm"] = np.ascontiguousarray(np.stack(wm), dtype=np.float32)
    sh["wmo"] = np.ascontiguousarray(inp["mlstm_w_out"], dtype=np.float32)
    W = inp["swa_w_in"][0]
    cols = []
    for g in range(4):
        q = []
        qr = []
        for hh in range(4):
            b = (4 * g + hh) * 64
            q += list(range(b, b + 64))
            qr += _rot_idx(b)
        kb_ = 1024 + g * 64
        k = list(range(kb_, kb_ + 64))
        kr = _rot_idx(kb_)
        v = list(range(1280 + g * 64, 1280 + g * 64 + 64))
        cols += q + qr + k + k + kr + kr + v
    sh["ws"] = np.ascontiguousarray(W[:, cols][None], dtype=np.float32)
    sh["wso"] = np.ascontiguousarray(inp["swa_w_out"], dtype=np.float32)
    W = inp["diff_w_in"][0]
    cols = []
    for h in range(8):
        qb_ = h * 128
        kb_ = 1024 + h * 128
        cols += list(range(qb_, qb_ + 128)) + _rot_idx(qb_) + _rot_idx(qb_ + 64)
        cols += list(range(kb_, kb_ + 128)) + _rot_idx(kb_) + _rot_idx(kb_ + 64)
        cols += list(range(2048 + h * 128, 2048 + h * 128 + 128))
    sh["wd"] = np.ascontiguousarray(W[:, cols][None], dtype=np.float32)
    sh["wdo"] = np.ascontiguousarray(inp["diff_w_out"], dtype=np.float32)
    small = np.zeros((128, NS), np.float32)
    for L in range(DEPTH):
        small[:, L * SL:L * SL + 8] = _pcol(inp["norm_mix"][L])
        small[:, L * SL + 8:L * SL + 16] = _pcol(inp["norm_ffn"][L])
        small[:, L * SL + 16:L * SL + 64] = _pcol(inp["ada_b"][L])
    small[:, S_FINAL:S_FINAL + 8] = _pcol(inp["final_norm"])
    for s in range(inp["mlstm_head_norm"].shape[0]):
        small[:, S_MHN + s * 8:S_MHN + s * 8 + 8] = _pcol(inp["mlstm_head_norm"][s])
        small[:, S_MGB + s * 32:S_MGB + s * 32 + 32] = np.broadcast_to(np.asarray(inp["mlstm_gate_b"][s], np.float32).reshape(1, 32), (128, 32))
    sink = np.asarray(inp["swa_sink"][0], np.float32)
    sord = [4 * g + PSORD[p] for g in range(4) for p in range(4)]
    small[:, S_SINK:S_SINK + 16] = np.broadcast_to(sink[sord][None, :], (128, 16))
    small[:, S_DHN:S_DHN + 8] = _pcol(inp["diff_head_norm"][0])
    lamv = np.concatenate([np.asarray(inp[k][0], np.float32) for k in ("diff_lambda_q1", "diff_lambda_k1", "diff_lambda_q2", "diff_lambda_k2")])
    small[:, S_DLAM:S_DLAM + 256] = np.broadcast_to(lamv[None, :], (128, 256))
    sh["small"] = small
    return sh


def prepare_core(inp, b):
    xin = np.ascontiguousarray(np.concatenate([inp["ctx"][b], inp["x"][b]], axis=0), dtype=np.float32)
    cv = np.zeros((128, 16), np.float32)
    cv[:, 0::2] = _pcol(inp["c"][b])
    cv[:, 1::2] = _pcol(inp["c_ctx"])
    return {"xin": xin, "cv": cv}


_NC_CACHE = {}


def kernel(**inputs):
    inp = {k: np.asarray(v) for k, v in inputs.items()}
    B = inp["x"].shape[0]
    shared = prepare_shared(inp)
    if "full" not in _NC_CACHE:
        _NC_CACHE["full"] = build_program()
    nc = _NC_CACHE["full"]
    in_maps = []
    for b in range(B):
        m = dict(shared)
        m.update(prepare_core(inp, b))
        in_maps.append({k: m[k] for k in nc._declared_inputs})
    res = run_bass_kernel_spmd(nc, in_maps, core_ids=list(range(B)))
    out = np.stack([np.asarray(r["out"], dtype=np.float32) for r in res.results], axis=0)
    return out
```

```python
import numpy as np
from contextlib import ExitStack
import concourse.bass as bass
import concourse.mybir as mybir
from concourse.bass_utils import run_bass_kernel_spmd

F32 = mybir.dt.float32
BF16 = mybir.dt.bfloat16
ALU = mybir.AluOpType
AF = mybir.ActivationFunctionType

D = 1024
NCTX = 256
NLAT = 2048
T = NCTX + NLAT
NT = T // 128
DEPTH = 4
EPS = 1e-6
NDS = 24
BLOCKS = [(0, 256, 1), (256, 512, 0), (768, 512, 0), (1280, 512, 0), (1792, 512, 0)]

SL = 64
S_FINAL = 4 * SL
S_MHN = S_FINAL + 8
S_MGB = S_MHN + 16
S_SINK = S_MGB + 64
S_DHN = S_SINK + 16
S_DLAM = S_DHN + 8
NS = S_DLAM + 256
NCONST = 5 * 128
PSORD = [0, 2, 1, 3]
FFN_ITEMS = 1000
SWA_STOP = 9
SWA_SUB = 9
SWA_QB = (0, 1)
SWA_NQ = 2


class Buf:
    __slots__ = ("name", "w", "r")

    def __init__(self, name):
        self.name = name
        self.w = None
        self.r = {}


class KB:
    def __init__(self, nc, es):
        self.nc = nc
        self.es = es
        self.E = {"pe": nc.tensor, "act": nc.scalar, "dve": nc.vector, "pool": nc.gpsimd, "sp": nc.sync}
        self.sem = {e: es.enter_context(nc.semaphore("s_" + e)) for e in self.E}
        self.cnt = {e: 0 for e in self.E}
        self.seen = {e: {} for e in self.E}
        self.dsem = [es.enter_context(nc.semaphore("d%d" % i)) for i in range(NDS)]
        self.dcnt = [0] * NDS
        self.dnext = {"sp": 0, "pool": 0}

    def _wait(self, eng, deps):
        for (sk, val) in deps:
            if sk == eng and eng == "pe":
                continue
            if self.seen[eng].get(sk, 0) >= val:
                continue
            semh = self.sem[sk] if isinstance(sk, str) else self.dsem[sk]
            self.E[eng].wait_ge(semh, val)
            self.seen[eng][sk] = val

    @staticmethod
    def _deps(reads, writes):
        deps = []
        for b in reads:
            if b.w is not None:
                deps.append(b.w)
        for b in writes:
            if b.w is not None:
                deps.append(b.w)
            deps.extend(b.r.values())
        return deps

    def op(self, eng, fn, reads=(), writes=(), inc=True):
        self._wait(eng, self._deps(reads, writes))
        ins = fn(self.E[eng])
        if inc:
            self.cnt[eng] += 1
            ins.then_inc(self.sem[eng], 1)
            tk = (eng, self.cnt[eng])
        else:
            tk = (eng, self.cnt[eng] + 1)
        for b in reads:
            b.r[eng] = tk
        for b in writes:
            b.w = tk
            b.r = {}

    def dma(self, q, out, in_, reads=(), writes=()):
        half = NDS // 2
        i = self.dnext[q] + (0 if q == "sp" else half)
        self.dnext[q] = (self.dnext[q] + 1) % half
        deps = self._deps(reads, writes)
        if self.dcnt[i] > 0:
            deps.append((i, self.dcnt[i]))
        self._wait(q, deps)
        self.E[q].dma_start(out=out, in_=in_).then_inc(self.dsem[i], 16)
        self.dcnt[i] += 16
        tk = (i, self.dcnt[i])
        for b in reads:
            b.r[("d", i)] = tk
        for b in writes:
            b.w = tk
            b.r = {}

    def wait_all_dma(self, eng):
        self._wait(eng, [(i, self.dcnt[i]) for i in range(NDS) if self.dcnt[i] > 0])

    def barrier(self):
        deps = [(e, self.cnt[e]) for e in self.E if e != "sp" and self.cnt[e] > 0]
        deps += [(i, self.dcnt[i]) for i in range(NDS) if self.dcnt[i] > 0]
        self._wait("sp", deps)
        self.cnt["sp"] += 1
        self.E["sp"].sem_inc(self.sem["sp"], 1)
        for e in self.E:
            if e != "sp":
                self._wait(e, [("sp", self.cnt["sp"])])


def build_program(layers=(0, 1, 2, 3), do_mixer=True, do_ffn=True, final_norm=True, in_ctx=True):
    nc = bass.Bass("TRN2", target_bir_lowering=False)

    declared = []
    only = None if len(layers) == DEPTH and do_mixer and do_ffn else set()

    class _Lazy:
        def __init__(self, name, shape):
            self.name, self.shape, self.ap_ = name, list(shape), None

        def get(self):
            if self.ap_ is None:
                self.ap_ = nc.dram_tensor(self.name, self.shape, F32, kind="ExternalInput").ap()
                declared.append(self.name)
            return self.ap_

        def __getitem__(self, k):
            return self.get()[k]

    def din(name, shape):
        return _Lazy(name, shape)

    xin = din("xin", [T, D]).get()
    cv_d = din("cv", [128, 16]).get()
    small_d = din("small", [128, NS]).get()
    cst_d = din("cst", [128, NCONST]).get()
    ropec_l = din("ropec", [128, NLAT])
    ropes_l = din("ropes", [128, NLAT])
    ada_w = din("ada_w", [DEPTH, D, 6 * D])
    w1_d = din("ffn_w1", [DEPTH, D, 4 * D])
    w2_d = din("ffn_w2", [DEPTH, 4 * D, D])
    wm_d = din("wm", [2, D, 3616])
    wmo_d = din("wmo", [2, D, D])
    ws_d = din("ws", [1, D, 3328])
    wso_d = din("wso", [1, D, D])
    wd_d = din("wd", [1, D, 5120])
    wdo_d = din("wdo", [1, D, D])
    out_d = nc.dram_tensor("out", [NLAT, D], F32, kind="ExternalOutput").ap()

    with ExitStack() as es:
        kb = KB(nc, es)
        op = kb.op
        dma = kb.dma
        cnt = [0]

        def sbt(es_, shape, dt, name=None):
            cnt[0] += 1
            return es_.enter_context(nc.sbuf_tensor("%s_%d" % (name or "t", cnt[0]), list(shape), dt))

        hT = sbt(es, [128, 8, T], F32, "hT")
        uT = sbt(es, [128, 8, T], BF16, "uT")
        hb = [Buf("h%d" % i) for i in range(NT)]
        ub = [Buf("u%d" % i) for i in range(len(BLOCKS))]
        cst = sbt(es, [128, NCONST], F32, "cst")
        small = sbt(es, [128, NS], F32, "small")
        cv = sbt(es, [128, 16], F32, "cv")
        cond = sbt(es, [128, 8, 2], BF16, "cond")
        ones_bf = sbt(es, [128, 128], BF16, "ones")
        maskf_bf = sbt(es, [128, 128], BF16, "maskf")
        maskb_bf = sbt(es, [128, 128], BF16, "maskb")
        nhalf = sbt(es, [128, 128], F32, "nhalf")
        modsbs = [sbt(es, [128, 48, 2], F32, "modsb") for _ in range(2)]
        amixs = [sbt(es, [128, 8, 2], F32, "amix") for _ in range(2)]
        affns = [sbt(es, [128, 8, 2], F32, "affn") for _ in range(2)]
        modbs = [Buf("mod0"), Buf("mod1")]
        cbuf = Buf("consts")

        class M:
            pass

        def set_mod(L):
            M.modsb, M.amix, M.affn, M.modb = modsbs[L % 2], amixs[L % 2], affns[L % 2], modbs[L % 2]
        ident = cst[:, 0:128]
        maskf = cst[:, 128:256]
        maskb = cst[:, 256:384]
        ntricf = cst[:, 384:512]
        ntricb = cst[:, 512:640]
        pb = [es.enter_context(nc.psum_tensor("pb%d" % i, [128, 512], F32)) for i in range(8)]
        pbb = [Buf("pb%d" % i) for i in range(8)]

        def hbs(t0, n):
            return [hb[i] for i in range(t0 // 128, (t0 + n) // 128)]

        dma("sp", cst[:], cst_d, writes=[cbuf])
        dma("sp", small[:], small_d, writes=[cbuf])
        dma("sp", cv[:], cv_d, writes=[cbuf])
        op("dve", lambda e: e.memset(ones_bf[:], 1.0), writes=[cbuf])
        op("dve", lambda e: e.memset(nhalf[:], -0.5), writes=[cbuf])
        op("dve", lambda e: e.tensor_copy(out=maskf_bf[:], in_=maskf), reads=[cbuf], writes=[cbuf])
        op("dve", lambda e: e.tensor_copy(out=maskb_bf[:], in_=maskb), reads=[cbuf], writes=[cbuf])
        with ExitStack() as ps:
            tmpc = sbt(ps, [128, 16], F32, "tmpc")
            tb_ = Buf("tmpc")
            op("act", lambda e: e.activation(out=tmpc[:], in_=cv[:], func=AF.Exp, scale=-1.0), reads=[cbuf], writes=[tb_])
            op("dve", lambda e: e.tensor_scalar(out=tmpc[:], in0=tmpc[:], scalar1=1.0, scalar2=None, op0=ALU.add), reads=[tb_], writes=[tb_])
            op("dve", lambda e: e.reciprocal(out=tmpc[:], in_=tmpc[:]), reads=[tb_], writes=[tb_])
            op("dve", lambda e: e.tensor_tensor(out=cond[:].rearrange("p k j -> p (k j)"), in0=tmpc[:], in1=cv[:], op=ALU.mult),
               reads=[tb_, cbuf], writes=[cbuf])
            stg = [sbt(ps, [128, D], F32, "stg") for _ in range(2)]
            stb = [Buf("stg0"), Buf("stg1")]
            for tt in range(NT):
                s = tt % 2
                dma("sp", stg[s][:], xin[tt * 128:(tt + 1) * 128, :], writes=[stb[s]])
                for half in range(2):
                    bk = (tt * 2 + half) % 4
                    for c4 in range(4):
                        c = half * 4 + c4
                        op("pe", lambda e, c=c, c4=c4, bk=bk, s=s: e.transpose(pb[bk][:, c4 * 128:(c4 + 1) * 128], stg[s][:, c * 128:(c + 1) * 128], ident),
                           reads=[stb[s], cbuf], writes=[pbb[bk]], inc=(c4 == 3))
                    eng = "act" if half == 0 else "dve"
                    if eng == "act":
                        op("act", lambda e, bk=bk, half=half, tt=tt: e.activation(out=hT[:, half * 4:half * 4 + 4, tt * 128:(tt + 1) * 128],
                                                                                 in_=pb[bk][:].rearrange("p (c n) -> p c n", c=4), func=AF.Identity),
                           reads=[pbb[bk]], writes=[hb[tt]])
                    else:
                        op("dve", lambda e, bk=bk, half=half, tt=tt: e.tensor_copy(out=hT[:, half * 4:half * 4 + 4, tt * 128:(tt + 1) * 128],
                                                                                   in_=pb[bk][:].rearrange("p (c n) -> p c n", c=4)),
                           reads=[pbb[bk]], writes=[hb[tt]])
            kb.barrier()

        def rstd_from_psum(ps_ap, out_ap, n, dim, pbuf, obuf):
            op("act", lambda e: e.activation(out=out_ap, in_=ps_ap, func=AF.Ln, scale=1.0 / dim, bias=EPS), reads=[pbuf], writes=[obuf])
            op("act", lambda e: e.activation(out=out_ap, in_=out_ap, func=AF.Exp, scale=-0.5), reads=[obuf], writes=[obuf])

        def ada_parts(L, scope):
            slots = [sbt(scope, [128, 8, 512], BF16, "adaw") for _ in range(3)]
            sbf = [Buf("adaw%d" % i) for i in range(3)]
            awv = ada_w[L].rearrange("(c p) n -> p c n", p=128)
            pm = pb[7][:, 0:96].rearrange("p (m j) -> p m j", j=2)
            msb, amx, afn, mb = modsbs[L % 2], amixs[L % 2], affns[L % 2], modbs[L % 2]

            def a_dma(g):
                s = g % 3
                dma("pool", slots[s][:], awv[:, :, g * 512:(g + 1) * 512], writes=[sbf[s]])

            def a_mm(g):
                s = g % 3
                for jj in range(4):
                    m = g * 4 + jj
                    for k in range(8):
                        op("pe", lambda e: e.matmul(pm[:, m, :], lhsT=slots[s][:, k, jj * 128:(jj + 1) * 128], rhs=cond[:, k, :], start=(k == 0), stop=(k == 7)),
                           reads=[sbf[s], cbuf], writes=[pbb[7]], inc=(k == 7))

            def a_fin():
                base = L * SL
                op("dve", lambda e: e.tensor_tensor(out=msb[:], in0=pm, in1=small[:, base + 16:base + 64].unsqueeze(2).broadcast_to([128, 48, 2]), op=ALU.add),
                   reads=[pbb[7], cbuf], writes=[mb])
                op("dve", lambda e: e.scalar_tensor_tensor(out=amx[:], in0=msb[:, 8:16, :], scalar=1.0,
                                                           in1=small[:, base:base + 8].unsqueeze(2).broadcast_to([128, 8, 2]), op0=ALU.add, op1=ALU.mult),
                   reads=[mb, cbuf], writes=[mb])
                op("dve", lambda e: e.scalar_tensor_tensor(out=afn[:], in0=msb[:, 32:40, :], scalar=1.0,
                                                           in1=small[:, base + 8:base + 16].unsqueeze(2).broadcast_to([128, 8, 2]), op0=ALU.add, op1=ALU.mult),
                   reads=[mb, cbuf], writes=[mb])

            return a_dma, a_mm, a_fin

        def ada_phase(L):
            with ExitStack() as ps:
                a_dma, a_mm, a_fin = ada_parts(L, ps)
                for g in range(12):
                    a_dma(g)
                    a_mm(g)
                a_fin()
                kb.barrier()

        def norm_mod(a_t, shift_off, blocks):
            with ExitStack() as ps:
                sq = [sbt(ps, [128, 8, 512], BF16, "sq") for _ in range(2)]
                sqb = [Buf("sq0"), Buf("sq1")]
                rs = [sbt(ps, [128, 512], F32, "rs") for _ in range(2)]
                rsb = [Buf("rs0"), Buf("rs1")]
                tmp = [sbt(ps, [128, 4, 512], F32, "nt") for _ in range(2)]
                tmb = [Buf("nt0"), Buf("nt1")]
                for bi, (t0, n, j) in enumerate(blocks):
                    s = bi % 2
                    bk = 6 + s
                    bidx = BLOCKS.index((t0, n, j))
                    op("act", lambda e, s=s: e.activation(out=sq[s][:, :, 0:n], in_=hT[:, :, t0:t0 + n], func=AF.Square), reads=hbs(t0, n), writes=[sqb[s]])
                    for c in range(8):
                        op("pe", lambda e, s=s, c=c, bk=bk: e.matmul(pb[bk][:, 0:n], lhsT=ones_bf[:], rhs=sq[s][:, c, 0:n], start=(c == 0), stop=(c == 7)),
                           reads=[sqb[s], cbuf], writes=[pbb[bk]], inc=(c == 7))
                    rstd_from_psum(pb[bk][:, 0:n], rs[s][:, 0:n], n, float(D), pbb[bk], rsb[s])
                    for half in range(2):
                        op("dve", lambda e, s=s, half=half: e.tensor_tensor(out=tmp[half][:, :, 0:n], in0=hT[:, half * 4:half * 4 + 4, t0:t0 + n],
                                                                            in1=rs[s][:, 0:n].unsqueeze(1).broadcast_to([128, 4, n]), op=ALU.mult),
                           reads=hbs(t0, n) + [rsb[s]], writes=[tmb[half]])
                        for c4 in range(4):
                            c = half * 4 + c4
                            op("act", lambda e, half=half, c4=c4, c=c: e.activation(out=uT[:, c, t0:t0 + n], in_=tmp[half][:, c4, 0:n], func=AF.Identity,
                                                                                    scale=a_t()[:, c, j:j + 1], bias=M.modsb[:, shift_off + c, j:j + 1]),
                               reads=[tmb[half], M.modb], writes=[ub[bidx]])
                kb.barrier()

        def ffn_phase(L, blocks, prefetch=None):
            with ExitStack() as ps:
                pre = ada_parts(prefetch, ps) if prefetch is not None else None
                w1s = [sbt(ps, [128, 8, 512], BF16, "w1s") for _ in range(3)]
                w2s = [sbt(ps, [128, 4, D], BF16, "w2s") for _ in range(3)]
                wb = [Buf("ffw%d" % i) for i in range(3)]
                hid = [sbt(ps, [128, 4, 512], BF16, "hid") for _ in range(2)]
                hib = [Buf("hid0"), Buf("hid1")]
                sqv = [sbt(ps, [128, 512], F32, "sqv") for _ in range(2)]
                sqvb = [Buf("sqv0"), Buf("sqv1")]
                w1v = w1_d[L].rearrange("(c p) n -> p c n", p=128)
                w2v = w2_d[L].rearrange("(c p) n -> p c n", p=128)
                items = [(e8, blk) for e8 in range(8) for blk in blocks][:FFN_ITEMS]
                loaded = set()

                def load(e8):
                    if e8 in loaded or e8 >= 8:
                        return
                    loaded.add(e8)
                    s = e8 % 3
                    dma("pool", w1s[s][:], w1v[:, :, e8 * 512:(e8 + 1) * 512], writes=[wb[s]])
                    dma("pool", w2s[s][:], w2v[:, e8 * 4:(e8 + 1) * 4, :], writes=[wb[s]])

                sqi = [0]

                def stage_h(i):
                    e8, (t0, n, j) = items[i]
                    s = e8 % 3
                    bidx = BLOCKS.index((t0, n, j))
                    for jj in range(4):
                        for k in range(8):
                            op("pe", lambda e, jj=jj, k=k: e.matmul(pb[jj][:, 0:n], lhsT=w1s[s][:, k, jj * 128:(jj + 1) * 128], rhs=uT[:, k, t0:t0 + n],
                                                                    start=(k == 0), stop=(k == 7)),
                               reads=[wb[s], ub[bidx]], writes=[pbb[jj]], inc=(k == 7))
                        q = sqi[0] % 2
                        sqi[0] += 1
                        op("act", lambda e, jj=jj, q=q: e.activation(out=sqv[q][:, 0:n], in_=pb[jj][:, 0:n], func=AF.Square), reads=[pbb[jj]], writes=[sqvb[q]])
                        op("dve", lambda e, jj=jj, q=q: e.scalar_tensor_tensor(out=hid[i % 2][:, jj, 0:n], in0=pb[jj][:, 0:n], scalar=0.0, in1=sqv[q][:, 0:n],
                                                                               op0=ALU.is_gt, op1=ALU.mult),
                           reads=[pbb[jj], sqvb[q]], writes=[hib[i % 2]])

                oi = [0]

                def stage_o(i):
                    e8, (t0, n, j) = items[i]
                    s = e8 % 3
                    for f in range(8):
                        bk = 4 + oi[0] % 3
                        oi[0] += 1
                        for jj in range(4):
                            op("pe", lambda e, jj=jj, f=f, bk=bk: e.matmul(pb[bk][:, 0:n], lhsT=w2s[s][:, jj, f * 128:(f + 1) * 128], rhs=hid[i % 2][:, jj, 0:n],
                                                                           start=(jj == 0), stop=(jj == 3)),
                               reads=[wb[s], hib[i % 2]], writes=[pbb[bk]], inc=(jj == 3))
                        op("dve", lambda e, f=f, bk=bk: e.scalar_tensor_tensor(out=hT[:, f, t0:t0 + n], in0=pb[bk][:, 0:n], scalar=M.modsb[:, 40 + f, j:j + 1],
                                                                               in1=hT[:, f, t0:t0 + n], op0=ALU.mult, op1=ALU.add),
                           reads=[pbb[bk], M.modb] + hbs(t0, n), writes=hbs(t0, n))

                load(0)
                load(1)
                stage_h(0)
                for i in range(len(items)):
                    if pre is not None:
                        if i % 3 == 0 and i // 3 < 12:
                            pre[0](i // 3)
                        if i >= 6 and (i - 6) % 3 == 0 and (i - 6) // 3 < 12:
                            pre[1]((i - 6) // 3)
                    if i + 1 < len(items):
                        load(items[i + 1][0] + 1)
                        stage_h(i + 1)
                    stage_o(i)
                if pre is not None:
                    pre[2]()
                kb.barrier()

        def outproj_acc(pbank, lhs_list, rhs_list, t0, n, j, reads, stop_bufs=None):
            for f in range(8):
                bk = pbank[f % len(pbank)]
                nmm = len(lhs_list)
                for i in range(nmm):
                    op("pe", lambda e, i=i, f=f, bk=bk: e.matmul(pb[bk][:, 0:n], lhsT=lhs_list[i][:, f * 128:(f + 1) * 128], rhs=rhs_list[i],
                                                                 start=(i == 0), stop=(i == nmm - 1)),
                       reads=reads, writes=[pbb[bk]], inc=(i == nmm - 1))
                op("dve", lambda e, f=f, bk=bk: e.scalar_tensor_tensor(out=hT[:, f, t0:t0 + n], in0=pb[bk][:, 0:n], scalar=M.modsb[:, 16 + f, j:j + 1],
                                                                       in1=hT[:, f, t0:t0 + n], op0=ALU.mult, op1=ALU.add),
                   reads=[pbb[bk], M.modb] + hbs(t0, n), writes=hbs(t0, n))

        def mlstm_phase(L, slot, need_ctx):
            out_blocks = BLOCKS if need_ctx else BLOCKS[1:]
            with ExitStack() as ps:
                G = sbt(ps, [128, NT, 32], F32, "G")
                SP = sbt(ps, [128, NT, 16], F32, "SP")
                LI = sbt(ps, [128, NT, 16], F32, "LI")
                KS = sbt(ps, [128, NT, 16], F32, "KS")
                KSb = sbt(ps, [128, NT, 16], BF16, "KSb")
                EBH = sbt(ps, [128, NT, 16], F32, "EBH")
                EBL = sbt(ps, [128, NT, 16], F32, "EBL")
                gbuf = Buf("gates")
                wg = sbt(ps, [128, 8, 32], BF16, "wg")
                wgb = Buf("wg")
                wmv = wm_d[slot].rearrange("(c p) n -> p c n", p=128)
                dma("pool", wg[:], wmv[:, :, 3584:3616], writes=[wgb])
                gbias = small[:, S_MGB + slot * 32:S_MGB + slot * 32 + 32]
                for tt in range(NT):
                    bk = 0 if tt < 16 else 1
                    o = (tt % 16) * 32
                    for k in range(8):
                        op("pe", lambda e, tt=tt, k=k, bk=bk, o=o: e.matmul(pb[bk][:, o:o + 32], lhsT=uT[:, k, tt * 128:(tt + 1) * 128], rhs=wg[:, k, :],
                                                                            start=(k == 0), stop=(k == 7)),
                           reads=[wgb] + ub, writes=[pbb[bk]], inc=(k == 7))
                op("dve", lambda e: e.tensor_tensor(out=G[:, 0:16, :], in0=pb[0][:].rearrange("p (t g) -> p t g", g=32),
                                                    in1=gbias.unsqueeze(1).broadcast_to([128, 16, 32]), op=ALU.add), reads=[pbb[0], cbuf], writes=[gbuf])
                op("dve", lambda e: e.tensor_tensor(out=G[:, 16:18, :], in0=pb[1][:, 0:64].rearrange("p (t g) -> p t g", g=32),
                                                    in1=gbias.unsqueeze(1).broadcast_to([128, 2, 32]), op=ALU.add), reads=[pbb[1], cbuf], writes=[gbuf])
                for d in range(2):
                    op("dve", lambda e, d=d: e.tensor_copy(out=LI[:, :, d * 8:d * 8 + 8], in_=G[:, :, d * 16:d * 16 + 8]), reads=[gbuf], writes=[gbuf])
                    op("act", lambda e, d=d: e.activation(out=SP[:, :, d * 8:d * 8 + 8], in_=G[:, :, d * 16 + 8:d * 16 + 16], func=AF.Exp, scale=-1.0),
                       reads=[gbuf], writes=[gbuf])
                op("act", lambda e: e.activation(out=SP[:], in_=SP[:], func=AF.Ln, scale=1.0, bias=1.0), reads=[gbuf], writes=[gbuf])
                for tt in range(NT):
                    bk = 2 if tt < 16 else 3
                    o = (tt % 16) * 32
                    op("pe", lambda e, tt=tt, bk=bk, o=o: e.matmul(pb[bk][:, o:o + 8], lhsT=ntricf, rhs=SP[:, tt, 0:8], start=True, stop=True),
                       reads=[gbuf, cbuf], writes=[pbb[bk]], inc=False)
                    op("pe", lambda e, tt=tt, bk=bk, o=o: e.matmul(pb[bk][:, o + 8:o + 16], lhsT=ntricb, rhs=SP[:, tt, 8:16], start=True, stop=True),
                       reads=[gbuf, cbuf], writes=[pbb[bk]], inc=False)
                    op("pe", lambda e, tt=tt, bk=bk, o=o: e.matmul(pb[bk][:, o + 16:o + 32], lhsT=nhalf[:], rhs=SP[:, tt, :], start=True, stop=True),
                       reads=[gbuf, cbuf], writes=[pbb[bk]], inc=True)
                for (bk, a, b_) in ((2, 0, 16), (3, 16, 18)):
                    nn = b_ - a
                    pv = pb[bk][:, 0:nn * 32].rearrange("p (t g) -> p t g", g=32)
                    op("dve", lambda e, pv=pv, a=a, b_=b_: e.tensor_tensor(out=KS[:, a:b_, :], in0=LI[:, a:b_, :], in1=pv[:, :, 0:16], op=ALU.subtract),
                       reads=[pbb[bk], gbuf], writes=[gbuf])
                    op("act", lambda e, pv=pv, a=a, b_=b_: e.activation(out=EBH[:, a:b_, :], in_=pv[:, :, 16:32], func=AF.Exp), reads=[pbb[bk]], writes=[gbuf])
                    op("act", lambda e, pv=pv, a=a, b_=b_: e.activation(out=EBL[:, a:b_, :], in_=pv[:, :, 16:32], func=AF.Exp, scale=2.0), reads=[pbb[bk]], writes=[gbuf])
                op("act", lambda e: e.activation(out=KS[:], in_=KS[:], func=AF.Exp), reads=[gbuf], writes=[gbuf])
                op("dve", lambda e: e.tensor_copy(out=KSb[:], in_=KS[:]), reads=[gbuf], writes=[gbuf])

                wh = [sbt(ps, [128, 8, 448], BF16, "wh") for _ in range(2)]
                whb = [Buf("wh0"), Buf("wh1")]
                wo = [sbt(ps, [128, D], BF16, "wo") for _ in range(2)]
                wob = [Buf("wo0"), Buf("wo1")]
                Qb = [sbt(ps, [64, T], BF16, "Qb") for _ in range(2)]
                qbb = [Buf("Qbf"), Buf("Qbb")]
                KT = sbt(ps, [64, T], BF16, "KT")
                ktb = Buf("KT")
                Ktok = sbt(ps, [128, NT, 64], BF16, "Ktok")
                kkb = Buf("Ktok")
                Va = [sbt(ps, [128, NT, 128], BF16, "Va") for _ in range(2)]
                vab = [Buf("Vaf"), Buf("Vab")]
                HSd = [sbt(ps, [128, T], F32, "HSf"), sbt(ps, [128, T], F32, "HSb")]
                HS = HSd[0]
                hsd = [[Buf("hs%d_%d" % (d_, i)) for i in range(NT)] for d_ in range(2)]
                hsb = hsd[0]
                Cst = [sbt(ps, [64, 256], F32, "Cst") for _ in range(2)]
                csb = [Buf("Cf"), Buf("Cb")]
                Cbf = [[sbt(ps, [64, 256], BF16, "Cbf") for _ in range(2)] for _ in range(2)]
                cbb = [[Buf("Cbf%d_%d" % (d_, i)) for i in range(2)] for d_ in range(2)]
                ctmp = [sbt(ps, [64, 256], F32, "ctmp") for _ in range(2)]
                ctb = [Buf("ct0"), Buf("ct1")]
                ebt = [sbt(ps, [64, 512], F32, "ebt") for _ in range(2)]
                ebb = [Buf("eb0"), Buf("eb1")]
                Pm = [sbt(ps, [128, 128], BF16, "Pm") for _ in range(4)]
                pmb = [Buf("Pm%d" % i) for i in range(4)]
                adn = [sbt(ps, [128, 128], F32, "adn") for _ in range(2)]
                adb = [Buf("ad0"), Buf("ad1")]
                htm = [sbt(ps, [128, 128], F32, "htm") for _ in range(2)]
                htb = [Buf("ht0"), Buf("ht1")]
                sqh = sbt(ps, [128, 512], BF16, "sqh")
                sqhb = Buf("sqh")
                rsh = sbt(ps, [128, 512], F32, "rsh")
                rshb = Buf("rsh")
                sig = sbt(ps, [128, 512], F32, "sig")
                sigb = Buf("sig")
                hn = sbt(ps, [128, 512], F32, "hn")
                hnb = Buf("hn")
                ao = [sbt(ps, [128, 512], BF16, "ao") for _ in range(2)]
                aob = [Buf("ao0"), Buf("ao1")]
                wov = wmo_d[slot]

                def load_head(h):
                    s = h % 2
                    dma("pool", wh[s][:], wmv[:, :, h * 448:(h + 1) * 448], writes=[whb[s]])
                    dma("pool", wo[s][:], wov[h * 128:(h + 1) * 128, :], writes=[wob[s]])

                order = [list(range(NT)), [1, 0] + list(range(NT - 1, 1, -1))]
                load_head(0)
                for h in range(8):
                    s = h % 2
                    if h + 1 < 8:
                        load_head(h + 1)
                    w = wh[s]
                    for (t0, n, j) in BLOCKS:
                        bidx = BLOCKS.index((t0, n, j))
                        for d in range(2):
                            ntr = ntricf if d == 0 else ntricb
                            for ti in range(n // 128):
                                tt = t0 // 128 + ti
                                op("pe", lambda e, d=d, tt=tt, ti=ti, ntr=ntr: e.matmul(pb[d][0:64, ti * 128:(ti + 1) * 128],
                                                                                         lhsT=SP[:, tt, d * 8 + h:d * 8 + h + 1].broadcast_to([128, 64]), rhs=ntr,
                                                                                         start=True, stop=True),
                                   reads=[gbuf, cbuf], writes=[pbb[d]], inc=(ti == n // 128 - 1))
                            op("act", lambda e, d=d: e.activation(out=ebt[d][:, 0:n], in_=pb[d][0:64, 0:n], func=AF.Exp), reads=[pbb[d]], writes=[ebb[d]])
                        for k in range(8):
                            op("pe", lambda e, k=k: e.matmul(pb[2][0:64, 0:n], lhsT=w[:, k, 0:64], rhs=uT[:, k, t0:t0 + n], start=(k == 0), stop=(k == 7)),
                               reads=[whb[s], ub[bidx]], writes=[pbb[2]], inc=(k == 7))
                        for d in range(2):
                            op("dve", lambda e, d=d: e.tensor_tensor(out=Qb[d][:, t0:t0 + n], in0=pb[2][0:64, 0:n], in1=ebt[d][:, 0:n], op=ALU.mult),
                               reads=[pbb[2], ebb[d]], writes=[qbb[d]])
                        for k in range(8):
                            op("pe", lambda e, k=k: e.matmul(pb[3][0:64, 0:n], lhsT=w[:, k, 64:128], rhs=uT[:, k, t0:t0 + n], start=(k == 0), stop=(k == 7)),
                               reads=[whb[s], ub[bidx]], writes=[pbb[3]], inc=(k == 7))
                        op("act", lambda e: e.activation(out=KT[:, t0:t0 + n], in_=pb[3][0:64, 0:n], func=AF.Identity, scale=0.125), reads=[pbb[3]], writes=[ktb])
                    for tt in range(NT):
                        bk = 4 + tt % 2
                        bidx = 0 if tt < 2 else 1 + (tt - 2) // 4
                        for k in range(8):
                            op("pe", lambda e, k=k, tt=tt, bk=bk: e.matmul(pb[bk][:, 0:192], lhsT=uT[:, k, tt * 128:(tt + 1) * 128], rhs=w[:, k, 128:320],
                                                                           start=(k == 0), stop=(k == 7)),
                               reads=[whb[s], ub[bidx]], writes=[pbb[bk]], inc=(k == 7))
                        op("act", lambda e, tt=tt, bk=bk: e.activation(out=Ktok[:, tt, :], in_=pb[bk][:, 0:64], func=AF.Identity, scale=0.125), reads=[pbb[bk]], writes=[kkb])
                        for d in range(2):
                            op("dve", lambda e, tt=tt, bk=bk, d=d: e.tensor_scalar(out=Va[d][:, tt, :], in0=pb[bk][:, 64:192], scalar1=KS[:, tt, d * 8 + h:d * 8 + h + 1],
                                                                                    scalar2=None, op0=ALU.mult),
                               reads=[pbb[bk], gbuf], writes=[vab[d]])
                    for d in range(2):
                        op("dve", lambda e, d=d: e.memset(Cst[d][:], 0.0), writes=[csb[d]])
                    def s1(step):
                        for d in range(2):
                            tt = order[d][step]
                            col = d * 8 + h
                            tsl = slice(tt * 128, (tt + 1) * 128)
                            if not (need_ctx or tt >= 2):
                                continue
                            cq = step % 2
                            if step > 0:
                                op("dve", lambda e: e.tensor_scalar(out=Cbf[d][cq][:], in0=Cst[d][:], scalar1=EBH[0:64, tt, col:col + 1], scalar2=None, op0=ALU.mult),
                                   reads=[csb[d], gbuf], writes=[cbb[d][cq]])
                            bS = d
                            p_ = (step % 2) * 2 + d
                            op("pe", lambda e: e.matmul(pb[bS][:, 0:128], lhsT=KT[:, tsl], rhs=Qb[d][:, tsl], start=True, stop=True),
                               reads=[ktb, qbb[d]], writes=[pbb[bS]])
                            mk = maskf_bf if d == 0 else maskb_bf
                            op("dve", lambda e: e.tensor_tensor(out=Pm[p_][:], in0=pb[bS][:, 0:128], in1=mk[:], op=ALU.mult),
                               reads=[pbb[bS], cbuf], writes=[pmb[p_]])

                    def s3(step):
                        if step >= NT - 1:
                            return
                        for d in range(2):
                            tt = order[d][step]
                            col = d * 8 + h
                            ksl = KSb[:, tt, col:col + 1].broadcast_to([128, 128])
                            bK = 6 + d
                            op("pe", lambda e: e.matmul(pb[bK][0:64, 0:128], lhsT=Ktok[:, tt, :], rhs=Va[d][:, tt, :], start=True, stop=True),
                               reads=[kkb, vab[d]], writes=[pbb[bK]], inc=False)
                            op("pe", lambda e: e.matmul(pb[bK][0:64, 128:256], lhsT=Ktok[:, tt, :], rhs=ksl, start=True, stop=True),
                               reads=[kkb, gbuf], writes=[pbb[bK]])
                            op("dve", lambda e: e.tensor_scalar(out=ctmp[d][:], in0=pb[bK][0:64, 0:256], scalar1=EBH[0:64, tt, col:col + 1], scalar2=None, op0=ALU.mult),
                               reads=[pbb[bK], gbuf], writes=[ctb[d]])
                            op("dve", lambda e: e.scalar_tensor_tensor(out=Cst[d][:], in0=Cst[d][:], scalar=EBL[0:64, tt, col:col + 1], in1=ctmp[d][:], op0=ALU.mult, op1=ALU.add),
                               reads=[ctb[d], gbuf, csb[d]], writes=[csb[d]])

                    def s2(step):
                        for d in range(2):
                            tt = order[d][step]
                            col = d * 8 + h
                            tsl = slice(tt * 128, (tt + 1) * 128)
                            if not (need_ctx or tt >= 2):
                                continue
                            ksl = KSb[:, tt, col:col + 1].broadcast_to([128, 128])
                            cq = step % 2
                            bN = 2 + d
                            bD = 4 + d
                            p_ = (step % 2) * 2 + d
                            last = (step == 0)
                            op("pe", lambda e: e.matmul(pb[bN][:, 0:128], lhsT=Va[d][:, tt, :], rhs=Pm[p_][:], start=True, stop=last),
                               reads=[vab[d], pmb[p_]], writes=[pbb[bN]], inc=last)
                            if not last:
                                op("pe", lambda e: e.matmul(pb[bN][:, 0:128], lhsT=Cbf[d][cq][:, 0:128], rhs=Qb[d][:, tsl], start=False, stop=True),
                                   reads=[cbb[d][cq], qbb[d]], writes=[pbb[bN]])
                            op("pe", lambda e: e.matmul(pb[bD][:, 0:128], lhsT=ksl, rhs=Pm[p_][:], start=True, stop=last),
                               reads=[gbuf, pmb[p_]], writes=[pbb[bD]], inc=last)
                            if not last:
                                op("pe", lambda e: e.matmul(pb[bD][:, 0:128], lhsT=Cbf[d][cq][:, 128:256], rhs=Qb[d][:, tsl], start=False, stop=True),
                                   reads=[cbb[d][cq], qbb[d]], writes=[pbb[bD]])
                            op("act", lambda e: e.activation(out=adn[d][:], in_=pb[bD][:, 0:128], func=AF.Abs), reads=[pbb[bD]], writes=[adb[d]])
                            op("dve", lambda e: e.tensor_scalar(out=adn[d][:], in0=adn[d][:], scalar1=1.0, scalar2=None, op0=ALU.max), reads=[adb[d]], writes=[adb[d]])
                            op("dve", lambda e: e.reciprocal(out=adn[d][:], in_=adn[d][:]), reads=[adb[d]], writes=[adb[d]])
                            op("dve", lambda e: e.tensor_tensor(out=HSd[d][:, tsl], in0=pb[bN][:, 0:128], in1=adn[d][:], op=ALU.mult),
                               reads=[pbb[bN], adb[d]], writes=[hsd[d][tt]])

                    s1(0)
                    for step in range(NT):
                        s3(step)
                        if step + 1 < NT:
                            s1(step + 1)
                        s2(step)
                    hnw = small[:, S_MHN + slot * 8 + h:S_MHN + slot * 8 + h + 1]
                    for bi, (t0, n, j) in enumerate(out_blocks):
                        bidx = BLOCKS.index((t0, n, j))
                        hsl = [hsb[i] for i in range(t0 // 128, (t0 + n) // 128)]
                        hsl2 = [hsd[1][i] for i in range(t0 // 128, (t0 + n) // 128)]
                        op("dve", lambda e: e.tensor_tensor(out=HS[:, t0:t0 + n], in0=HS[:, t0:t0 + n], in1=HSd[1][:, t0:t0 + n], op=ALU.add), reads=hsl + hsl2, writes=hsl)
                        op("act", lambda e: e.activation(out=sqh[:, 0:n], in_=HS[:, t0:t0 + n], func=AF.Square), reads=hsl, writes=[sqhb])
                        op("pe", lambda e: e.matmul(pb[0][:, 0:n], lhsT=ones_bf[:], rhs=sqh[:, 0:n], start=True, stop=True), reads=[sqhb, cbuf], writes=[pbb[0]])
                        rstd_from_psum(pb[0][:, 0:n], rsh[:, 0:n], n, 128.0, pbb[0], rshb)
                        for k in range(8):
                            op("pe", lambda e, k=k: e.matmul(pb[1][:, 0:n], lhsT=w[:, k, 320:448], rhs=uT[:, k, t0:t0 + n], start=(k == 0), stop=(k == 7)),
                               reads=[whb[s], ub[bidx]], writes=[pbb[1]], inc=(k == 7))
                        op("act", lambda e: e.activation(out=sig[:, 0:n], in_=pb[1][:, 0:n], func=AF.Exp, scale=-1.0), reads=[pbb[1]], writes=[sigb])
                        op("dve", lambda e: e.tensor_scalar(out=sig[:, 0:n], in0=sig[:, 0:n], scalar1=1.0, scalar2=None, op0=ALU.add), reads=[sigb], writes=[sigb])
                        op("dve", lambda e: e.reciprocal(out=sig[:, 0:n], in_=sig[:, 0:n]), reads=[sigb], writes=[sigb])
                        op("dve", lambda e: e.scalar_tensor_tensor(out=hn[:, 0:n], in0=HS[:, t0:t0 + n], scalar=hnw, in1=rsh[:, 0:n], op0=ALU.mult, op1=ALU.mult),
                           reads=hsl + [rshb, cbuf], writes=[hnb])
                        a_ = bi % 2
                        op("dve", lambda e, a_=a_: e.tensor_tensor(out=ao[a_][:, 0:n], in0=hn[:, 0:n], in1=sig[:, 0:n], op=ALU.mult), reads=[hnb, sigb], writes=[aob[a_]])
                        outproj_acc([6, 7], [wo[s]], [ao[a_][:, 0:n]], t0, n, j, [wob[s], aob[a_]])
                kb.barrier()

        def proj_rope(ps_res, w, c0, cr0, dst_ap_fn, dbuf, wbuf, banks, tmps, tmpb, rope_tiles, scale_ctx_copy=True):
            for (t0, n, j) in BLOCKS:
                bidx = BLOCKS.index((t0, n, j))
                b0, b1 = banks
                for k in range(8):
                    op("pe", lambda e, k=k: e.matmul(pb[b0][:, 0:n], lhsT=w[:, k, c0:c0 + 128], rhs=uT[:, k, t0:t0 + n], start=(k == 0), stop=(k == 7)),
                       reads=[wbuf, ub[bidx]], writes=[pbb[b0]], inc=(k == 7))
                if j == 1:
                    op("act", lambda e: e.activation(out=dst_ap_fn(t0, n), in_=pb[b0][:, 0:n], func=AF.Identity), reads=[pbb[b0]], writes=[dbuf])
                    continue
                for k in range(8):
                    op("pe", lambda e, k=k: e.matmul(pb[b1][:, 0:n], lhsT=w[:, k, cr0:cr0 + 128], rhs=uT[:, k, t0:t0 + n], start=(k == 0), stop=(k == 7)),
                       reads=[wbuf, ub[bidx]], writes=[pbb[b1]], inc=(k == 7))
                rc, rs_, rb = rope_tiles(t0)
                op("dve", lambda e: e.tensor_tensor(out=tmps[0][:, 0:n], in0=pb[b0][:, 0:n], in1=rc, op=ALU.mult), reads=[pbb[b0], rb], writes=[tmpb[0]])
                op("dve", lambda e: e.tensor_tensor(out=tmps[1][:, 0:n], in0=pb[b1][:, 0:n], in1=rs_, op=ALU.mult), reads=[pbb[b1], rb], writes=[tmpb[1]])
                op("dve", lambda e: e.tensor_tensor(out=dst_ap_fn(t0, n), in0=tmps[0][:, 0:n], in1=tmps[1][:, 0:n], op=ALU.add), reads=[tmpb[0], tmpb[1]], writes=[dbuf])

        def load_rope(ps):
            rc = sbt(ps, [128, NLAT], F32, "ropec")
            rs_ = sbt(ps, [128, NLAT], F32, "ropes")
            rb = Buf("rope")
            dma("sp", rc[:], ropec_l.get(), writes=[rb])
            dma("sp", rs_[:], ropes_l.get(), writes=[rb])

            def tiles(t0):
                l0 = t0 - NCTX
                return rc[:, l0:l0 + 512], rs_[:, l0:l0 + 512], rb
            return tiles

        def swa_phase(L, need_ctx):
            with ExitStack() as ps:
                rope_tiles = load_rope(ps)
                wsv = ws_d[0].rearrange("(c p) n -> p c n", p=128)
                wov = wso_d[0].rearrange("(h p) n -> p h n", p=64)
                wg_ = [sbt(ps, [128, 8, 832], BF16, "wsg") for _ in range(1)]
                wgb = [Buf("wsg0"), Buf("wsg1")]
                wo = [sbt(ps, [64, 4, D], BF16, "wso") for _ in range(1)]
                wob = [Buf("wso0"), Buf("wso1")]
                QT = sbt(ps, [128, 2, T], BF16, "QT")
                qtb = Buf("QT")
                KT = sbt(ps, [128, T], BF16, "KT")
                ktb = Buf("KT")
                Va = sbt(ps, [128, NT, 128], BF16, "Va")
                vab = Buf("Va")
                aog = sbt(ps, [64, 4, T], BF16, "aog")
                aob = [Buf("aog%d" % i) for i in range(NT)]
                tmps = [sbt(ps, [128, 512], F32, "rt") for _ in range(2)]
                tmpb = [Buf("rt0"), Buf("rt1")]
                Pt = [sbt(ps, [128, 512], BF16, "Pt") for _ in range(3)]
                ptb = [Buf("Pt%d" % i) for i in range(3)]
                dn = sbt(ps, [128, 512], F32, "dn")
                dnb = Buf("dn")
                ES = sbt(ps, [128, 16], F32, "ES")
                esb = Buf("ES")
                op("act", lambda e: e.activation(out=ES[:], in_=small[:, S_SINK:S_SINK + 16], func=AF.Exp), reads=[cbuf], writes=[esb])
                op("dve", lambda e: e.memset(Va[:, :, 64:128], 1.0), writes=[vab])

                def load_g(g):
                    s = 0
                    dma("pool", wg_[s][:], wsv[:, :, g * 832:(g + 1) * 832], writes=[wgb[s]])
                    dma("pool", wo[s][:], wov[:, g * 4:(g + 1) * 4, :], writes=[wob[s]])

                pti = 0
                for g in range(4):
                    s = 0
                    load_g(g)
                    w = wg_[s]
                    if SWA_STOP < 1:
                        continue
                    proj_rope(ps, w, 512, 640, lambda t0, n: KT[:, t0:t0 + n], ktb, wgb[s], (0, 1), tmps, tmpb, rope_tiles)
                    for jq in range(SWA_NQ):
                        if SWA_SUB < 1:
                            continue
                        proj_rope(ps, w, jq * 128, 256 + jq * 128, lambda t0, n, jq=jq: QT[:, jq, t0:t0 + n], qtb, wgb[s], SWA_QB, tmps, tmpb, rope_tiles)
                    for tt in range(NT):
                        if SWA_SUB < 2:
                            continue
                        bk = 4 + (tt // 8) % 2
                        o = (tt % 8) * 64
                        bidx = 0 if tt < 2 else 1 + (tt - 2) // 4
                        for k in range(8):
                            op("pe", lambda e, k=k, tt=tt, bk=bk, o=o: e.matmul(pb[bk][:, o:o + 64], lhsT=uT[:, k, tt * 128:(tt + 1) * 128], rhs=w[:, k, 768:832],
                                                                                start=(k == 0), stop=(k == 7)),
                               reads=[wgb[s], ub[bidx]], writes=[pbb[bk]], inc=(k == 7))
                        if tt % 8 == 7 or tt == NT - 1:
                            a = (tt // 8) * 8
                            nn = tt + 1 - a
                            op("act", lambda e, a=a, nn=nn, bk=bk: e.activation(out=Va[:, a:a + nn, 0:64], in_=pb[bk][:, 0:nn * 64].rearrange("p (t d) -> p t d", d=64),
                                                                                func=AF.Identity), reads=[pbb[bk]], writes=[vab])
                    aitems = []
                    for qt in range(NT):
                        if SWA_STOP < 2:
                            continue
                        if qt < 2:
                            if not need_ctx:
                                continue
                            kts = [0, 1]
                        else:
                            kts = [0, 1] + [kt for kt in (qt - 1, qt, qt + 1) if 2 <= kt < NT]
                        for ki, kt in enumerate(kts):
                            aitems.append((qt, kt, ki, len(kts)))

                    def s_stage(i):
                        qt, kt, ki, nk = aitems[i]
                        qsl = slice(qt * 128, (qt + 1) * 128)
                        ksl = slice(kt * 128, (kt + 1) * 128)
                        bS = (i % 2) * 2
                        p_ = i % 3
                        op("pe", lambda e: e.matmul(pb[bS][:, 0:256], lhsT=KT[0:64, ksl], rhs=QT[0:64, :, qsl], start=True, stop=True),
                           reads=[ktb, qtb], writes=[pbb[bS]])
                        op("pe", lambda e: e.matmul(pb[bS + 1][:, 0:256], lhsT=KT[64:128, ksl], rhs=QT[64:128, :, qsl], start=True, stop=True),
                           reads=[ktb, qtb], writes=[pbb[bS + 1]])
                        op("act", lambda e: e.activation(out=Pt[p_][:, 0:256], in_=pb[bS][:, 0:256], func=AF.Exp, scale=0.125), reads=[pbb[bS]], writes=[ptb[p_]])
                        op("act", lambda e: e.activation(out=Pt[p_][:, 256:512], in_=pb[bS + 1][:, 0:256], func=AF.Exp, scale=0.125), reads=[pbb[bS + 1]], writes=[ptb[p_]])
                        if qt >= 2 and kt >= 2 and kt != qt:
                            mk = maskb_bf if kt == qt - 1 else maskf_bf
                            op("dve", lambda e: e.tensor_tensor(out=Pt[p_][:].rearrange("p (h q) -> p h q", h=4), in0=Pt[p_][:].rearrange("p (h q) -> p h q", h=4),
                                                                in1=mk[:].unsqueeze(1).broadcast_to([128, 4, 128]), op=ALU.mult),
                               reads=[ptb[p_], cbuf], writes=[ptb[p_]])

                    def pv_stage(i):
                        qt, kt, ki, nk = aitems[i]
                        qsl = slice(qt * 128, (qt + 1) * 128)
                        p_ = i % 3
                        bO = 6 + qt % 2
                        op("pe", lambda e: e.matmul(pb[bO][:], lhsT=Va[:, kt, :], rhs=Pt[p_][:], start=(ki == 0), stop=(ki == nk - 1)),
                           reads=[vab, ptb[p_]], writes=[pbb[bO]], inc=(ki == nk - 1))
                        if ki == nk - 1:
                            op("dve", lambda e: e.tensor_tensor(out=dn[64:128, :].rearrange("p (h q) -> p h q", h=4), in0=pb[bO][64:128, :].rearrange("p (h q) -> p h q", h=4),
                                                                in1=ES[64:128, g * 4:(g + 1) * 4].unsqueeze(2).broadcast_to([64, 4, 128]), op=ALU.add),
                               reads=[pbb[bO], esb], writes=[dnb])
                            op("dve", lambda e: e.reciprocal(out=dn[64:128, :], in_=dn[64:128, :]), reads=[dnb], writes=[dnb])
                            op("dve", lambda e: e.tensor_tensor(out=aog[:, :, qsl], in0=pb[bO][0:64, :].rearrange("p (h q) -> p h q", h=4),
                                                                in1=dn[64:128, :].rearrange("p (h q) -> p h q", h=4), op=ALU.mult),
                               reads=[pbb[bO], dnb], writes=[aob[qt]])

                    if aitems:
                        s_stage(0)
                    for i in range(len(aitems)):
                        if i + 1 < len(aitems):
                            s_stage(i + 1)
                        pv_stage(i)
                    for (t0, n, j) in (BLOCKS if need_ctx else BLOCKS[1:]):
                        if SWA_STOP < 3:
                            continue
                        ab = [aob[i] for i in range(t0 // 128, (t0 + n) // 128)]
                        outproj_acc([0, 1, 2, 3], [wo[s][:, PSORD[pos], :] for pos in range(4)], [aog[:, pos, t0:t0 + n] for pos in range(4)], t0, n, j, [wob[s]] + ab)
                kb.barrier()

        def diff_phase(L, need_ctx):
            lam_init = 0.8 - 0.6 * float(np.exp(-0.3 * L))
            with ExitStack() as ps:
                rope_tiles = load_rope(ps)
                wdv = wd_d[0].rearrange("(c p) n -> p c n", p=128)
                wov = wdo_d[0]
                wh = [sbt(ps, [128, 8, 640], BF16, "wdh") for _ in range(2)]
                whb = [Buf("wdh0"), Buf("wdh1")]
                wo = [sbt(ps, [128, D], BF16, "wdo") for _ in range(2)]
                wob = [Buf("wdo0"), Buf("wdo1")]
                QT = sbt(ps, [128, T], BF16, "QT")
                qtb = Buf("QT")
                KT = sbt(ps, [128, T], BF16, "KT")
                ktb = Buf("KT")
                Vt = sbt(ps, [128, NT, 128], BF16, "Vt")
                vtb = Buf("Vt")
                tmps = [sbt(ps, [128, 512], F32, "rt") for _ in range(2)]
                tmpb = [Buf("rt0"), Buf("rt1")]
                Pt = [sbt(ps, [128, 512], BF16, "Pt") for _ in range(4)]
                ptb = [Buf("Pt%d" % i) for i in range(4)]
                rr = [sbt(ps, [128, 512], F32, "rr") for _ in range(2)]
                rrb = [Buf("rr0"), Buf("rr1")]
                od = sbt(ps, [128, 512], F32, "od")
                odb = Buf("od")
                sqh = sbt(ps, [128, 512], BF16, "sqh")
                sqhb = Buf("sqh")
                rsh = sbt(ps, [128, 512], F32, "rsh")
                rshb = Buf("rsh")
                ao = [sbt(ps, [128, 512], BF16, "ao") for _ in range(2)]
                aob = [Buf("ao0"), Buf("ao1")]
                lam = sbt(ps, [128, 8], F32, "lam")
                lamb = Buf("lam")
                hnw = sbt(ps, [128, 8], F32, "hnw")
                lv = small[:, S_DLAM:S_DLAM + 256]
                op("dve", lambda e: e.tensor_tensor(out=tmps[0][:, 0:64], in0=lv[:, 0:64], in1=lv[:, 64:128], op=ALU.mult), reads=[cbuf], writes=[tmpb[0]])
                op("dve", lambda e: e.tensor_tensor(out=tmps[0][:, 64:128], in0=lv[:, 128:192], in1=lv[:, 192:256], op=ALU.mult), reads=[cbuf], writes=[tmpb[0]])
                op("dve", lambda e: e.tensor_reduce(out=lam[:, 0:2], in_=tmps[0][:, 0:128].rearrange("p (a b) -> p a b", a=2), axis=mybir.AxisListType.X, op=ALU.add),
                   reads=[tmpb[0]], writes=[lamb])
                op("act", lambda e: e.activation(out=lam[:, 0:2], in_=lam[:, 0:2], func=AF.Exp), reads=[lamb], writes=[lamb])
                op("dve", lambda e: e.tensor_tensor(out=lam[:, 2:3], in0=lam[:, 1:2], in1=lam[:, 0:1], op=ALU.subtract), reads=[lamb], writes=[lamb])
                op("dve", lambda e: e.tensor_scalar(out=lam[:, 3:4], in0=lam[:, 2:3], scalar1=-lam_init, scalar2=None, op0=ALU.add), reads=[lamb], writes=[lamb])
                op("dve", lambda e: e.tensor_scalar(out=hnw[:], in0=small[:, S_DHN:S_DHN + 8], scalar1=(1.0 - lam_init), scalar2=None, op0=ALU.mult), reads=[cbuf], writes=[lamb])
                nlam = lam[:, 3:4]

                def load_h(h):
                    s = h % 2
                    dma("pool", wh[s][:], wdv[:, :, h * 640:(h + 1) * 640], writes=[whb[s]])
                    dma("pool", wo[s][:], wov[h * 128:(h + 1) * 128, :], writes=[wob[s]])

                load_h(0)
                pti = 0
                oi = 0
                for h in range(8):
                    s = h % 2
                    if h + 1 < 8:
                        load_h(h + 1)
                    w = wh[s]
                    proj_rope(ps, w, 256, 384, lambda t0, n: KT[:, t0:t0 + n], ktb, whb[s], (0, 1), tmps, tmpb, rope_tiles)
                    proj_rope(ps, w, 0, 128, lambda t0, n: QT[:, t0:t0 + n], qtb, whb[s], (2, 3), tmps, tmpb, rope_tiles)
                    for tt in range(NT):
                        bk = 4 + (tt // 4) % 2
                        o = (tt % 4) * 128
                        bidx = 0 if tt < 2 else 1 + (tt - 2) // 4
                        for k in range(8):
                            op("pe", lambda e, k=k, tt=tt, bk=bk, o=o: e.matmul(pb[bk][:, o:o + 128], lhsT=uT[:, k, tt * 128:(tt + 1) * 128], rhs=w[:, k, 512:640],
                                                                                start=(k == 0), stop=(k == 7)),
                               reads=[whb[s], ub[bidx]], writes=[pbb[bk]], inc=(k == 7))
                        if tt % 4 == 3 or tt == NT - 1:
                            a = (tt // 4) * 4
                            nn = tt + 1 - a
                            op("act", lambda e, a=a, nn=nn, bk=bk: e.activation(out=Vt[:, a:a + nn, :], in_=pb[bk][:, 0:nn * 128].rearrange("p (t d) -> p t d", d=128),
                                                                                func=AF.Identity), reads=[pbb[bk]], writes=[vtb])
                    for bi, (t0, n, j) in enumerate(BLOCKS if need_ctx else BLOCKS[1:]):
                        kts = [0, 1] if j == 1 else list(range(NT))
                        nk = len(kts)

                        def s_stage(ki):
                            kt = kts[ki]
                            ksl = slice(kt * 128, (kt + 1) * 128)
                            for m in range(2):
                                bS = m * 2 + (ki % 2)
                                p_ = (ki % 2) * 2 + m
                                rsl = slice(m * 64, (m + 1) * 64)
                                op("pe", lambda e, bS=bS, ksl=ksl, rsl=rsl: e.matmul(pb[bS][:, 0:n], lhsT=KT[rsl, ksl], rhs=QT[rsl, t0:t0 + n], start=True, stop=True),
                                   reads=[ktb, qtb], writes=[pbb[bS]])
                            for m in range(2):
                                bS = m * 2 + (ki % 2)
                                p_ = (ki % 2) * 2 + m
                                op("act", lambda e, bS=bS, p_=p_: e.activation(out=Pt[p_][:, 0:n], in_=pb[bS][:, 0:n], func=AF.Exp, scale=0.125), reads=[pbb[bS]], writes=[ptb[p_]])

                        def pv_stage(ki):
                            kt = kts[ki]
                            first = (ki == 0)
                            lastk = (ki == nk - 1)
                            for m in range(2):
                                p_ = (ki % 2) * 2 + m
                                op("pe", lambda e, kt=kt, p_=p_, m=m: e.matmul(pb[4 + m][:, 0:n], lhsT=Vt[:, kt, :], rhs=Pt[p_][:, 0:n], start=first, stop=lastk),
                                   reads=[vtb, ptb[p_]], writes=[pbb[4 + m]], inc=lastk)
                                op("pe", lambda e, p_=p_, m=m: e.matmul(pb[6 + m][:, 0:n], lhsT=ones_bf[:], rhs=Pt[p_][:, 0:n], start=first, stop=lastk),
                                   reads=[cbuf, ptb[p_]], writes=[pbb[6 + m]], inc=lastk)

                        s_stage(0)
                        for ki in range(nk):
                            if ki + 1 < nk:
                                s_stage(ki + 1)
                            pv_stage(ki)
                        for m in range(2):
                            op("dve", lambda e, m=m: e.reciprocal(out=rr[m][:, 0:n], in_=pb[6 + m][:, 0:n]), reads=[pbb[6 + m]], writes=[rrb[m]])
                            op("dve", lambda e, m=m: e.tensor_tensor(out=rr[m][:, 0:n], in0=pb[4 + m][:, 0:n], in1=rr[m][:, 0:n], op=ALU.mult), reads=[pbb[4 + m], rrb[m]], writes=[rrb[m]])
                        op("dve", lambda e: e.scalar_tensor_tensor(out=od[:, 0:n], in0=rr[1][:, 0:n], scalar=nlam, in1=rr[0][:, 0:n], op0=ALU.mult, op1=ALU.add),
                           reads=[rrb[0], rrb[1], lamb], writes=[odb])
                        op("act", lambda e: e.activation(out=sqh[:, 0:n], in_=od[:, 0:n], func=AF.Square), reads=[odb], writes=[sqhb])
                        op("pe", lambda e: e.matmul(pb[0][:, 0:n], lhsT=ones_bf[:], rhs=sqh[:, 0:n], start=True, stop=True), reads=[sqhb, cbuf], writes=[pbb[0]])
                        rstd_from_psum(pb[0][:, 0:n], rsh[:, 0:n], n, 128.0, pbb[0], rshb)
                        a_ = oi % 2
                        oi += 1
                        op("dve", lambda e, a_=a_: e.scalar_tensor_tensor(out=ao[a_][:, 0:n], in0=od[:, 0:n], scalar=hnw[:, h:h + 1], in1=rsh[:, 0:n], op0=ALU.mult, op1=ALU.mult),
                           reads=[odb, rshb, lamb], writes=[aob[a_]])
                        outproj_acc([1, 2, 3], [wo[s]], [ao[a_][:, 0:n]], t0, n, j, [wob[s], aob[a_]])
                kb.barrier()

        def final_phase(do_norm):
            with ExitStack() as ps:
                sq = sbt(ps, [128, 8, 512], BF16, "fsq")
                sqb = Buf("fsq")
                rs = sbt(ps, [128, 512], F32, "frs")
                rsb = Buf("frs")
                tmp = sbt(ps, [128, 8, 512], F32, "ftmp")
                tmb = Buf("ftmp")
                ost = [sbt(ps, [128, D], F32, "ost") for _ in range(2)]
                osb = [Buf("ost0"), Buf("ost1")]
                oi = 0
                for (t0, n, j) in BLOCKS[1:]:
                    if do_norm:
                        op("act", lambda e: e.activation(out=sq[:, :, 0:n], in_=hT[:, :, t0:t0 + n], func=AF.Square), reads=hbs(t0, n), writes=[sqb])
                        for c in range(8):
                            op("pe", lambda e, c=c: e.matmul(pb[7][:, 0:n], lhsT=ones_bf[:], rhs=sq[:, c, 0:n], start=(c == 0), stop=(c == 7)),
                               reads=[sqb, cbuf], writes=[pbb[7]], inc=(c == 7))
                        rstd_from_psum(pb[7][:, 0:n], rs[:, 0:n], n, float(D), pbb[7], rsb)
                        for c in range(8):
                            op("dve", lambda e, c=c: e.scalar_tensor_tensor(out=tmp[:, c, 0:n], in0=hT[:, c, t0:t0 + n], scalar=small[:, S_FINAL + c:S_FINAL + c + 1],
                                                                            in1=rs[:, 0:n], op0=ALU.mult, op1=ALU.mult),
                               reads=hbs(t0, n) + [rsb, cbuf], writes=[tmb])
                    else:
                        op("dve", lambda e: e.tensor_copy(out=tmp[:, :, 0:n], in_=hT[:, :, t0:t0 + n]), reads=hbs(t0, n), writes=[tmb])
                    for ti in range(n // 128):
                        o_ = oi % 2
                        oi += 1
                        for half in range(2):
                            bk = (oi * 2 + half) % 4
                            for c4 in range(4):
                                c = half * 4 + c4
                                op("pe", lambda e, c=c, c4=c4, bk=bk, ti=ti: e.transpose(pb[bk][:, c4 * 128:(c4 + 1) * 128], tmp[:, c, ti * 128:(ti + 1) * 128], ident),
                                   reads=[tmb, cbuf], writes=[pbb[bk]], inc=(c4 == 3))
                            if half == 0:
                                op("act", lambda e, bk=bk, o_=o_, half=half: e.activation(out=ost[o_][:, half * 512:(half + 1) * 512], in_=pb[bk][:], func=AF.Identity),
                                   reads=[pbb[bk]], writes=[osb[o_]])
                            else:
                                op("dve", lambda e, bk=bk, o_=o_, half=half: e.tensor_copy(out=ost[o_][:, half * 512:(half + 1) * 512], in_=pb[bk][:]),
                                   reads=[pbb[bk]], writes=[osb[o_]])
                        r0 = t0 - NCTX + ti * 128
                        dma("sp", out_d[r0:r0 + 128, :], ost[o_][:], reads=[osb[o_]])
                kb.wait_all_dma("sp")

        prefetched = set()
        for li, L in enumerate(layers):
            kind, slot = L % 3, L // 3
            need_ctx = L < DEPTH - 1
            blocks = BLOCKS if need_ctx else BLOCKS[1:]
            set_mod(L)
            if L not in prefetched:
                ada_phase(L)
            if do_mixer:
                norm_mod(lambda: M.amix, 0, BLOCKS)
                if kind == 0:
                    mlstm_phase(L, slot, need_ctx)
                elif kind == 1:
                    swa_phase(L, need_ctx)
                else:
                    diff_phase(L, need_ctx)
            if do_ffn:
                norm_mod(lambda: M.affn, 24, blocks)
                if do_ffn != "norm":
                    nxt = layers[li + 1] if li + 1 < len(layers) else None
                    can = nxt is not None and len(blocks) == len(BLOCKS) and FFN_ITEMS >= 40
                    ffn_phase(L, blocks, prefetch=nxt if can else None)
                    if can:
                        prefetched.add(nxt)
        final_phase(final_norm)
    nc._declared_inputs = list(declared)
    return nc


def _pcol(v):
    return np.ascontiguousarray(np.asarray(v, np.float32).reshape(-1, 128).T)


def _rot_idx(base):
    return list(range(base + 32, base + 64)) + list(range(base, base + 32))


def _consts():
    i = np.arange(128)
    ident = (i[:, None] == i[None, :]).astype(np.float32)
    maskf = (i[:, None] <= i[None, :]).astype(np.float32)
    maskb = (i[:, None] >= i[None, :]).astype(np.float32)
    return np.ascontiguousarray(np.concatenate([ident, maskf, maskb, -(maskf - 0.5), -(maskb - 0.5)], axis=1))


def _rope_tables():
    rows = NLAT // 64
    row = np.repeat(np.arange(rows), 64).astype(np.float32)
    col = np.tile(np.arange(64), rows).astype(np.float32)
    quarter = 16
    inv = (np.float32(10000.0) ** (-np.arange(quarter, dtype=np.float32) / np.float32(quarter))).astype(np.float32)
    ang = np.concatenate([row[:, None] * inv, col[:, None] * inv], axis=-1).astype(np.float32)
    cos = np.cos(ang).astype(np.float32).T
    sin = np.sin(ang).astype(np.float32).T
    c64 = np.concatenate([cos, cos], 0)
    s64 = np.concatenate([-sin, sin], 0)
    return np.ascontiguousarray(np.concatenate([c64, c64], 0)), np.ascontiguousarray(np.concatenate([s64, s64], 0))


def prepare_shared(inp):
    sh = {}
    sh["cst"] = _consts()
    sh["ropec"], sh["ropes"] = _rope_tables()
    sh["ada_w"] = np.ascontiguousarray(inp["ada_w"], dtype=np.float32)
    sh["ffn_w1"] = np.ascontiguousarray(inp["ffn_w1"], dtype=np.float32)
    sh["ffn_w2"] = np.ascontiguousarray(inp["ffn_w2"], dtype=np.float32)
    wm = []
    for s in range(inp["mlstm_w_in"].shape[0]):
        W = inp["mlstm_w_in"][s]
        cols = []
        for h in range(8):
            cols += list(range(h * 64, h * 64 + 64))
            cols += list(range(512 + h * 64, 512 + h * 64 + 64))
            cols += list(range(512 + h * 64, 512 + h * 64 + 64))
            cols += list(range(1024 + h * 128, 1024 + h * 128 + 128))
            cols += list(range(2048 + h * 128, 2048 + h * 128 + 128))
        cols += list(range(3072, 3104))
        wm.append(W[:, cols])
    sh["wm"] = np.ascontiguousarray(np.stack(wm), dtype=np.float32)
    sh["wmo"] = np.ascontiguousarray(inp["mlstm_w_out"], dtype=np.float32)
    W = inp["swa_w_in"][0]
    cols = []
    for g in range(4):
        q = []
        qr = []
        for hh in range(4):
            b = (4 * g + hh) * 64
            q += list(range(b, b + 64))
            qr += _rot_idx(b)
        kb_ = 1024 + g * 64
        k = list(range(kb_, kb_ + 64))
        kr = _rot_idx(kb_)
        v = list(range(1280 + g * 64, 1280 + g * 64 + 64))
        cols += q + qr + k + k + kr + kr + v
    sh["ws"] = np.ascontiguousarray(W[:, cols][None], dtype=np.float32)
    sh["wso"] = np.ascontiguousarray(inp["swa_w_out"], dtype=np.float32)
    W = inp["diff_w_in"][0]
    cols = []
    for h in range(8):
        qb_ = h * 128
        kb_ = 1024 + h * 128
        cols += list(range(qb_, qb_ + 128)) + _rot_idx(qb_) + _rot_idx(qb_ + 64)
        cols += list(range(kb_, kb_ + 128)) + _rot_idx(kb_) + _rot_idx(kb_ + 64)
        cols += list(range(2048 + h * 128, 2048 + h * 128 + 128))
    sh["wd"] = np.ascontiguousarray(W[:, cols][None], dtype=np.float32)
    sh["wdo"] = np.ascontiguousarray(inp["diff_w_out"], dtype=np.float32)
    small = np.zeros((128, NS), np.float32)
    for L in range(DEPTH):
        small[:, L * SL:L * SL + 8] = _pcol(inp["norm_mix"][L])
        small[:, L * SL + 8:L * SL + 16] = _pcol(inp["norm_ffn"][L])
        small[:, L * SL + 16:L * SL + 64] = _pcol(inp["ada_b"][L])
    small[:, S_FINAL:S_FINAL + 8] = _pcol(inp["final_norm"])
    for s in range(inp["mlstm_head_norm"].shape[0]):
        small[:, S_MHN + s * 8:S_MHN + s * 8 + 8] = _pcol(inp["mlstm_head_norm"][s])
        small[:, S_MGB + s * 32:S_MGB + s * 32 + 32] = np.broadcast_to(np.asarray(inp["mlstm_gate_b"][s], np.float32).reshape(1, 32), (128, 32))
    sink = np.asarray(inp["swa_sink"][0], np.float32)
    sord = [4 * g + PSORD[p] for g in range(4) for p in range(4)]
    small[:, S_SINK:S_SINK + 16] = np.broadcast_to(sink[sord][None, :], (128, 16))
    small[:, S_DHN:S_DHN + 8] = _pcol(inp["diff_head_norm"][0])
    lamv = np.concatenate([np.asarray(inp[k][0], np.float32) for k in ("diff_lambda_q1", "diff_lambda_k1", "diff_lambda_q2", "diff_lambda_k2")])
    small[:, S_DLAM:S_DLAM + 256] = np.broadcast_to(lamv[None, :], (128, 256))
    sh["small"] = small
    return sh


def prepare_core(inp, b):
    xin = np.ascontiguousarray(np.concatenate([inp["ctx"][b], inp["x"][b]], axis=0), dtype=np.float32)
    cv = np.zeros((128, 16), np.float32)
    cv[:, 0::2] = _pcol(inp["c"][b])
    cv[:, 1::2] = _pcol(inp["c_ctx"])
    return {"xin": xin, "cv": cv}


_NC_CACHE = {}


def kernel(**inputs):
    inp = {k: np.asarray(v) for k, v in inputs.items()}
    B = inp["x"].shape[0]
    shared = prepare_shared(inp)
    if "full" not in _NC_CACHE:
        _NC_CACHE["full"] = build_program()
    nc = _NC_CACHE["full"]
    in_maps = []
    for b in range(B):
        m = dict(shared)
        m.update(prepare_core(inp, b))
        in_maps.append({k: m[k] for k in nc._declared_inputs})
    res = run_bass_kernel_spmd(nc, in_maps, core_ids=list(range(B)))
    out = np.stack([np.asarray(r["out"], dtype=np.float32) for r in res.results], axis=0)
    return out
```

```python
import numpy as np
from contextlib import ExitStack
import concourse.bass as bass
import concourse.mybir as mybir
from concourse.bass_utils import run_bass_kernel_spmd

F32 = mybir.dt.float32
BF16 = mybir.dt.bfloat16
ALU = mybir.AluOpType
AF = mybir.ActivationFunctionType

D = 1024
NCTX = 256
NLAT = 2048
T = NCTX + NLAT
NT = T // 128
DEPTH = 4
EPS = 1e-6
NDS = 24
BLOCKS = [(0, 256, 1), (256, 512, 0), (768, 512, 0), (1280, 512, 0), (1792, 512, 0)]

SL = 64
S_FINAL = 4 * SL
S_MHN = S_FINAL + 8
S_MGB = S_MHN + 16
S_SINK = S_MGB + 64
S_DHN = S_SINK + 16
S_DLAM = S_DHN + 8
NS = S_DLAM + 256
NCONST = 5 * 128
PSORD = [0, 2, 1, 3]
FFN_ITEMS = 1000
SWA_STOP = 9
SWA_SUB = 9
SWA_QB = (0, 1)
SWA_NQ = 2


class Buf:
    __slots__ = ("name", "w", "r")

    def __init__(self, name):
        self.name = name
        self.w = None
        self.r = {}


class KB:
    def __init__(self, nc, es):
        self.nc = nc
        self.es = es
        self.E = {"pe": nc.tensor, "act": nc.scalar, "dve": nc.vector, "pool": nc.gpsimd, "sp": nc.sync}
        self.sem = {e: es.enter_context(nc.semaphore("s_" + e)) for e in self.E}
        self.cnt = {e: 0 for e in self.E}
        self.seen = {e: {} for e in self.E}
        self.dsem = [es.enter_context(nc.semaphore("d%d" % i)) for i in range(NDS)]
        self.dcnt = [0] * NDS
        self.dnext = {"sp": 0, "pool": 0}

    def _wait(self, eng, deps):
        for (sk, val) in deps:
            if sk == eng and eng == "pe":
                continue
            if self.seen[eng].get(sk, 0) >= val:
                continue
            semh = self.sem[sk] if isinstance(sk, str) else self.dsem[sk]
            self.E[eng].wait_ge(semh, val)
            self.seen[eng][sk] = val

    @staticmethod
    def _deps(reads, writes):
        deps = []
        for b in reads:
            if b.w is not None:
                deps.append(b.w)
        for b in writes:
            if b.w is not None:
                deps.append(b.w)
            deps.extend(b.r.values())
        return deps

    def op(self, eng, fn, reads=(), writes=(), inc=True):
        self._wait(eng, self._deps(reads, writes))
        ins = fn(self.E[eng])
        if inc:
            self.cnt[eng] += 1
            ins.then_inc(self.sem[eng], 1)
            tk = (eng, self.cnt[eng])
        else:
            tk = (eng, self.cnt[eng] + 1)
        for b in reads:
            b.r[eng] = tk
        for b in writes:
            b.w = tk
            b.r = {}

    def dma(self, q, out, in_, reads=(), writes=()):
        half = NDS // 2
        i = self.dnext[q] + (0 if q == "sp" else half)
        self.dnext[q] = (self.dnext[q] + 1) % half
        deps = self._deps(reads, writes)
        if self.dcnt[i] > 0:
            deps.append((i, self.dcnt[i]))
        self._wait(q, deps)
        self.E[q].dma_start(out=out, in_=in_).then_inc(self.dsem[i], 16)
        self.dcnt[i] += 16
        tk = (i, self.dcnt[i])
        for b in reads:
            b.r[("d", i)] = tk
        for b in writes:
            b.w = tk
            b.r = {}

    def wait_all_dma(self, eng):
        self._wait(eng, [(i, self.dcnt[i]) for i in range(NDS) if self.dcnt[i] > 0])

    def barrier(self):
        deps = [(e, self.cnt[e]) for e in self.E if e != "sp" and self.cnt[e] > 0]
        deps += [(i, self.dcnt[i]) for i in range(NDS) if self.dcnt[i] > 0]
        self._wait("sp", deps)
        self.cnt["sp"] += 1
        self.E["sp"].sem_inc(self.sem["sp"], 1)
        for e in self.E:
            if e != "sp":
                self._wait(e, [("sp", self.cnt["sp"])])


def build_program(layers=(0, 1, 2, 3), do_mixer=True, do_ffn=True, final_norm=True, in_ctx=True):
    nc = bass.Bass("TRN2", target_bir_lowering=False)

    declared = []
    only = None if len(layers) == DEPTH and do_mixer and do_ffn else set()

    class _Lazy:
        def __init__(self, name, shape):
            self.name, self.shape, self.ap_ = name, list(shape), None

        def get(self):
            if self.ap_ is None:
                self.ap_ = nc.dram_tensor(self.name, self.shape, F32, kind="ExternalInput").ap()
                declared.append(self.name)
            return self.ap_

        def __getitem__(self, k):
            return self.get()[k]

    def din(name, shape):
        return _Lazy(name, shape)

    xin = din("xin", [T, D]).get()
    cv_d = din("cv", [128, 16]).get()
    small_d = din("small", [128, NS]).get()
    cst_d = din("cst", [128, NCONST]).get()
    ropec_l = din("ropec", [128, NLAT])
    ropes_l = din("ropes", [128, NLAT])
    ada_w = din("ada_w", [DEPTH, D, 6 * D])
    w1_d = din("ffn_w1", [DEPTH, D, 4 * D])
    w2_d = din("ffn_w2", [DEPTH, 4 * D, D])
    wm_d = din("wm", [2, D, 3616])
    wmo_d = din("wmo", [2, D, D])
    ws_d = din("ws", [1, D, 3328])
    wso_d = din("wso", [1, D, D])
    wd_d = din("wd", [1, D, 5120])
    wdo_d = din("wdo", [1, D, D])
    out_d = nc.dram_tensor("out", [NLAT, D], F32, kind="ExternalOutput").ap()

    with ExitStack() as es:
        kb = KB(nc, es)
        op = kb.op
        dma = kb.dma
        cnt = [0]

        def sbt(es_, shape, dt, name=None):
            cnt[0] += 1
            return es_.enter_context(nc.sbuf_tensor("%s_%d" % (name or "t", cnt[0]), list(shape), dt))

        hT = sbt(es, [128, 8, T], F32, "hT")
        uT = sbt(es, [128, 8, T], BF16, "uT")
        hb = [Buf("h%d" % i) for i in range(NT)]
        ub = [Buf("u%d" % i) for i in range(len(BLOCKS))]
        cst = sbt(es, [128, NCONST], F32, "cst")
        small = sbt(es, [128, NS], F32, "small")
        cv = sbt(es, [128, 16], F32, "cv")
        cond = sbt(es, [128, 8, 2], BF16, "cond")
        ones_bf = sbt(es, [128, 128], BF16, "ones")
        maskf_bf = sbt(es, [128, 128], BF16, "maskf")
        maskb_bf = sbt(es, [128, 128], BF16, "maskb")
        nhalf = sbt(es, [128, 128], F32, "nhalf")
        modsbs = [sbt(es, [128, 48, 2], F32, "modsb") for _ in range(2)]
        amixs = [sbt(es, [128, 8, 2], F32, "amix") for _ in range(2)]
        affns = [sbt(es, [128, 8, 2], F32, "affn") for _ in range(2)]
        modbs = [Buf("mod0"), Buf("mod1")]
        cbuf = Buf("consts")

        class M:
            pass

        def set_mod(L):
            M.modsb, M.amix, M.affn, M.modb = modsbs[L % 2], amixs[L % 2], affns[L % 2], modbs[L % 2]
        ident = cst[:, 0:128]
        maskf = cst[:, 128:256]
        maskb = cst[:, 256:384]
        ntricf = cst[:, 384:512]
        ntricb = cst[:, 512:640]
        pb = [es.enter_context(nc.psum_tensor("pb%d" % i, [128, 512], F32)) for i in range(8)]
        pbb = [Buf("pb%d" % i) for i in range(8)]

        def hbs(t0, n):
            return [hb[i] for i in range(t0 // 128, (t0 + n) // 128)]

        dma("sp", cst[:], cst_d, writes=[cbuf])
        dma("sp", small[:], small_d, writes=[cbuf])
        dma("sp", cv[:], cv_d, writes=[cbuf])
        op("dve", lambda e: e.memset(ones_bf[:], 1.0), writes=[cbuf])
        op("dve", lambda e: e.memset(nhalf[:], -0.5), writes=[cbuf])
        op("dve", lambda e: e.tensor_copy(out=maskf_bf[:], in_=maskf), reads=[cbuf], writes=[cbuf])
        op("dve", lambda e: e.tensor_copy(out=maskb_bf[:], in_=maskb), reads=[cbuf], writes=[cbuf])
        with ExitStack() as ps:
            tmpc = sbt(ps, [128, 16], F32, "tmpc")
            tb_ = Buf("tmpc")
            op("act", lambda e: e.activation(out=tmpc[:], in_=cv[:], func=AF.Exp, scale=-1.0), reads=[cbuf], writes=[tb_])
            op("dve", lambda e: e.tensor_scalar(out=tmpc[:], in0=tmpc[:], scalar1=1.0, scalar2=None, op0=ALU.add), reads=[tb_], writes=[tb_])
            op("dve", lambda e: e.reciprocal(out=tmpc[:], in_=tmpc[:]), reads=[tb_], writes=[tb_])
            op("dve", lambda e: e.tensor_tensor(out=cond[:].rearrange("p k j -> p (k j)"), in0=tmpc[:], in1=cv[:], op=ALU.mult),
               reads=[tb_, cbuf], writes=[cbuf])
            stg = [sbt(ps, [128, D], F32, "stg") for _ in range(2)]
            stb = [Buf("stg0"), Buf("stg1")]
            for tt in range(NT):
                s = tt % 2
                dma("sp", stg[s][:], xin[tt * 128:(tt + 1) * 128, :], writes=[stb[s]])
                for half in range(2):
                    bk = (tt * 2 + half) % 4
                    for c4 in range(4):
                        c = half * 4 + c4
                        op("pe", lambda e, c=c, c4=c4, bk=bk, s=s: e.transpose(pb[bk][:, c4 * 128:(c4 + 1) * 128], stg[s][:, c * 128:(c + 1) * 128], ident),
                           reads=[stb[s], cbuf], writes=[pbb[bk]], inc=(c4 == 3))
                    eng = "act" if half == 0 else "dve"
                    if eng == "act":
                        op("act", lambda e, bk=bk, half=half, tt=tt: e.activation(out=hT[:, half * 4:half * 4 + 4, tt * 128:(tt + 1) * 128],
                                                                                 in_=pb[bk][:].rearrange("p (c n) -> p c n", c=4), func=AF.Identity),
                           reads=[pbb[bk]], writes=[hb[tt]])
                    else:
                        op("dve", lambda e, bk=bk, half=half, tt=tt: e.tensor_copy(out=hT[:, half * 4:half * 4 + 4, tt * 128:(tt + 1) * 128],
                                                                                   in_=pb[bk][:].rearrange("p (c n) -> p c n", c=4)),
                           reads=[pbb[bk]], writes=[hb[tt]])
            kb.barrier()

        def rstd_from_psum(ps_ap, out_ap, n, dim, pbuf, obuf):
            op("act", lambda e: e.activation(out=out_ap, in_=ps_ap, func=AF.Ln, scale=1.0 / dim, bias=EPS), reads=[pbuf], writes=[obuf])
            op("act", lambda e: e.activation(out=out_ap, in_=out_ap, func=AF.Exp, scale=-0.5), reads=[obuf], writes=[obuf])

        def ada_parts(L, scope):
            slots = [sbt(scope, [128, 8, 512], BF16, "adaw") for _ in range(3)]
            sbf = [Buf("adaw%d" % i) for i in range(3)]
            awv = ada_w[L].rearrange("(c p) n -> p c n", p=128)
            pm = pb[7][:, 0:96].rearrange("p (m j) -> p m j", j=2)
            msb, amx, afn, mb = modsbs[L % 2], amixs[L % 2], affns[L % 2], modbs[L % 2]

            def a_dma(g):
                s = g % 3
                dma("pool", slots[s][:], awv[:, :, g * 512:(g + 1) * 512], writes=[sbf[s]])

            def a_mm(g):
                s = g % 3
                for jj in range(4):
                    m = g * 4 + jj
                    for k in range(8):
                        op("pe", lambda e: e.matmul(pm[:, m, :], lhsT=slots[s][:, k, jj * 128:(jj + 1) * 128], rhs=cond[:, k, :], start=(k == 0), stop=(k == 7)),
                           reads=[sbf[s], cbuf], writes=[pbb[7]], inc=(k == 7))

            def a_fin():
                base = L * SL
                op("dve", lambda e: e.tensor_tensor(out=msb[:], in0=pm, in1=small[:, base + 16:base + 64].unsqueeze(2).broadcast_to([128, 48, 2]), op=ALU.add),
                   reads=[pbb[7], cbuf], writes=[mb])
                op("dve", lambda e: e.scalar_tensor_tensor(out=amx[:], in0=msb[:, 8:16, :], scalar=1.0,
                                                           in1=small[:, base:base + 8].unsqueeze(2).broadcast_to([128, 8, 2]), op0=ALU.add, op1=ALU.mult),
                   reads=[mb, cbuf], writes=[mb])
                op("dve", lambda e: e.scalar_tensor_tensor(out=afn[:], in0=msb[:, 32:40, :], scalar=1.0,
                                                           in1=small[:, base + 8:base + 16].unsqueeze(2).broadcast_to([128, 8, 2]), op0=ALU.add, op1=ALU.mult),
                   reads=[mb, cbuf], writes=[mb])

            return a_dma, a_mm, a_fin

        def ada_phase(L):
            with ExitStack() as ps:
                a_dma, a_mm, a_fin = ada_parts(L, ps)
                for g in range(12):
                    a_dma(g)
                    a_mm(g)
                a_fin()
                kb.barrier()

        def norm_mod(a_t, shift_off, blocks):
            with ExitStack() as ps:
                sq = [sbt(ps, [128, 8, 512], BF16, "sq") for _ in range(2)]
                sqb = [Buf("sq0"), Buf("sq1")]
                rs = [sbt(ps, [128, 512], F32, "rs") for _ in range(2)]
                rsb = [Buf("rs0"), Buf("rs1")]
                tmp = [sbt(ps, [128, 4, 512], F32, "nt") for _ in range(2)]
                tmb = [Buf("nt0"), Buf("nt1")]
                for bi, (t0, n, j) in enumerate(blocks):
                    s = bi % 2
                    bk = 6 + s
                    bidx = BLOCKS.index((t0, n, j))
                    op("act", lambda e, s=s: e.activation(out=sq[s][:, :, 0:n], in_=hT[:, :, t0:t0 + n], func=AF.Square), reads=hbs(t0, n), writes=[sqb[s]])
                    for c in range(8):
                        op("pe", lambda e, s=s, c=c, bk=bk: e.matmul(pb[bk][:, 0:n], lhsT=ones_bf[:], rhs=sq[s][:, c, 0:n], start=(c == 0), stop=(c == 7)),
                           reads=[sqb[s], cbuf], writes=[pbb[bk]], inc=(c == 7))
                    rstd_from_psum(pb[bk][:, 0:n], rs[s][:, 0:n], n, float(D), pbb[bk], rsb[s])
                    for half in range(2):
                        op("dve", lambda e, s=s, half=half: e.tensor_tensor(out=tmp[half][:, :, 0:n], in0=hT[:, half * 4:half * 4 + 4, t0:t0 + n],
                                                                            in1=rs[s][:, 0:n].unsqueeze(1).broadcast_to([128, 4, n]), op=ALU.mult),
                           reads=hbs(t0, n) + [rsb[s]], writes=[tmb[half]])
                        for c4 in range(4):
                            c = half * 4 + c4
                            op("act", lambda e, half=half, c4=c4, c=c: e.activation(out=uT[:, c, t0:t0 + n], in_=tmp[half][:, c4, 0:n], func=AF.Identity,
                                                                                    scale=a_t()[:, c, j:j + 1], bias=M.modsb[:, shift_off + c, j:j + 1]),
                               reads=[tmb[half], M.modb], writes=[ub[bidx]])
                kb.barrier()

        def ffn_phase(L, blocks, prefetch=None):
            with ExitStack() as ps:
                pre = ada_parts(prefetch, ps) if prefetch is not None else None
                w1s = [sbt(ps, [128, 8, 512], BF16, "w1s") for _ in range(3)]
                w2s = [sbt(ps, [128, 4, D], BF16, "w2s") for _ in range(3)]
                wb = [Buf("ffw%d" % i) for i in range(3)]
                hid = [sbt(ps, [128, 4, 512], BF16, "hid") for _ in range(2)]
                hib = [Buf("hid0"), Buf("hid1")]
                sqv = [sbt(ps, [128, 512], F32, "sqv") for _ in range(2)]
                sqvb = [Buf("sqv0"), Buf("sqv1")]
                w1v = w1_d[L].rearrange("(c p) n -> p c n", p=128)
                w2v = w2_d[L].rearrange("(c p) n -> p c n", p=128)
                items = [(e8, blk) for e8 in range(8) for blk in blocks][:FFN_ITEMS]
                loaded = set()

                def load(e8):
                    if e8 in loaded or e8 >= 8:
                        return
                    loaded.add(e8)
                    s = e8 % 3
                    dma("pool", w1s[s][:], w1v[:, :, e8 * 512:(e8 + 1) * 512], writes=[wb[s]])
                    dma("pool", w2s[s][:], w2v[:, e8 * 4:(e8 + 1) * 4, :], writes=[wb[s]])

                sqi = [0]

                def stage_h(i):
                    e8, (t0, n, j) = items[i]
                    s = e8 % 3
                    bidx = BLOCKS.index((t0, n, j))
                    for jj in range(4):
                        for k in range(8):
                            op("pe", lambda e, jj=jj, k=k: e.matmul(pb[jj][:, 0:n], lhsT=w1s[s][:, k, jj * 128:(jj + 1) * 128], rhs=uT[:, k, t0:t0 + n],
                                                                    start=(k == 0), stop=(k == 7)),
                               reads=[wb[s], ub[bidx]], writes=[pbb[jj]], inc=(k == 7))
                        q = sqi[0] % 2
                        sqi[0] += 1
                        op("act", lambda e, jj=jj, q=q: e.activation(out=sqv[q][:, 0:n], in_=pb[jj][:, 0:n], func=AF.Square), reads=[pbb[jj]], writes=[sqvb[q]])
                        op("dve", lambda e, jj=jj, q=q: e.scalar_tensor_tensor(out=hid[i % 2][:, jj, 0:n], in0=pb[jj][:, 0:n], scalar=0.0, in1=sqv[q][:, 0:n],
                                                                               op0=ALU.is_gt, op1=ALU.mult),
                           reads=[pbb[jj], sqvb[q]], writes=[hib[i % 2]])

                oi = [0]

                def stage_o(i):
                    e8, (t0, n, j) = items[i]
                    s = e8 % 3
                    for f in range(8):
                        bk = 4 + oi[0] % 3
                        oi[0] += 1
                        for jj in range(4):
                            op("pe", lambda e, jj=jj, f=f, bk=bk: e.matmul(pb[bk][:, 0:n], lhsT=w2s[s][:, jj, f * 128:(f + 1) * 128], rhs=hid[i % 2][:, jj, 0:n],
                                                                           start=(jj == 0), stop=(jj == 3)),
                               reads=[wb[s], hib[i % 2]], writes=[pbb[bk]], inc=(jj == 3))
                        op("dve", lambda e, f=f, bk=bk: e.scalar_tensor_tensor(out=hT[:, f, t0:t0 + n], in0=pb[bk][:, 0:n], scalar=M.modsb[:, 40 + f, j:j + 1],
                                                                               in1=hT[:, f, t0:t0 + n], op0=ALU.mult, op1=ALU.add),
                           reads=[pbb[bk], M.modb] + hbs(t0, n), writes=hbs(t0, n))

                load(0)
                load(1)
                stage_h(0)
                for i in range(len(items)):
                    if pre is not None:
                        if i % 3 == 0 and i // 3 < 12:
                            pre[0](i // 3)
                        if i >= 6 and (i - 6) % 3 == 0 and (i - 6) // 3 < 12:
                            pre[1]((i - 6) // 3)
                    if i + 1 < len(items):
                        load(items[i + 1][0] + 1)
                        stage_h(i + 1)
                    stage_o(i)
                if pre is not None:
                    pre[2]()
                kb.barrier()

        def outproj_acc(pbank, lhs_list, rhs_list, t0, n, j, reads, stop_bufs=None):
            for f in range(8):
                bk = pbank[f % len(pbank)]
                nmm = len(lhs_list)
                for i in range(nmm):
                    op("pe", lambda e, i=i, f=f, bk=bk: e.matmul(pb[bk][:, 0:n], lhsT=lhs_list[i][:, f * 128:(f + 1) * 128], rhs=rhs_list[i],
                                                                 start=(i == 0), stop=(i == nmm - 1)),
                       reads=reads, writes=[pbb[bk]], inc=(i == nmm - 1))
                op("dve", lambda e, f=f, bk=bk: e.scalar_tensor_tensor(out=hT[:, f, t0:t0 + n], in0=pb[bk][:, 0:n], scalar=M.modsb[:, 16 + f, j:j + 1],
                                                                       in1=hT[:, f, t0:t0 + n], op0=ALU.mult, op1=ALU.add),
                   reads=[pbb[bk], M.modb] + hbs(t0, n), writes=hbs(t0, n))

        def mlstm_phase(L, slot, need_ctx):
            out_blocks = BLOCKS if need_ctx else BLOCKS[1:]
            with ExitStack() as ps:
                G = sbt(ps, [128, NT, 32], F32, "G")
                SP = sbt(ps, [128, NT, 16], F32, "SP")
                LI = sbt(ps, [128, NT, 16], F32, "LI")
                KS = sbt(ps, [128, NT, 16], F32, "KS")
                KSb = sbt(ps, [128, NT, 16], BF16, "KSb")
                EBH = sbt(ps, [128, NT, 16], F32, "EBH")
                EBL = sbt(ps, [128, NT, 16], F32, "EBL")
                gbuf = Buf("gates")
                wg = sbt(ps, [128, 8, 32], BF16, "wg")
                wgb = Buf("wg")
                wmv = wm_d[slot].rearrange("(c p) n -> p c n", p=128)
                dma("pool", wg[:], wmv[:, :, 3584:3616], writes=[wgb])
                gbias = small[:, S_MGB + slot * 32:S_MGB + slot * 32 + 32]
                for tt in range(NT):
                    bk = 0 if tt < 16 else 1
                    o = (tt % 16) * 32
                    for k in range(8):
                        op("pe", lambda e, tt=tt, k=k, bk=bk, o=o: e.matmul(pb[bk][:, o:o + 32], lhsT=uT[:, k, tt * 128:(tt + 1) * 128], rhs=wg[:, k, :],
                                                                            start=(k == 0), stop=(k == 7)),
                           reads=[wgb] + ub, writes=[pbb[bk]], inc=(k == 7))
                op("dve", lambda e: e.tensor_tensor(out=G[:, 0:16, :], in0=pb[0][:].rearrange("p (t g) -> p t g", g=32),
                                                    in1=gbias.unsqueeze(1).broadcast_to([128, 16, 32]), op=ALU.add), reads=[pbb[0], cbuf], writes=[gbuf])
                op("dve", lambda e: e.tensor_tensor(out=G[:, 16:18, :], in0=pb[1][:, 0:64].rearrange("p (t g) -> p t g", g=32),
                                                    in1=gbias.unsqueeze(1).broadcast_to([128, 2, 32]), op=ALU.add), reads=[pbb[1], cbuf], writes=[gbuf])
                for d in range(2):
                    op("dve", lambda e, d=d: e.tensor_copy(out=LI[:, :, d * 8:d * 8 + 8], in_=G[:, :, d * 16:d * 16 + 8]), reads=[gbuf], writes=[gbuf])
                    op("act", lambda e, d=d: e.activation(out=SP[:, :, d * 8:d * 8 + 8], in_=G[:, :, d * 16 + 8:d * 16 + 16], func=AF.Exp, scale=-1.0),
                       reads=[gbuf], writes=[gbuf])
                op("act", lambda e: e.activation(out=SP[:], in_=SP[:], func=AF.Ln, scale=1.0, bias=1.0), reads=[gbuf], writes=[gbuf])
                for tt in range(NT):
                    bk = 2 if tt < 16 else 3
                    o = (tt % 16) * 32
                    op("pe", lambda e, tt=tt, bk=bk, o=o: e.matmul(pb[bk][:, o:o + 8], lhsT=ntricf, rhs=SP[:, tt, 0:8], start=True, stop=True),
                       reads=[gbuf, cbuf], writes=[pbb[bk]], inc=False)
                    op("pe", lambda e, tt=tt, bk=bk, o=o: e.matmul(pb[bk][:, o + 8:o + 16], lhsT=ntricb, rhs=SP[:, tt, 8:16], start=True, stop=True),
                       reads=[gbuf, cbuf], writes=[pbb[bk]], inc=False)
                    op("pe", lambda e, tt=tt, bk=bk, o=o: e.matmul(pb[bk][:, o + 16:o + 32], lhsT=nhalf[:], rhs=SP[:, tt, :], start=True, stop=True),
                       reads=[gbuf, cbuf], writes=[pbb[bk]], inc=True)
                for (bk, a, b_) in ((2, 0, 16), (3, 16, 18)):
                    nn = b_ - a
                    pv = pb[bk][:, 0:nn * 32].rearrange("p (t g) -> p t g", g=32)
                    op("dve", lambda e, pv=pv, a=a, b_=b_: e.tensor_tensor(out=KS[:, a:b_, :], in0=LI[:, a:b_, :], in1=pv[:, :, 0:16], op=ALU.subtract),
                       reads=[pbb[bk], gbuf], writes=[gbuf])
                    op("act", lambda e, pv=pv, a=a, b_=b_: e.activation(out=EBH[:, a:b_, :], in_=pv[:, :, 16:32], func=AF.Exp), reads=[pbb[bk]], writes=[gbuf])
                    op("act", lambda e, pv=pv, a=a, b_=b_: e.activation(out=EBL[:, a:b_, :], in_=pv[:, :, 16:32], func=AF.Exp, scale=2.0), reads=[pbb[bk]], writes=[gbuf])
                op("act", lambda e: e.activation(out=KS[:], in_=KS[:], func=AF.Exp), reads=[gbuf], writes=[gbuf])
                op("dve", lambda e: e.tensor_copy(out=KSb[:], in_=KS[:]), reads=[gbuf], writes=[gbuf])

                wh = [sbt(ps, [128, 8, 448], BF16, "wh") for _ in range(2)]
                whb = [Buf("wh0"), Buf("wh1")]
                wo = [sbt(ps, [128, D], BF16, "wo") for _ in range(2)]
                wob = [Buf("wo0"), Buf("wo1")]
                Qb = [sbt(ps, [64, T], BF16, "Qb") for _ in range(2)]
                qbb = [Buf("Qbf"), Buf("Qbb")]
                KT = sbt(ps, [64, T], BF16, "KT")
                ktb = Buf("KT")
                Ktok = sbt(ps, [128, NT, 64], BF16, "Ktok")
                kkb = Buf("Ktok")
                Va = [sbt(ps, [128, NT, 128], BF16, "Va") for _ in range(2)]
                vab = [Buf("Vaf"), Buf("Vab")]
                HSd = [sbt(ps, [128, T], F32, "HSf"), sbt(ps, [128, T], F32, "HSb")]
                HS = HSd[0]
                hsd = [[Buf("hs%d_%d" % (d_, i)) for i in range(NT)] for d_ in range(2)]
                hsb = hsd[0]
                Cst = [sbt(ps, [64, 256], F32, "Cst") for _ in range(2)]
                csb = [Buf("Cf"), Buf("Cb")]
                Cbf = [[sbt(ps, [64, 256], BF16, "Cbf") for _ in range(2)] for _ in range(2)]
                cbb = [[Buf("Cbf%d_%d" % (d_, i)) for i in range(2)] for d_ in range(2)]
                ctmp = [sbt(ps, [64, 256], F32, "ctmp") for _ in range(2)]
                ctb = [Buf("ct0"), Buf("ct1")]
                ebt = [sbt(ps, [64, 512], F32, "ebt") for _ in range(2)]
                ebb = [Buf("eb0"), Buf("eb1")]
                Pm = [sbt(ps, [128, 128], BF16, "Pm") for _ in range(4)]
                pmb = [Buf("Pm%d" % i) for i in range(4)]
                adn = [sbt(ps, [128, 128], F32, "adn") for _ in range(2)]
                adb = [Buf("ad0"), Buf("ad1")]
                htm = [sbt(ps, [128, 128], F32, "htm") for _ in range(2)]
                htb = [Buf("ht0"), Buf("ht1")]
                sqh = sbt(ps, [128, 512], BF16, "sqh")
                sqhb = Buf("sqh")
                rsh = sbt(ps, [128, 512], F32, "rsh")
                rshb = Buf("rsh")
                sig = sbt(ps, [128, 512], F32, "sig")
                sigb = Buf("sig")
                hn = sbt(ps, [128, 512], F32, "hn")
                hnb = Buf("hn")
                ao = [sbt(ps, [128, 512], BF16, "ao") for _ in range(2)]
                aob = [Buf("ao0"), Buf("ao1")]
                wov = wmo_d[slot]

                def load_head(h):
                    s = h % 2
                    dma("pool", wh[s][:], wmv[:, :, h * 448:(h + 1) * 448], writes=[whb[s]])
                    dma("pool", wo[s][:], wov[h * 128:(h + 1) * 128, :], writes=[wob[s]])

                order = [list(range(NT)), [1, 0] + list(range(NT - 1, 1, -1))]
                load_head(0)
                for h in range(8):
                    s = h % 2
                    if h + 1 < 8:
                        load_head(h + 1)
                    w = wh[s]
                    for (t0, n, j) in BLOCKS:
                        bidx = BLOCKS.index((t0, n, j))
                        for d in range(2):
                            ntr = ntricf if d == 0 else ntricb
                            for ti in range(n // 128):
                                tt = t0 // 128 + ti
                                op("pe", lambda e, d=d, tt=tt, ti=ti, ntr=ntr: e.matmul(pb[d][0:64, ti * 128:(ti + 1) * 128],
                                                                                         lhsT=SP[:, tt, d * 8 + h:d * 8 + h + 1].broadcast_to([128, 64]), rhs=ntr,
                                                                                         start=True, stop=True),
                                   reads=[gbuf, cbuf], writes=[pbb[d]], inc=(ti == n // 128 - 1))
                            op("act", lambda e, d=d: e.activation(out=ebt[d][:, 0:n], in_=pb[d][0:64, 0:n], func=AF.Exp), reads=[pbb[d]], writes=[ebb[d]])
                        for k in range(8):
                            op("pe", lambda e, k=k: e.matmul(pb[2][0:64, 0:n], lhsT=w[:, k, 0:64], rhs=uT[:, k, t0:t0 + n], start=(k == 0), stop=(k == 7)),
                               reads=[whb[s], ub[bidx]], writes=[pbb[2]], inc=(k == 7))
                        for d in range(2):
                            op("dve", lambda e, d=d: e.tensor_tensor(out=Qb[d][:, t0:t0 + n], in0=pb[2][0:64, 0:n], in1=ebt[d][:, 0:n], op=ALU.mult),
                               reads=[pbb[2], ebb[d]], writes=[qbb[d]])
                        for k in range(8):
                            op("pe", lambda e, k=k: e.matmul(pb[3][0:64, 0:n], lhsT=w[:, k, 64:128], rhs=uT[:, k, t0:t0 + n], start=(k == 0), stop=(k == 7)),
                               reads=[whb[s], ub[bidx]], writes=[pbb[3]], inc=(k == 7))
                        op("act", lambda e: e.activation(out=KT[:, t0:t0 + n], in_=pb[3][0:64, 0:n], func=AF.Identity, scale=0.125), reads=[pbb[3]], writes=[ktb])
                    for tt in range(NT):
                        bk = 4 + tt % 2
                        bidx = 0 if tt < 2 else 1 + (tt - 2) // 4
                        for k in range(8):
                            op("pe", lambda e, k=k, tt=tt, bk=bk: e.matmul(pb[bk][:, 0:192], lhsT=uT[:, k, tt * 128:(tt + 1) * 128], rhs=w[:, k, 128:320],
                                                                           start=(k == 0), stop=(k == 7)),
                               reads=[whb[s], ub[bidx]], writes=[pbb[bk]], inc=(k == 7))
                        op("act", lambda e, tt=tt, bk=bk: e.activation(out=Ktok[:, tt, :], in_=pb[bk][:, 0:64], func=AF.Identity, scale=0.125), reads=[pbb[bk]], writes=[kkb])
                        for d in range(2):
                            op("dve", lambda e, tt=tt, bk=bk, d=d: e.tensor_scalar(out=Va[d][:, tt, :], in0=pb[bk][:, 64:192], scalar1=KS[:, tt, d * 8 + h:d * 8 + h + 1],
                                                                                    scalar2=None, op0=ALU.mult),
                               reads=[pbb[bk], gbuf], writes=[vab[d]])
                    for d in range(2):
                        op("dve", lambda e, d=d: e.memset(Cst[d][:], 0.0), writes=[csb[d]])
                    def unit(step, d):
                        tt = order[d][step]
                        return tt, d * 8 + h, slice(tt * 128, (tt + 1) * 128), (need_ctx or tt >= 2)

                    def s1(step):
                        cq = step % 2
                        for d in range(2):
                            tt, col, tsl, outp = unit(step, d)
                            if outp and step > 0:
                                op("dve", lambda e: e.tensor_scalar(out=Cbf[d][cq][:], in0=Cst[d][:], scalar1=EBH[0:64, tt, col:col + 1], scalar2=None, op0=ALU.mult),
                                   reads=[csb[d], gbuf], writes=[cbb[d][cq]])
                        for d in range(2):
                            tt, col, tsl, outp = unit(step, d)
                            if outp:
                                op("pe", lambda e: e.matmul(pb[d][:, 0:128], lhsT=KT[:, tsl], rhs=Qb[d][:, tsl], start=True, stop=True),
                                   reads=[ktb, qbb[d]], writes=[pbb[d]])
                        for d in range(2):
                            tt, col, tsl, outp = unit(step, d)
                            if outp:
                                p_ = (step % 2) * 2 + d
                                mk = maskf_bf if d == 0 else maskb_bf
                                op("dve", lambda e: e.tensor_tensor(out=Pm[p_][:], in0=pb[d][:, 0:128], in1=mk[:], op=ALU.mult),
                                   reads=[pbb[d], cbuf], writes=[pmb[p_]])

                    def s3(step):
                        if step >= NT - 1:
                            return
                        for d in range(2):
                            tt, col, tsl, outp = unit(step, d)
                            ksl = KSb[:, tt, col:col + 1].broadcast_to([128, 128])
                            bK = 6 + d
                            op("pe", lambda e: e.matmul(pb[bK][0:64, 0:128], lhsT=Ktok[:, tt, :], rhs=Va[d][:, tt, :], start=True, stop=True),
                               reads=[kkb, vab[d]], writes=[pbb[bK]], inc=False)
                            op("pe", lambda e: e.matmul(pb[bK][0:64, 128:256], lhsT=Ktok[:, tt, :], rhs=ksl, start=True, stop=True),
                               reads=[kkb, gbuf], writes=[pbb[bK]])
                        for d in range(2):
                            tt, col, tsl, outp = unit(step, d)
                            bK = 6 + d
                            op("dve", lambda e: e.tensor_scalar(out=ctmp[d][:], in0=pb[bK][0:64, 0:256], scalar1=EBH[0:64, tt, col:col + 1], scalar2=None, op0=ALU.mult),
                               reads=[pbb[bK], gbuf], writes=[ctb[d]])
                        for d in range(2):
                            tt, col, tsl, outp = unit(step, d)
                            op("dve", lambda e: e.scalar_tensor_tensor(out=Cst[d][:], in0=Cst[d][:], scalar=EBL[0:64, tt, col:col + 1], in1=ctmp[d][:], op0=ALU.mult, op1=ALU.add),
                               reads=[ctb[d], gbuf, csb[d]], writes=[csb[d]])

                    def s2(step):
                        cq = step % 2
                        last = (step == 0)
                        act = [d for d in range(2) if unit(step, d)[3]]
                        for d in act:
                            tt, col, tsl, outp = unit(step, d)
                            ksl = KSb[:, tt, col:col + 1].broadcast_to([128, 128])
                            bN, bD, p_ = 2 + d, 4 + d, (step % 2) * 2 + d
                            op("pe", lambda e: e.matmul(pb[bN][:, 0:128], lhsT=Va[d][:, tt, :], rhs=Pm[p_][:], start=True, stop=last),
                               reads=[vab[d], pmb[p_]], writes=[pbb[bN]], inc=last)
                            if not last:
                                op("pe", lambda e: e.matmul(pb[bN][:, 0:128], lhsT=Cbf[d][cq][:, 0:128], rhs=Qb[d][:, tsl], start=False, stop=True),
                                   reads=[cbb[d][cq], qbb[d]], writes=[pbb[bN]])
                            op("pe", lambda e: e.matmul(pb[bD][:, 0:128], lhsT=ksl, rhs=Pm[p_][:], start=True, stop=last),
                               reads=[gbuf, pmb[p_]], writes=[pbb[bD]], inc=last)
                            if not last:
                                op("pe", lambda e: e.matmul(pb[bD][:, 0:128], lhsT=Cbf[d][cq][:, 128:256], rhs=Qb[d][:, tsl], start=False, stop=True),
                                   reads=[cbb[d][cq], qbb[d]], writes=[pbb[bD]])
                        for d in act:
                            op("act", lambda e: e.activation(out=adn[d][:], in_=pb[4 + d][:, 0:128], func=AF.Abs), reads=[pbb[4 + d]], writes=[adb[d]])
                        for d in act:
                            op("dve", lambda e: e.tensor_scalar(out=adn[d][:], in0=adn[d][:], scalar1=1.0, scalar2=None, op0=ALU.max), reads=[adb[d]], writes=[adb[d]])
                        for d in act:
                            op("dve", lambda e: e.reciprocal(out=adn[d][:], in_=adn[d][:]), reads=[adb[d]], writes=[adb[d]])
                        for d in act:
                            tt, col, tsl, outp = unit(step, d)
                            op("dve", lambda e: e.tensor_tensor(out=HSd[d][:, tsl], in0=pb[2 + d][:, 0:128], in1=adn[d][:], op=ALU.mult),
                               reads=[pbb[2 + d], adb[d]], writes=[hsd[d][tt]])

                    s1(0)
                    for step in range(NT):
                        s3(step)
                        if step + 1 < NT:
                            s1(step + 1)
                        s2(step)
                    hnw = small[:, S_MHN + slot * 8 + h:S_MHN + slot * 8 + h + 1]
                    for bi, (t0, n, j) in enumerate(out_blocks):
                        bidx = BLOCKS.index((t0, n, j))
                        hsl = [hsb[i] for i in range(t0 // 128, (t0 + n) // 128)]
                        hsl2 = [hsd[1][i] for i in range(t0 // 128, (t0 + n) // 128)]
                        op("dve", lambda e: e.tensor_tensor(out=HS[:, t0:t0 + n], in0=HS[:, t0:t0 + n], in1=HSd[1][:, t0:t0 + n], op=ALU.add), reads=hsl + hsl2, writes=hsl)
                        op("act", lambda e: e.activation(out=sqh[:, 0:n], in_=HS[:, t0:t0 + n], func=AF.Square), reads=hsl, writes=[sqhb])
                        op("pe", lambda e: e.matmul(pb[0][:, 0:n], lhsT=ones_bf[:], rhs=sqh[:, 0:n], start=True, stop=True), reads=[sqhb, cbuf], writes=[pbb[0]])
                        rstd_from_psum(pb[0][:, 0:n], rsh[:, 0:n], n, 128.0, pbb[0], rshb)
                        for k in range(8):
                            op("pe", lambda e, k=k: e.matmul(pb[1][:, 0:n], lhsT=w[:, k, 320:448], rhs=uT[:, k, t0:t0 + n], start=(k == 0), stop=(k == 7)),
                               reads=[whb[s], ub[bidx]], writes=[pbb[1]], inc=(k == 7))
                        op("act", lambda e: e.activation(out=sig[:, 0:n], in_=pb[1][:, 0:n], func=AF.Exp, scale=-1.0), reads=[pbb[1]], writes=[sigb])
                        op("dve", lambda e: e.tensor_scalar(out=sig[:, 0:n], in0=sig[:, 0:n], scalar1=1.0, scalar2=None, op0=ALU.add), reads=[sigb], writes=[sigb])
                        op("dve", lambda e: e.reciprocal(out=sig[:, 0:n], in_=sig[:, 0:n]), reads=[sigb], writes=[sigb])
                        op("dve", lambda e: e.scalar_tensor_tensor(out=hn[:, 0:n], in0=HS[:, t0:t0 + n], scalar=hnw, in1=rsh[:, 0:n], op0=ALU.mult, op1=ALU.mult),
                           reads=hsl + [rshb, cbuf], writes=[hnb])
                        a_ = bi % 2
                        op("dve", lambda e, a_=a_: e.tensor_tensor(out=ao[a_][:, 0:n], in0=hn[:, 0:n], in1=sig[:, 0:n], op=ALU.mult), reads=[hnb, sigb], writes=[aob[a_]])
                        outproj_acc([6, 7], [wo[s]], [ao[a_][:, 0:n]], t0, n, j, [wob[s], aob[a_]])
                kb.barrier()

        def proj_rope(ps_res, w, c0, cr0, dst_ap_fn, dbuf, wbuf, banks, tmps, tmpb, rope_tiles, scale_ctx_copy=True):
            for (t0, n, j) in BLOCKS:
                bidx = BLOCKS.index((t0, n, j))
                b0, b1 = banks
                for k in range(8):
                    op("pe", lambda e, k=k: e.matmul(pb[b0][:, 0:n], lhsT=w[:, k, c0:c0 + 128], rhs=uT[:, k, t0:t0 + n], start=(k == 0), stop=(k == 7)),
                       reads=[wbuf, ub[bidx]], writes=[pbb[b0]], inc=(k == 7))
                if j == 1:
                    op("act", lambda e: e.activation(out=dst_ap_fn(t0, n), in_=pb[b0][:, 0:n], func=AF.Identity), reads=[pbb[b0]], writes=[dbuf])
                    continue
                for k in range(8):
                    op("pe", lambda e, k=k: e.matmul(pb[b1][:, 0:n], lhsT=w[:, k, cr0:cr0 + 128], rhs=uT[:, k, t0:t0 + n], start=(k == 0), stop=(k == 7)),
                       reads=[wbuf, ub[bidx]], writes=[pbb[b1]], inc=(k == 7))
                rc, rs_, rb = rope_tiles(t0)
                op("dve", lambda e: e.tensor_tensor(out=tmps[0][:, 0:n], in0=pb[b0][:, 0:n], in1=rc, op=ALU.mult), reads=[pbb[b0], rb], writes=[tmpb[0]])
                op("dve", lambda e: e.tensor_tensor(out=tmps[1][:, 0:n], in0=pb[b1][:, 0:n], in1=rs_, op=ALU.mult), reads=[pbb[b1], rb], writes=[tmpb[1]])
                op("dve", lambda e: e.tensor_tensor(out=dst_ap_fn(t0, n), in0=tmps[0][:, 0:n], in1=tmps[1][:, 0:n], op=ALU.add), reads=[tmpb[0], tmpb[1]], writes=[dbuf])

        def load_rope(ps):
            rc = sbt(ps, [128, NLAT], F32, "ropec")
            rs_ = sbt(ps, [128, NLAT], F32, "ropes")
            rb = Buf("rope")
            dma("sp", rc[:], ropec_l.get(), writes=[rb])
            dma("sp", rs_[:], ropes_l.get(), writes=[rb])

            def tiles(t0):
                l0 = t0 - NCTX
                return rc[:, l0:l0 + 512], rs_[:, l0:l0 + 512], rb
            return tiles

        def swa_phase(L, need_ctx):
            with ExitStack() as ps:
                rope_tiles = load_rope(ps)
                wsv = ws_d[0].rearrange("(c p) n -> p c n", p=128)
                wov = wso_d[0].rearrange("(h p) n -> p h n", p=64)
                wg_ = [sbt(ps, [128, 8, 832], BF16, "wsg") for _ in range(1)]
                wgb = [Buf("wsg0"), Buf("wsg1")]
                wo = [sbt(ps, [64, 4, D], BF16, "wso") for _ in range(1)]
                wob = [Buf("wso0"), Buf("wso1")]
                QT = sbt(ps, [128, 2, T], BF16, "QT")
                qtb = Buf("QT")
                KT = sbt(ps, [128, T], BF16, "KT")
                ktb = Buf("KT")
                Va = sbt(ps, [128, NT, 128], BF16, "Va")
                vab = Buf("Va")
                aog = sbt(ps, [64, 4, T], BF16, "aog")
                aob = [Buf("aog%d" % i) for i in range(NT)]
                tmps = [sbt(ps, [128, 512], F32, "rt") for _ in range(2)]
                tmpb = [Buf("rt0"), Buf("rt1")]
                Pt = [sbt(ps, [128, 512], BF16, "Pt") for _ in range(3)]
                ptb = [Buf("Pt%d" % i) for i in range(3)]
                dn = sbt(ps, [128, 512], F32, "dn")
                dnb = Buf("dn")
                ES = sbt(ps, [128, 16], F32, "ES")
                esb = Buf("ES")
                op("act", lambda e: e.activation(out=ES[:], in_=small[:, S_SINK:S_SINK + 16], func=AF.Exp), reads=[cbuf], writes=[esb])
                op("dve", lambda e: e.memset(Va[:, :, 64:128], 1.0), writes=[vab])

                def load_g(g):
                    s = 0
                    dma("pool", wg_[s][:], wsv[:, :, g * 832:(g + 1) * 832], writes=[wgb[s]])
                    dma("pool", wo[s][:], wov[:, g * 4:(g + 1) * 4, :], writes=[wob[s]])

                pti = 0
                for g in range(4):
                    s = 0
                    load_g(g)
                    w = wg_[s]
                    if SWA_STOP < 1:
                        continue
                    proj_rope(ps, w, 512, 640, lambda t0, n: KT[:, t0:t0 + n], ktb, wgb[s], (0, 1), tmps, tmpb, rope_tiles)
                    for jq in range(SWA_NQ):
                        if SWA_SUB < 1:
                            continue
                        proj_rope(ps, w, jq * 128, 256 + jq * 128, lambda t0, n, jq=jq: QT[:, jq, t0:t0 + n], qtb, wgb[s], SWA_QB, tmps, tmpb, rope_tiles)
                    for tt in range(NT):
                        if SWA_SUB < 2:
                            continue
                        bk = 4 + (tt // 8) % 2
                        o = (tt % 8) * 64
                        bidx = 0 if tt < 2 else 1 + (tt - 2) // 4
                        for k in range(8):
                            op("pe", lambda e, k=k, tt=tt, bk=bk, o=o: e.matmul(pb[bk][:, o:o + 64], lhsT=uT[:, k, tt * 128:(tt + 1) * 128], rhs=w[:, k, 768:832],
                                                                                start=(k == 0), stop=(k == 7)),
                               reads=[wgb[s], ub[bidx]], writes=[pbb[bk]], inc=(k == 7))
                        if tt % 8 == 7 or tt == NT - 1:
                            a = (tt // 8) * 8
                            nn = tt + 1 - a
                            op("act", lambda e, a=a, nn=nn, bk=bk: e.activation(out=Va[:, a:a + nn, 0:64], in_=pb[bk][:, 0:nn * 64].rearrange("p (t d) -> p t d", d=64),
                                                                                func=AF.Identity), reads=[pbb[bk]], writes=[vab])
                    aitems = []
                    for qt in range(NT):
                        if SWA_STOP < 2:
                            continue
                        if qt < 2:
                            if not need_ctx:
                                continue
                            kts = [0, 1]
                        else:
                            kts = [0, 1] + [kt for kt in (qt - 1, qt, qt + 1) if 2 <= kt < NT]
                        for ki, kt in enumerate(kts):
                            aitems.append((qt, kt, ki, len(kts)))

                    def s_stage(i):
                        qt, kt, ki, nk = aitems[i]
                        qsl = slice(qt * 128, (qt + 1) * 128)
                        ksl = slice(kt * 128, (kt + 1) * 128)
                        bS = (i % 2) * 2
                        p_ = i % 3
                        op("pe", lambda e: e.matmul(pb[bS][:, 0:256], lhsT=KT[0:64, ksl], rhs=QT[0:64, :, qsl], start=True, stop=True),
                           reads=[ktb, qtb], writes=[pbb[bS]])
                        op("pe", lambda e: e.matmul(pb[bS + 1][:, 0:256], lhsT=KT[64:128, ksl], rhs=QT[64:128, :, qsl], start=True, stop=True),
                           reads=[ktb, qtb], writes=[pbb[bS + 1]])
                        op("act", lambda e: e.activation(out=Pt[p_][:, 0:256], in_=pb[bS][:, 0:256], func=AF.Exp, scale=0.125), reads=[pbb[bS]], writes=[ptb[p_]])
                        op("act", lambda e: e.activation(out=Pt[p_][:, 256:512], in_=pb[bS + 1][:, 0:256], func=AF.Exp, scale=0.125), reads=[pbb[bS + 1]], writes=[ptb[p_]])
                        if qt >= 2 and kt >= 2 and kt != qt:
                            mk = maskb_bf if kt == qt - 1 else maskf_bf
                            op("dve", lambda e: e.tensor_tensor(out=Pt[p_][:].rearrange("p (h q) -> p h q", h=4), in0=Pt[p_][:].rearrange("p (h q) -> p h q", h=4),
                                                                in1=mk[:].unsqueeze(1).broadcast_to([128, 4, 128]), op=ALU.mult),
                               reads=[ptb[p_], cbuf], writes=[ptb[p_]])

                    def pv_stage(i):
                        qt, kt, ki, nk = aitems[i]
                        qsl = slice(qt * 128, (qt + 1) * 128)
                        p_ = i % 3
                        bO = 6 + qt % 2
                        op("pe", lambda e: e.matmul(pb[bO][:], lhsT=Va[:, kt, :], rhs=Pt[p_][:], start=(ki == 0), stop=(ki == nk - 1)),
                           reads=[vab, ptb[p_]], writes=[pbb[bO]], inc=(ki == nk - 1))
                        if ki == nk - 1:
                            op("dve", lambda e: e.tensor_tensor(out=dn[64:128, :].rearrange("p (h q) -> p h q", h=4), in0=pb[bO][64:128, :].rearrange("p (h q) -> p h q", h=4),
                                                                in1=ES[64:128, g * 4:(g + 1) * 4].unsqueeze(2).broadcast_to([64, 4, 128]), op=ALU.add),
                               reads=[pbb[bO], esb], writes=[dnb])
                            op("dve", lambda e: e.reciprocal(out=dn[64:128, :], in_=dn[64:128, :]), reads=[dnb], writes=[dnb])
                            op("dve", lambda e: e.tensor_tensor(out=aog[:, :, qsl], in0=pb[bO][0:64, :].rearrange("p (h q) -> p h q", h=4),
                                                                in1=dn[64:128, :].rearrange("p (h q) -> p h q", h=4), op=ALU.mult),
                               reads=[pbb[bO], dnb], writes=[aob[qt]])

                    if aitems:
                        s_stage(0)
                    for i in range(len(aitems)):
                        if i + 1 < len(aitems):
                            s_stage(i + 1)
                        pv_stage(i)
                    for (t0, n, j) in (BLOCKS if need_ctx else BLOCKS[1:]):
                        if SWA_STOP < 3:
                            continue
                        ab = [aob[i] for i in range(t0 // 128, (t0 + n) // 128)]
                        outproj_acc([0, 1, 2, 3], [wo[s][:, PSORD[pos], :] for pos in range(4)], [aog[:, pos, t0:t0 + n] for pos in range(4)], t0, n, j, [wob[s]] + ab)
                kb.barrier()

        def diff_phase(L, need_ctx):
            lam_init = 0.8 - 0.6 * float(np.exp(-0.3 * L))
            with ExitStack() as ps:
                rope_tiles = load_rope(ps)
                wdv = wd_d[0].rearrange("(c p) n -> p c n", p=128)
                wov = wdo_d[0]
                wh = [sbt(ps, [128, 8, 640], BF16, "wdh") for _ in range(2)]
                whb = [Buf("wdh0"), Buf("wdh1")]
                wo = [sbt(ps, [128, D], BF16, "wdo") for _ in range(2)]
                wob = [Buf("wdo0"), Buf("wdo1")]
                QT = sbt(ps, [128, T], BF16, "QT")
                qtb = Buf("QT")
                KT = sbt(ps, [128, T], BF16, "KT")
                ktb = Buf("KT")
                Vt = sbt(ps, [128, NT, 128], BF16, "Vt")
                vtb = Buf("Vt")
                tmps = [sbt(ps, [128, 512], F32, "rt") for _ in range(2)]
                tmpb = [Buf("rt0"), Buf("rt1")]
                Pt = [sbt(ps, [128, 512], BF16, "Pt") for _ in range(4)]
                ptb = [Buf("Pt%d" % i) for i in range(4)]
                rr = [sbt(ps, [128, 512], F32, "rr") for _ in range(2)]
                rrb = [Buf("rr0"), Buf("rr1")]
                od = sbt(ps, [128, 512], F32, "od")
                odb = Buf("od")
                sqh = sbt(ps, [128, 512], BF16, "sqh")
                sqhb = Buf("sqh")
                rsh = sbt(ps, [128, 512], F32, "rsh")
                rshb = Buf("rsh")
                ao = [sbt(ps, [128, 512], BF16, "ao") for _ in range(2)]
                aob = [Buf("ao0"), Buf("ao1")]
                lam = sbt(ps, [128, 8], F32, "lam")
                lamb = Buf("lam")
                hnw = sbt(ps, [128, 8], F32, "hnw")
                lv = small[:, S_DLAM:S_DLAM + 256]
                op("dve", lambda e: e.tensor_tensor(out=tmps[0][:, 0:64], in0=lv[:, 0:64], in1=lv[:, 64:128], op=ALU.mult), reads=[cbuf], writes=[tmpb[0]])
                op("dve", lambda e: e.tensor_tensor(out=tmps[0][:, 64:128], in0=lv[:, 128:192], in1=lv[:, 192:256], op=ALU.mult), reads=[cbuf], writes=[tmpb[0]])
                op("dve", lambda e: e.tensor_reduce(out=lam[:, 0:2], in_=tmps[0][:, 0:128].rearrange("p (a b) -> p a b", a=2), axis=mybir.AxisListType.X, op=ALU.add),
                   reads=[tmpb[0]], writes=[lamb])
                op("act", lambda e: e.activation(out=lam[:, 0:2], in_=lam[:, 0:2], func=AF.Exp), reads=[lamb], writes=[lamb])
                op("dve", lambda e: e.tensor_tensor(out=lam[:, 2:3], in0=lam[:, 1:2], in1=lam[:, 0:1], op=ALU.subtract), reads=[lamb], writes=[lamb])
                op("dve", lambda e: e.tensor_scalar(out=lam[:, 3:4], in0=lam[:, 2:3], scalar1=-lam_init, scalar2=None, op0=ALU.add), reads=[lamb], writes=[lamb])
                op("dve", lambda e: e.tensor_scalar(out=hnw[:], in0=small[:, S_DHN:S_DHN + 8], scalar1=(1.0 - lam_init), scalar2=None, op0=ALU.mult), reads=[cbuf], writes=[lamb])
                nlam = lam[:, 3:4]

                def load_h(h):
                    s = h % 2
                    dma("pool", wh[s][:], wdv[:, :, h * 640:(h + 1) * 640], writes=[whb[s]])
                    dma("pool", wo[s][:], wov[h * 128:(h + 1) * 128, :], writes=[wob[s]])

                load_h(0)
                pti = 0
                oi = 0
                for h in range(8):
                    s = h % 2
                    if h + 1 < 8:
                        load_h(h + 1)
                    w = wh[s]
                    proj_rope(ps, w, 256, 384, lambda t0, n: KT[:, t0:t0 + n], ktb, whb[s], (0, 1), tmps, tmpb, rope_tiles)
                    proj_rope(ps, w, 0, 128, lambda t0, n: QT[:, t0:t0 + n], qtb, whb[s], (2, 3), tmps, tmpb, rope_tiles)
                    for tt in range(NT):
                        bk = 4 + (tt // 4) % 2
                        o = (tt % 4) * 128
                        bidx = 0 if tt < 2 else 1 + (tt - 2) // 4
                        for k in range(8):
                            op("pe", lambda e, k=k, tt=tt, bk=bk, o=o: e.matmul(pb[bk][:, o:o + 128], lhsT=uT[:, k, tt * 128:(tt + 1) * 128], rhs=w[:, k, 512:640],
                                                                                start=(k == 0), stop=(k == 7)),
                               reads=[whb[s], ub[bidx]], writes=[pbb[bk]], inc=(k == 7))
                        if tt % 4 == 3 or tt == NT - 1:
                            a = (tt // 4) * 4
                            nn = tt + 1 - a
                            op("act", lambda e, a=a, nn=nn, bk=bk: e.activation(out=Vt[:, a:a + nn, :], in_=pb[bk][:, 0:nn * 128].rearrange("p (t d) -> p t d", d=128),
                                                                                func=AF.Identity), reads=[pbb[bk]], writes=[vtb])
                    for bi, (t0, n, j) in enumerate(BLOCKS if need_ctx else BLOCKS[1:]):
                        kts = [0, 1] if j == 1 else list(range(NT))
                        nk = len(kts)

                        def s_stage(ki):
                            kt = kts[ki]
                            ksl = slice(kt * 128, (kt + 1) * 128)
                            for m in range(2):
                                bS = m * 2 + (ki % 2)
                                p_ = (ki % 2) * 2 + m
                                rsl = slice(m * 64, (m + 1) * 64)
                                op("pe", lambda e, bS=bS, ksl=ksl, rsl=rsl: e.matmul(pb[bS][:, 0:n], lhsT=KT[rsl, ksl], rhs=QT[rsl, t0:t0 + n], start=True, stop=True),
                                   reads=[ktb, qtb], writes=[pbb[bS]])
                            for m in range(2):
                                bS = m * 2 + (ki % 2)
                                p_ = (ki % 2) * 2 + m
                                op("act", lambda e, bS=bS, p_=p_: e.activation(out=Pt[p_][:, 0:n], in_=pb[bS][:, 0:n], func=AF.Exp, scale=0.125), reads=[pbb[bS]], writes=[ptb[p_]])

                        def pv_stage(ki):
                            kt = kts[ki]
                            first = (ki == 0)
                            lastk = (ki == nk - 1)
                            for m in range(2):
                                p_ = (ki % 2) * 2 + m
                                op("pe", lambda e, kt=kt, p_=p_, m=m: e.matmul(pb[4 + m][:, 0:n], lhsT=Vt[:, kt, :], rhs=Pt[p_][:, 0:n], start=first, stop=lastk),
                                   reads=[vtb, ptb[p_]], writes=[pbb[4 + m]], inc=lastk)
                                op("pe", lambda e, p_=p_, m=m: e.matmul(pb[6 + m][:, 0:n], lhsT=ones_bf[:], rhs=Pt[p_][:, 0:n], start=first, stop=lastk),
                                   reads=[cbuf, ptb[p_]], writes=[pbb[6 + m]], inc=lastk)

                        s_stage(0)
                        for ki in range(nk):
                            if ki + 1 < nk:
                                s_stage(ki + 1)
                            pv_stage(ki)
                        for m in range(2):
                            op("dve", lambda e, m=m: e.reciprocal(out=rr[m][:, 0:n], in_=pb[6 + m][:, 0:n]), reads=[pbb[6 + m]], writes=[rrb[m]])
                            op("dve", lambda e, m=m: e.tensor_tensor(out=rr[m][:, 0:n], in0=pb[4 + m][:, 0:n], in1=rr[m][:, 0:n], op=ALU.mult), reads=[pbb[4 + m], rrb[m]], writes=[rrb[m]])
                        op("dve", lambda e: e.scalar_tensor_tensor(out=od[:, 0:n], in0=rr[1][:, 0:n], scalar=nlam, in1=rr[0][:, 0:n], op0=ALU.mult, op1=ALU.add),
                           reads=[rrb[0], rrb[1], lamb], writes=[odb])
                        op("act", lambda e: e.activation(out=sqh[:, 0:n], in_=od[:, 0:n], func=AF.Square), reads=[odb], writes=[sqhb])
                        op("pe", lambda e: e.matmul(pb[0][:, 0:n], lhsT=ones_bf[:], rhs=sqh[:, 0:n], start=True, stop=True), reads=[sqhb, cbuf], writes=[pbb[0]])
                        rstd_from_psum(pb[0][:, 0:n], rsh[:, 0:n], n, 128.0, pbb[0], rshb)
                        a_ = oi % 2
                        oi += 1
                        op("dve", lambda e, a_=a_: e.scalar_tensor_tensor(out=ao[a_][:, 0:n], in0=od[:, 0:n], scalar=hnw[:, h:h + 1], in1=rsh[:, 0:n], op0=ALU.mult, op1=ALU.mult),
                           reads=[odb, rshb, lamb], writes=[aob[a_]])
                        outproj_acc([1, 2, 3], [wo[s]], [ao[a_][:, 0:n]], t0, n, j, [wob[s], aob[a_]])
                kb.barrier()

        def final_phase(do_norm):
            with ExitStack() as ps:
                sq = sbt(ps, [128, 8, 512], BF16, "fsq")
                sqb = Buf("fsq")
                rs = sbt(ps, [128, 512], F32, "frs")
                rsb = Buf("frs")
                tmp = sbt(ps, [128, 8, 512], F32, "ftmp")
                tmb = Buf("ftmp")
                ost = [sbt(ps, [128, D], F32, "ost") for _ in range(2)]
                osb = [Buf("ost0"), Buf("ost1")]
                oi = 0
                for (t0, n, j) in BLOCKS[1:]:
                    if do_norm:
                        op("act", lambda e: e.activation(out=sq[:, :, 0:n], in_=hT[:, :, t0:t0 + n], func=AF.Square), reads=hbs(t0, n), writes=[sqb])
                        for c in range(8):
                            op("pe", lambda e, c=c: e.matmul(pb[7][:, 0:n], lhsT=ones_bf[:], rhs=sq[:, c, 0:n], start=(c == 0), stop=(c == 7)),
                               reads=[sqb, cbuf], writes=[pbb[7]], inc=(c == 7))
                        rstd_from_psum(pb[7][:, 0:n], rs[:, 0:n], n, float(D), pbb[7], rsb)
                        for c in range(8):
                            op("dve", lambda e, c=c: e.scalar_tensor_tensor(out=tmp[:, c, 0:n], in0=hT[:, c, t0:t0 + n], scalar=small[:, S_FINAL + c:S_FINAL + c + 1],
                                                                            in1=rs[:, 0:n], op0=ALU.mult, op1=ALU.mult),
                               reads=hbs(t0, n) + [rsb, cbuf], writes=[tmb])
                    else:
                        op("dve", lambda e: e.tensor_copy(out=tmp[:, :, 0:n], in_=hT[:, :, t0:t0 + n]), reads=hbs(t0, n), writes=[tmb])
                    for ti in range(n // 128):
                        o_ = oi % 2
                        oi += 1
                        for half in range(2):
                            bk = (oi * 2 + half) % 4
                            for c4 in range(4):
                                c = half * 4 + c4
                                op("pe", lambda e, c=c, c4=c4, bk=bk, ti=ti: e.transpose(pb[bk][:, c4 * 128:(c4 + 1) * 128], tmp[:, c, ti * 128:(ti + 1) * 128], ident),
                                   reads=[tmb, cbuf], writes=[pbb[bk]], inc=(c4 == 3))
                            if half == 0:
                                op("act", lambda e, bk=bk, o_=o_, half=half: e.activation(out=ost[o_][:, half * 512:(half + 1) * 512], in_=pb[bk][:], func=AF.Identity),
                                   reads=[pbb[bk]], writes=[osb[o_]])
                            else:
                                op("dve", lambda e, bk=bk, o_=o_, half=half: e.tensor_copy(out=ost[o_][:, half * 512:(half + 1) * 512], in_=pb[bk][:]),
                                   reads=[pbb[bk]], writes=[osb[o_]])
                        r0 = t0 - NCTX + ti * 128
                        dma("sp", out_d[r0:r0 + 128, :], ost[o_][:], reads=[osb[o_]])
                kb.wait_all_dma("sp")

        prefetched = set()
        for li, L in enumerate(layers):
            kind, slot = L % 3, L // 3
            need_ctx = L < DEPTH - 1
            blocks = BLOCKS if need_ctx else BLOCKS[1:]
            set_mod(L)
            if L not in prefetched:
                ada_phase(L)
            if do_mixer:
                norm_mod(lambda: M.amix, 0, BLOCKS)
                if kind == 0:
                    mlstm_phase(L, slot, need_ctx)
                elif kind == 1:
                    swa_phase(L, need_ctx)
                else:
                    diff_phase(L, need_ctx)
            if do_ffn:
                norm_mod(lambda: M.affn, 24, blocks)
                if do_ffn != "norm":
                    nxt = layers[li + 1] if li + 1 < len(layers) else None
                    can = nxt is not None and len(blocks) == len(BLOCKS) and FFN_ITEMS >= 40
                    ffn_phase(L, blocks, prefetch=nxt if can else None)
                    if can:
                        prefetched.add(nxt)
        final_phase(final_norm)
    nc._declared_inputs = list(declared)
    return nc


def _pcol(v):
    return np.ascontiguousarray(np.asarray(v, np.float32).reshape(-1, 128).T)


def _rot_idx(base):
    return list(range(base + 32, base + 64)) + list(range(base, base + 32))


def _consts():
    i = np.arange(128)
    ident = (i[:, None] == i[None, :]).astype(np.float32)
    maskf = (i[:, None] <= i[None, :]).astype(np.float32)
    maskb = (i[:, None] >= i[None, :]).astype(np.float32)
    return np.ascontiguousarray(np.concatenate([ident, maskf, maskb, -(maskf - 0.5), -(maskb - 0.5)], axis=1))


def _rope_tables():
    rows = NLAT // 64
    row = np.repeat(np.arange(rows), 64).astype(np.float32)
    col = np.tile(np.arange(64), rows).astype(np.float32)
    quarter = 16
    inv = (np.float32(10000.0) ** (-np.arange(quarter, dtype=np.float32) / np.float32(quarter))).astype(np.float32)
    ang = np.concatenate([row[:, None] * inv, col[:, None] * inv], axis=-1).astype(np.float32)
    cos = np.cos(ang).astype(np.float32).T
    sin = np.sin(ang).astype(np.float32).T
    c64 = np.concatenate([cos, cos], 0)
    s64 = np.concatenate([-sin, sin], 0)
    return np.ascontiguousarray(np.concatenate([c64, c64], 0)), np.ascontiguousarray(np.concatenate([s64, s64], 0))


def prepare_shared(inp):
    sh = {}
    sh["cst"] = _consts()
    sh["ropec"], sh["ropes"] = _rope_tables()
    sh["ada_w"] = np.ascontiguousarray(inp["ada_w"], dtype=np.float32)
    sh["ffn_w1"] = np.ascontiguousarray(inp["ffn_w1"], dtype=np.float32)
    sh["ffn_w2"] = np.ascontiguousarray(inp["ffn_w2"], dtype=np.float32)
    wm = []
    for s in range(inp["mlstm_w_in"].shape[0]):
        W = inp["mlstm_w_in"][s]
        cols = []
        for h in range(8):
            cols += list(range(h * 64, h * 64 + 64))
            cols += list(range(512 + h * 64, 512 + h * 64 + 64))
            cols += list(range(512 + h * 64, 512 + h * 64 + 64))
            cols += list(range(1024 + h * 128, 1024 + h * 128 + 128))
            cols += list(range(2048 + h * 128, 2048 + h * 128 + 128))
        cols += list(range(3072, 3104))
        wm.append(W[:, cols])
    sh["wm"] = np.ascontiguousarray(np.stack(wm), dtype=np.float32)
    sh["wmo"] = np.ascontiguousarray(inp["mlstm_w_out"], dtype=np.float32)
    W = inp["swa_w_in"][0]
    cols = []
    for g in range(4):
        q = []
        qr = []
        for hh in range(4):
            b = (4 * g + hh) * 64
            q += list(range(b, b + 64))
            qr += _rot_idx(b)
        kb_ = 1024 + g * 64
        k = list(range(kb_, kb_ + 64))
        kr = _rot_idx(kb_)
        v = list(range(1280 + g * 64, 1280 + g * 64 + 64))
        cols += q + qr + k + k + kr + kr + v
    sh["ws"] = np.ascontiguousarray(W[:, cols][None], dtype=np.float32)
    sh["wso"] = np.ascontiguousarray(inp["swa_w_out"], dtype=np.float32)
    W = inp["diff_w_in"][0]
    cols = []
    for h in range(8):
        qb_ = h * 128
        kb_ = 1024 + h * 128
        cols += list(range(qb_, qb_ + 128)) + _rot_idx(qb_) + _rot_idx(qb_ + 64)
        cols += list(range(kb_, kb_ + 128)) + _rot_idx(kb_) + _rot_idx(kb_ + 64)
        cols += list(range(2048 + h * 128, 2048 + h * 128 + 128))
    sh["wd"] = np.ascontiguousarray(W[:, cols][None], dtype=np.float32)
    sh["wdo"] = np.ascontiguousarray(inp["diff_w_out"], dtype=np.float32)
    small = np.zeros((128, NS), np.float32)
    for L in range(DEPTH):
        small[:, L * SL:L * SL + 8] = _pcol(inp["norm_mix"][L])
        small[:, L * SL + 8:L * SL + 16] = _pcol(inp["norm_ffn"][L])
        small[:, L * SL + 16:L * SL + 64] = _pcol(inp["ada_b"][L])
    small[:, S_FINAL:S_FINAL + 8] = _pcol(inp["final_norm"])
    for s in range(inp["mlstm_head_norm"].shape[0]):
        small[:, S_MHN + s * 8:S_MHN + s * 8 + 8] = _pcol(inp["mlstm_head_norm"][s])
        small[:, S_MGB + s * 32:S_MGB + s * 32 + 32] = np.broadcast_to(np.asarray(inp["mlstm_gate_b"][s], np.float32).reshape(1, 32), (128, 32))
    sink = np.asarray(inp["swa_sink"][0], np.float32)
    sord = [4 * g + PSORD[p] for g in range(4) for p in range(4)]
    small[:, S_SINK:S_SINK + 16] = np.broadcast_to(sink[sord][None, :], (128, 16))
    small[:, S_DHN:S_DHN + 8] = _pcol(inp["diff_head_norm"][0])
    lamv = np.concatenate([np.asarray(inp[k][0], np.float32) for k in ("diff_lambda_q1", "diff_lambda_k1", "diff_lambda_q2", "diff_lambda_k2")])
    small[:, S_DLAM:S_DLAM + 256] = np.broadcast_to(lamv[None, :], (128, 256))
    sh["small"] = small
    return sh


def prepare_core(inp, b):
    xin = np.ascontiguousarray(np.concatenate([inp["ctx"][b], inp["x"][b]], axis=0), dtype=np.float32)
    cv = np.zeros((128, 16), np.float32)
    cv[:, 0::2] = _pcol(inp["c"][b])
    cv[:, 1::2] = _pcol(inp["c_ctx"])
    return {"xin": xin, "cv": cv}


_NC_CACHE = {}


def kernel(**inputs):
    inp = {k: np.asarray(v) for k, v in inputs.items()}
    B = inp["x"].shape[0]
    shared = prepare_shared(inp)
    if "full" not in _NC_CACHE:
        _NC_CACHE["full"] = build_program()
    nc = _NC_CACHE["full"]
    in_maps = []
    for b in range(B):
        m = dict(shared)
        m.update(prepare_core(inp, b))
        in_maps.append({k: m[k] for k in nc._declared_inputs})
    res = run_bass_kernel_spmd(nc, in_maps, core_ids=list(range(B)))
    out = np.stack([np.asarray(r["out"], dtype=np.float32) for r in res.results], axis=0)
    return out
```

```python
import numpy as np
from contextlib import ExitStack
import concourse.bass as bass
import concourse.mybir as mybir
from concourse.bass_utils import run_bass_kernel_spmd

F32 = mybir.dt.float32
BF16 = mybir.dt.bfloat16
ALU = mybir.AluOpType
AF = mybir.ActivationFunctionType

D = 1024
NCTX = 256
NLAT = 2048
T = NCTX + NLAT
NT = T // 128
DEPTH = 4
EPS = 1e-6
NDS = 24
BLOCKS = [(0, 256, 1), (256, 512, 0), (768, 512, 0), (1280, 512, 0), (1792, 512, 0)]

SL = 64
S_FINAL = 4 * SL
S_MHN = S_FINAL + 8
S_MGB = S_MHN + 16
S_SINK = S_MGB + 64
S_DHN = S_SINK + 16
S_DLAM = S_DHN + 8
NS = S_DLAM + 256
NCONST = 5 * 128
PSORD = [0, 2, 1, 3]
FFN_ITEMS = 1000
SWA_STOP = 9
SWA_SUB = 9
SWA_QB = (0, 1)
SWA_NQ = 2


class Buf:
    __slots__ = ("name", "w", "r")

    def __init__(self, name):
        self.name = name
        self.w = None
        self.r = {}


class KB:
    def __init__(self, nc, es):
        self.nc = nc
        self.es = es
        self.E = {"pe": nc.tensor, "act": nc.scalar, "dve": nc.vector, "pool": nc.gpsimd, "sp": nc.sync}
        self.sem = {e: es.enter_context(nc.semaphore("s_" + e)) for e in self.E}
        self.cnt = {e: 0 for e in self.E}
        self.seen = {e: {} for e in self.E}
        self.dsem = [es.enter_context(nc.semaphore("d%d" % i)) for i in range(NDS)]
        self.dcnt = [0] * NDS
        self.dnext = {"sp": 0, "pool": 0}

    def _wait(self, eng, deps):
        for (sk, val) in deps:
            if sk == eng and eng == "pe":
                continue
            if self.seen[eng].get(sk, 0) >= val:
                continue
            semh = self.sem[sk] if isinstance(sk, str) else self.dsem[sk]
            self.E[eng].wait_ge(semh, val)
            self.seen[eng][sk] = val

    @staticmethod
    def _deps(reads, writes):
        deps = []
        for b in reads:
            if b.w is not None:
                deps.append(b.w)
        for b in writes:
            if b.w is not None:
                deps.append(b.w)
            deps.extend(b.r.values())
        return deps

    def op(self, eng, fn, reads=(), writes=(), inc=True):
        self._wait(eng, self._deps(reads, writes))
        ins = fn(self.E[eng])
        if inc:
            self.cnt[eng] += 1
            ins.then_inc(self.sem[eng], 1)
            tk = (eng, self.cnt[eng])
        else:
            tk = (eng, self.cnt[eng] + 1)
        for b in reads:
            b.r[eng] = tk
        for b in writes:
            b.w = tk
            b.r = {}

    def dma(self, q, out, in_, reads=(), writes=()):
        half = NDS // 2
        i = self.dnext[q] + (0 if q == "sp" else half)
        self.dnext[q] = (self.dnext[q] + 1) % half
        deps = self._deps(reads, writes)
        if self.dcnt[i] > 0:
            deps.append((i, self.dcnt[i]))
        self._wait(q, deps)
        self.E[q].dma_start(out=out, in_=in_).then_inc(self.dsem[i], 16)
        self.dcnt[i] += 16
        tk = (i, self.dcnt[i])
        for b in reads:
            b.r[("d", i)] = tk
        for b in writes:
            b.w = tk
            b.r = {}

    def wait_all_dma(self, eng):
        self._wait(eng, [(i, self.dcnt[i]) for i in range(NDS) if self.dcnt[i] > 0])

    def barrier(self):
        deps = [(e, self.cnt[e]) for e in self.E if e != "sp" and self.cnt[e] > 0]
        deps += [(i, self.dcnt[i]) for i in range(NDS) if self.dcnt[i] > 0]
        self._wait("sp", deps)
        self.cnt["sp"] += 1
        self.E["sp"].sem_inc(self.sem["sp"], 1)
        for e in self.E:
            if e != "sp":
                self._wait(e, [("sp", self.cnt["sp"])])


def build_program(layers=(0, 1, 2, 3), do_mixer=True, do_ffn=True, final_norm=True, in_ctx=True):
    nc = bass.Bass("TRN2", target_bir_lowering=False)

    declared = []
    only = None if len(layers) == DEPTH and do_mixer and do_ffn else set()

    class _Lazy:
        def __init__(self, name, shape):
            self.name, self.shape, self.ap_ = name, list(shape), None

        def get(self):
            if self.ap_ is None:
                self.ap_ = nc.dram_tensor(self.name, self.shape, F32, kind="ExternalInput").ap()
                declared.append(self.name)
            return self.ap_

        def __getitem__(self, k):
            return self.get()[k]

    def din(name, shape):
        return _Lazy(name, shape)

    xin = din("xin", [T, D]).get()
    cv_d = din("cv", [128, 16]).get()
    small_d = din("small", [128, NS]).get()
    cst_d = din("cst", [128, NCONST]).get()
    ropec_l = din("ropec", [128, NLAT])
    ropes_l = din("ropes", [128, NLAT])
    ada_w = din("ada_w", [DEPTH, D, 6 * D])
    w1_d = din("ffn_w1", [DEPTH, D, 4 * D])
    w2_d = din("ffn_w2", [DEPTH, 4 * D, D])
    wm_d = din("wm", [2, D, 3616])
    wmo_d = din("wmo", [2, D, D])
    ws_d = din("ws", [1, D, 3328])
    wso_d = din("wso", [1, D, D])
    wd_d = din("wd", [1, D, 5120])
    wdo_d = din("wdo", [1, D, D])
    out_d = nc.dram_tensor("out", [NLAT, D], F32, kind="ExternalOutput").ap()

    with ExitStack() as es:
        kb = KB(nc, es)
        op = kb.op
        dma = kb.dma
        cnt = [0]

        def sbt(es_, shape, dt, name=None):
            cnt[0] += 1
            return es_.enter_context(nc.sbuf_tensor("%s_%d" % (name or "t", cnt[0]), list(shape), dt))

        hT = sbt(es, [128, 8, T], F32, "hT")
        uT = sbt(es, [128, 8, T], BF16, "uT")
        hb = [Buf("h%d" % i) for i in range(NT)]
        ub = [Buf("u%d" % i) for i in range(len(BLOCKS))]
        cst = sbt(es, [128, NCONST], F32, "cst")
        small = sbt(es, [128, NS], F32, "small")
        cv = sbt(es, [128, 16], F32, "cv")
        cond = sbt(es, [128, 8, 2], BF16, "cond")
        ones_bf = sbt(es, [128, 128], BF16, "ones")
        maskf_bf = sbt(es, [128, 128], BF16, "maskf")
        maskb_bf = sbt(es, [128, 128], BF16, "maskb")
        nhalf = sbt(es, [128, 128], F32, "nhalf")
        modsbs = [sbt(es, [128, 48, 2], F32, "modsb") for _ in range(2)]
        amixs = [sbt(es, [128, 8, 2], F32, "amix") for _ in range(2)]
        affns = [sbt(es, [128, 8, 2], F32, "affn") for _ in range(2)]
        modbs = [Buf("mod0"), Buf("mod1")]
        cbuf = Buf("consts")

        class M:
            pass

        def set_mod(L):
            M.modsb, M.amix, M.affn, M.modb = modsbs[L % 2], amixs[L % 2], affns[L % 2], modbs[L % 2]
        ident = cst[:, 0:128]
        maskf = cst[:, 128:256]
        maskb = cst[:, 256:384]
        ntricf = cst[:, 384:512]
        ntricb = cst[:, 512:640]
        pb = [es.enter_context(nc.psum_tensor("pb%d" % i, [128, 512], F32)) for i in range(8)]
        pbb = [Buf("pb%d" % i) for i in range(8)]

        def hbs(t0, n):
            return [hb[i] for i in range(t0 // 128, (t0 + n) // 128)]

        dma("sp", cst[:], cst_d, writes=[cbuf])
        dma("sp", small[:], small_d, writes=[cbuf])
        dma("sp", cv[:], cv_d, writes=[cbuf])
        op("dve", lambda e: e.memset(ones_bf[:], 1.0), writes=[cbuf])
        op("dve", lambda e: e.memset(nhalf[:], -0.5), writes=[cbuf])
        op("dve", lambda e: e.tensor_copy(out=maskf_bf[:], in_=maskf), reads=[cbuf], writes=[cbuf])
        op("dve", lambda e: e.tensor_copy(out=maskb_bf[:], in_=maskb), reads=[cbuf], writes=[cbuf])
        with ExitStack() as ps:
            tmpc = sbt(ps, [128, 16], F32, "tmpc")
            tb_ = Buf("tmpc")
            op("act", lambda e: e.activation(out=tmpc[:], in_=cv[:], func=AF.Exp, scale=-1.0), reads=[cbuf], writes=[tb_])
            op("dve", lambda e: e.tensor_scalar(out=tmpc[:], in0=tmpc[:], scalar1=1.0, scalar2=None, op0=ALU.add), reads=[tb_], writes=[tb_])
            op("dve", lambda e: e.reciprocal(out=tmpc[:], in_=tmpc[:]), reads=[tb_], writes=[tb_])
            op("dve", lambda e: e.tensor_tensor(out=cond[:].rearrange("p k j -> p (k j)"), in0=tmpc[:], in1=cv[:], op=ALU.mult),
               reads=[tb_, cbuf], writes=[cbuf])
            stg = [sbt(ps, [128, D], F32, "stg") for _ in range(2)]
            stb = [Buf("stg0"), Buf("stg1")]
            for tt in range(NT):
                s = tt % 2
                dma("sp", stg[s][:], xin[tt * 128:(tt + 1) * 128, :], writes=[stb[s]])
                for half in range(2):
                    bk = (tt * 2 + half) % 4
                    for c4 in range(4):
                        c = half * 4 + c4
                        op("pe", lambda e, c=c, c4=c4, bk=bk, s=s: e.transpose(pb[bk][:, c4 * 128:(c4 + 1) * 128], stg[s][:, c * 128:(c + 1) * 128], ident),
                           reads=[stb[s], cbuf], writes=[pbb[bk]], inc=(c4 == 3))
                    eng = "act" if half == 0 else "dve"
                    if eng == "act":
                        op("act", lambda e, bk=bk, half=half, tt=tt: e.activation(out=hT[:, half * 4:half * 4 + 4, tt * 128:(tt + 1) * 128],
                                                                                 in_=pb[bk][:].rearrange("p (c n) -> p c n", c=4), func=AF.Identity),
                           reads=[pbb[bk]], writes=[hb[tt]])
                    else:
                        op("dve", lambda e, bk=bk, half=half, tt=tt: e.tensor_copy(out=hT[:, half * 4:half * 4 + 4, tt * 128:(tt + 1) * 128],
                                                                                   in_=pb[bk][:].rearrange("p (c n) -> p c n", c=4)),
                           reads=[pbb[bk]], writes=[hb[tt]])
            kb.barrier()

        def rstd_from_psum(ps_ap, out_ap, n, dim, pbuf, obuf):
            op("act", lambda e: e.activation(out=out_ap, in_=ps_ap, func=AF.Ln, scale=1.0 / dim, bias=EPS), reads=[pbuf], writes=[obuf])
            op("act", lambda e: e.activation(out=out_ap, in_=out_ap, func=AF.Exp, scale=-0.5), reads=[obuf], writes=[obuf])

        def ada_parts(L, scope):
            slots = [sbt(scope, [128, 8, 512], BF16, "adaw") for _ in range(3)]
            sbf = [Buf("adaw%d" % i) for i in range(3)]
            awv = ada_w[L].rearrange("(c p) n -> p c n", p=128)
            pm = pb[7][:, 0:96].rearrange("p (m j) -> p m j", j=2)
            msb, amx, afn, mb = modsbs[L % 2], amixs[L % 2], affns[L % 2], modbs[L % 2]

            def a_dma(g):
                s = g % 3
                dma("pool", slots[s][:], awv[:, :, g * 512:(g + 1) * 512], writes=[sbf[s]])

            def a_mm(g):
                s = g % 3
                for jj in range(4):
                    m = g * 4 + jj
                    for k in range(8):
                        op("pe", lambda e: e.matmul(pm[:, m, :], lhsT=slots[s][:, k, jj * 128:(jj + 1) * 128], rhs=cond[:, k, :], start=(k == 0), stop=(k == 7)),
                           reads=[sbf[s], cbuf], writes=[pbb[7]], inc=(k == 7))

            def a_fin():
                base = L * SL
                op("dve", lambda e: e.tensor_tensor(out=msb[:], in0=pm, in1=small[:, base + 16:base + 64].unsqueeze(2).broadcast_to([128, 48, 2]), op=ALU.add),
                   reads=[pbb[7], cbuf], writes=[mb])
                op("dve", lambda e: e.scalar_tensor_tensor(out=amx[:], in0=msb[:, 8:16, :], scalar=1.0,
                                                           in1=small[:, base:base + 8].unsqueeze(2).broadcast_to([128, 8, 2]), op0=ALU.add, op1=ALU.mult),
                   reads=[mb, cbuf], writes=[mb])
                op("dve", lambda e: e.scalar_tensor_tensor(out=afn[:], in0=msb[:, 32:40, :], scalar=1.0,
                                                           in1=small[:, base + 8:base + 16].unsqueeze(2).broadcast_to([128, 8, 2]), op0=ALU.add, op1=ALU.mult),
                   reads=[mb, cbuf], writes=[mb])

            return a_dma, a_mm, a_fin

        def ada_phase(L):
            with ExitStack() as ps:
                a_dma, a_mm, a_fin = ada_parts(L, ps)
                for g in range(12):
                    a_dma(g)
                    a_mm(g)
                a_fin()
                kb.barrier()

        def norm_mod(a_t, shift_off, blocks):
            with ExitStack() as ps:
                sq = [sbt(ps, [128, 8, 512], BF16, "sq") for _ in range(2)]
                sqb = [Buf("sq0"), Buf("sq1")]
                rs = [sbt(ps, [128, 512], F32, "rs") for _ in range(2)]
                rsb = [Buf("rs0"), Buf("rs1")]
                tmp = [sbt(ps, [128, 4, 512], F32, "nt") for _ in range(2)]
                tmb = [Buf("nt0"), Buf("nt1")]
                for bi, (t0, n, j) in enumerate(blocks):
                    s = bi % 2
                    bk = 6 + s
                    bidx = BLOCKS.index((t0, n, j))
                    op("act", lambda e, s=s: e.activation(out=sq[s][:, :, 0:n], in_=hT[:, :, t0:t0 + n], func=AF.Square), reads=hbs(t0, n), writes=[sqb[s]])
                    for c in range(8):
                        op("pe", lambda e, s=s, c=c, bk=bk: e.matmul(pb[bk][:, 0:n], lhsT=ones_bf[:], rhs=sq[s][:, c, 0:n], start=(c == 0), stop=(c == 7)),
                           reads=[sqb[s], cbuf], writes=[pbb[bk]], inc=(c == 7))
                    rstd_from_psum(pb[bk][:, 0:n], rs[s][:, 0:n], n, float(D), pbb[bk], rsb[s])
                    for half in range(2):
                        op("dve", lambda e, s=s, half=half: e.tensor_tensor(out=tmp[half][:, :, 0:n], in0=hT[:, half * 4:half * 4 + 4, t0:t0 + n],
                                                                            in1=rs[s][:, 0:n].unsqueeze(1).broadcast_to([128, 4, n]), op=ALU.mult),
                           reads=hbs(t0, n) + [rsb[s]], writes=[tmb[half]])
                        for c4 in range(4):
                            c = half * 4 + c4
                            op("act", lambda e, half=half, c4=c4, c=c: e.activation(out=uT[:, c, t0:t0 + n], in_=tmp[half][:, c4, 0:n], func=AF.Identity,
                                                                                    scale=a_t()[:, c, j:j + 1], bias=M.modsb[:, shift_off + c, j:j + 1]),
                               reads=[tmb[half], M.modb], writes=[ub[bidx]])
                kb.barrier()

        def ffn_phase(L, blocks, prefetch=None):
            with ExitStack() as ps:
                pre = ada_parts(prefetch, ps) if prefetch is not None else None
                w1s = [sbt(ps, [128, 8, 512], BF16, "w1s") for _ in range(3)]
                w2s = [sbt(ps, [128, 4, D], BF16, "w2s") for _ in range(3)]
                wb = [Buf("ffw%d" % i) for i in range(3)]
                hid = [sbt(ps, [128, 4, 512], BF16, "hid") for _ in range(2)]
                hib = [Buf("hid0"), Buf("hid1")]
                sqv = [sbt(ps, [128, 512], F32, "sqv") for _ in range(2)]
                sqvb = [Buf("sqv0"), Buf("sqv1")]
                w1v = w1_d[L].rearrange("(c p) n -> p c n", p=128)
                w2v = w2_d[L].rearrange("(c p) n -> p c n", p=128)
                items = [(e8, blk) for e8 in range(8) for blk in blocks][:FFN_ITEMS]
                loaded = set()

                def load(e8):
                    if e8 in loaded or e8 >= 8:
                        return
                    loaded.add(e8)
                    s = e8 % 3
                    dma("pool", w1s[s][:], w1v[:, :, e8 * 512:(e8 + 1) * 512], writes=[wb[s]])
                    dma("pool", w2s[s][:], w2v[:, e8 * 4:(e8 + 1) * 4, :], writes=[wb[s]])

                sqi = [0]

                def stage_h(i):
                    e8, (t0, n, j) = items[i]
                    s = e8 % 3
                    bidx = BLOCKS.index((t0, n, j))
                    for jj in range(4):
                        for k in range(8):
                            op("pe", lambda e, jj=jj, k=k: e.matmul(pb[jj][:, 0:n], lhsT=w1s[s][:, k, jj * 128:(jj + 1) * 128], rhs=uT[:, k, t0:t0 + n],
                                                                    start=(k == 0), stop=(k == 7)),
                               reads=[wb[s], ub[bidx]], writes=[pbb[jj]], inc=(k == 7))
                        q = sqi[0] % 2
                        sqi[0] += 1
                        op("act", lambda e, jj=jj, q=q: e.activation(out=sqv[q][:, 0:n], in_=pb[jj][:, 0:n], func=AF.Square), reads=[pbb[jj]], writes=[sqvb[q]])
                        op("dve", lambda e, jj=jj, q=q: e.scalar_tensor_tensor(out=hid[i % 2][:, jj, 0:n], in0=pb[jj][:, 0:n], scalar=0.0, in1=sqv[q][:, 0:n],
                                                                               op0=ALU.is_gt, op1=ALU.mult),
                           reads=[pbb[jj], sqvb[q]], writes=[hib[i % 2]])

                oi = [0]

                def stage_o(i):
                    e8, (t0, n, j) = items[i]
                    s = e8 % 3
                    for f in range(8):
                        bk = 4 + oi[0] % 3
                        oi[0] += 1
                        for jj in range(4):
                            op("pe", lambda e, jj=jj, f=f, bk=bk: e.matmul(pb[bk][:, 0:n], lhsT=w2s[s][:, jj, f * 128:(f + 1) * 128], rhs=hid[i % 2][:, jj, 0:n],
                                                                           start=(jj == 0), stop=(jj == 3)),
                               reads=[wb[s], hib[i % 2]], writes=[pbb[bk]], inc=(jj == 3))
                        op("dve", lambda e, f=f, bk=bk: e.scalar_tensor_tensor(out=hT[:, f, t0:t0 + n], in0=pb[bk][:, 0:n], scalar=M.modsb[:, 40 + f, j:j + 1],
                                                                               in1=hT[:, f, t0:t0 + n], op0=ALU.mult, op1=ALU.add),
                           reads=[pbb[bk], M.modb] + hbs(t0, n), writes=hbs(t0, n))

                load(0)
                load(1)
                stage_h(0)
                for i in range(len(items)):
                    if pre is not None:
                        if i % 3 == 0 and i // 3 < 12:
                            pre[0](i // 3)
                        if i >= 6 and (i - 6) % 3 == 0 and (i - 6) // 3 < 12:
                            pre[1]((i - 6) // 3)
                    if i + 1 < len(items):
                        load(items[i + 1][0] + 1)
                        stage_h(i + 1)
                    stage_o(i)
                if pre is not None:
                    pre[2]()
                kb.barrier()

        def outproj_acc(pbank, lhs_list, rhs_list, t0, n, j, reads, stop_bufs=None):
            for f in range(8):
                bk = pbank[f % len(pbank)]
                nmm = len(lhs_list)
                for i in range(nmm):
                    op("pe", lambda e, i=i, f=f, bk=bk: e.matmul(pb[bk][:, 0:n], lhsT=lhs_list[i][:, f * 128:(f + 1) * 128], rhs=rhs_list[i],
                                                                 start=(i == 0), stop=(i == nmm - 1)),
                       reads=reads, writes=[pbb[bk]], inc=(i == nmm - 1))
                op("dve", lambda e, f=f, bk=bk: e.scalar_tensor_tensor(out=hT[:, f, t0:t0 + n], in0=pb[bk][:, 0:n], scalar=M.modsb[:, 16 + f, j:j + 1],
                                                                       in1=hT[:, f, t0:t0 + n], op0=ALU.mult, op1=ALU.add),
                   reads=[pbb[bk], M.modb] + hbs(t0, n), writes=hbs(t0, n))

        def mlstm_phase(L, slot, need_ctx):
            out_blocks = BLOCKS if need_ctx else BLOCKS[1:]
            with ExitStack() as ps:
                G = sbt(ps, [128, NT, 32], F32, "G")
                SP = sbt(ps, [128, NT, 16], F32, "SP")
                LI = sbt(ps, [128, NT, 16], F32, "LI")
                KS = sbt(ps, [128, NT, 16], F32, "KS")
                KSb = sbt(ps, [128, NT, 16], BF16, "KSb")
                EBH = sbt(ps, [128, NT, 16], F32, "EBH")
                EBL = sbt(ps, [128, NT, 16], F32, "EBL")
                gbuf = Buf("gates")
                wg = sbt(ps, [128, 8, 32], BF16, "wg")
                wgb = Buf("wg")
                wmv = wm_d[slot].rearrange("(c p) n -> p c n", p=128)
                dma("pool", wg[:], wmv[:, :, 3584:3616], writes=[wgb])
                gbias = small[:, S_MGB + slot * 32:S_MGB + slot * 32 + 32]
                for tt in range(NT):
                    bk = 0 if tt < 16 else 1
                    o = (tt % 16) * 32
                    for k in range(8):
                        op("pe", lambda e, tt=tt, k=k, bk=bk, o=o: e.matmul(pb[bk][:, o:o + 32], lhsT=uT[:, k, tt * 128:(tt + 1) * 128], rhs=wg[:, k, :],
                                                                            start=(k == 0), stop=(k == 7)),
                           reads=[wgb] + ub, writes=[pbb[bk]], inc=(k == 7))
                op("dve", lambda e: e.tensor_tensor(out=G[:, 0:16, :], in0=pb[0][:].rearrange("p (t g) -> p t g", g=32),
                                                    in1=gbias.unsqueeze(1).broadcast_to([128, 16, 32]), op=ALU.add), reads=[pbb[0], cbuf], writes=[gbuf])
                op("dve", lambda e: e.tensor_tensor(out=G[:, 16:18, :], in0=pb[1][:, 0:64].rearrange("p (t g) -> p t g", g=32),
                                                    in1=gbias.unsqueeze(1).broadcast_to([128, 2, 32]), op=ALU.add), reads=[pbb[1], cbuf], writes=[gbuf])
                for d in range(2):
                    op("dve", lambda e, d=d: e.tensor_copy(out=LI[:, :, d * 8:d * 8 + 8], in_=G[:, :, d * 16:d * 16 + 8]), reads=[gbuf], writes=[gbuf])
                    op("act", lambda e, d=d: e.activation(out=SP[:, :, d * 8:d * 8 + 8], in_=G[:, :, d * 16 + 8:d * 16 + 16], func=AF.Exp, scale=-1.0),
                       reads=[gbuf], writes=[gbuf])
                op("act", lambda e: e.activation(out=SP[:], in_=SP[:], func=AF.Ln, scale=1.0, bias=1.0), reads=[gbuf], writes=[gbuf])
                for tt in range(NT):
                    bk = 2 if tt < 16 else 3
                    o = (tt % 16) * 32
                    op("pe", lambda e, tt=tt, bk=bk, o=o: e.matmul(pb[bk][:, o:o + 8], lhsT=ntricf, rhs=SP[:, tt, 0:8], start=True, stop=True),
                       reads=[gbuf, cbuf], writes=[pbb[bk]], inc=False)
                    op("pe", lambda e, tt=tt, bk=bk, o=o: e.matmul(pb[bk][:, o + 8:o + 16], lhsT=ntricb, rhs=SP[:, tt, 8:16], start=True, stop=True),
                       reads=[gbuf, cbuf], writes=[pbb[bk]], inc=False)
                    op("pe", lambda e, tt=tt, bk=bk, o=o: e.matmul(pb[bk][:, o + 16:o + 32], lhsT=nhalf[:], rhs=SP[:, tt, :], start=True, stop=True),
                       reads=[gbuf, cbuf], writes=[pbb[bk]], inc=True)
                for (bk, a, b_) in ((2, 0, 16), (3, 16, 18)):
                    nn = b_ - a
                    pv = pb[bk][:, 0:nn * 32].rearrange("p (t g) -> p t g", g=32)
                    op("dve", lambda e, pv=pv, a=a, b_=b_: e.tensor_tensor(out=KS[:, a:b_, :], in0=LI[:, a:b_, :], in1=pv[:, :, 0:16], op=ALU.subtract),
                       reads=[pbb[bk], gbuf], writes=[gbuf])
                    op("act", lambda e, pv=pv, a=a, b_=b_: e.activation(out=EBH[:, a:b_, :], in_=pv[:, :, 16:32], func=AF.Exp), reads=[pbb[bk]], writes=[gbuf])
                    op("act", lambda e, pv=pv, a=a, b_=b_: e.activation(out=EBL[:, a:b_, :], in_=pv[:, :, 16:32], func=AF.Exp, scale=2.0), reads=[pbb[bk]], writes=[gbuf])
                op("act", lambda e: e.activation(out=KS[:], in_=KS[:], func=AF.Exp), reads=[gbuf], writes=[gbuf])
                op("dve", lambda e: e.tensor_copy(out=KSb[:], in_=KS[:]), reads=[gbuf], writes=[gbuf])

                wh = [sbt(ps, [128, 8, 448], BF16, "wh") for _ in range(2)]
                whb = [Buf("wh0"), Buf("wh1")]
                wo = [sbt(ps, [128, D], BF16, "wo") for _ in range(2)]
                wob = [Buf("wo0"), Buf("wo1")]
                Qb = [sbt(ps, [64, T], BF16, "Qb") for _ in range(2)]
                qbb = [Buf("Qbf"), Buf("Qbb")]
                KT = sbt(ps, [64, T], BF16, "KT")
                ktb = Buf("KT")
                Ktok = sbt(ps, [128, NT, 64], BF16, "Ktok")
                kkb = Buf("Ktok")
                Va = [sbt(ps, [128, NT, 128], BF16, "Va") for _ in range(2)]
                vab = [Buf("Vaf"), Buf("Vab")]
                HSd = [sbt(ps, [128, T], F32, "HSf"), sbt(ps, [128, T], F32, "HSb")]
                HS = HSd[0]
                hsd = [[Buf("hs%d_%d" % (d_, i)) for i in range(NT)] for d_ in range(2)]
                hsb = hsd[0]
                Cst = [sbt(ps, [64, 256], F32, "Cst") for _ in range(2)]
                csb = [Buf("Cf"), Buf("Cb")]
                Cbf = [[sbt(ps, [64, 256], BF16, "Cbf") for _ in range(2)] for _ in range(2)]
                cbb = [[Buf("Cbf%d_%d" % (d_, i)) for i in range(2)] for d_ in range(2)]
                ctmp = [sbt(ps, [64, 256], F32, "ctmp") for _ in range(2)]
                ctb = [Buf("ct0"), Buf("ct1")]
                ebt = [sbt(ps, [64, 512], F32, "ebt") for _ in range(2)]
                ebb = [Buf("eb0"), Buf("eb1")]
                Pm = [sbt(ps, [128, 128], BF16, "Pm") for _ in range(4)]
                pmb = [Buf("Pm%d" % i) for i in range(4)]
                adn = [sbt(ps, [128, 128], F32, "adn") for _ in range(2)]
                adb = [Buf("ad0"), Buf("ad1")]
                htm = [sbt(ps, [128, 128], F32, "htm") for _ in range(2)]
                htb = [Buf("ht0"), Buf("ht1")]
                sqh = sbt(ps, [128, 512], BF16, "sqh")
                sqhb = Buf("sqh")
                rsh = sbt(ps, [128, 512], F32, "rsh")
                rshb = Buf("rsh")
                sig = sbt(ps, [128, 512], F32, "sig")
                sigb = Buf("sig")
                hn = sbt(ps, [128, 512], F32, "hn")
                hnb = Buf("hn")
                ao = [sbt(ps, [128, 512], BF16, "ao") for _ in range(2)]
                aob = [Buf("ao0"), Buf("ao1")]
                wov = wmo_d[slot]

                def load_head(h):
                    s = h % 2
                    dma("pool", wh[s][:], wmv[:, :, h * 448:(h + 1) * 448], writes=[whb[s]])
                    dma("pool", wo[s][:], wov[h * 128:(h + 1) * 128, :], writes=[wob[s]])

                order = [list(range(NT)), [1, 0] + list(range(NT - 1, 1, -1))]
                load_head(0)
                for h in range(8):
                    s = h % 2
                    if h + 1 < 8:
                        load_head(h + 1)
                    w = wh[s]
                    for (t0, n, j) in BLOCKS:
                        bidx = BLOCKS.index((t0, n, j))
                        for d in range(2):
                            ntr = ntricf if d == 0 else ntricb
                            for ti in range(n // 128):
                                tt = t0 // 128 + ti
                                op("pe", lambda e, d=d, tt=tt, ti=ti, ntr=ntr: e.matmul(pb[d][0:64, ti * 128:(ti + 1) * 128],
                                                                                         lhsT=SP[:, tt, d * 8 + h:d * 8 + h + 1].broadcast_to([128, 64]), rhs=ntr,
                                                                                         start=True, stop=True),
                                   reads=[gbuf, cbuf], writes=[pbb[d]], inc=(ti == n // 128 - 1))
                            op("act", lambda e, d=d: e.activation(out=ebt[d][:, 0:n], in_=pb[d][0:64, 0:n], func=AF.Exp), reads=[pbb[d]], writes=[ebb[d]])
                        for k in range(8):
                            op("pe", lambda e, k=k: e.matmul(pb[2][0:64, 0:n], lhsT=w[:, k, 0:64], rhs=uT[:, k, t0:t0 + n], start=(k == 0), stop=(k == 7)),
                               reads=[whb[s], ub[bidx]], writes=[pbb[2]], inc=(k == 7))
                        for d in range(2):
                            op("dve", lambda e, d=d: e.tensor_tensor(out=Qb[d][:, t0:t0 + n], in0=pb[2][0:64, 0:n], in1=ebt[d][:, 0:n], op=ALU.mult),
                               reads=[pbb[2], ebb[d]], writes=[qbb[d]])
                        for k in range(8):
                            op("pe", lambda e, k=k: e.matmul(pb[3][0:64, 0:n], lhsT=w[:, k, 64:128], rhs=uT[:, k, t0:t0 + n], start=(k == 0), stop=(k == 7)),
                               reads=[whb[s], ub[bidx]], writes=[pbb[3]], inc=(k == 7))
                        op("act", lambda e: e.activation(out=KT[:, t0:t0 + n], in_=pb[3][0:64, 0:n], func=AF.Identity, scale=0.125), reads=[pbb[3]], writes=[ktb])
                    for tt in range(NT):
                        bk = 4 + tt % 2
                        bidx = 0 if tt < 2 else 1 + (tt - 2) // 4
                        for k in range(8):
                            op("pe", lambda e, k=k, tt=tt, bk=bk: e.matmul(pb[bk][:, 0:192], lhsT=uT[:, k, tt * 128:(tt + 1) * 128], rhs=w[:, k, 128:320],
                                                                           start=(k == 0), stop=(k == 7)),
                               reads=[whb[s], ub[bidx]], writes=[pbb[bk]], inc=(k == 7))
                        op("act", lambda e, tt=tt, bk=bk: e.activation(out=Ktok[:, tt, :], in_=pb[bk][:, 0:64], func=AF.Identity, scale=0.125), reads=[pbb[bk]], writes=[kkb])
                        for d in range(2):
                            op("dve", lambda e, tt=tt, bk=bk, d=d: e.tensor_scalar(out=Va[d][:, tt, :], in0=pb[bk][:, 64:192], scalar1=KS[:, tt, d * 8 + h:d * 8 + h + 1],
                                                                                    scalar2=None, op0=ALU.mult),
                               reads=[pbb[bk], gbuf], writes=[vab[d]])
                    for d in range(2):
                        op("dve", lambda e, d=d: e.memset(Cst[d][:], 0.0), writes=[csb[d]])
                    def unit(step, d):
                        tt = order[d][step]
                        return tt, d * 8 + h, slice(tt * 128, (tt + 1) * 128), (need_ctx or tt >= 2)

                    def s1(step):
                        cq = step % 2
                        for d in range(2):
                            tt, col, tsl, outp = unit(step, d)
                            if outp and step > 0:
                                op("dve", lambda e: e.tensor_scalar(out=Cbf[d][cq][:], in0=Cst[d][:], scalar1=EBH[0:64, tt, col:col + 1], scalar2=None, op0=ALU.mult),
                                   reads=[csb[d], gbuf], writes=[cbb[d][cq]])
                        for d in range(2):
                            tt, col, tsl, outp = unit(step, d)
                            if outp:
                                op("pe", lambda e: e.matmul(pb[d][:, 0:128], lhsT=KT[:, tsl], rhs=Qb[d][:, tsl], start=True, stop=True),
                                   reads=[ktb, qbb[d]], writes=[pbb[d]])
                        for d in range(2):
                            tt, col, tsl, outp = unit(step, d)
                            if outp:
                                p_ = (step % 2) * 2 + d
                                mk = maskf_bf if d == 0 else maskb_bf
                                op("dve", lambda e: e.tensor_tensor(out=Pm[p_][:], in0=pb[d][:, 0:128], in1=mk[:], op=ALU.mult),
                                   reads=[pbb[d], cbuf], writes=[pmb[p_]])

                    def s3(step):
                        if step >= NT - 1:
                            return
                        for d in range(2):
                            tt, col, tsl, outp = unit(step, d)
                            ksl = KSb[:, tt, col:col + 1].broadcast_to([128, 128])
                            bK = 6 + d
                            op("pe", lambda e: e.matmul(pb[bK][0:64, 0:128], lhsT=Ktok[:, tt, :], rhs=Va[d][:, tt, :], start=True, stop=True),
                               reads=[kkb, vab[d]], writes=[pbb[bK]], inc=False)
                            op("pe", lambda e: e.matmul(pb[bK][0:64, 128:256], lhsT=Ktok[:, tt, :], rhs=ksl, start=True, stop=True),
                               reads=[kkb, gbuf], writes=[pbb[bK]])
                        for d in range(2):
                            tt, col, tsl, outp = unit(step, d)
                            bK = 6 + d
                            op("dve", lambda e: e.tensor_scalar(out=ctmp[d][:], in0=pb[bK][0:64, 0:256], scalar1=EBH[0:64, tt, col:col + 1], scalar2=None, op0=ALU.mult),
                               reads=[pbb[bK], gbuf], writes=[ctb[d]])
                        for d in range(2):
                            tt, col, tsl, outp = unit(step, d)
                            op("dve", lambda e: e.scalar_tensor_tensor(out=Cst[d][:], in0=Cst[d][:], scalar=EBL[0:64, tt, col:col + 1], in1=ctmp[d][:], op0=ALU.mult, op1=ALU.add),
                               reads=[ctb[d], gbuf, csb[d]], writes=[csb[d]])

                    def s2(step):
                        cq = step % 2
                        last = (step == 0)
                        act = [d for d in range(2) if unit(step, d)[3]]
                        for d in act:
                            tt, col, tsl, outp = unit(step, d)
                            ksl = KSb[:, tt, col:col + 1].broadcast_to([128, 128])
                            bN, bD, p_ = 2 + d, 4 + d, (step % 2) * 2 + d
                            op("pe", lambda e: e.matmul(pb[bN][:, 0:128], lhsT=Va[d][:, tt, :], rhs=Pm[p_][:], start=True, stop=last),
                               reads=[vab[d], pmb[p_]], writes=[pbb[bN]], inc=last)
                            if not last:
                                op("pe", lambda e: e.matmul(pb[bN][:, 0:128], lhsT=Cbf[d][cq][:, 0:128], rhs=Qb[d][:, tsl], start=False, stop=True),
                                   reads=[cbb[d][cq], qbb[d]], writes=[pbb[bN]])
                            op("pe", lambda e: e.matmul(pb[bD][:, 0:128], lhsT=ksl, rhs=Pm[p_][:], start=True, stop=last),
                               reads=[gbuf, pmb[p_]], writes=[pbb[bD]], inc=last)
                            if not last:
                                op("pe", lambda e: e.matmul(pb[bD][:, 0:128], lhsT=Cbf[d][cq][:, 128:256], rhs=Qb[d][:, tsl], start=False, stop=True),
                                   reads=[cbb[d][cq], qbb[d]], writes=[pbb[bD]])
                        for d in act:
                            op("act", lambda e: e.activation(out=adn[d][:], in_=pb[4 + d][:, 0:128], func=AF.Abs), reads=[pbb[4 + d]], writes=[adb[d]])
                        for d in act:
                            op("dve", lambda e: e.tensor_scalar(out=adn[d][:], in0=adn[d][:], scalar1=1.0, scalar2=None, op0=ALU.max), reads=[adb[d]], writes=[adb[d]])
                        for d in act:
                            op("dve", lambda e: e.reciprocal(out=adn[d][:], in_=adn[d][:]), reads=[adb[d]], writes=[adb[d]])
                        for d in act:
                            tt, col, tsl, outp = unit(step, d)
                            op("dve", lambda e: e.tensor_tensor(out=HSd[d][:, tsl], in0=pb[2 + d][:, 0:128], in1=adn[d][:], op=ALU.mult),
                               reads=[pbb[2 + d], adb[d]], writes=[hsd[d][tt]])

                    s1(0)
                    for step in range(NT):
                        s3(step)
                        if step + 1 < NT:
                            s1(step + 1)
                        s2(step)
                    hnw = small[:, S_MHN + slot * 8 + h:S_MHN + slot * 8 + h + 1]
                    for bi, (t0, n, j) in enumerate(out_blocks):
                        bidx = BLOCKS.index((t0, n, j))
                        hsl = [hsb[i] for i in range(t0 // 128, (t0 + n) // 128)]
                        hsl2 = [hsd[1][i] for i in range(t0 // 128, (t0 + n) // 128)]
                        op("dve", lambda e: e.tensor_tensor(out=HS[:, t0:t0 + n], in0=HS[:, t0:t0 + n], in1=HSd[1][:, t0:t0 + n], op=ALU.add), reads=hsl + hsl2, writes=hsl)
                        op("act", lambda e: e.activation(out=sqh[:, 0:n], in_=HS[:, t0:t0 + n], func=AF.Square), reads=hsl, writes=[sqhb])
                        op("pe", lambda e: e.matmul(pb[0][:, 0:n], lhsT=ones_bf[:], rhs=sqh[:, 0:n], start=True, stop=True), reads=[sqhb, cbuf], writes=[pbb[0]])
                        rstd_from_psum(pb[0][:, 0:n], rsh[:, 0:n], n, 128.0, pbb[0], rshb)
                        for k in range(8):
                            op("pe", lambda e, k=k: e.matmul(pb[1][:, 0:n], lhsT=w[:, k, 320:448], rhs=uT[:, k, t0:t0 + n], start=(k == 0), stop=(k == 7)),
                               reads=[whb[s], ub[bidx]], writes=[pbb[1]], inc=(k == 7))
                        op("act", lambda e: e.activation(out=sig[:, 0:n], in_=pb[1][:, 0:n], func=AF.Exp, scale=-1.0), reads=[pbb[1]], writes=[sigb])
                        op("dve", lambda e: e.tensor_scalar(out=sig[:, 0:n], in0=sig[:, 0:n], scalar1=1.0, scalar2=None, op0=ALU.add), reads=[sigb], writes=[sigb])
                        op("dve", lambda e: e.reciprocal(out=sig[:, 0:n], in_=sig[:, 0:n]), reads=[sigb], writes=[sigb])
                        op("dve", lambda e: e.scalar_tensor_tensor(out=hn[:, 0:n], in0=HS[:, t0:t0 + n], scalar=hnw, in1=rsh[:, 0:n], op0=ALU.mult, op1=ALU.mult),
                           reads=hsl + [rshb, cbuf], writes=[hnb])
                        a_ = bi % 2
                        op("dve", lambda e, a_=a_: e.tensor_tensor(out=ao[a_][:, 0:n], in0=hn[:, 0:n], in1=sig[:, 0:n], op=ALU.mult), reads=[hnb, sigb], writes=[aob[a_]])
                        outproj_acc([6, 7], [wo[s]], [ao[a_][:, 0:n]], t0, n, j, [wob[s], aob[a_]])
                kb.barrier()

        def proj_rope(ps_res, w, c0, cr0, dst_ap_fn, dbuf, wbuf, banks, tmps, tmpb, rope_tiles, scale_ctx_copy=True):
            for (t0, n, j) in BLOCKS:
                bidx = BLOCKS.index((t0, n, j))
                b0, b1 = banks
                for k in range(8):
                    op("pe", lambda e, k=k: e.matmul(pb[b0][:, 0:n], lhsT=w[:, k, c0:c0 + 128], rhs=uT[:, k, t0:t0 + n], start=(k == 0), stop=(k == 7)),
                       reads=[wbuf, ub[bidx]], writes=[pbb[b0]], inc=(k == 7))
                if j == 1:
                    op("act", lambda e: e.activation(out=dst_ap_fn(t0, n), in_=pb[b0][:, 0:n], func=AF.Identity), reads=[pbb[b0]], writes=[dbuf])
                    continue
                for k in range(8):
                    op("pe", lambda e, k=k: e.matmul(pb[b1][:, 0:n], lhsT=w[:, k, cr0:cr0 + 128], rhs=uT[:, k, t0:t0 + n], start=(k == 0), stop=(k == 7)),
                       reads=[wbuf, ub[bidx]], writes=[pbb[b1]], inc=(k == 7))
                rc, rs_, rb = rope_tiles(t0)
                op("dve", lambda e: e.tensor_tensor(out=tmps[0][:, 0:n], in0=pb[b0][:, 0:n], in1=rc, op=ALU.mult), reads=[pbb[b0], rb], writes=[tmpb[0]])
                op("dve", lambda e: e.tensor_tensor(out=tmps[1][:, 0:n], in0=pb[b1][:, 0:n], in1=rs_, op=ALU.mult), reads=[pbb[b1], rb], writes=[tmpb[1]])
                op("dve", lambda e: e.tensor_tensor(out=dst_ap_fn(t0, n), in0=tmps[0][:, 0:n], in1=tmps[1][:, 0:n], op=ALU.add), reads=[tmpb[0], tmpb[1]], writes=[dbuf])

        def load_rope(ps):
            rc = sbt(ps, [128, NLAT], F32, "ropec")
            rs_ = sbt(ps, [128, NLAT], F32, "ropes")
            rb = Buf("rope")
            dma("sp", rc[:], ropec_l.get(), writes=[rb])
            dma("sp", rs_[:], ropes_l.get(), writes=[rb])

            def tiles(t0):
                l0 = t0 - NCTX
                return rc[:, l0:l0 + 512], rs_[:, l0:l0 + 512], rb
            return tiles

        def swa_phase(L, need_ctx):
            with ExitStack() as ps:
                rope_tiles = load_rope(ps)
                wsv = ws_d[0].rearrange("(c p) n -> p c n", p=128)
                wov = wso_d[0].rearrange("(h p) n -> p h n", p=64)
                wg_ = [sbt(ps, [128, 8, 832], BF16, "wsg") for _ in range(1)]
                wgb = [Buf("wsg0"), Buf("wsg1")]
                wo = [sbt(ps, [64, 4, D], BF16, "wso") for _ in range(1)]
                wob = [Buf("wso0"), Buf("wso1")]
                QT = sbt(ps, [128, 2, T], BF16, "QT")
                qtb = Buf("QT")
                KT = sbt(ps, [128, T], BF16, "KT")
                ktb = Buf("KT")
                Va = sbt(ps, [128, NT, 128], BF16, "Va")
                vab = Buf("Va")
                aog = sbt(ps, [64, 4, T], BF16, "aog")
                aob = [Buf("aog%d" % i) for i in range(NT)]
                tmps = [sbt(ps, [128, 512], F32, "rt") for _ in range(2)]
                tmpb = [Buf("rt0"), Buf("rt1")]
                Pt = [sbt(ps, [128, 512], BF16, "Pt") for _ in range(3)]
                ptb = [Buf("Pt%d" % i) for i in range(3)]
                dn = sbt(ps, [128, 512], F32, "dn")
                dnb = Buf("dn")
                ES = sbt(ps, [128, 16], F32, "ES")
                esb = Buf("ES")
                op("act", lambda e: e.activation(out=ES[:], in_=small[:, S_SINK:S_SINK + 16], func=AF.Exp), reads=[cbuf], writes=[esb])
                op("dve", lambda e: e.memset(Va[:, :, 64:128], 1.0), writes=[vab])

                def load_g(g):
                    s = 0
                    dma("pool", wg_[s][:], wsv[:, :, g * 832:(g + 1) * 832], writes=[wgb[s]])
                    dma("pool", wo[s][:], wov[:, g * 4:(g + 1) * 4, :], writes=[wob[s]])

                pti = 0
                for g in range(4):
                    s = 0
                    load_g(g)
                    w = wg_[s]
                    if SWA_STOP < 1:
                        continue
                    proj_rope(ps, w, 512, 640, lambda t0, n: KT[:, t0:t0 + n], ktb, wgb[s], (0, 1), tmps, tmpb, rope_tiles)
                    for jq in range(SWA_NQ):
                        if SWA_SUB < 1:
                            continue
                        proj_rope(ps, w, jq * 128, 256 + jq * 128, lambda t0, n, jq=jq: QT[:, jq, t0:t0 + n], qtb, wgb[s], SWA_QB, tmps, tmpb, rope_tiles)
                    for tt in range(NT):
                        if SWA_SUB < 2:
                            continue
                        bk = 4 + (tt // 8) % 2
                        o = (tt % 8) * 64
                        bidx = 0 if tt < 2 else 1 + (tt - 2) // 4
                        for k in range(8):
                            op("pe", lambda e, k=k, tt=tt, bk=bk, o=o: e.matmul(pb[bk][:, o:o + 64], lhsT=uT[:, k, tt * 128:(tt + 1) * 128], rhs=w[:, k, 768:832],
                                                                                start=(k == 0), stop=(k == 7)),
                               reads=[wgb[s], ub[bidx]], writes=[pbb[bk]], inc=(k == 7))
                        if tt % 8 == 7 or tt == NT - 1:
                            a = (tt // 8) * 8
                            nn = tt + 1 - a
                            op("act", lambda e, a=a, nn=nn, bk=bk: e.activation(out=Va[:, a:a + nn, 0:64], in_=pb[bk][:, 0:nn * 64].rearrange("p (t d) -> p t d", d=64),
                                                                                func=AF.Identity), reads=[pbb[bk]], writes=[vab])
                    aitems = []
                    for qt in range(NT):
                        if SWA_STOP < 2:
                            continue
                        if qt < 2:
                            if not need_ctx:
                                continue
                            kts = [0, 1]
                        else:
                            kts = [0, 1] + [kt for kt in (qt - 1, qt, qt + 1) if 2 <= kt < NT]
                        for ki, kt in enumerate(kts):
                            aitems.append((qt, kt, ki, len(kts)))

                    def s_stage(i):
                        qt, kt, ki, nk = aitems[i]
                        qsl = slice(qt * 128, (qt + 1) * 128)
                        ksl = slice(kt * 128, (kt + 1) * 128)
                        bS = (i % 2) * 2
                        p_ = i % 3
                        op("pe", lambda e: e.matmul(pb[bS][:, 0:256], lhsT=KT[0:64, ksl], rhs=QT[0:64, :, qsl], start=True, stop=True),
                           reads=[ktb, qtb], writes=[pbb[bS]])
                        op("pe", lambda e: e.matmul(pb[bS + 1][:, 0:256], lhsT=KT[64:128, ksl], rhs=QT[64:128, :, qsl], start=True, stop=True),
                           reads=[ktb, qtb], writes=[pbb[bS + 1]])
                        op("act", lambda e: e.activation(out=Pt[p_][:, 0:256], in_=pb[bS][:, 0:256], func=AF.Exp, scale=0.125), reads=[pbb[bS]], writes=[ptb[p_]])
                        op("act", lambda e: e.activation(out=Pt[p_][:, 256:512], in_=pb[bS + 1][:, 0:256], func=AF.Exp, scale=0.125), reads=[pbb[bS + 1]], writes=[ptb[p_]])
                        if qt >= 2 and kt >= 2 and kt != qt:
                            mk = maskb_bf if kt == qt - 1 else maskf_bf
                            op("dve", lambda e: e.tensor_tensor(out=Pt[p_][:].rearrange("p (h q) -> p h q", h=4), in0=Pt[p_][:].rearrange("p (h q) -> p h q", h=4),
                                                                in1=mk[:].unsqueeze(1).broadcast_to([128, 4, 128]), op=ALU.mult),
                               reads=[ptb[p_], cbuf], writes=[ptb[p_]])

                    def pv_stage(i):
                        qt, kt, ki, nk = aitems[i]
                        qsl = slice(qt * 128, (qt + 1) * 128)
                        p_ = i % 3
                        bO = 6 + qt % 2
                        op("pe", lambda e: e.matmul(pb[bO][:], lhsT=Va[:, kt, :], rhs=Pt[p_][:], start=(ki == 0), stop=(ki == nk - 1)),
                           reads=[vab, ptb[p_]], writes=[pbb[bO]], inc=(ki == nk - 1))
                        if ki == nk - 1:
                            op("dve", lambda e: e.tensor_tensor(out=dn[64:128, :].rearrange("p (h q) -> p h q", h=4), in0=pb[bO][64:128, :].rearrange("p (h q) -> p h q", h=4),
                                                                in1=ES[64:128, g * 4:(g + 1) * 4].unsqueeze(2).broadcast_to([64, 4, 128]), op=ALU.add),
                               reads=[pbb[bO], esb], writes=[dnb])
                            op("dve", lambda e: e.reciprocal(out=dn[64:128, :], in_=dn[64:128, :]), reads=[dnb], writes=[dnb])
                            op("dve", lambda e: e.tensor_tensor(out=aog[:, :, qsl], in0=pb[bO][0:64, :].rearrange("p (h q) -> p h q", h=4),
                                                                in1=dn[64:128, :].rearrange("p (h q) -> p h q", h=4), op=ALU.mult),
                               reads=[pbb[bO], dnb], writes=[aob[qt]])

                    if aitems:
                        s_stage(0)
                    for i in range(len(aitems)):
                        if i + 1 < len(aitems):
                            s_stage(i + 1)
                        pv_stage(i)
                    for (t0, n, j) in (BLOCKS if need_ctx else BLOCKS[1:]):
                        if SWA_STOP < 3:
                            continue
                        ab = [aob[i] for i in range(t0 // 128, (t0 + n) // 128)]
                        outproj_acc([0, 1, 2, 3], [wo[s][:, PSORD[pos], :] for pos in range(4)], [aog[:, pos, t0:t0 + n] for pos in range(4)], t0, n, j, [wob[s]] + ab)
                kb.barrier()

        def diff_phase(L, need_ctx):
            lam_init = 0.8 - 0.6 * float(np.exp(-0.3 * L))
            with ExitStack() as ps:
                rope_tiles = load_rope(ps)
                wdv = wd_d[0].rearrange("(c p) n -> p c n", p=128)
                wov = wdo_d[0]
                wh = [sbt(ps, [128, 8, 640], BF16, "wdh") for _ in range(2)]
                whb = [Buf("wdh0"), Buf("wdh1")]
                wo = [sbt(ps, [128, D], BF16, "wdo") for _ in range(2)]
                wob = [Buf("wdo0"), Buf("wdo1")]
                QT = sbt(ps, [128, T], BF16, "QT")
                qtb = Buf("QT")
                KT = sbt(ps, [128, T], BF16, "KT")
                ktb = Buf("KT")
                Vt = sbt(ps, [128, NT, 128], BF16, "Vt")
                vtb = Buf("Vt")
                tmps = [sbt(ps, [128, 512], F32, "rt") for _ in range(2)]
                tmpb = [Buf("rt0"), Buf("rt1")]
                Pt = [sbt(ps, [128, 512], BF16, "Pt") for _ in range(4)]
                ptb = [Buf("Pt%d" % i) for i in range(4)]
                rr = [sbt(ps, [128, 512], F32, "rr") for _ in range(2)]
                rrb = [Buf("rr0"), Buf("rr1")]
                od = sbt(ps, [128, 512], F32, "od")
                odb = Buf("od")
                sqh = sbt(ps, [128, 512], BF16, "sqh")
                sqhb = Buf("sqh")
                rsh = sbt(ps, [128, 512], F32, "rsh")
                rshb = Buf("rsh")
                ao = [sbt(ps, [128, 512], BF16, "ao") for _ in range(2)]
                aob = [Buf("ao0"), Buf("ao1")]
                lam = sbt(ps, [128, 8], F32, "lam")
                lamb = Buf("lam")
                hnw = sbt(ps, [128, 8], F32, "hnw")
                lv = small[:, S_DLAM:S_DLAM + 256]
                op("dve", lambda e: e.tensor_tensor(out=tmps[0][:, 0:64], in0=lv[:, 0:64], in1=lv[:, 64:128], op=ALU.mult), reads=[cbuf], writes=[tmpb[0]])
                op("dve", lambda e: e.tensor_tensor(out=tmps[0][:, 64:128], in0=lv[:, 128:192], in1=lv[:, 192:256], op=ALU.mult), reads=[cbuf], writes=[tmpb[0]])
                op("dve", lambda e: e.tensor_reduce(out=lam[:, 0:2], in_=tmps[0][:, 0:128].rearrange("p (a b) -> p a b", a=2), axis=mybir.AxisListType.X, op=ALU.add),
                   reads=[tmpb[0]], writes=[lamb])
                op("act", lambda e: e.activation(out=lam[:, 0:2], in_=lam[:, 0:2], func=AF.Exp), reads=[lamb], writes=[lamb])
                op("dve", lambda e: e.tensor_tensor(out=lam[:, 2:3], in0=lam[:, 1:2], in1=lam[:, 0:1], op=ALU.subtract), reads=[lamb], writes=[lamb])
                op("dve", lambda e: e.tensor_scalar(out=lam[:, 3:4], in0=lam[:, 2:3], scalar1=-lam_init, scalar2=None, op0=ALU.add), reads=[lamb], writes=[lamb])
                op("dve", lambda e: e.tensor_scalar(out=hnw[:], in0=small[:, S_DHN:S_DHN + 8], scalar1=(1.0 - lam_init), scalar2=None, op0=ALU.mult), reads=[cbuf], writes=[lamb])
                nlam = lam[:, 3:4]

                def load_h(h):
                    s = h % 2
                    dma("pool", wh[s][:], wdv[:, :, h * 640:(h + 1) * 640], writes=[whb[s]])
                    dma("pool", wo[s][:], wov[h * 128:(h + 1) * 128, :], writes=[wob[s]])

                load_h(0)
                pti = 0
                oi = 0
                for h in range(8):
                    s = h % 2
                    if h + 1 < 8:
                        load_h(h + 1)
                    w = wh[s]
                    proj_rope(ps, w, 256, 384, lambda t0, n: KT[:, t0:t0 + n], ktb, whb[s], (0, 1), tmps, tmpb, rope_tiles)
                    proj_rope(ps, w, 0, 128, lambda t0, n: QT[:, t0:t0 + n], qtb, whb[s], (2, 3), tmps, tmpb, rope_tiles)
                    for tt in range(NT):
                        bk = 4 + (tt // 4) % 2
                        o = (tt % 4) * 128
                        bidx = 0 if tt < 2 else 1 + (tt - 2) // 4
                        for k in range(8):
                            op("pe", lambda e, k=k, tt=tt, bk=bk, o=o: e.matmul(pb[bk][:, o:o + 128], lhsT=uT[:, k, tt * 128:(tt + 1) * 128], rhs=w[:, k, 512:640],
                                                                                start=(k == 0), stop=(k == 7)),
                               reads=[whb[s], ub[bidx]], writes=[pbb[bk]], inc=(k == 7))
                        if tt % 4 == 3 or tt == NT - 1:
                            a = (tt // 4) * 4
                            nn = tt + 1 - a
                            op("act", lambda e, a=a, nn=nn, bk=bk: e.activation(out=Vt[:, a:a + nn, :], in_=pb[bk][:, 0:nn * 128].rearrange("p (t d) -> p t d", d=128),
                                                                                func=AF.Identity), reads=[pbb[bk]], writes=[vtb])
                    for bi, (t0, n, j) in enumerate(BLOCKS if need_ctx else BLOCKS[1:]):
                        kts = [0, 1] if j == 1 else list(range(NT))
                        nk = len(kts)

                        def s_stage(ki):
                            kt = kts[ki]
                            ksl = slice(kt * 128, (kt + 1) * 128)
                            for m in range(2):
                                bS = m * 2 + (ki % 2)
                                p_ = (ki % 2) * 2 + m
                                rsl = slice(m * 64, (m + 1) * 64)
                                op("pe", lambda e, bS=bS, ksl=ksl, rsl=rsl: e.matmul(pb[bS][:, 0:n], lhsT=KT[rsl, ksl], rhs=QT[rsl, t0:t0 + n], start=True, stop=True),
                                   reads=[ktb, qtb], writes=[pbb[bS]])
                            for m in range(2):
                                bS = m * 2 + (ki % 2)
                                p_ = (ki % 2) * 2 + m
                                op("act", lambda e, bS=bS, p_=p_: e.activation(out=Pt[p_][:, 0:n], in_=pb[bS][:, 0:n], func=AF.Exp, scale=0.125), reads=[pbb[bS]], writes=[ptb[p_]])

                        def pv_stage(ki):
                            kt = kts[ki]
                            first = (ki == 0)
                            lastk = (ki == nk - 1)
                            for m in range(2):
                                p_ = (ki % 2) * 2 + m
                                op("pe", lambda e, kt=kt, p_=p_, m=m: e.matmul(pb[4 + m][:, 0:n], lhsT=Vt[:, kt, :], rhs=Pt[p_][:, 0:n], start=first, stop=lastk),
                                   reads=[vtb, ptb[p_]], writes=[pbb[4 + m]], inc=lastk)
                                op("pe", lambda e, p_=p_, m=m: e.matmul(pb[6 + m][:, 0:n], lhsT=ones_bf[:], rhs=Pt[p_][:, 0:n], start=first, stop=lastk),
                                   reads=[cbuf, ptb[p_]], writes=[pbb[6 + m]], inc=lastk)

                        s_stage(0)
                        for ki in range(nk):
                            if ki + 1 < nk:
                                s_stage(ki + 1)
                            pv_stage(ki)
                        for m in range(2):
                            op("dve", lambda e, m=m: e.reciprocal(out=rr[m][:, 0:n], in_=pb[6 + m][:, 0:n]), reads=[pbb[6 + m]], writes=[rrb[m]])
                        for m in range(2):
                            op("dve", lambda e, m=m: e.tensor_tensor(out=rr[m][:, 0:n], in0=pb[4 + m][:, 0:n], in1=rr[m][:, 0:n], op=ALU.mult), reads=[pbb[4 + m], rrb[m]], writes=[rrb[m]])
                        op("dve", lambda e: e.scalar_tensor_tensor(out=od[:, 0:n], in0=rr[1][:, 0:n], scalar=nlam, in1=rr[0][:, 0:n], op0=ALU.mult, op1=ALU.add),
                           reads=[rrb[0], rrb[1], lamb], writes=[odb])
                        op("act", lambda e: e.activation(out=sqh[:, 0:n], in_=od[:, 0:n], func=AF.Square), reads=[odb], writes=[sqhb])
                        op("pe", lambda e: e.matmul(pb[0][:, 0:n], lhsT=ones_bf[:], rhs=sqh[:, 0:n], start=True, stop=True), reads=[sqhb, cbuf], writes=[pbb[0]])
                        rstd_from_psum(pb[0][:, 0:n], rsh[:, 0:n], n, 128.0, pbb[0], rshb)
                        a_ = oi % 2
                        oi += 1
                        op("dve", lambda e, a_=a_: e.scalar_tensor_tensor(out=ao[a_][:, 0:n], in0=od[:, 0:n], scalar=hnw[:, h:h + 1], in1=rsh[:, 0:n], op0=ALU.mult, op1=ALU.mult),
                           reads=[odb, rshb, lamb], writes=[aob[a_]])
                        outproj_acc([1, 2, 3], [wo[s]], [ao[a_][:, 0:n]], t0, n, j, [wob[s], aob[a_]])
                kb.barrier()

        def final_phase(do_norm):
            with ExitStack() as ps:
                sq = sbt(ps, [128, 8, 512], BF16, "fsq")
                sqb = Buf("fsq")
                rs = sbt(ps, [128, 512], F32, "frs")
                rsb = Buf("frs")
                tmp = sbt(ps, [128, 8, 512], F32, "ftmp")
                tmb = Buf("ftmp")
                ost = [sbt(ps, [128, D], F32, "ost") for _ in range(2)]
                osb = [Buf("ost0"), Buf("ost1")]
                oi = 0
                for (t0, n, j) in BLOCKS[1:]:
                    if do_norm:
                        op("act", lambda e: e.activation(out=sq[:, :, 0:n], in_=hT[:, :, t0:t0 + n], func=AF.Square), reads=hbs(t0, n), writes=[sqb])
                        for c in range(8):
                            op("pe", lambda e, c=c: e.matmul(pb[7][:, 0:n], lhsT=ones_bf[:], rhs=sq[:, c, 0:n], start=(c == 0), stop=(c == 7)),
                               reads=[sqb, cbuf], writes=[pbb[7]], inc=(c == 7))
                        rstd_from_psum(pb[7][:, 0:n], rs[:, 0:n], n, float(D), pbb[7], rsb)
                        for c in range(8):
                            op("dve", lambda e, c=c: e.scalar_tensor_tensor(out=tmp[:, c, 0:n], in0=hT[:, c, t0:t0 + n], scalar=small[:, S_FINAL + c:S_FINAL + c + 1],
                                                                            in1=rs[:, 0:n], op0=ALU.mult, op1=ALU.mult),
                               reads=hbs(t0, n) + [rsb, cbuf], writes=[tmb])
                    else:
                        op("dve", lambda e: e.tensor_copy(out=tmp[:, :, 0:n], in_=hT[:, :, t0:t0 + n]), reads=hbs(t0, n), writes=[tmb])
                    for ti in range(n // 128):
                        o_ = oi % 2
                        oi += 1
                        for half in range(2):
                            bk = (oi * 2 + half) % 4
                            for c4 in range(4):
                                c = half * 4 + c4
                                op("pe", lambda e, c=c, c4=c4, bk=bk, ti=ti: e.transpose(pb[bk][:, c4 * 128:(c4 + 1) * 128], tmp[:, c, ti * 128:(ti + 1) * 128], ident),
                                   reads=[tmb, cbuf], writes=[pbb[bk]], inc=(c4 == 3))
                            if half == 0:
                                op("act", lambda e, bk=bk, o_=o_, half=half: e.activation(out=ost[o_][:, half * 512:(half + 1) * 512], in_=pb[bk][:], func=AF.Identity),
                                   reads=[pbb[bk]], writes=[osb[o_]])
                            else:
                                op("dve", lambda e, bk=bk, o_=o_, half=half: e.tensor_copy(out=ost[o_][:, half * 512:(half + 1) * 512], in_=pb[bk][:]),
                                   reads=[pbb[bk]], writes=[osb[o_]])
                        r0 = t0 - NCTX + ti * 128
                        dma("sp", out_d[r0:r0 + 128, :], ost[o_][:], reads=[osb[o_]])
                kb.wait_all_dma("sp")

        prefetched = set()
        for li, L in enumerate(layers):
            kind, slot = L % 3, L // 3
            need_ctx = L < DEPTH - 1
            blocks = BLOCKS if need_ctx else BLOCKS[1:]
            set_mod(L)
            if L not in prefetched:
                ada_phase(L)
            if do_mixer:
                norm_mod(lambda: M.amix, 0, BLOCKS)
                if kind == 0:
                    mlstm_phase(L, slot, need_ctx)
                elif kind == 1:
                    swa_phase(L, need_ctx)
                else:
                    diff_phase(L, need_ctx)
            if do_ffn:
                norm_mod(lambda: M.affn, 24, blocks)
                if do_ffn != "norm":
                    nxt = layers[li + 1] if li + 1 < len(layers) else None
                    can = nxt is not None and len(blocks) == len(BLOCKS) and FFN_ITEMS >= 40
                    ffn_phase(L, blocks, prefetch=nxt if can else None)
                    if can:
                        prefetched.add(nxt)
        final_phase(final_norm)
    nc._declared_inputs = list(declared)
    return nc


def _pcol(v):
    return np.ascontiguousarray(np.asarray(v, np.float32).reshape(-1, 128).T)


def _rot_idx(base):
    return list(range(base + 32, base + 64)) + list(range(base, base + 32))


def _consts():
    i = np.arange(128)
    ident = (i[:, None] == i[None, :]).astype(np.float32)
    maskf = (i[:, None] <= i[None, :]).astype(np.float32)
    maskb = (i[:, None] >= i[None, :]).astype(np.float32)
    return np.ascontiguousarray(np.concatenate([ident, maskf, maskb, -(maskf - 0.5), -(maskb - 0.5)], axis=1))


def _rope_tables():
    rows = NLAT // 64
    row = np.repeat(np.arange(rows), 64).astype(np.float32)
    col = np.tile(np.arange(64), rows).astype(np.float32)
    quarter = 16
    inv = (np.float32(10000.0) ** (-np.arange(quarter, dtype=np.float32) / np.float32(quarter))).astype(np.float32)
    ang = np.concatenate([row[:, None] * inv, col[:, None] * inv], axis=-1).astype(np.float32)
    cos = np.cos(ang).astype(np.float32).T
    sin = np.sin(ang).astype(np.float32).T
    c64 = np.concatenate([cos, cos], 0)
    s64 = np.concatenate([-sin, sin], 0)
    return np.ascontiguousarray(np.concatenate([c64, c64], 0)), np.ascontiguousarray(np.concatenate([s64, s64], 0))


def prepare_shared(inp):
    sh = {}
    sh["cst"] = _consts()
    sh["ropec"], sh["ropes"] = _rope_tables()
    sh["ada_w"] = np.ascontiguousarray(inp["ada_w"], dtype=np.float32)
    sh["ffn_w1"] = np.ascontiguousarray(inp["ffn_w1"], dtype=np.float32)
    sh["ffn_w2"] = np.ascontiguousarray(inp["ffn_w2"], dtype=np.float32)
    wm = []
    for s in range(inp["mlstm_w_in"].shape[0]):
        W = inp["mlstm_w_in"][s]
        cols = []
        for h in range(8):
            cols += list(range(h * 64, h * 64 + 64))
            cols += list(range(512 + h * 64, 512 + h * 64 + 64))
            cols += list(range(512 + h * 64, 512 + h * 64 + 64))
            cols += list(range(1024 + h * 128, 1024 + h * 128 + 128))
            cols += list(range(2048 + h * 128, 2048 + h * 128 + 128))
        cols += list(range(3072, 3104))
        wm.append(W[:, cols])
    sh["wm"] = np.ascontiguousarray(np.stack(wm), dtype=np.float32)
    sh["wmo"] = np.ascontiguousarray(inp["mlstm_w_out"], dtype=np.float32)
    W = inp["swa_w_in"][0]
    cols = []
    for g in range(4):
        q = []
        qr = []
        for hh in range(4):
            b = (4 * g + hh) * 64
            q += list(range(b, b + 64))
            qr += _rot_idx(b)
        kb_ = 1024 + g * 64
        k = list(range(kb_, kb_ + 64))
        kr = _rot_idx(kb_)
        v = list(range(1280 + g * 64, 1280 + g * 64 + 64))
        cols += q + qr + k + k + kr + kr + v
    sh["ws"] = np.ascontiguousarray(W[:, cols][None], dtype=np.float32)
    sh["wso"] = np.ascontiguousarray(inp["swa_w_out"], dtype=np.float32)
    W = inp["diff_w_in"][0]
    cols = []
    for h in range(8):
        qb_ = h * 128
        kb_ = 1024 + h * 128
        cols += list(range(qb_, qb_ + 128)) + _rot_idx(qb_) + _rot_idx(qb_ + 64)
        cols += list(range(kb_, kb_ + 128)) + _rot_idx(kb_) + _rot_idx(kb_ + 64)
        cols += list(range(2048 + h * 128, 2048 + h * 128 + 128))
    sh["wd"] = np.ascontiguousarray(W[:, cols][None], dtype=np.float32)
    sh["wdo"] = np.ascontiguousarray(inp["diff_w_out"], dtype=np.float32)
    small = np.zeros((128, NS), np.float32)
    for L in range(DEPTH):
        small[:, L * SL:L * SL + 8] = _pcol(inp["norm_mix"][L])
        small[:, L * SL + 8:L * SL + 16] = _pcol(inp["norm_ffn"][L])
        small[:, L * SL + 16:L * SL + 64] = _pcol(inp["ada_b"][L])
    small[:, S_FINAL:S_FINAL + 8] = _pcol(inp["final_norm"])
    for s in range(inp["mlstm_head_norm"].shape[0]):
        small[:, S_MHN + s * 8:S_MHN + s * 8 + 8] = _pcol(inp["mlstm_head_norm"][s])
        small[:, S_MGB + s * 32:S_MGB + s * 32 + 32] = np.broadcast_to(np.asarray(inp["mlstm_gate_b"][s], np.float32).reshape(1, 32), (128, 32))
    sink = np.asarray(inp["swa_sink"][0], np.float32)
    sord = [4 * g + PSORD[p] for g in range(4) for p in range(4)]
    small[:, S_SINK:S_SINK + 16] = np.broadcast_to(sink[sord][None, :], (128, 16))
    small[:, S_DHN:S_DHN + 8] = _pcol(inp["diff_head_norm"][0])
    lamv = np.concatenate([np.asarray(inp[k][0], np.float32) for k in ("diff_lambda_q1", "diff_lambda_k1", "diff_lambda_q2", "diff_lambda_k2")])
    small[:, S_DLAM:S_DLAM + 256] = np.broadcast_to(lamv[None, :], (128, 256))
    sh["small"] = small
    return sh


def prepare_core(inp, b):
    xin = np.ascontiguousarray(np.concatenate([inp["ctx"][b], inp["x"][b]], axis=0), dtype=np.float32)
    cv = np.zeros((128, 16), np.float32)
    cv[:, 0::2] = _pcol(inp["c"][b])
    cv[:, 1::2] = _pcol(inp["c_ctx"])
    return {"xin": xin, "cv": cv}


_NC_CACHE = {}


def kernel(**inputs):
    inp = {k: np.asarray(v) for k, v in inputs.items()}
    B = inp["x"].shape[0]
    shared = prepare_shared(inp)
    if "full" not in _NC_CACHE:
        _NC_CACHE["full"] = build_program()
    nc = _NC_CACHE["full"]
    in_maps = []
    for b in range(B):
        m = dict(shared)
        m.update(prepare_core(inp, b))
        in_maps.append({k: m[k] for k in nc._declared_inputs})
    res = run_bass_kernel_spmd(nc, in_maps, core_ids=list(range(B)))
    out = np.stack([np.asarray(r["out"], dtype=np.float32) for r in res.results], axis=0)
    return out
```
